# Optimizing a Trainium2 kernel written in Bass

```python
import jax, jax.numpy as jnp
from jax import lax
import numpy as np

D_MODEL = 1024
BATCH = 4
SEQ = 4096
DEPTH = 1

CHUNK = 64
EPS = 1e-6
D_FF = 2816
S5_WIDTH = D_MODEL
S5_GROUP = 16
S5_GROUPS = S5_WIDTH // S5_GROUP
S5_STATE = 64
D_INNER = 2 * D_MODEL
SSD_HEADDIM = 64
SSD_HEADS = D_INNER // SSD_HEADDIM
SSD_GROUPS = 8
SSD_HPG = SSD_HEADS // SSD_GROUPS
SSD_STATE = 128
CONV_K = 4
CONV_DIM = D_INNER + 2 * SSD_GROUPS * SSD_STATE
N_BRANCH = 2
IN_SPLITS = (S5_WIDTH, D_INNER, CONV_DIM, SSD_HEADS, N_BRANCH * D_MODEL)
D_IN = S5_WIDTH + D_INNER + CONV_DIM + SSD_HEADS + N_BRANCH * D_MODEL

kernel_name = "hybrid_s5_ssd_gated_macaron"


def rmsnorm(x, g):
    xf = x.astype(jnp.float32)
    y = xf * lax.rsqrt(jnp.mean(xf * xf, axis=-1, keepdims=True) + EPS)
    return (y * g.astype(jnp.float32)).astype(x.dtype)


def swiglu(h, w_gate, w_up, w_down):
    return (jax.nn.silu(h @ w_gate) * (h @ w_up)) @ w_down


def s5_branch(u, A_re, A_im, log_dt, B_re, B_im, C_re, C_im, d_skip, w_glu, b_glu):
    f32 = jnp.float32
    bsz, L, _ = u.shape
    uf = u.astype(f32).reshape(bsz, L, S5_GROUPS, S5_GROUP)
    dt = jnp.exp(log_dt.astype(f32))[:, None]
    lr, li = A_re.astype(f32), A_im.astype(f32)
    mag = jnp.exp(lr * dt)
    ar, ai = mag * jnp.cos(li * dt), mag * jnp.sin(li * dt)
    den = lr * lr + li * li
    cr = ((ar - 1.0) * lr + ai * li) / den
    ci = (ai * lr - (ar - 1.0) * li) / den
    Br, Bi = B_re.astype(f32), B_im.astype(f32)
    bbr = cr[..., None] * Br - ci[..., None] * Bi
    bbi = cr[..., None] * Bi + ci[..., None] * Br
    bu_r = jnp.einsum('blgm,gnm->blgn', uf, bbr)
    bu_i = jnp.einsum('blgm,gnm->blgn', uf, bbi)
    a_r = jnp.broadcast_to(ar, bu_r.shape)
    a_i = jnp.broadcast_to(ai, bu_i.shape)

    def combine(e1, e2):
        a1r, a1i, b1r, b1i = e1
        a2r, a2i, b2r, b2i = e2
        return (a2r * a1r - a2i * a1i,
                a2r * a1i + a2i * a1r,
                a2r * b1r - a2i * b1i + b2r,
                a2r * b1i + a2i * b1r + b2i)

    _, _, s_r, s_i = lax.associative_scan(combine, (a_r, a_i, bu_r, bu_i), axis=1)
    y = (jnp.einsum('blgn,gmn->blgm', s_r, C_re.astype(f32))
         - jnp.einsum('blgn,gmn->blgm', s_i, C_im.astype(f32)))
    y = y.reshape(bsz, L, S5_WIDTH) + d_skip.astype(f32) * u.astype(f32)
    g = jax.nn.gelu(y)
    out = g * jax.nn.sigmoid(g @ w_glu.astype(f32) + b_glu.astype(f32))
    return out.astype(u.dtype)


def causal_depthwise_conv(x, w, b):
    y = lax.conv_general_dilated(
        x, w[:, None, :].astype(x.dtype), window_strides=(1,),
        padding=[(CONV_K - 1, 0)], dimension_numbers=('NWC', 'WIO', 'NWC'),
        feature_group_count=CONV_DIM)
    return y + b.astype(x.dtype)


def ssd_branch(z, xbc, dt_raw, conv_w, conv_b, A_log, dt_bias, d_head, norm_w):
    f32 = jnp.float32
    bsz, L, _ = z.shape
    nc = L // CHUNK
    xbc = jax.nn.silu(causal_depthwise_conv(xbc, conv_w, conv_b))
    xs, Bm, Cm = jnp.split(xbc, [D_INNER, D_INNER + SSD_GROUPS * SSD_STATE], axis=-1)
    x = xs.reshape(bsz, nc, CHUNK, SSD_GROUPS, SSD_HPG, SSD_HEADDIM).astype(f32)
    Bm = Bm.reshape(bsz, nc, CHUNK, SSD_GROUPS, SSD_STATE).astype(f32)
    Cm = Cm.reshape(bsz, nc, CHUNK, SSD_GROUPS, SSD_STATE).astype(f32)
    dt = jax.nn.softplus(dt_raw.astype(f32) + dt_bias.astype(f32))
    dt = dt.reshape(bsz, nc, CHUNK, SSD_GROUPS, SSD_HPG)
    A = -jnp.exp(A_log.astype(f32)).reshape(SSD_GROUPS, SSD_HPG)
    a_cum = jnp.cumsum(dt * A, axis=2)
    seg = a_cum[:, :, :, None] - a_cum[:, :, None]
    mask = jnp.tril(jnp.ones((CHUNK, CHUNK), dtype=bool))[:, :, None, None]
    Lmat = jnp.exp(jnp.where(mask, seg, -jnp.inf))
    cb = jnp.einsum('bcign,bcjgn->bcijg', Cm, Bm)
    w = cb[..., None] * Lmat * dt[:, :, None]
    y_diag = jnp.einsum('bcijgk,bcjgkp->bcigkp', w, x)
    decay_to_end = jnp.exp(a_cum[:, :, -1:] - a_cum)
    xw = x * (decay_to_end * dt)[..., None]
    states = jnp.einsum('bclgn,bclgkp->bcgkpn', Bm, xw)
    chunk_decay = jnp.exp(a_cum[:, :, -1])

    def step(h, inp):
        dec, s = inp
        return dec[..., None, None] * h + s, h

    h0 = jnp.zeros((bsz, SSD_GROUPS, SSD_HPG, SSD_HEADDIM, SSD_STATE), f32)
    _, prev = lax.scan(step, h0, (jnp.moveaxis(chunk_decay, 1, 0), jnp.moveaxis(states, 1, 0)))
    prev = jnp.moveaxis(prev, 0, 1)
    y_off = jnp.einsum('bcign,bcgkpn->bcigkp', Cm, prev) * jnp.exp(a_cum)[..., None]
    y = y_diag + y_off + d_head.astype(f32).reshape(SSD_GROUPS, SSD_HPG)[:, :, None] * x
    y = y.reshape(bsz, L, D_INNER).astype(z.dtype)
    return rmsnorm(y * jax.nn.silu(z), norm_w)


def setup_inputs(seed: int = 0) -> dict:
    key = jax.random.key(seed)
    ks = iter(jax.random.split(key, 40))
    f32 = jnp.float32

    def nrm(shape, scale):
        return jax.random.normal(next(ks), shape, f32) * scale

    def gain(shape):
        return 1.0 + nrm(shape, 0.02)

    Dp = DEPTH
    x = jax.random.normal(next(ks), (BATCH, SEQ, D_MODEL), f32)
    inp = {"x": x}
    inp["ffn1_norm"] = gain((Dp, D_MODEL))
    inp["ffn1_w_gate"] = nrm((Dp, D_MODEL, D_FF), D_MODEL ** -0.5)
    inp["ffn1_w_up"] = nrm((Dp, D_MODEL, D_FF), D_MODEL ** -0.5)
    inp["ffn1_w_down"] = nrm((Dp, D_FF, D_MODEL), D_FF ** -0.5)
    inp["mix_norm"] = gain((Dp, D_MODEL))
    inp["w_in"] = nrm((Dp, D_MODEL, D_IN), D_MODEL ** -0.5)
    inp["conv_w"] = nrm((Dp, CONV_K, CONV_DIM), CONV_K ** -0.5)
    inp["conv_b"] = nrm((Dp, CONV_DIM), 0.02)
    inp["s5_A_re"] = -0.5 + nrm((Dp, S5_GROUPS, S5_STATE), 0.01)
    inp["s5_A_im"] = jnp.pi * jnp.arange(S5_STATE, dtype=f32) + nrm((Dp, S5_GROUPS, S5_STATE), 0.01)
    inp["s5_log_dt"] = jax.random.uniform(next(ks), (Dp, S5_GROUPS), f32,
                                          np.log(0.001), np.log(0.1))
    inp["s5_B_re"] = nrm((Dp, S5_GROUPS, S5_STATE, S5_GROUP), (2 * S5_GROUP) ** -0.5)
    inp["s5_B_im"] = nrm((Dp, S5_GROUPS, S5_STATE, S5_GROUP), (2 * S5_GROUP) ** -0.5)
    inp["s5_C_re"] = nrm((Dp, S5_GROUPS, S5_GROUP, S5_STATE), S5_STATE ** -0.5)
    inp["s5_C_im"] = nrm((Dp, S5_GROUPS, S5_GROUP, S5_STATE), S5_STATE ** -0.5)
    inp["s5_D"] = nrm((Dp, S5_WIDTH), 1.0)
    inp["s5_w_glu"] = nrm((Dp, S5_WIDTH, S5_WIDTH), S5_WIDTH ** -0.5)
    inp["s5_b_glu"] = nrm((Dp, S5_WIDTH), 0.02)
    inp["ssd_A_log"] = jnp.log(jax.random.uniform(next(ks), (Dp, SSD_HEADS), f32, 1.0, 16.0))
    dt0 = jnp.exp(jax.random.uniform(next(ks), (Dp, SSD_HEADS), f32, np.log(0.001), np.log(0.1)))
    inp["ssd_dt_bias"] = dt0 + jnp.log(-jnp.expm1(-dt0))
    inp["ssd_D"] = gain((Dp, SSD_HEADS))
    inp["ssd_norm"] = gain((Dp, D_INNER))
    inp["w_proj_s5"] = nrm((Dp, S5_WIDTH, D_MODEL), S5_WIDTH ** -0.5)
    inp["w_proj_ssd"] = nrm((Dp, D_INNER, D_MODEL), D_INNER ** -0.5)
    inp["b_gate"] = nrm((Dp, N_BRANCH * D_MODEL), 0.02)
    inp["w_out"] = nrm((Dp, D_MODEL, D_MODEL), D_MODEL ** -0.5)
    inp["ffn2_norm"] = gain((Dp, D_MODEL))
    inp["ffn2_w_gate"] = nrm((Dp, D_MODEL, D_FF), D_MODEL ** -0.5)
    inp["ffn2_w_up"] = nrm((Dp, D_MODEL, D_FF), D_MODEL ** -0.5)
    inp["ffn2_w_down"] = nrm((Dp, D_FF, D_MODEL), D_FF ** -0.5)
    inp["final_norm"] = gain((D_MODEL,))
    return inp


def reference(x, ffn1_norm, ffn1_w_gate, ffn1_w_up, ffn1_w_down, mix_norm, w_in,
              conv_w, conv_b, s5_A_re, s5_A_im, s5_log_dt, s5_B_re, s5_B_im,
              s5_C_re, s5_C_im, s5_D, s5_w_glu, s5_b_glu, ssd_A_log, ssd_dt_bias,
              ssd_D, ssd_norm, w_proj_s5, w_proj_ssd, b_gate, w_out,
              ffn2_norm, ffn2_w_gate, ffn2_w_up, ffn2_w_down, final_norm):
    split_at = [sum(IN_SPLITS[:i + 1]) for i in range(len(IN_SPLITS) - 1)]
    for i in range(DEPTH):
        x = x + 0.5 * swiglu(rmsnorm(x, ffn1_norm[i]), ffn1_w_gate[i], ffn1_w_up[i], ffn1_w_down[i])
        h = rmsnorm(x, mix_norm[i])
        proj = h @ w_in[i]
        u_s5, z, xbc, dt_raw, gate_logits = jnp.split(proj, split_at, axis=-1)
        y_s5 = s5_branch(u_s5, s5_A_re[i], s5_A_im[i], s5_log_dt[i], s5_B_re[i], s5_B_im[i],
                         s5_C_re[i], s5_C_im[i], s5_D[i], s5_w_glu[i], s5_b_glu[i])
        y_ssd = ssd_branch(z, xbc, dt_raw, conv_w[i], conv_b[i], ssd_A_log[i], ssd_dt_bias[i],
                           ssd_D[i], ssd_norm[i])
        gates = jax.nn.sigmoid(gate_logits + b_gate[i])
        g_s5, g_ssd = jnp.split(gates, 2, axis=-1)
        merged = g_s5 * (y_s5 @ w_proj_s5[i]) + g_ssd * (y_ssd @ w_proj_ssd[i])
        x = x + merged @ w_out[i]
        x = x + 0.5 * swiglu(rmsnorm(x, ffn2_norm[i]), ffn2_w_gate[i], ffn2_w_up[i], ffn2_w_down[i])
    return rmsnorm(x, final_norm)
```

```python
import numpy as np
import ml_dtypes
from contextlib import ExitStack
import concourse.bass as bass
import concourse.mybir as mybir
from concourse.bass_utils import run_bass_kernel_spmd

F32 = mybir.dt.float32
BF16 = mybir.dt.bfloat16
I32 = mybir.dt.int32
U8 = mybir.dt.uint8
ALU = mybir.AluOpType
AF = mybir.ActivationFunctionType
DTSZ = {F32: 4, BF16: 2, I32: 4, U8: 1}

D = 1024
DFF = 2816
DIN = 9248
TT = 512
NT_PRE = 4
NT_OWN = 4
EPS = 1e-6
ARENA = 206 * 1024
LIGHT = True
DEBUG = False


def _prod(s):
    r = 1
    for v in s:
        r *= v
    return r


class Buf:
    def __init__(self, ap, key, space, lo, hi, shape, dt):
        self.ap, self.key, self.space, self.lo, self.hi, self.shape, self.dt = ap, key, space, lo, hi, shape, dt
        self.sub_bytes = (_prod(shape[2:]) * DTSZ[dt]) if len(shape) > 2 else None

    def __getitem__(self, idx):
        return self.ap[idx]


class Prog:
    ENG = ("pe", "act", "dve", "pool", "sp")
    EPOCH = 12000

    def __init__(self, nc, es):
        self.nc, self.es = nc, es
        self.ops = {e: [] for e in self.ENG}
        self.count = {e: 0 for e in self.ENG}
        self.esems = {e: [] for e in self.ENG}
        self.dsems = {}
        self.lastw = {}
        self.readers = {}
        self.spaces = {}
        self.waited = {e: {} for e in self.ENG}
        self.final = {}
        self.arena = es.enter_context(nc.sbuf_tensor("arena", [128, ARENA], U8))
        self.top = 0
        self.nps = 0

    def view(self, name, off, shape, dt):
        nb = _prod(shape[1:]) * DTSZ[dt]
        assert off + nb <= ARENA, (name, off, nb)
        ap = self.arena[0:shape[0], off:off + nb].bitcast(dt)
        if len(shape) == 3:
            ap = ap.rearrange("p (a b) -> p a b", a=shape[1])
        elif len(shape) == 4:
            ap = ap.rearrange("p (a b c) -> p a b c", a=shape[1], b=shape[2])
        return Buf(ap, "%s@%d" % (name, off), "arena", off, off + nb, shape, dt)

    def alloc(self, name, shape, dt):
        b = self.view(name, self.top, shape, dt)
        self.top = (b.hi + 31) // 32 * 32
        return b

    def psum(self, name):
        t = self.es.enter_context(self.nc.psum_tensor(name, [128, 512], F32))
        return Buf(t[:, :], name, name, 0, 2048, [128, 512], F32)

    def sem(self, name):
        return self.es.enter_context(self.nc.semaphore(name))

    def _norm(self, it):
        if isinstance(it, Buf):
            return (it.key, it.space, it.lo, it.hi)
        if isinstance(it, tuple):
            b = it[0]
            i0 = it[1]
            i1 = it[2] if len(it) > 2 else i0 + 1
            return ((b.key, i0, i1), b.space, b.lo + i0 * b.sub_bytes, b.lo + i1 * b.sub_bytes)
        return (it, it, 0, 1)

    def _related(self, n):
        key, space, lo, hi = n
        sp = self.spaces.setdefault(space, {})
        if key not in sp:
            sp[key] = (lo, hi)
        return [k for k, (l, h) in sp.items() if l < hi and lo < h]

    def op(self, eng, fn, r=(), w=(), dma=None):
        rn = [self._norm(x) for x in r]
        wn = [self._norm(x) for x in w]
        deps = []
        for n in rn:
            for k in self._related(n):
                if k in self.lastw:
                    deps.append(self.lastw[k])
        for n in wn:
            for k in self._related(n):
                if k in self.lastw:
                    deps.append(self.lastw[k])
                deps.extend(self.readers.get(k, {}).values())
        waits = []
        for (sem, val, seng) in deps:
            if seng == "pe" and eng == "pe" and dma is None:
                continue
            cur = self.waited[eng].get(id(sem), 0)
            if val > cur:
                self.waited[eng][id(sem)] = val
                waits.append((sem, val))
        if dma is not None:
            if dma not in self.dsems:
                self.dsems[dma] = [self.sem("d_" + str(dma)), 0]
            ds = self.dsems[dma]
            ds[1] += 16
            tok = (ds[0], ds[1], "dma")
            inc = (ds[0], 16)
        else:
            c = self.count[eng]
            ep = c // self.EPOCH
            while len(self.esems[eng]) <= ep:
                self.esems[eng].append(self.sem("e_%s_%d" % (eng, len(self.esems[eng]))))
            s = self.esems[eng][ep]
            self.count[eng] = c + 1
            tok = (s, c - ep * self.EPOCH + 1, eng)
            inc = (s, 1)
        self.ops[eng].append((waits, fn, inc))
        for n in wn:
            self.lastw[n[0]] = tok
            self.readers[n[0]] = {}
        for n in rn:
            self.readers.setdefault(n[0], {})[id(tok[0])] = tok
        return tok

    def emit(self):
        with self.nc.Block() as block:
            def mk(engname):
                def body(e):
                    for waits, fn, inc in self.ops[engname]:
                        for (s, v) in waits:
                            e.wait_ge(s, v)
                        fn(e).then_inc(inc[0], inc[1])
                    if engname == "sp":
                        for (s, v) in self.final.values():
                            e.wait_ge(s, v)
                return body
            block.tensor(mk("pe"))
            block.scalar(mk("act"))
            block.vector(mk("dve"))
            block.gpsimd(mk("pool"))
            block.sync(mk("sp"))


WSHAPES = {"ffn1_w_gate": (D, DFF), "ffn1_w_up": (D, DFF), "ffn1_w_down": (DFF, D),
           "ffn2_w_gate": (D, DFF), "ffn2_w_up": (D, DFF), "ffn2_w_down": (DFF, D),
           "w_in": (D, DIN), "s5_w_glu": (D, D), "w_proj_s5": (D, D), "w_proj_ssd": (2 * D, D),
           "w_out": (D, D)}
PSHAPES = {"pc8": [128, 6, 8], "pc16": [128, 3, 16], "convp": [128, 32, 5], "hp": [32, 2],
           "s5a": [128, 3, 64], "s5b": [128, 2, 64, 16], "s5c": [128, 2, 64, 16],
           "cst": [128, 5, 128], "cstv": [128, 4]}


def build_program():
    nc = bass.Bass("TRN2", target_bir_lowering=False)
    NTOK = (NT_PRE + NT_OWN) * TT
    din = {}
    for n, s in WSHAPES.items():
        din[n] = nc.dram_tensor(n, list(s), F32, kind="ExternalInput").ap()
    for n, s in PSHAPES.items():
        din[n] = nc.dram_tensor(n, list(s), F32, kind="ExternalInput").ap()
    xs = nc.dram_tensor("xs", [NTOK, D], F32, kind="ExternalInput").ap()
    out = nc.dram_tensor("out", [NT_OWN * TT, D], F32, kind="ExternalOutput").ap()
    wbf = {n: nc.dram_tensor(n + "_bf", list(s), BF16, kind="Internal").ap() for n, s in WSHAPES.items()}
    s5B = nc.dram_tensor("s5B_tab", [8, 128, 2, 8, 128], BF16, kind="Internal").ap()
    s5K = nc.dram_tensor("s5K_tab", [8, 128, 2, 8, 128], BF16, kind="Internal").ap()
    dbg = {}
    if DEBUG:
        for n, c in (("dbg_s5", 8), ("dbg_ssd", 16), ("dbg_m", 8), ("dbg_x1", 8), ("dbg_xs", 16), ("dbg_B", 8), ("dbg_C", 8), ("dbg_y", 16), ("dbg_dt", 1)):
            dbg[n] = nc.dram_tensor(n, [128, c, TT], F32, kind="ExternalOutput").ap()

    with ExitStack() as es:
        P = Prog(nc, es)
        psb = [P.psum("ps%d" % i) for i in range(8)]
        NROT = 7

        def next_ps():
            P.nps = (P.nps + 1) % NROT
            return psb[P.nps]

        def psbf(ps):
            return ps.ap.bitcast(BF16)

        cst = P.alloc("cst", [128, 5, 128], F32)
        IDN, MU, M1, ONE, KMASK = 0, 1, 2, 3, 4
        cstv = P.alloc("cstv", [128, 4], F32)
        SGN, SELT, SELB, FLAG = 0, 1, 2, 3
        pc8 = P.alloc("pc8", [128, 6, 8], F32)
        G_FFN1, G_MIX, G_FFN2, G_FIN, V_S5D, V_BGLU = range(6)
        pc16 = P.alloc("pc16", [128, 3, 16], F32)
        V_SSDN, V_BGATE, V_SSDD = range(3)
        convp = P.alloc("convp", [128, 32, 5], F32)
        hp = P.alloc("hp", [32, 2], F32)
        aneg = P.alloc("aneg", [32, 1], F32)
        identb = P.alloc("identb", [128, 128], BF16)
        onesb = P.alloc("onesb", [128, 128], BF16)
        mub = P.alloc("mub", [128, 128], BF16)
        m1b = P.alloc("m1b", [128, 128], BF16)
        hSb = P.alloc("hSb", [128, 32, 64], BF16)
        a8r = P.alloc("a8r", [128, 64], F32)
        a8s = P.alloc("a8s", [128, 64], F32)
        a8n = P.alloc("a8n", [128, 64], F32)
        Sst = [P.alloc("Sst%d" % i, [128, 64], F32) for i in range(2)]
        Ssw = [P.alloc("Ssw%d" % i, [128, 64], F32) for i in range(2)]
        hS = P.alloc("hS", [128, 32, 64], F32)
        tail = P.alloc("tail", [128, 32, 3], BF16)
        xres = P.alloc("xres", [128, 8, TT], F32)
        hT = P.alloc("hT", [128, 8, TT], BF16)
        rstd = P.alloc("rstd", [128, TT], F32)
        gsb = [P.alloc("gsb%d" % i, [128, TT], F32) for i in range(2)]
        NPAN = 3
        pans = [P.alloc("pan%d" % i, [128, 4096], BF16) for i in range(NPAN)]
        s5tab = P.alloc("s5tab", [128, 2, 8, 128], BF16)
        xin = P.alloc("xin", [128, D], F32)
        osb = [P.alloc("osb%d" % i, [128, D], F32) for i in range(1)]
        BASE = P.top
        st = {"pan": 0, "g": 0, "o": 0, "s": 0}
        S5T_OFF = ARENA - 8 * TT * 2
        MRG_OFF = BASE + 16 * TT * 2 + 64

        class Phase:
            def __init__(self, base=BASE):
                self.top = base

            def alloc(self, name, shape, dt):
                b = P.view(name, self.top, shape, dt)
                self.top = (b.hi + 31) // 32 * 32
                return b

        def load_params(items, semkey):
            for b, src in items:
                P.op("sp", lambda e, b=b, src=src: e.dma_start(out=b.ap, in_=src), w=[b], dma=semkey)
            sem, total = P.dsems[semkey]
            for b, _ in items:
                P.lastw[b.key] = (sem, total, "dma")

        load_params([(cst, din["cst"]), (cstv, din["cstv"]), (pc8, din["pc8"]), (pc16, din["pc16"]),
                     (convp, din["convp"]), (hp, din["hp"])], "par")
        P.op("dve", lambda e: e.tensor_copy(out=identb.ap, in_=cst[:, IDN, :]), r=[cst], w=[identb])
        P.op("pool", lambda e: e.memset(onesb.ap, 1.0), w=[onesb])
        P.op("dve", lambda e: e.tensor_copy(out=mub.ap, in_=cst[:, MU, :]), r=[cst], w=[mub])
        P.op("dve", lambda e: e.tensor_copy(out=m1b.ap, in_=cst[:, M1, :]), r=[cst], w=[m1b])
        P.op("pool", lambda e: e.memset(hSb.ap, 0.0), w=[hSb])
        P.op("pool", lambda e: e.memset(hS.ap, 0.0), w=[hS])
        P.op("pool", lambda e: e.memset(tail.ap, 0.0), w=[tail])
        P.op("pool", lambda e: e.memset(Sst[0].ap, 0.0), w=[Sst[0]])
        P.op("pool", lambda e: e.memset(Ssw[0].ap, 0.0), w=[Ssw[0]])
        P.op("act", lambda e: e.activation(out=aneg.ap, in_=hp[:, 0:1], func=AF.Exp), r=[hp], w=[aneg])
        P.op("dve", lambda e: e.tensor_scalar(out=aneg.ap, in0=aneg.ap, scalar1=-1.0, scalar2=None, op0=ALU.mult),
             r=[aneg], w=[aneg])

        import os as _os2
        for n, (k, m) in (WSHAPES.items() if int(_os2.environ.get("KSTOP", "9")) >= 0 else []):
            rows = 256
            for r0 in range(0, k, rows):
                rr = min(rows, k - r0)
                P.op("pool", lambda e, n=n, r0=r0, rr=rr: e.dma_start(
                    out=wbf[n][r0:r0 + rr, :], in_=din[n][r0:r0 + rr, :]), w=[n + "_bf"], dma="cast_" + n)

        def get_pan():
            p = pans[st["pan"] % NPAN]
            st["pan"] += 1
            return p

        def proj(wname, col0, ncols, KC, rhs, consume, pw=None):
            W = wbf[wname]
            if pw is None:
                pw = min(512, (4096 // KC) // 128 * 128)
            j = 0
            for p0 in range(0, ncols, pw):
                w_ = min(pw, ncols - p0)
                pan = get_pan()
                pv = pan.ap[:, 0:KC * w_].rearrange("p (c n) -> p c n", c=KC)
                P.op("sp", lambda e, pv=pv, p0=p0, w_=w_: e.dma_start(
                    out=pv, in_=W[:, col0 + p0:col0 + p0 + w_].rearrange("(c p) n -> p c n", p=128)),
                    r=[wname + "_bf"], w=[pan], dma=pan.key)
                for q0 in range(0, w_, 128):
                    cw = min(128, w_ - q0)
                    ps = next_ps()
                    for c in range(KC):
                        rap, rdep = rhs(c)
                        P.op("pe", lambda e, pv=pv, ps=ps, c=c, q0=q0, cw=cw, rap=rap: e.matmul(
                            ps[0:cw, :], lhsT=pv[:, c, q0:q0 + cw], rhs=rap, start=(c == 0), stop=(c == KC - 1)),
                            r=[pan, rdep], w=[ps])
                    consume(j, ps, cw)
                    j += 1

        def rstd_from(ps, scale):
            P.op("dve", lambda e, ps=ps: e.tensor_scalar(out=rstd.ap, in0=ps[:, :], scalar1=scale, scalar2=EPS,
                                                         op0=ALU.mult, op1=ALU.add), r=[ps], w=[rstd])
            P.op("act", lambda e: e.activation(out=rstd.ap, in_=rstd.ap, func=AF.Ln), r=[rstd], w=[rstd])
            P.op("act", lambda e: e.activation(out=rstd.ap, in_=rstd.ap, func=AF.Exp, scale=-0.5), r=[rstd], w=[rstd])

        def rmsnorm_to(dst, gcol, sqbuf):
            for c in range(8):
                P.op("act", lambda e, c=c: e.activation(out=sqbuf[:, c, :], in_=xres[:, c, :], func=AF.Square),
                     r=[(xres, c)], w=[(sqbuf, c)])
            ps = next_ps()
            for c in range(8):
                P.op("pe", lambda e, c=c, ps=ps: e.matmul(ps[:, :], lhsT=onesb.ap, rhs=sqbuf[:, c, :],
                                                            start=(c == 0), stop=(c == 7)),
                     r=[onesb, (sqbuf, c)], w=[ps])
            rstd_from(ps, 1.0 / D)
            for c in range(8):
                P.op("dve", lambda e, c=c: e.scalar_tensor_tensor(
                    out=dst[:, c, :], in0=xres[:, c, :], scalar=pc8[:, gcol, c:c + 1], in1=rstd.ap,
                    op0=ALU.mult, op1=ALU.mult), r=[(xres, c), pc8, rstd], w=[(dst, c)])

        def load_x(tile):
            for s in range(4):
                src = xs[tile * TT + s * 128: tile * TT + (s + 1) * 128, :]
                P.op("sp", lambda e, src=src: e.dma_start(out=xin.ap, in_=src), w=[xin], dma="xin")
                for c0 in range(0, 8, 4):
                    ps = next_ps()
                    for c in range(c0, c0 + 4):
                        P.op("pe", lambda e, c=c, c0=c0, ps=ps: e.transpose(
                            out=ps[:, (c - c0) * 128:(c - c0 + 1) * 128], in_=xin[:, c * 128:(c + 1) * 128],
                            identity=cst[:, IDN, :]), r=[xin, cst], w=[ps])
                    P.op("act", lambda e, c0=c0, s=s, ps=ps: e.activation(
                        out=xres[:, c0:c0 + 4, s * 128:(s + 1) * 128],
                        in_=ps[:, :].rearrange("p (c t) -> p c t", c=4), func=AF.Copy),
                        r=[ps], w=[(xres, c0, c0 + 4)])

        def dump(name, buf, nch):
            if not DEBUG:
                return
            for c in range(nch):
                g = gsb[st["g"] % 2]
                st["g"] += 1
                P.op("dve", lambda e, g=g, c=c: e.tensor_copy(out=g.ap, in_=buf[:, c, :]), r=[(buf, c)], w=[g])
                tok = P.op("sp", lambda e, g=g, c=c: e.dma_start(out=dbg[name][:, c, :], in_=g.ap),
                           r=[g], w=[name], dma=g.key)
                P.final[g.key] = (tok[0], tok[1])

        def ffn(pref, gcol):
            ph = Phase()
            act = ph.alloc("act", [128, 22, TT], BF16)
            sqb = ph.alloc("sqb", [128, 8, TT], BF16)
            rmsnorm_to(hT, gcol, sqb)
            for p0 in range(0, DFF, 512):
                pw = min(512, DFF - p0)
                pp = []
                for wn in (pref + "_w_gate", pref + "_w_up"):
                    pan = get_pan()
                    pv = pan.ap[:, 0:8 * pw].rearrange("p (c n) -> p c n", c=8)
                    P.op("sp", lambda e, pv=pv, wn=wn, p0=p0, pw=pw: e.dma_start(
                        out=pv, in_=wbf[wn][:, p0:p0 + pw].rearrange("(c p) n -> p c n", p=128)),
                        r=[wn + "_bf"], w=[pan], dma=pan.key)
                    pp.append((pan, pv))
                for j in range(pw // 128):
                    oc = (p0 // 128) + j
                    psg, psu = next_ps(), next_ps()
                    for (pan, pv), ps in ((pp[0], psg), (pp[1], psu)):
                        for c in range(8):
                            P.op("pe", lambda e, pv=pv, ps=ps, c=c, j=j: e.matmul(
                                ps[:, :], lhsT=pv[:, c, j * 128:(j + 1) * 128], rhs=hT[:, c, :],
                                start=(c == 0), stop=(c == 7)), r=[pan, (hT, c)], w=[ps])
                    g = gsb[st["g"] % 2]
                    st["g"] += 1
                    P.op("act", lambda e, g=g, psg=psg: e.activation(out=g.ap, in_=psg[:, :], func=AF.Silu),
                         r=[psg], w=[g])
                    P.op("dve", lambda e, g=g, psu=psu, oc=oc: e.tensor_tensor(
                        out=act[:, oc, :], in0=psu[:, :], in1=g.ap, op=ALU.mult), r=[psu, g], w=[(act, oc)])

            def fin(j, ps, cw):
                P.op("dve", lambda e, ps=ps, j=j: e.scalar_tensor_tensor(
                    out=xres[:, j, :], in0=ps[:, :], scalar=0.5, in1=xres[:, j, :],
                    op0=ALU.mult, op1=ALU.add), r=[ps, (xres, j)], w=[(xres, j)])
            proj(pref + "_w_down", 0, D, 22, lambda c: (act[:, c, :], (act, c)), fin, pw=128)

        def s5_setup():
            ph = Phase()
            A = ph.alloc("s5a", [128, 3, 64], F32)
            Bi = ph.alloc("s5b", [128, 2, 64, 16], F32)
            Ci = ph.alloc("s5c", [128, 2, 64, 16], F32)
            load_params([(A, din["s5a"]), (Bi, din["s5b"]), (Ci, din["s5c"])], "par2")
            T = {}
            TMP = Buf(None, "s5tmp", "arena", BASE, ARENA, [128, ARENA - BASE], U8)
            RD = [A, Bi, Ci, TMP, cstv]

            def t(name):
                if name not in T:
                    T[name] = ph.alloc("t_" + name, [128, 64], F32).ap
                return T[name]

            def tt(o, a, b, op):
                P.op("dve", lambda e: e.tensor_tensor(out=o, in0=a, in1=b, op=op), r=RD, w=[TMP])

            def ts(o, a, s1, s2, op0, op1=None):
                if op1 is None:
                    P.op("dve", lambda e: e.tensor_scalar(out=o, in0=a, scalar1=s1, scalar2=None, op0=op0), r=RD, w=[TMP])
                else:
                    P.op("dve", lambda e: e.tensor_scalar(out=o, in0=a, scalar1=s1, scalar2=s2, op0=op0, op1=op1),
                         r=RD, w=[TMP])

            def stt(o, a, s, b, op0, op1):
                P.op("dve", lambda e: e.scalar_tensor_tensor(out=o, in0=a, scalar=s, in1=b, op0=op0, op1=op1),
                     r=RD, w=[TMP])

            def ac(o, a, func, scale=1.0):
                P.op("act", lambda e: e.activation(out=o, in_=a, func=func, scale=scale), r=RD, w=[TMP])

            def rcp(o, a):
                P.op("dve", lambda e: e.reciprocal(out=o, in_=a), r=RD, w=[TMP])

            import os as _os
            SL = int(_os.environ.get("S5STOP", "99"))
            lr, li, ldt = A[:, 0, :], A[:, 1, :], A[:, 2, :]
            ac(t("dt"), ldt, AF.Exp)
            tt(t("lrdt"), lr, t("dt"), ALU.mult)
            tt(t("th"), li, t("dt"), ALU.mult)
            ac(t("mag"), t("lrdt"), AF.Exp)
            ts(t("v"), t("th"), 1.0 / (2 * np.pi), None, ALU.mult)
            if SL <= 1:
                return
            vi = ph.alloc("t_vi", [128, 64], I32).ap
            P.op("dve", lambda e: e.tensor_copy(out=vi, in_=t("v")), r=RD, w=[TMP])
            P.op("dve", lambda e: e.tensor_copy(out=t("vf"), in_=vi), r=RD, w=[TMP])
            tt(t("fr"), t("v"), t("vf"), ALU.subtract)
            ts(t("g"), t("fr"), 0.5, None, ALU.is_gt)
            tt(t("fr"), t("fr"), t("g"), ALU.subtract)
            ts(t("g"), t("fr"), -0.5, None, ALU.is_lt)
            tt(t("fr"), t("fr"), t("g"), ALU.add)
            if SL <= 2:
                return
            ac(t("sin"), t("fr"), AF.Sin, scale=6.28318)
            ts(t("frc"), t("fr"), 0.25, None, ALU.add)
            ts(t("g"), t("frc"), 0.5, None, ALU.is_gt)
            tt(t("frc"), t("frc"), t("g"), ALU.subtract)
            ac(t("cos"), t("frc"), AF.Sin, scale=6.28318)
            if SL <= 3:
                return
            ar, ai = t("ar"), t("ai")
            tt(ar, t("mag"), t("cos"), ALU.mult)
            tt(ai, t("mag"), t("sin"), ALU.mult)
            tt(t("den"), lr, lr, ALU.mult)
            tt(t("d2"), li, li, ALU.mult)
            tt(t("den"), t("den"), t("d2"), ALU.add)
            rcp(t("rden"), t("den"))
            ts(t("am1"), ar, -1.0, None, ALU.add)
            tt(t("x1"), t("am1"), lr, ALU.mult)
            tt(t("x2"), ai, li, ALU.mult)
            tt(t("x1"), t("x1"), t("x2"), ALU.add)
            tt(t("cr"), t("x1"), t("rden"), ALU.mult)
            tt(t("x1"), ai, lr, ALU.mult)
            tt(t("x2"), t("am1"), li, ALU.mult)
            tt(t("x1"), t("x1"), t("x2"), ALU.subtract)
            tt(t("ci"), t("x1"), t("rden"), ALU.mult)
            pr = ph.alloc("t_pr", [128, 9, 64], F32)
            pi = ph.alloc("t_pi", [128, 9, 64], F32)
            P.op("dve", lambda e: e.memset(pr[:, 0, :], 1.0), r=RD, w=[TMP])
            P.op("dve", lambda e: e.memset(pi[:, 0, :], 0.0), r=RD, w=[TMP])

            def cmul(orr, oi, xr, xi, yr, yi):
                tt(t("m1"), xr, yr, ALU.mult)
                tt(t("m2"), xi, yi, ALU.mult)
                tt(t("m3"), xr, yi, ALU.mult)
                tt(t("m4"), xi, yr, ALU.mult)
                tt(orr, t("m1"), t("m2"), ALU.subtract)
                tt(oi, t("m3"), t("m4"), ALU.add)
            for k in range(8):
                cmul(pr[:, k + 1, :], pi[:, k + 1, :], pr[:, k, :], pi[:, k, :], ar, ai)
            tt(t("m1"), pr[:, 8, :], pr[:, 8, :], ALU.mult)
            tt(t("m2"), pi[:, 8, :], pi[:, 8, :], ALU.mult)
            tt(t("m1"), t("m1"), t("m2"), ALU.add)
            rcp(t("rm"), t("m1"))
            tt(t("ir"), pr[:, 8, :], t("rm"), ALU.mult)
            tt(t("ii"), pi[:, 8, :], t("rm"), ALU.mult)
            ts(t("ii"), t("ii"), -1.0, None, ALU.mult)
            P.op("dve", lambda e: e.tensor_copy(out=a8r.ap, in_=pr[:, 8, :]), r=RD, w=[a8r])
            P.op("dve", lambda e: e.tensor_scalar(out=a8s.ap, in0=pi[:, 8, :], scalar1=cstv[:, SGN:SGN + 1], scalar2=None,
                                                  op0=ALU.mult), r=RD, w=[a8s])
            P.op("dve", lambda e: e.tensor_scalar(out=a8n.ap, in0=a8s.ap, scalar1=-1.0, scalar2=None, op0=ALU.mult),
                 r=[a8s], w=[a8n])
            if SL <= 4:
                return
            big = lambda name: ph.alloc(name, [128, 64, 16], F32).ap
            bbr, bbi, BB, BBs, CC1, CC2, w1, w2 = [big(n) for n in ("bbr", "bbi", "BB", "BBs", "CC1", "CC2", "w1", "w2")]

            def bc(ap2d):
                return ap2d.unsqueeze(2).to_broadcast([128, 64, 16])
            Br, Bim, Cr, Cim = Bi[:, 0, :, :], Bi[:, 1, :, :], Ci[:, 0, :, :], Ci[:, 1, :, :]
            crb, cib = bc(t("cr")), bc(t("ci"))
            tt(w1, Br, crb, ALU.mult); tt(w2, Bim, cib, ALU.mult); tt(bbr, w1, w2, ALU.subtract)
            tt(w1, Bim, crb, ALU.mult); tt(w2, Br, cib, ALU.mult); tt(bbi, w1, w2, ALU.add)
            sT, sB = cstv[:, SELT:SELT + 1], cstv[:, SELB:SELB + 1]
            ts(BB, bbr, sT, None, ALU.mult); stt(BB, bbi, sB, BB, ALU.mult, ALU.add)
            ts(BBs, bbi, sT, -1.0, ALU.mult, ALU.mult); stt(BBs, bbr, sB, BBs, ALU.mult, ALU.add)
            ts(CC1, Cim, sB, -1.0, ALU.mult, ALU.mult); stt(CC1, Cr, sT, CC1, ALU.mult, ALU.add)
            ts(CC2, Cim, sT, -1.0, ALU.mult, ALU.mult); ts(w1, Cr, sB, -1.0, ALU.mult, ALU.mult)
            tt(CC2, CC2, w1, ALU.add)
            if SL <= 5:
                return
            NH = 16
            Bpw = ph.alloc("Bpw", [128, NH, 8, 16], F32)
            BJ = ph.alloc("BJ", [128, NH, 8, 16], F32)
            Cpw = ph.alloc("Cpw", [128, NH, 8, 16], F32)
            stB = ph.alloc("stB", [128, 2, 8, 128], BF16)
            stK = ph.alloc("stK", [128, 2, 8, 128], BF16)
            BJh = ph.alloc("BJh", [128, NH, 8, 16], BF16)
            BJl = ph.alloc("BJl", [128, NH, 8, 16], BF16)
            Cph = ph.alloc("Cph", [128, NH, 8, 16], BF16)
            Cpl = ph.alloc("Cpl", [128, NH, 8, 16], BF16)
            fl = lambda b, g: b[:, g, :, :].rearrange("p a b -> p (a b)")

            def bch(ap2d, gs):
                return ap2d[:, gs].unsqueeze(2).to_broadcast([128, NH, 16])
            for hf in range(64 // NH):
                gs = slice(hf * NH, (hf + 1) * NH)
                w1h, w2h = w1[:, 0:NH, :], w2[:, 0:NH, :]
                for j in range(8):
                    zr, zi = bch(pr[:, 7 - j, :], gs), bch(pi[:, 7 - j, :], gs)
                    tt(w1h, BB[:, gs, :], zr, ALU.mult); tt(w2h, BBs[:, gs, :], zi, ALU.mult)
                    tt(Bpw[:, :, j, :], w1h, w2h, ALU.add)
                    cmul(t("qr"), t("qi"), pr[:, 7 - j, :], pi[:, 7 - j, :], t("ir"), t("ii"))
                    zr, zi = bch(t("qr"), gs), bch(t("qi"), gs)
                    tt(w1h, BB[:, gs, :], zr, ALU.mult); tt(w2h, BBs[:, gs, :], zi, ALU.mult)
                    tt(BJ[:, :, j, :], w1h, w2h, ALU.add)
                    zr, zi = bch(pr[:, j + 1, :], gs), bch(pi[:, j + 1, :], gs)
                    tt(w1h, CC1[:, gs, :], zr, ALU.mult); tt(w2h, CC2[:, gs, :], zi, ALU.mult)
                    tt(Cpw[:, :, j, :], w1h, w2h, ALU.add)
                if SL <= 6:
                    continue
                for src_, hi_, lo_ in ((BJ, BJh, BJl), (Cpw, Cph, Cpl)):
                    P.op("dve", lambda e, src_=src_, hi_=hi_: e.tensor_copy(out=hi_.ap, in_=src_.ap), r=RD, w=[TMP])
                    P.op("dve", lambda e, src_=src_, hi_=hi_: e.tensor_tensor(out=src_.ap, in0=src_.ap, in1=hi_.ap,
                                                                           op=ALU.subtract), r=RD, w=[TMP])
                    P.op("dve", lambda e, src_=src_, lo_=lo_: e.tensor_copy(out=lo_.ap, in_=src_.ap), r=RD, w=[TMP])
                for q in range(hf * NH // 8, (hf + 1) * NH // 8):
                    for gl in range(8):
                        g = 8 * q + gl - hf * NH
                        ps = next_ps()
                        psk = next_ps()
                        P.op("pe", lambda e, g=g, ps=ps: e.transpose(out=ps[:, 0:128], in_=fl(Bpw, g), identity=cst[:, IDN, :]),
                             r=[TMP, cst], w=[ps])
                        for ii, (l_, r_) in enumerate(((BJh, Cph), (BJh, Cpl), (BJl, Cph))):
                            P.op("pe", lambda e, g=g, psk=psk, ii=ii, l_=l_, r_=r_: e.matmul(
                                psk[:, 0:128], lhsT=fl(l_, g), rhs=fl(r_, g), start=(ii == 0), stop=(ii == 2)),
                                r=[TMP], w=[psk])
                        P.op("act", lambda e, gl=gl, ps=ps: e.activation(out=stB[:, 0, gl, :], in_=ps[:, 0:128], func=AF.Copy),
                             r=[ps], w=[stB])
                        P.op("act", lambda e, gl=gl, ps=ps: e.activation(out=stB[:, 1, gl, 0:64], in_=ps[:, 64:128], func=AF.Copy),
                             r=[ps], w=[stB])
                        P.op("act", lambda e, gl=gl, ps=ps: e.activation(out=stB[:, 1, gl, 64:128], in_=ps[:, 0:64], func=AF.Copy),
                             r=[ps], w=[stB])
                        P.op("dve", lambda e, gl=gl, psk=psk: e.tensor_tensor(
                            out=stK[:, 0, gl, :], in0=psk[:, 0:128], in1=cst[:, KMASK, :], op=ALU.mult),
                            r=[psk, cst], w=[stK])
                        P.op("dve", lambda e, gl=gl, g=g: e.tensor_copy(out=stK[:, 1, gl, :], in_=fl(Cph, g)),
                             r=[TMP], w=[stK])
                    if int(_os.environ.get("S7", "9")) >= 3:
                        P.op("sp", lambda e, q=q: e.dma_start(out=s5B[q], in_=stB.ap), r=[stB], w=["s5B"], dma="s5B")
                        P.op("sp", lambda e, q=q: e.dma_start(out=s5K[q], in_=stK.ap), r=[stK], w=["s5K"], dma="s5K")
            print("setup phase top", ph.top, "BASE", BASE)

        def s5_tile(light):
            ph = Phase()
            uT = ph.alloc("uT", [128, 8, TT], BF16)
            U2 = ph.alloc("U2", [64, 64, 8, 16], BF16)
            Ucol = ph.alloc("Ucol", [128, 64, 64], BF16)
            cS = ph.alloc("cS", [128, 64, 64], F32)
            cW = ph.alloc("cW", [128, 64, 64], F32)
            Xprev = ph.alloc("Xprev", [128, 64, 64], BF16)
            Ycol = ph.alloc("Ycol", [128, 64, 64], BF16)
            ytmp = ph.alloc("ytmp", [128, TT], F32)
            gtmp = ph.alloc("gtmp", [128, TT], F32)
            r1 = [ph.alloc("r1_%d" % i, [128, 64], F32) for i in range(2)]
            r2 = [ph.alloc("r2_%d" % i, [128, 64], F32) for i in range(2)]

            def cons_u(j, ps, cw):
                P.op("act", lambda e: e.activation(out=uT[:, j, :], in_=ps[:, :], func=AF.Copy), r=[ps], w=[(uT, j)])
            proj("w_in", 0, 1024, 8, lambda c: (hT[:, c, :], (hT, c)), cons_u)
            for q in range(8):
                ps = next_ps()
                pb = psbf(ps)
                for j in range(8):
                    P.op("pe", lambda e, q=q, j=j, pb=pb: e.transpose(
                        out=pb[0:64, j * 128:(j + 1) * 128], in_=uT[:, q, j:TT:8], identity=identb.ap),
                        r=[(uT, q), identb], w=[ps])
                P.op("dve", lambda e, q=q, pb=pb: e.tensor_copy(
                    out=U2[:, 8 * q:8 * q + 8, :, :],
                    in_=pb[0:64, :].rearrange("p (j g m) -> p g j m", j=8, g=8)), r=[ps], w=[U2])
            for g0 in range(0, 64, 16):
                ps = next_ps()
                pb = psbf(ps)
                for g in range(g0, g0 + 16):
                    P.op("pe", lambda e, g=g, g0=g0, pb=pb: e.transpose(
                        out=pb[:, (g - g0) * 64:(g - g0 + 1) * 64], in_=U2[:, g, :, :].rearrange("p a b -> p (a b)"),
                        identity=identb[0:64, 0:64]), r=[U2, identb], w=[ps])
                P.op("act", lambda e, g0=g0, pb=pb: e.activation(
                    out=Ucol[:, g0:g0 + 16, :], in_=pb[:, :].rearrange("p (g b) -> p g b", g=16), func=AF.Copy),
                    r=[ps], w=[(Ucol, g0, g0 + 16)])
            for q in range(8):
                P.op("sp", lambda e, q=q: e.dma_start(out=s5tab.ap, in_=s5B[q]), r=["s5B"], w=[s5tab], dma="s5tab")
                for which, dst in ((0, cS), (1, cW)):
                    ps = next_ps()
                    for gl in range(8):
                        g = 8 * q + gl
                        P.op("pe", lambda e, which=which, gl=gl, g=g, ps=ps: e.matmul(
                            ps[:, gl * 64:(gl + 1) * 64], lhsT=s5tab[:, which, gl, :], rhs=Ucol[:, g, :],
                            start=True, stop=True), r=[s5tab, (Ucol, g)], w=[ps])
                    P.op("act", lambda e, q=q, dst=dst, ps=ps: e.activation(
                        out=dst[:, 8 * q:8 * q + 8, :], in_=ps[:, :].rearrange("p (g b) -> p g b", g=8), func=AF.Copy),
                        r=[ps], w=[(dst, 8 * q, 8 * q + 8)])
            for b in range(64):
                S0, W0 = Sst[st["s"] % 2], Ssw[st["s"] % 2]
                S1, W1 = Sst[(st["s"] + 1) % 2], Ssw[(st["s"] + 1) % 2]
                st["s"] += 1
                if not light:
                    P.op("act", lambda e, b=b, S0=S0: e.activation(out=Xprev[:, :, b], in_=S0.ap, func=AF.Copy),
                         r=[S0], w=[Xprev])
                A, Bq = r1[b % 2], r2[b % 2]
                P.op("dve", lambda e, A=A, S0=S0: e.tensor_tensor(out=A.ap, in0=S0.ap, in1=a8r.ap, op=ALU.mult),
                     r=[S0, a8r], w=[A])
                P.op("pool", lambda e, Bq=Bq, W0=W0: e.tensor_tensor(out=Bq.ap, in0=W0.ap, in1=a8r.ap, op=ALU.mult),
                     r=[W0, a8r], w=[Bq])
                P.op("dve", lambda e, W0=W0, S1=S1: e.tensor_tensor(out=S1.ap, in0=W0.ap, in1=a8s.ap, op=ALU.mult),
                     r=[W0, a8s], w=[S1])
                P.op("pool", lambda e, S0=S0, W1=W1: e.tensor_tensor(out=W1.ap, in0=S0.ap, in1=a8n.ap, op=ALU.mult),
                     r=[S0, a8n], w=[W1])
                P.op("dve", lambda e, A=A, S1=S1: e.tensor_tensor(out=S1.ap, in0=S1.ap, in1=A.ap, op=ALU.add),
                     r=[S1, A], w=[S1])
                P.op("pool", lambda e, Bq=Bq, W1=W1: e.tensor_tensor(out=W1.ap, in0=W1.ap, in1=Bq.ap, op=ALU.add),
                     r=[W1, Bq], w=[W1])
                P.op("dve", lambda e, b=b, S1=S1: e.tensor_tensor(out=S1.ap, in0=S1.ap, in1=cS[:, :, b], op=ALU.add),
                     r=[S1, cS], w=[S1])
                P.op("pool", lambda e, b=b, W1=W1: e.tensor_tensor(out=W1.ap, in0=W1.ap, in1=cW[:, :, b], op=ALU.add),
                     r=[W1, cW], w=[W1])
            if light:
                return None
            s5T = P.view("s5T", S5T_OFF, [128, 8, TT], BF16)
            gel = ph.alloc("gel", [128, 8, TT], BF16)
            assert ph.top <= S5T_OFF, ph.top
            for q in range(8):
                P.op("sp", lambda e, q=q: e.dma_start(out=s5tab.ap, in_=s5K[q]), r=["s5K"], w=[s5tab], dma="s5tab")
                ps = next_ps()
                for gl in range(8):
                    g = 8 * q + gl
                    P.op("pe", lambda e, gl=gl, g=g, ps=ps: e.matmul(
                        ps[:, gl * 64:(gl + 1) * 64], lhsT=s5tab[:, 0, gl, :], rhs=Ucol[:, g, :],
                        start=True, stop=False), r=[s5tab, (Ucol, g)], w=[ps])
                    P.op("pe", lambda e, gl=gl, g=g, ps=ps: e.matmul(
                        ps[:, gl * 64:(gl + 1) * 64], lhsT=s5tab[:, 1, gl, :], rhs=Xprev[:, g, :],
                        start=False, stop=True), r=[s5tab, Xprev], w=[ps])
                P.op("act", lambda e, q=q, ps=ps: e.activation(
                    out=Ycol[:, 8 * q:8 * q + 8, :], in_=ps[:, :].rearrange("p (g b) -> p g b", g=8), func=AF.Copy),
                    r=[ps], w=[(Ycol, 8 * q, 8 * q + 8)])
            Y2 = P.view("U2", U2.lo, [64, 8, 64, 16], BF16)
            for g0 in range(0, 64, 8):
                ps = next_ps()
                pb = psbf(ps)
                for g in range(g0, g0 + 8):
                    P.op("pe", lambda e, g=g, g0=g0, pb=pb: e.transpose(
                        out=pb[0:64, (g - g0) * 128:(g - g0 + 1) * 128], in_=Ycol[:, g, :], identity=identb.ap),
                        r=[(Ycol, g), identb], w=[ps])
                P.op("dve", lambda e, g0=g0, pb=pb: e.tensor_copy(
                    out=Y2[:, :, g0:g0 + 8, :],
                    in_=pb[0:64, :].rearrange("p (g j m) -> p j g m", g=8, j=8)), r=[ps], w=[Y2])
            for q in range(8):
                ps = next_ps()
                pb = psbf(ps)
                for j in range(8):
                    P.op("pe", lambda e, q=q, j=j, pb=pb: e.transpose(
                        out=pb[:, j * 64:(j + 1) * 64],
                        in_=Y2[:, j, 8 * q:8 * q + 8, :].rearrange("p a b -> p (a b)"),
                        identity=identb[0:64, 0:64]), r=[Y2, identb], w=[ps])
                P.op("dve", lambda e, q=q, pb=pb: e.scalar_tensor_tensor(
                    out=ytmp.ap.rearrange("p (b j) -> p j b", j=8),
                    in0=uT[:, q, :].rearrange("p (b j) -> p j b", j=8),
                    scalar=pc8[:, V_S5D, q:q + 1],
                    in1=pb[:, 0:512].rearrange("p (j b) -> p j b", j=8),
                    op0=ALU.mult, op1=ALU.add), r=[ps, (uT, q), pc8], w=[ytmp])
                P.op("pool", lambda e: e.tensor_tensor(out=gtmp.ap, in0=ytmp.ap, in1=ytmp.ap, op=ALU.mult),
                     r=[ytmp], w=[gtmp])
                P.op("pool", lambda e: e.tensor_scalar(out=gtmp.ap, in0=gtmp.ap, scalar1=0.044715, scalar2=1.0,
                                                       op0=ALU.mult, op1=ALU.add), r=[gtmp], w=[gtmp])
                P.op("pool", lambda e: e.tensor_tensor(out=gtmp.ap, in0=gtmp.ap, in1=ytmp.ap, op=ALU.mult),
                     r=[gtmp, ytmp], w=[gtmp])
                P.op("act", lambda e: e.activation(out=gtmp.ap, in_=gtmp.ap, func=AF.Sigmoid, scale=1.5957691216),
                     r=[gtmp], w=[gtmp])
                P.op("dve", lambda e, q=q: e.tensor_tensor(out=gel[:, q, :], in0=gtmp.ap, in1=ytmp.ap, op=ALU.mult),
                     r=[gtmp, ytmp], w=[(gel, q)])

            def cons_glu(j, ps, cw):
                g = gsb[st["g"] % 2]
                st["g"] += 1
                P.op("act", lambda e: e.activation(out=g.ap, in_=ps[:, :], func=AF.Sigmoid,
                                                   bias=pc8[:, V_BGLU, j:j + 1]), r=[ps, pc8], w=[g])
                P.op("dve", lambda e: e.tensor_tensor(out=s5T[:, j, :], in0=gel[:, j, :], in1=g.ap, op=ALU.mult),
                     r=[g, (gel, j)], w=[(s5T, j)])
            proj("s5_w_glu", 0, 1024, 8, lambda c: (gel[:, c, :], (gel, c)), cons_glu)
            return s5T

        def ssd_tile(light):
            ph = Phase()
            xbcp = ph.alloc("xbcp", [128, 32, TT + 3], BF16)
            ySS = P.view("ySS", xbcp.lo, [128, 16, TT], BF16)
            xsT = ph.alloc("xsT", [128, 16, TT], BF16)
            Bfm = ph.alloc("Bfm", [128, 8, TT], BF16)
            Cfm = ph.alloc("Cfm", [128, 8, TT], BF16)
            xtok = ph.alloc("xtok", [128, 2048], BF16)
            Btok = ph.alloc("Btok", [128, 1024], BF16)
            dtT = ph.alloc("dtT", [32, TT], F32)
            dAT = ph.alloc("dAT", [32, TT], F32)
            dtok = ph.alloc("dtok", [128, 4, 64], F32)
            dAh = ph.alloc("dAh", [128, 4, 32], BF16)
            dAl = ph.alloc("dAl", [128, 4, 32], BF16)
            dAr = ph.alloc("dAr", [128, 4, 32], F32)
            acs = ph.alloc("acs", [128, 64], F32)
            wgt = ph.alloc("wgt", [128, 32], F32)
            cdec = ph.alloc("cdec", [128, 32], F32)
            dg = [ph.alloc("dg%d" % i, [128, 4, 128], BF16) for i in range(2)]
            cbm = [ph.alloc("cbm%d" % i, [128, 128], F32) for i in range(2)]
            rhsM = [ph.alloc("rhsM%d" % i, [128, 2, 128], BF16) for i in range(2)]
            ed = [ph.alloc("ed%d" % i, [128, 256], F32) for i in range(2)]
            wTb = [ph.alloc("wT%d" % i, [128, 128], BF16) for i in range(2)]
            Csb = [ph.alloc("Cs%d" % i, [128, 128], BF16) for i in range(2)]
            xwb = [ph.alloc("xw%d" % i, [128, 256], BF16) for i in range(2)]
            htmp = [ph.alloc("htmp%d" % i, [128, 4, 64], F32) for i in range(2)]
            sqz = [ph.alloc("sqz%d" % i, [128, TT], BF16) for i in range(2)]
            assert ph.top <= S5T_OFF, ph.top
            cnt = {"h": 0, "g": 0}

            P.op("pool", lambda e: e.tensor_copy(out=xbcp[:, :, 0:3], in_=tail.ap), r=[tail], w=[xbcp])

            def cons_xbc(j, ps, cw):
                P.op("act", lambda e: e.activation(out=xbcp[:, j, 3:TT + 3], in_=ps[:, :], func=AF.Copy),
                     r=[ps], w=[(xbcp, j)])
            proj("w_in", 3072, 4096, 8, lambda c: (hT[:, c, :], (hT, c)), cons_xbc)
            P.op("pool", lambda e: e.tensor_copy(out=tail.ap, in_=xbcp[:, :, TT:TT + 3]), r=[xbcp], w=[tail])
            for oc in range(32):
                d = dg[oc % 2]
                for k in range(4):
                    P.op("pool", lambda e, d=d, k=k, oc=oc: e.tensor_scalar(
                        out=d[:, k, :], in0=identb.ap, scalar1=convp[:, oc, k:k + 1], scalar2=None, op0=ALU.mult),
                        r=[identb, convp], w=[(d, k)])
                ps = next_ps()
                for k in range(4):
                    P.op("pe", lambda e, d=d, k=k, oc=oc, ps=ps: e.matmul(
                        ps[:, :], lhsT=d[:, k, :], rhs=xbcp[:, oc, k:k + TT], start=(k == 0), stop=(k == 3)),
                        r=[(d, k), (xbcp, oc)], w=[ps])
                if oc < 16:
                    dst, dk = xsT[:, oc, :], (xsT, oc)
                elif oc < 24:
                    dst, dk = Bfm[:, oc - 16, :], (Bfm, oc - 16)
                else:
                    dst, dk = Cfm[:, oc - 24, :], (Cfm, oc - 24)
                P.op("act", lambda e, dst=dst, ps=ps, oc=oc: e.activation(
                    out=dst, in_=ps[:, :], func=AF.Silu, bias=convp[:, oc, 4:5]), r=[ps, convp], w=[dk])

            def cons_dt(j, ps, cw):
                P.op("act", lambda e: e.activation(out=dtT.ap, in_=ps[0:32, :], func=AF.Exp, bias=hp[:, 1:2]),
                     r=[ps, hp], w=[dtT])
                P.op("act", lambda e: e.activation(out=dtT.ap, in_=dtT.ap, func=AF.Ln, bias=1.0), r=[dtT], w=[dtT])
                P.op("dve", lambda e: e.tensor_scalar(out=dAT.ap, in0=dtT.ap, scalar1=aneg[:, 0:1], scalar2=None,
                                                      op0=ALU.mult), r=[dtT, aneg], w=[dAT])
            proj("w_in", 7168, 32, 8, lambda c: (hT[:, c, :], (hT, c)), cons_dt, pw=128)
            ps = next_ps()
            for s in range(4):
                P.op("pe", lambda e, s=s, ps=ps: e.transpose(
                    out=ps[:, s * 64:s * 64 + 32], in_=dtT[:, s * 128:(s + 1) * 128], identity=cst[0:32, IDN, 0:32]),
                    r=[dtT, cst], w=[ps])
                P.op("pe", lambda e, s=s, ps=ps: e.transpose(
                    out=ps[:, s * 64 + 32:s * 64 + 64], in_=dAT[:, s * 128:(s + 1) * 128], identity=cst[0:32, IDN, 0:32]),
                    r=[dAT, cst], w=[ps])
            P.op("dve", lambda e, ps=ps: e.tensor_copy(out=dtok.ap, in_=ps[:, 0:256].rearrange("p (s c) -> p s c", s=4)),
                 r=[ps], w=[dtok])
            P.op("dve", lambda e: e.tensor_copy(out=dAh.ap, in_=dtok[:, :, 32:64]), r=[dtok], w=[dAh])
            P.op("dve", lambda e: e.tensor_tensor(out=dAr.ap, in0=dtok[:, :, 32:64], in1=dAh.ap, op=ALU.subtract),
                 r=[dtok, dAh], w=[dAr])
            P.op("dve", lambda e: e.tensor_copy(out=dAl.ap, in_=dAr.ap), r=[dAr], w=[dAl])

            for s in range(4):
                tk = slice(s * 128, (s + 1) * 128)
                for c0 in range(0, 24, 8):
                    ps = next_ps()
                    pb = psbf(ps)
                    for c in range(c0, c0 + 8):
                        src, sk = (xsT[:, c, tk], (xsT, c)) if c < 16 else (Bfm[:, c - 16, tk], (Bfm, c - 16))
                        P.op("pe", lambda e, c=c, c0=c0, pb=pb, src=src: e.transpose(
                            out=pb[:, (c - c0) * 128:(c - c0 + 1) * 128], in_=src, identity=identb.ap),
                            r=[sk, identb], w=[ps])
                    if c0 < 16:
                        P.op("act", lambda e, c0=c0, pb=pb: e.activation(
                            out=xtok[:, c0 * 128:(c0 + 8) * 128], in_=pb[:, :], func=AF.Copy), r=[ps], w=[xtok])
                    else:
                        P.op("act", lambda e, pb=pb: e.activation(out=Btok.ap, in_=pb[:, :], func=AF.Copy),
                             r=[ps], w=[Btok])
                ps = next_ps()
                for ii, dd in enumerate((dAh, dAl)):
                    P.op("pe", lambda e, s=s, ps=ps, ii=ii, dd=dd: e.matmul(ps[:, 0:32], lhsT=mub.ap, rhs=dd[:, s, :],
                                                                            start=(ii == 0), stop=(ii == 1)), r=[mub, dd], w=[ps])
                for ii, dd in enumerate((dAh, dAl)):
                    P.op("pe", lambda e, s=s, ps=ps, ii=ii, dd=dd: e.matmul(ps[:, 32:64], lhsT=onesb.ap, rhs=dd[:, s, :],
                                                                            start=(ii == 0), stop=(ii == 1)), r=[onesb, dd], w=[ps])
                P.op("act", lambda e, ps=ps: e.activation(out=acs.ap, in_=ps[:, 0:64], func=AF.Copy), r=[ps], w=[acs])
                P.op("dve", lambda e: e.tensor_tensor(out=wgt.ap, in0=acs[:, 32:64], in1=acs[:, 0:32], op=ALU.subtract),
                     r=[acs], w=[wgt])
                P.op("act", lambda e: e.activation(out=wgt.ap, in_=wgt.ap, func=AF.Exp), r=[wgt], w=[wgt])
                P.op("dve", lambda e, s=s: e.tensor_tensor(out=wgt.ap, in0=wgt.ap, in1=dtok[:, s, 0:32], op=ALU.mult),
                     r=[wgt, dtok], w=[wgt])
                P.op("act", lambda e: e.activation(out=cdec.ap, in_=acs[:, 32:64], func=AF.Exp), r=[acs], w=[cdec])
                for g in range(8):
                    if not light:
                        cb = cbm[cnt["g"] % 2]
                        ps = next_ps()
                        P.op("pe", lambda e, g=g, ps=ps, tk=tk: e.matmul(ps[:, 0:128], lhsT=Bfm[:, g, tk], rhs=Cfm[:, g, tk],
                                                                    start=True, stop=True),
                             r=[(Bfm, g), (Cfm, g)], w=[ps])
                        P.op("dve", lambda e, cb=cb, ps=ps: e.tensor_tensor(out=cb.ap, in0=ps[:, 0:128], in1=cst[:, MU, :],
                                                                            op=ALU.mult), r=[ps, cst], w=[cb])
                        psy = next_ps()
                        for k in range(4):
                            h = 4 * g + k
                            i = cnt["h"] % 2
                            cnt["h"] += 1
                            rm, e_, wt, cs = rhsM[i], ed[i], wTb[i], Csb[i]
                            for ii, dd in enumerate((dAh, dAl)):
                                P.op("pool", lambda e, rm=rm, s=s, h=h, ii=ii, dd=dd: e.tensor_scalar(
                                    out=rm[:, ii, :], in0=mub.ap, scalar1=dd[:, s, h:h + 1], scalar2=None,
                                    op0=ALU.mult), r=[mub, dd], w=[(rm, ii)])
                            ps2 = next_ps()
                            for ii in range(2):
                                P.op("pe", lambda e, rm=rm, ps2=ps2, ii=ii: e.matmul(
                                    ps2[:, 0:128], lhsT=m1b.ap, rhs=rm[:, ii, :], start=(ii == 0), stop=(ii == 1)),
                                    r=[m1b, (rm, ii)], w=[ps2])
                            for ii in range(2):
                                P.op("pe", lambda e, rm=rm, ps2=ps2, ii=ii: e.matmul(
                                    ps2[:, 128:256], lhsT=onesb.ap, rhs=rm[:, ii, :], start=(ii == 0), stop=(ii == 1)),
                                    r=[onesb, (rm, ii)], w=[ps2])
                            P.op("act", lambda e, e_=e_, ps2=ps2: e.activation(out=e_.ap, in_=ps2[:, 0:256], func=AF.Exp),
                                 r=[ps2], w=[e_])
                            P.op("dve", lambda e, wt=wt, e_=e_, cb=cb, s=s, h=h: e.scalar_tensor_tensor(
                                out=wt.ap, in0=e_[:, 0:128], scalar=dtok[:, s, h:h + 1], in1=cb.ap,
                                op0=ALU.mult, op1=ALU.mult), r=[e_, cb, dtok], w=[wt])
                            P.op("pool", lambda e, cs=cs, e_=e_, g=g, tk=tk: e.tensor_tensor(
                                out=cs.ap, in0=Cfm[:, g, tk], in1=e_[:, 128:256], op=ALU.mult),
                                r=[(Cfm, g), e_], w=[cs])
                            po, co = 64 * (k % 2), 128 * (k // 2)
                            P.op("pe", lambda e, psy=psy, po=po, co=co, h=h, wt=wt: e.matmul(
                                psy[po:po + 64, co:co + 128], lhsT=xtok[:, h * 64:(h + 1) * 64], rhs=wt.ap,
                                start=True, stop=False), r=[xtok, wt], w=[psy])
                            P.op("pe", lambda e, psy=psy, po=po, co=co, h=h, cs=cs: e.matmul(
                                psy[po:po + 64, co:co + 128], lhsT=hSb[:, h, :], rhs=cs.ap,
                                start=False, stop=True), r=[(hSb, h), cs], w=[psy])
                        for cl in range(2):
                            cc = 2 * g + cl
                            P.op("dve", lambda e, psy=psy, cl=cl, cc=cc, tk=tk: e.scalar_tensor_tensor(
                                out=ySS[:, cc, tk], in0=xsT[:, cc, tk], scalar=pc16[:, V_SSDD, cc:cc + 1],
                                in1=psy[:, cl * 128:(cl + 1) * 128], op0=ALU.mult, op1=ALU.add),
                                r=[psy, (xsT, cc), pc16], w=[(ySS, cc)])
                    xw = xwb[cnt["g"] % 2]
                    ht = htmp[cnt["g"] % 2]
                    cnt["g"] += 1
                    P.op("dve", lambda e, xw=xw, g=g: e.tensor_tensor(
                        out=xw.ap.rearrange("p (k d) -> p k d", k=4),
                        in0=xtok[:, g * 256:(g + 1) * 256].rearrange("p (k d) -> p k d", k=4),
                        in1=wgt[:, 4 * g:4 * g + 4].unsqueeze(2).to_broadcast([128, 4, 64]), op=ALU.mult),
                        r=[xtok, wgt], w=[xw])
                    ps3 = next_ps()
                    P.op("pe", lambda e, ps3=ps3, g=g, xw=xw: e.matmul(
                        ps3[:, 0:256], lhsT=Btok[:, g * 128:(g + 1) * 128], rhs=xw.ap, start=True, stop=True),
                        r=[Btok, xw], w=[ps3])
                    P.op("pool", lambda e, ht=ht, g=g: e.tensor_tensor(
                        out=ht.ap, in0=hS[:, 4 * g:4 * g + 4, :],
                        in1=cdec[:, 4 * g:4 * g + 4].unsqueeze(2).to_broadcast([128, 4, 64]), op=ALU.mult),
                        r=[(hS, 4 * g, 4 * g + 4), cdec], w=[ht])
                    P.op("dve", lambda e, ht=ht, g=g, ps3=ps3: e.tensor_tensor(
                        out=hS[:, 4 * g:4 * g + 4, :], in0=ps3[:, 0:256].rearrange("p (k d) -> p k d", k=4),
                        in1=ht.ap, op=ALU.add), r=[ps3, ht], w=[(hS, 4 * g, 4 * g + 4)])
                    P.op("act", lambda e, g=g: e.activation(out=hSb[:, 4 * g:4 * g + 4, :], in_=hS[:, 4 * g:4 * g + 4, :],
                                                            func=AF.Copy), r=[(hS, 4 * g, 4 * g + 4)], w=[(hSb, 4 * g, 4 * g + 4)])
            if light:
                return None
            if DEBUG and st.get("dbg_ssd_done") is None:
                st["dbg_ssd_done"] = 1
                dump("dbg_xs", xsT, 16); dump("dbg_B", Bfm, 8); dump("dbg_C", Cfm, 8); dump("dbg_y", ySS, 16)
            pss = psb[7]

            def cons_z(j, ps, cw):
                g = gsb[st["g"] % 2]
                st["g"] += 1
                sq_ = sqz[j % 2]
                P.op("act", lambda e: e.activation(out=g.ap, in_=ps[:, :], func=AF.Silu), r=[ps], w=[g])
                P.op("dve", lambda e: e.tensor_tensor(out=ySS[:, j, :], in0=ySS[:, j, :], in1=g.ap, op=ALU.mult),
                     r=[(ySS, j), g], w=[(ySS, j)])
                P.op("act", lambda e: e.activation(out=sq_.ap, in_=ySS[:, j, :], func=AF.Square), r=[(ySS, j)], w=[sq_])
                P.op("pe", lambda e: e.matmul(pss[:, :], lhsT=onesb.ap, rhs=sq_.ap, start=(j == 0), stop=(j == 15)),
                     r=[onesb, sq_], w=[pss])
            proj("w_in", 1024, 2048, 8, lambda c: (hT[:, c, :], (hT, c)), cons_z)
            rstd_from(pss, 1.0 / 2048)
            for c in range(16):
                P.op("dve", lambda e, c=c: e.scalar_tensor_tensor(
                    out=ySS[:, c, :], in0=ySS[:, c, :], scalar=pc16[:, V_SSDN, c:c + 1], in1=rstd.ap,
                    op0=ALU.mult, op1=ALU.mult), r=[(ySS, c), pc16, rstd], w=[(ySS, c)])
            return ySS

        def merge_tile(s5T, ynT):
            G = P.view("G", MRG_OFF, [128, 16, TT], BF16)
            Q = P.view("Q", MRG_OFF + 16 * TT * 2, [128, 8, TT], BF16)
            mT = P.view("mT", MRG_OFF + 24 * TT * 2, [128, 8, TT], BF16)
            assert MRG_OFF + 32 * TT * 2 <= S5T_OFF and MRG_OFF >= ynT.hi, (MRG_OFF, ynT.hi, S5T_OFF)

            def cons_g(j, ps, cw):
                P.op("act", lambda e: e.activation(out=G[:, j, :], in_=ps[:, :], func=AF.Sigmoid,
                                                   bias=pc16[:, V_BGATE, j:j + 1]), r=[ps, pc16], w=[(G, j)])
            proj("w_in", 7200, 2048, 8, lambda c: (hT[:, c, :], (hT, c)), cons_g)

            def cons_q(j, ps, cw):
                P.op("dve", lambda e: e.tensor_tensor(out=Q[:, j, :], in0=ps[:, :], in1=G[:, j, :], op=ALU.mult),
                     r=[ps, (G, j)], w=[(Q, j)])
            proj("w_proj_s5", 0, 1024, 8, lambda c: (s5T[:, c, :], (s5T, c)), cons_q)

            def cons_m(j, ps, cw):
                g = gsb[st["g"] % 2]
                st["g"] += 1
                P.op("dve", lambda e: e.tensor_tensor(out=g.ap, in0=ps[:, :], in1=G[:, 8 + j, :], op=ALU.mult),
                     r=[ps, (G, 8 + j)], w=[g])
                P.op("pool", lambda e: e.tensor_tensor(out=mT[:, j, :], in0=g.ap, in1=Q[:, j, :], op=ALU.add),
                     r=[g, (Q, j)], w=[(mT, j)])
            proj("w_proj_ssd", 0, 1024, 16, lambda c: (ynT[:, c, :], (ynT, c)), cons_m)

            def cons_o(j, ps, cw):
                P.op("dve", lambda e: e.tensor_tensor(out=xres[:, j, :], in0=ps[:, :], in1=xres[:, j, :], op=ALU.add),
                     r=[ps, (xres, j)], w=[(xres, j)])
            proj("w_out", 0, 1024, 8, lambda c: (mT[:, c, :], (mT, c)), cons_o)
            return mT

        def store_out(t_own):
            ph = Phase()
            sqb = ph.alloc("sqb", [128, 8, TT], BF16)
            rmsnorm_to(xres, G_FIN, sqb)
            for s in range(4):
                ob = osb[0]
                for c0 in range(0, 8, 4):
                    ps = next_ps()
                    for c in range(c0, c0 + 4):
                        P.op("pe", lambda e, ps=ps, c=c, c0=c0, s=s: e.transpose(
                            out=ps[:, (c - c0) * 128:(c - c0 + 1) * 128], in_=xres[:, c, s * 128:(s + 1) * 128],
                            identity=cst[:, IDN, :]), r=[(xres, c), cst], w=[ps])
                    P.op("act", lambda e, ps=ps, c0=c0, ob=ob: e.activation(
                        out=ob[:, c0 * 128:(c0 + 4) * 128], in_=ps[:, :], func=AF.Copy), r=[ps], w=[ob])
                r0 = t_own * TT + s * 128
                tok = P.op("sp", lambda e, ob=ob, r0=r0: e.dma_start(out=out[r0:r0 + 128, :], in_=ob.ap),
                           r=[ob], w=["outdram"], dma=ob.key)
                P.final[ob.key] = (tok[0], tok[1])

        import os as _os
        STOP = int(_os.environ.get("KSTOP", "9"))
        if STOP >= 1 or STOP == -1:
            s5_setup()
        for tile in range(NT_PRE + NT_OWN if STOP >= 0 else 0):
            own = tile >= NT_PRE
            light = LIGHT and not own
            if tile == NT_PRE:
                fl_ = cstv[:, FLAG:FLAG + 1]
                for b_ in (Sst[st["s"] % 2], Ssw[st["s"] % 2], hS, hSb, tail):
                    P.op("dve", lambda e, b_=b_: e.tensor_scalar(out=b_.ap, in0=b_.ap, scalar1=fl_, scalar2=None,
                                                                 op0=ALU.mult), r=[b_, cstv], w=[b_])
            load_x(tile)
            ffn("ffn1", G_FFN1)
            if tile == NT_PRE:
                dump("dbg_x1", xres, 8)
            s5T = ynT = None
            if STOP >= 2:
                ph = Phase()
                sqb = ph.alloc("sqb", [128, 8, TT], BF16)
                rmsnorm_to(hT, G_MIX, sqb)
                s5T = s5_tile(light)
            if STOP >= 3:
                ynT = ssd_tile(light)
            if own:
                if STOP >= 4:
                    if tile == NT_PRE:
                        dump("dbg_s5", s5T, 8)
                        dump("dbg_ssd", ynT, 16)
                    mT = merge_tile(s5T, ynT)
                    if tile == NT_PRE:
                        dump("dbg_m", mT, 8)
                    ffn("ffn2", G_FFN2)
                store_out(tile - NT_PRE)
        P.emit()
    return nc


_CACHE = {}


def _pc(v, n):
    return np.ascontiguousarray(np.asarray(v, np.float32).reshape(n, 128).T)


def prep_inputs(inputs):
    f = lambda k: np.asarray(inputs[k], dtype=np.float32)
    common = {}
    for n in WSHAPES:
        common[n] = np.ascontiguousarray(f(n)[0])
    pc8 = np.stack([_pc(f("ffn1_norm")[0], 8), _pc(f("mix_norm")[0], 8), _pc(f("ffn2_norm")[0], 8),
                    _pc(f("final_norm"), 8), _pc(f("s5_D")[0], 8), _pc(f("s5_b_glu")[0], 8)], axis=1)
    common["pc8"] = np.ascontiguousarray(pc8)
    pc16 = np.stack([_pc(f("ssd_norm")[0], 16), _pc(f("b_gate")[0], 16),
                     _pc(np.repeat(f("ssd_D")[0], 64), 16)], axis=1)
    common["pc16"] = np.ascontiguousarray(pc16)
    cw = f("conv_w")[0]
    convp = np.zeros((128, 32, 5), np.float32)
    for k in range(4):
        convp[:, :, k] = _pc(cw[k], 32)
    convp[:, :, 4] = _pc(f("conv_b")[0], 32)
    common["convp"] = convp
    common["hp"] = np.ascontiguousarray(np.stack([f("ssd_A_log")[0], f("ssd_dt_bias")[0]], axis=1))
    dup = lambda a: np.concatenate([a, a], axis=0)
    s5a = np.stack([dup(f("s5_A_re")[0].T), dup(f("s5_A_im")[0].T),
                    np.broadcast_to(f("s5_log_dt")[0][None, :], (128, 64))], axis=1)
    common["s5a"] = np.ascontiguousarray(s5a)
    common["s5b"] = np.ascontiguousarray(np.stack([dup(f("s5_B_re")[0].transpose(1, 0, 2)),
                                                   dup(f("s5_B_im")[0].transpose(1, 0, 2))], axis=1))
    common["s5c"] = np.ascontiguousarray(np.stack([dup(f("s5_C_re")[0].transpose(2, 0, 1)),
                                                   dup(f("s5_C_im")[0].transpose(2, 0, 1))], axis=1))
    i = np.arange(128)
    cst = np.zeros((128, 5, 128), np.float32)
    cst[:, 0, :] = np.eye(128)
    cst[:, 1, :] = (i[:, None] <= i[None, :])
    cst[:, 2, :] = (i[:, None] > i[None, :])
    cst[:, 3, :] = 1.0
    cst[:, 4, :] = ((i[None, :] // 16) >= (i[:, None] // 16))
    common["cst"] = cst
    return common


def kernel(**inputs):
    x = np.asarray(inputs["x"], dtype=np.float32)
    B, L, _ = x.shape
    half = L // 2
    if "nc" not in _CACHE:
        _CACHE["nc"] = build_program()
    nc = _CACHE["nc"]
    common = prep_inputs(inputs)
    in_maps = []
    for c in range(8):
        b, h = c // 2, c % 2
        xs = np.zeros((L, D), np.float32)
        if h == 0:
            xs[half:] = x[b, :half]
        else:
            xs[:] = x[b]
        m = dict(common)
        m["xs"] = xs
        cv = np.zeros((128, 4), np.float32)
        cv[:64, 0], cv[64:, 0] = -1.0, 1.0
        cv[:64, 1] = 1.0
        cv[64:, 2] = 1.0
        cv[:, 3] = float(h)
        m["cstv"] = cv
        in_maps.append(m)
    res = run_bass_kernel_spmd(nc, in_maps, core_ids=list(range(8)))
    _CACHE["res"] = res
    outp = np.zeros((B, L, D), np.float32)
    for c in range(8):
        b, h = c // 2, c % 2
        outp[b, h * half:(h + 1) * half] = np.asarray(res.results[c]["out"], dtype=np.float32)
    return outp
```

```python
import numpy as np
import ml_dtypes
from contextlib import ExitStack
import concourse.bass as bass
import concourse.mybir as mybir
from concourse.bass_utils import run_bass_kernel_spmd

F32 = mybir.dt.float32
BF16 = mybir.dt.bfloat16
I32 = mybir.dt.int32
U8 = mybir.dt.uint8
ALU = mybir.AluOpType
AF = mybir.ActivationFunctionType
DTSZ = {F32: 4, BF16: 2, I32: 4, U8: 1}

D = 1024
DFF = 2816
DIN = 9248
TT = 512
NT_PRE = 4
NT_OWN = 4
EPS = 1e-6
ARENA = 206 * 1024
LIGHT = True
DEBUG = False


def _prod(s):
    r = 1
    for v in s:
        r *= v
    return r


class Buf:
    def __init__(self, ap, key, space, lo, hi, shape, dt):
        self.ap, self.key, self.space, self.lo, self.hi, self.shape, self.dt = ap, key, space, lo, hi, shape, dt
        self.sub_bytes = (_prod(shape[2:]) * DTSZ[dt]) if len(shape) > 2 else None

    def __getitem__(self, idx):
        return self.ap[idx]


class Prog:
    ENG = ("pe", "act", "dve", "pool", "sp")
    EPOCH = 12000

    def __init__(self, nc, es):
        self.nc, self.es = nc, es
        self.ops = {e: [] for e in self.ENG}
        self.count = {e: 0 for e in self.ENG}
        self.esems = {e: [] for e in self.ENG}
        self.dsems = {}
        self.lastw = {}
        self.readers = {}
        self.spaces = {}
        self.waited = {e: {} for e in self.ENG}
        self.final = {}
        self.arena = es.enter_context(nc.sbuf_tensor("arena", [128, ARENA], U8))
        self.top = 0
        self.nps = 0

    def view(self, name, off, shape, dt):
        nb = _prod(shape[1:]) * DTSZ[dt]
        assert off + nb <= ARENA, (name, off, nb)
        ap = self.arena[0:shape[0], off:off + nb].bitcast(dt)
        if len(shape) == 3:
            ap = ap.rearrange("p (a b) -> p a b", a=shape[1])
        elif len(shape) == 4:
            ap = ap.rearrange("p (a b c) -> p a b c", a=shape[1], b=shape[2])
        return Buf(ap, "%s@%d" % (name, off), "arena", off, off + nb, shape, dt)

    def alloc(self, name, shape, dt):
        b = self.view(name, self.top, shape, dt)
        self.top = (b.hi + 31) // 32 * 32
        return b

    def psum(self, name):
        t = self.es.enter_context(self.nc.psum_tensor(name, [128, 512], F32))
        return Buf(t[:, :], name, name, 0, 2048, [128, 512], F32)

    def sem(self, name):
        return self.es.enter_context(self.nc.semaphore(name))

    def _norm(self, it):
        if isinstance(it, Buf):
            return (it.key, it.space, it.lo, it.hi)
        if isinstance(it, tuple):
            b = it[0]
            i0 = it[1]
            i1 = it[2] if len(it) > 2 else i0 + 1
            return ((b.key, i0, i1), b.space, b.lo + i0 * b.sub_bytes, b.lo + i1 * b.sub_bytes)
        return (it, it, 0, 1)

    def _related(self, n):
        key, space, lo, hi = n
        sp = self.spaces.setdefault(space, {})
        if key not in sp:
            sp[key] = (lo, hi)
        return [k for k, (l, h) in sp.items() if l < hi and lo < h]

    def op(self, eng, fn, r=(), w=(), dma=None):
        rn = [self._norm(x) for x in r]
        wn = [self._norm(x) for x in w]
        deps = []
        for n in rn:
            for k in self._related(n):
                if k in self.lastw:
                    deps.append(self.lastw[k])
        for n in wn:
            for k in self._related(n):
                if k in self.lastw:
                    deps.append(self.lastw[k])
                deps.extend(self.readers.get(k, {}).values())
        waits = []
        for (sem, val, seng) in deps:
            if seng == "pe" and eng == "pe" and dma is None:
                continue
            cur = self.waited[eng].get(id(sem), 0)
            if val > cur:
                self.waited[eng][id(sem)] = val
                waits.append((sem, val))
        if dma is not None:
            if dma not in self.dsems:
                self.dsems[dma] = [self.sem("d_" + str(dma)), 0]
            ds = self.dsems[dma]
            ds[1] += 16
            tok = (ds[0], ds[1], "dma")
            inc = (ds[0], 16)
        else:
            c = self.count[eng]
            ep = c // self.EPOCH
            while len(self.esems[eng]) <= ep:
                self.esems[eng].append(self.sem("e_%s_%d" % (eng, len(self.esems[eng]))))
            s = self.esems[eng][ep]
            self.count[eng] = c + 1
            tok = (s, c - ep * self.EPOCH + 1, eng)
            inc = (s, 1)
        self.ops[eng].append((waits, fn, inc))
        for n in wn:
            self.lastw[n[0]] = tok
            self.readers[n[0]] = {}
        for n in rn:
            self.readers.setdefault(n[0], {})[id(tok[0])] = tok
        return tok

    def emit(self):
        with self.nc.Block() as block:
            def mk(engname):
                def body(e):
                    for waits, fn, inc in self.ops[engname]:
                        for (s, v) in waits:
                            e.wait_ge(s, v)
                        fn(e).then_inc(inc[0], inc[1])
                    if engname == "sp":
                        for (s, v) in self.final.values():
                            e.wait_ge(s, v)
                return body
            block.tensor(mk("pe"))
            block.scalar(mk("act"))
            block.vector(mk("dve"))
            block.gpsimd(mk("pool"))
            block.sync(mk("sp"))


WSHAPES = {"ffn1_w_gate": (D, DFF), "ffn1_w_up": (D, DFF), "ffn1_w_down": (DFF, D),
           "w_in": (D, DIN), "s5_w_glu": (D, D), "w_proj_s5": (D, D), "w_proj_ssd": (2 * D, D),
           "w_out": (D, D),
           "ffn2_w_gate": (D, DFF), "ffn2_w_up": (D, DFF), "ffn2_w_down": (DFF, D)}
PSHAPES = {"pc8": [128, 6, 8], "pc16": [128, 3, 16], "convp": [128, 32, 5], "hp": [32, 2],
           "s5a": [128, 3, 64], "s5b": [128, 2, 64, 16], "s5c": [128, 2, 64, 16],
           "cst": [128, 5, 128], "cstv": [128, 4]}


def build_program():
    nc = bass.Bass("TRN2", target_bir_lowering=False)
    NTOK = (NT_PRE + NT_OWN) * TT
    din = {}
    for n, s in WSHAPES.items():
        din[n] = nc.dram_tensor(n, list(s), F32, kind="ExternalInput").ap()
    for n, s in PSHAPES.items():
        din[n] = nc.dram_tensor(n, list(s), F32, kind="ExternalInput").ap()
    xs = nc.dram_tensor("xs", [NTOK, D], F32, kind="ExternalInput").ap()
    out = nc.dram_tensor("out", [NT_OWN * TT, D], F32, kind="ExternalOutput").ap()
    wbf = {n: nc.dram_tensor(n + "_bf", list(s), BF16, kind="Internal").ap() for n, s in WSHAPES.items()}
    s5B = nc.dram_tensor("s5B_tab", [8, 128, 2, 8, 128], BF16, kind="Internal").ap()
    s5K = nc.dram_tensor("s5K_tab", [8, 128, 2, 8, 128], BF16, kind="Internal").ap()
    dbg = {}
    if DEBUG:
        for n, c in (("dbg_s5", 8), ("dbg_ssd", 16), ("dbg_m", 8), ("dbg_x1", 8), ("dbg_xs", 16), ("dbg_B", 8), ("dbg_C", 8), ("dbg_y", 16), ("dbg_dt", 1)):
            dbg[n] = nc.dram_tensor(n, [128, c, TT], F32, kind="ExternalOutput").ap()

    with ExitStack() as es:
        P = Prog(nc, es)
        psb = [P.psum("ps%d" % i) for i in range(8)]
        NROT = 7

        def next_ps():
            P.nps = (P.nps + 1) % NROT
            return psb[P.nps]

        def psbf(ps):
            return ps.ap.bitcast(BF16)

        cst = P.alloc("cst", [128, 5, 128], F32)
        IDN, MU, M1, ONE, KMASK = 0, 1, 2, 3, 4
        cstv = P.alloc("cstv", [128, 4], F32)
        SGN, SELT, SELB, FLAG = 0, 1, 2, 3
        pc8 = P.alloc("pc8", [128, 6, 8], F32)
        G_FFN1, G_MIX, G_FFN2, G_FIN, V_S5D, V_BGLU = range(6)
        pc16 = P.alloc("pc16", [128, 3, 16], F32)
        V_SSDN, V_BGATE, V_SSDD = range(3)
        convp = P.alloc("convp", [128, 32, 5], F32)
        hp = P.alloc("hp", [32, 2], F32)
        aneg = P.alloc("aneg", [32, 1], F32)
        identb = P.alloc("identb", [128, 128], BF16)
        onesb = P.alloc("onesb", [128, 128], BF16)
        mub = P.alloc("mub", [128, 128], BF16)
        m1b = P.alloc("m1b", [128, 128], BF16)
        hSb = P.alloc("hSb", [128, 32, 64], BF16)
        a8r = P.alloc("a8r", [128, 64], F32)
        a8s = P.alloc("a8s", [128, 64], F32)
        a8n = P.alloc("a8n", [128, 64], F32)
        Sst = [P.alloc("Sst%d" % i, [128, 64], F32) for i in range(2)]
        Ssw = [P.alloc("Ssw%d" % i, [128, 64], F32) for i in range(2)]
        hS = P.alloc("hS", [128, 32, 64], F32)
        tail = P.alloc("tail", [128, 32, 3], BF16)
        xres = P.alloc("xres", [128, 8, TT], F32)
        hT = P.alloc("hT", [128, 8, TT], BF16)
        rstd = P.alloc("rstd", [128, TT], F32)
        gsb = [P.alloc("gsb%d" % i, [128, TT], F32) for i in range(2)]
        NPAN = 3
        pans = [P.alloc("pan%d" % i, [128, 4096], BF16) for i in range(NPAN)]
        s5tab = P.alloc("s5tab", [128, 2, 8, 128], BF16)
        xin = P.alloc("xin", [128, D], F32)
        osb = [P.alloc("osb%d" % i, [128, D], F32) for i in range(1)]
        BASE = P.top
        st = {"pan": 0, "g": 0, "o": 0, "s": 0}
        S5T_OFF = ARENA - 8 * TT * 2
        MRG_OFF = BASE + 16 * TT * 2 + 64

        class Phase:
            def __init__(self, base=BASE):
                self.top = base

            def alloc(self, name, shape, dt):
                b = P.view(name, self.top, shape, dt)
                self.top = (b.hi + 31) // 32 * 32
                return b

        def load_params(items, semkey):
            for b, src in items:
                P.op("sp", lambda e, b=b, src=src: e.dma_start(out=b.ap, in_=src), w=[b], dma=semkey)
            sem, total = P.dsems[semkey]
            for b, _ in items:
                P.lastw[b.key] = (sem, total, "dma")

        load_params([(cst, din["cst"]), (cstv, din["cstv"]), (pc8, din["pc8"]), (pc16, din["pc16"]),
                     (convp, din["convp"]), (hp, din["hp"])], "par")
        P.op("dve", lambda e: e.tensor_copy(out=identb.ap, in_=cst[:, IDN, :]), r=[cst], w=[identb])
        P.op("pool", lambda e: e.memset(onesb.ap, 1.0), w=[onesb])
        P.op("dve", lambda e: e.tensor_copy(out=mub.ap, in_=cst[:, MU, :]), r=[cst], w=[mub])
        P.op("dve", lambda e: e.tensor_copy(out=m1b.ap, in_=cst[:, M1, :]), r=[cst], w=[m1b])
        P.op("pool", lambda e: e.memset(hSb.ap, 0.0), w=[hSb])
        P.op("pool", lambda e: e.memset(hS.ap, 0.0), w=[hS])
        P.op("pool", lambda e: e.memset(tail.ap, 0.0), w=[tail])
        P.op("pool", lambda e: e.memset(Sst[0].ap, 0.0), w=[Sst[0]])
        P.op("pool", lambda e: e.memset(Ssw[0].ap, 0.0), w=[Ssw[0]])
        P.op("act", lambda e: e.activation(out=aneg.ap, in_=hp[:, 0:1], func=AF.Exp), r=[hp], w=[aneg])
        P.op("dve", lambda e: e.tensor_scalar(out=aneg.ap, in0=aneg.ap, scalar1=-1.0, scalar2=None, op0=ALU.mult),
             r=[aneg], w=[aneg])

        import os as _os2
        for n, (k, m) in (WSHAPES.items() if int(_os2.environ.get("KSTOP", "9")) >= 0 else []):
            rows = 256
            for r0 in range(0, k, rows):
                rr = min(rows, k - r0)
                P.op("pool", lambda e, n=n, r0=r0, rr=rr: e.dma_start(
                    out=wbf[n][r0:r0 + rr, :], in_=din[n][r0:r0 + rr, :]), w=[n + "_bf"], dma="cast_" + n)

        def get_pan():
            p = pans[st["pan"] % NPAN]
            st["pan"] += 1
            return p

        def proj(wname, col0, ncols, KC, rhs, consume, pw=None):
            W = wbf[wname]
            if pw is None:
                pw = min(512, (4096 // KC) // 128 * 128)
            j = 0
            for p0 in range(0, ncols, pw):
                w_ = min(pw, ncols - p0)
                pan = get_pan()
                pv = pan.ap[:, 0:KC * w_].rearrange("p (c n) -> p c n", c=KC)
                P.op("sp", lambda e, pv=pv, p0=p0, w_=w_: e.dma_start(
                    out=pv, in_=W[:, col0 + p0:col0 + p0 + w_].rearrange("(c p) n -> p c n", p=128)),
                    r=[wname + "_bf"], w=[pan], dma=pan.key)
                for q0 in range(0, w_, 128):
                    cw = min(128, w_ - q0)
                    ps = next_ps()
                    for c in range(KC):
                        rap, rdep = rhs(c)
                        P.op("pe", lambda e, pv=pv, ps=ps, c=c, q0=q0, cw=cw, rap=rap: e.matmul(
                            ps[0:cw, :], lhsT=pv[:, c, q0:q0 + cw], rhs=rap, start=(c == 0), stop=(c == KC - 1)),
                            r=[pan, rdep], w=[ps])
                    consume(j, ps, cw)
                    j += 1

        def rstd_from(ps, scale):
            P.op("dve", lambda e, ps=ps: e.tensor_scalar(out=rstd.ap, in0=ps[:, :], scalar1=scale, scalar2=EPS,
                                                         op0=ALU.mult, op1=ALU.add), r=[ps], w=[rstd])
            P.op("act", lambda e: e.activation(out=rstd.ap, in_=rstd.ap, func=AF.Ln), r=[rstd], w=[rstd])
            P.op("act", lambda e: e.activation(out=rstd.ap, in_=rstd.ap, func=AF.Exp, scale=-0.5), r=[rstd], w=[rstd])

        def rmsnorm_to(dst, gcol, sqbuf):
            for c in range(8):
                P.op("act", lambda e, c=c: e.activation(out=sqbuf[:, c, :], in_=xres[:, c, :], func=AF.Square),
                     r=[(xres, c)], w=[(sqbuf, c)])
            ps = next_ps()
            for c in range(8):
                P.op("pe", lambda e, c=c, ps=ps: e.matmul(ps[:, :], lhsT=onesb.ap, rhs=sqbuf[:, c, :],
                                                            start=(c == 0), stop=(c == 7)),
                     r=[onesb, (sqbuf, c)], w=[ps])
            rstd_from(ps, 1.0 / D)
            for c in range(8):
                P.op("dve", lambda e, c=c: e.scalar_tensor_tensor(
                    out=dst[:, c, :], in0=xres[:, c, :], scalar=pc8[:, gcol, c:c + 1], in1=rstd.ap,
                    op0=ALU.mult, op1=ALU.mult), r=[(xres, c), pc8, rstd], w=[(dst, c)])

        def load_x(tile):
            for s in range(4):
                src = xs[tile * TT + s * 128: tile * TT + (s + 1) * 128, :]
                P.op("sp", lambda e, src=src: e.dma_start(out=xin.ap, in_=src), w=[xin], dma="xin")
                for c0 in range(0, 8, 4):
                    ps = next_ps()
                    for c in range(c0, c0 + 4):
                        P.op("pe", lambda e, c=c, c0=c0, ps=ps: e.transpose(
                            out=ps[:, (c - c0) * 128:(c - c0 + 1) * 128], in_=xin[:, c * 128:(c + 1) * 128],
                            identity=cst[:, IDN, :]), r=[xin, cst], w=[ps])
                    P.op("act", lambda e, c0=c0, s=s, ps=ps: e.activation(
                        out=xres[:, c0:c0 + 4, s * 128:(s + 1) * 128],
                        in_=ps[:, :].rearrange("p (c t) -> p c t", c=4), func=AF.Copy),
                        r=[ps], w=[(xres, c0, c0 + 4)])

        def dump(name, buf, nch):
            if not DEBUG:
                return
            for c in range(nch):
                g = gsb[st["g"] % 2]
                st["g"] += 1
                P.op("dve", lambda e, g=g, c=c: e.tensor_copy(out=g.ap, in_=buf[:, c, :]), r=[(buf, c)], w=[g])
                tok = P.op("sp", lambda e, g=g, c=c: e.dma_start(out=dbg[name][:, c, :], in_=g.ap),
                           r=[g], w=[name], dma=g.key)
                P.final[g.key] = (tok[0], tok[1])

        def ffn(pref, gcol):
            ph = Phase()
            act = ph.alloc("act", [128, 22, TT], BF16)
            sqb = ph.alloc("sqb", [128, 8, TT], BF16)
            rmsnorm_to(hT, gcol, sqb)
            for p0 in range(0, DFF, 512):
                pw = min(512, DFF - p0)
                pp = []
                for wn in (pref + "_w_gate", pref + "_w_up"):
                    pan = get_pan()
                    pv = pan.ap[:, 0:8 * pw].rearrange("p (c n) -> p c n", c=8)
                    P.op("sp", lambda e, pv=pv, wn=wn, p0=p0, pw=pw: e.dma_start(
                        out=pv, in_=wbf[wn][:, p0:p0 + pw].rearrange("(c p) n -> p c n", p=128)),
                        r=[wn + "_bf"], w=[pan], dma=pan.key)
                    pp.append((pan, pv))
                for j in range(pw // 128):
                    oc = (p0 // 128) + j
                    psg, psu = next_ps(), next_ps()
                    for (pan, pv), ps in ((pp[0], psg), (pp[1], psu)):
                        for c in range(8):
                            P.op("pe", lambda e, pv=pv, ps=ps, c=c, j=j: e.matmul(
                                ps[:, :], lhsT=pv[:, c, j * 128:(j + 1) * 128], rhs=hT[:, c, :],
                                start=(c == 0), stop=(c == 7)), r=[pan, (hT, c)], w=[ps])
                    g = gsb[st["g"] % 2]
                    st["g"] += 1
                    P.op("act", lambda e, g=g, psg=psg: e.activation(out=g.ap, in_=psg[:, :], func=AF.Silu),
                         r=[psg], w=[g])
                    P.op("dve", lambda e, g=g, psu=psu, oc=oc: e.tensor_tensor(
                        out=act[:, oc, :], in0=psu[:, :], in1=g.ap, op=ALU.mult), r=[psu, g], w=[(act, oc)])

            def fin(j, ps, cw):
                P.op("dve", lambda e, ps=ps, j=j: e.scalar_tensor_tensor(
                    out=xres[:, j, :], in0=ps[:, :], scalar=0.5, in1=xres[:, j, :],
                    op0=ALU.mult, op1=ALU.add), r=[ps, (xres, j)], w=[(xres, j)])
            proj(pref + "_w_down", 0, D, 22, lambda c: (act[:, c, :], (act, c)), fin, pw=128)

        def s5_setup():
            ph = Phase()
            A = ph.alloc("s5a", [128, 3, 64], F32)
            Bi = ph.alloc("s5b", [128, 2, 64, 16], F32)
            Ci = ph.alloc("s5c", [128, 2, 64, 16], F32)
            load_params([(A, din["s5a"]), (Bi, din["s5b"]), (Ci, din["s5c"])], "par2")
            T = {}
            TMP = Buf(None, "s5tmp", "arena", BASE, ARENA, [128, ARENA - BASE], U8)
            RD = [A, Bi, Ci, TMP, cstv]

            def t(name):
                if name not in T:
                    T[name] = ph.alloc("t_" + name, [128, 64], F32).ap
                return T[name]

            def tt(o, a, b, op):
                P.op("dve", lambda e: e.tensor_tensor(out=o, in0=a, in1=b, op=op), r=RD, w=[TMP])

            def ts(o, a, s1, s2, op0, op1=None):
                if op1 is None:
                    P.op("dve", lambda e: e.tensor_scalar(out=o, in0=a, scalar1=s1, scalar2=None, op0=op0), r=RD, w=[TMP])
                else:
                    P.op("dve", lambda e: e.tensor_scalar(out=o, in0=a, scalar1=s1, scalar2=s2, op0=op0, op1=op1),
                         r=RD, w=[TMP])

            def stt(o, a, s, b, op0, op1):
                P.op("dve", lambda e: e.scalar_tensor_tensor(out=o, in0=a, scalar=s, in1=b, op0=op0, op1=op1),
                     r=RD, w=[TMP])

            def ac(o, a, func, scale=1.0):
                P.op("act", lambda e: e.activation(out=o, in_=a, func=func, scale=scale), r=RD, w=[TMP])

            def rcp(o, a):
                P.op("dve", lambda e: e.reciprocal(out=o, in_=a), r=RD, w=[TMP])

            import os as _os
            SL = int(_os.environ.get("S5STOP", "99"))
            lr, li, ldt = A[:, 0, :], A[:, 1, :], A[:, 2, :]
            ac(t("dt"), ldt, AF.Exp)
            tt(t("lrdt"), lr, t("dt"), ALU.mult)
            tt(t("th"), li, t("dt"), ALU.mult)
            ac(t("mag"), t("lrdt"), AF.Exp)
            ts(t("v"), t("th"), 1.0 / (2 * np.pi), None, ALU.mult)
            if SL <= 1:
                return
            vi = ph.alloc("t_vi", [128, 64], I32).ap
            P.op("dve", lambda e: e.tensor_copy(out=vi, in_=t("v")), r=RD, w=[TMP])
            P.op("dve", lambda e: e.tensor_copy(out=t("vf"), in_=vi), r=RD, w=[TMP])
            tt(t("fr"), t("v"), t("vf"), ALU.subtract)
            ts(t("g"), t("fr"), 0.5, None, ALU.is_gt)
            tt(t("fr"), t("fr"), t("g"), ALU.subtract)
            ts(t("g"), t("fr"), -0.5, None, ALU.is_lt)
            tt(t("fr"), t("fr"), t("g"), ALU.add)
            if SL <= 2:
                return
            ac(t("sin"), t("fr"), AF.Sin, scale=6.28318)
            ts(t("frc"), t("fr"), 0.25, None, ALU.add)
            ts(t("g"), t("frc"), 0.5, None, ALU.is_gt)
            tt(t("frc"), t("frc"), t("g"), ALU.subtract)
            ac(t("cos"), t("frc"), AF.Sin, scale=6.28318)
            if SL <= 3:
                return
            ar, ai = t("ar"), t("ai")
            tt(ar, t("mag"), t("cos"), ALU.mult)
            tt(ai, t("mag"), t("sin"), ALU.mult)
            tt(t("den"), lr, lr, ALU.mult)
            tt(t("d2"), li, li, ALU.mult)
            tt(t("den"), t("den"), t("d2"), ALU.add)
            rcp(t("rden"), t("den"))
            ts(t("am1"), ar, -1.0, None, ALU.add)
            tt(t("x1"), t("am1"), lr, ALU.mult)
            tt(t("x2"), ai, li, ALU.mult)
            tt(t("x1"), t("x1"), t("x2"), ALU.add)
            tt(t("cr"), t("x1"), t("rden"), ALU.mult)
            tt(t("x1"), ai, lr, ALU.mult)
            tt(t("x2"), t("am1"), li, ALU.mult)
            tt(t("x1"), t("x1"), t("x2"), ALU.subtract)
            tt(t("ci"), t("x1"), t("rden"), ALU.mult)
            pr = ph.alloc("t_pr", [128, 9, 64], F32)
            pi = ph.alloc("t_pi", [128, 9, 64], F32)
            P.op("dve", lambda e: e.memset(pr[:, 0, :], 1.0), r=RD, w=[TMP])
            P.op("dve", lambda e: e.memset(pi[:, 0, :], 0.0), r=RD, w=[TMP])

            def cmul(orr, oi, xr, xi, yr, yi):
                tt(t("m1"), xr, yr, ALU.mult)
                tt(t("m2"), xi, yi, ALU.mult)
                tt(t("m3"), xr, yi, ALU.mult)
                tt(t("m4"), xi, yr, ALU.mult)
                tt(orr, t("m1"), t("m2"), ALU.subtract)
                tt(oi, t("m3"), t("m4"), ALU.add)
            for k in range(8):
                cmul(pr[:, k + 1, :], pi[:, k + 1, :], pr[:, k, :], pi[:, k, :], ar, ai)
            tt(t("m1"), pr[:, 8, :], pr[:, 8, :], ALU.mult)
            tt(t("m2"), pi[:, 8, :], pi[:, 8, :], ALU.mult)
            tt(t("m1"), t("m1"), t("m2"), ALU.add)
            rcp(t("rm"), t("m1"))
            tt(t("ir"), pr[:, 8, :], t("rm"), ALU.mult)
            tt(t("ii"), pi[:, 8, :], t("rm"), ALU.mult)
            ts(t("ii"), t("ii"), -1.0, None, ALU.mult)
            P.op("dve", lambda e: e.tensor_copy(out=a8r.ap, in_=pr[:, 8, :]), r=RD, w=[a8r])
            P.op("dve", lambda e: e.tensor_scalar(out=a8s.ap, in0=pi[:, 8, :], scalar1=cstv[:, SGN:SGN + 1], scalar2=None,
                                                  op0=ALU.mult), r=RD, w=[a8s])
            P.op("dve", lambda e: e.tensor_scalar(out=a8n.ap, in0=a8s.ap, scalar1=-1.0, scalar2=None, op0=ALU.mult),
                 r=[a8s], w=[a8n])
            if SL <= 4:
                return
            big = lambda name: ph.alloc(name, [128, 64, 16], F32).ap
            bbr, bbi, BB, BBs, CC1, CC2, w1, w2 = [big(n) for n in ("bbr", "bbi", "BB", "BBs", "CC1", "CC2", "w1", "w2")]

            def bc(ap2d):
                return ap2d.unsqueeze(2).to_broadcast([128, 64, 16])
            Br, Bim, Cr, Cim = Bi[:, 0, :, :], Bi[:, 1, :, :], Ci[:, 0, :, :], Ci[:, 1, :, :]
            crb, cib = bc(t("cr")), bc(t("ci"))
            tt(w1, Br, crb, ALU.mult); tt(w2, Bim, cib, ALU.mult); tt(bbr, w1, w2, ALU.subtract)
            tt(w1, Bim, crb, ALU.mult); tt(w2, Br, cib, ALU.mult); tt(bbi, w1, w2, ALU.add)
            sT, sB = cstv[:, SELT:SELT + 1], cstv[:, SELB:SELB + 1]
            ts(BB, bbr, sT, None, ALU.mult); stt(BB, bbi, sB, BB, ALU.mult, ALU.add)
            ts(BBs, bbi, sT, -1.0, ALU.mult, ALU.mult); stt(BBs, bbr, sB, BBs, ALU.mult, ALU.add)
            ts(CC1, Cim, sB, -1.0, ALU.mult, ALU.mult); stt(CC1, Cr, sT, CC1, ALU.mult, ALU.add)
            ts(CC2, Cim, sT, -1.0, ALU.mult, ALU.mult); ts(w1, Cr, sB, -1.0, ALU.mult, ALU.mult)
            tt(CC2, CC2, w1, ALU.add)
            if SL <= 5:
                return
            NH = 16
            Bpw = ph.alloc("Bpw", [128, NH, 8, 16], F32)
            BJ = ph.alloc("BJ", [128, NH, 8, 16], F32)
            Cpw = ph.alloc("Cpw", [128, NH, 8, 16], F32)
            stB = ph.alloc("stB", [128, 2, 8, 128], BF16)
            stK = ph.alloc("stK", [128, 2, 8, 128], BF16)
            BJh = ph.alloc("BJh", [128, NH, 8, 16], BF16)
            BJl = ph.alloc("BJl", [128, NH, 8, 16], BF16)
            Cph = ph.alloc("Cph", [128, NH, 8, 16], BF16)
            Cpl = ph.alloc("Cpl", [128, NH, 8, 16], BF16)
            fl = lambda b, g: b[:, g, :, :].rearrange("p a b -> p (a b)")

            def bch(ap2d, gs):
                return ap2d[:, gs].unsqueeze(2).to_broadcast([128, NH, 16])
            for hf in range(64 // NH):
                gs = slice(hf * NH, (hf + 1) * NH)
                w1h, w2h = w1[:, 0:NH, :], w2[:, 0:NH, :]
                for j in range(8):
                    zr, zi = bch(pr[:, 7 - j, :], gs), bch(pi[:, 7 - j, :], gs)
                    tt(w1h, BB[:, gs, :], zr, ALU.mult); tt(w2h, BBs[:, gs, :], zi, ALU.mult)
                    tt(Bpw[:, :, j, :], w1h, w2h, ALU.add)
                    cmul(t("qr"), t("qi"), pr[:, 7 - j, :], pi[:, 7 - j, :], t("ir"), t("ii"))
                    zr, zi = bch(t("qr"), gs), bch(t("qi"), gs)
                    tt(w1h, BB[:, gs, :], zr, ALU.mult); tt(w2h, BBs[:, gs, :], zi, ALU.mult)
                    tt(BJ[:, :, j, :], w1h, w2h, ALU.add)
                    zr, zi = bch(pr[:, j + 1, :], gs), bch(pi[:, j + 1, :], gs)
                    tt(w1h, CC1[:, gs, :], zr, ALU.mult); tt(w2h, CC2[:, gs, :], zi, ALU.mult)
                    tt(Cpw[:, :, j, :], w1h, w2h, ALU.add)
                if SL <= 6:
                    continue
                for src_, hi_, lo_ in ((BJ, BJh, BJl), (Cpw, Cph, Cpl)):
                    P.op("dve", lambda e, src_=src_, hi_=hi_: e.tensor_copy(out=hi_.ap, in_=src_.ap), r=RD, w=[TMP])
                    P.op("dve", lambda e, src_=src_, hi_=hi_: e.tensor_tensor(out=src_.ap, in0=src_.ap, in1=hi_.ap,
                                                                           op=ALU.subtract), r=RD, w=[TMP])
                    P.op("dve", lambda e, src_=src_, lo_=lo_: e.tensor_copy(out=lo_.ap, in_=src_.ap), r=RD, w=[TMP])
                for q in range(hf * NH // 8, (hf + 1) * NH // 8):
                    for gl in range(8):
                        g = 8 * q + gl - hf * NH
                        ps = next_ps()
                        psk = next_ps()
                        P.op("pe", lambda e, g=g, ps=ps: e.transpose(out=ps[:, 0:128], in_=fl(Bpw, g), identity=cst[:, IDN, :]),
                             r=[TMP, cst], w=[ps])
                        for ii, (l_, r_) in enumerate(((BJh, Cph), (BJh, Cpl), (BJl, Cph))):
                            P.op("pe", lambda e, g=g, psk=psk, ii=ii, l_=l_, r_=r_: e.matmul(
                                psk[:, 0:128], lhsT=fl(l_, g), rhs=fl(r_, g), start=(ii == 0), stop=(ii == 2)),
                                r=[TMP], w=[psk])
                        P.op("act", lambda e, gl=gl, ps=ps: e.activation(out=stB[:, 0, gl, :], in_=ps[:, 0:128], func=AF.Copy),
                             r=[ps], w=[stB])
                        P.op("act", lambda e, gl=gl, ps=ps: e.activation(out=stB[:, 1, gl, 0:64], in_=ps[:, 64:128], func=AF.Copy),
                             r=[ps], w=[stB])
                        P.op("act", lambda e, gl=gl, ps=ps: e.activation(out=stB[:, 1, gl, 64:128], in_=ps[:, 0:64], func=AF.Copy),
                             r=[ps], w=[stB])
                        P.op("dve", lambda e, gl=gl, psk=psk: e.tensor_tensor(
                            out=stK[:, 0, gl, :], in0=psk[:, 0:128], in1=cst[:, KMASK, :], op=ALU.mult),
                            r=[psk, cst], w=[stK])
                        P.op("dve", lambda e, gl=gl, g=g: e.tensor_copy(out=stK[:, 1, gl, :], in_=fl(Cph, g)),
                             r=[TMP], w=[stK])
                    if int(_os.environ.get("S7", "9")) >= 3:
                        P.op("sp", lambda e, q=q: e.dma_start(out=s5B[q], in_=stB.ap), r=[stB], w=["s5B"], dma="s5B")
                        P.op("sp", lambda e, q=q: e.dma_start(out=s5K[q], in_=stK.ap), r=[stK], w=["s5K"], dma="s5K")
            print("setup phase top", ph.top, "BASE", BASE)

        def s5_tile(light):
            ph = Phase()
            uT = ph.alloc("uT", [128, 8, TT], BF16)
            U2 = ph.alloc("U2", [64, 64, 8, 16], BF16)
            Ucol = ph.alloc("Ucol", [128, 64, 64], BF16)
            cS = ph.alloc("cS", [128, 64, 64], F32)
            cW = ph.alloc("cW", [128, 64, 64], F32)
            Xprev = ph.alloc("Xprev", [128, 64, 64], BF16)
            Ycol = ph.alloc("Ycol", [128, 64, 64], BF16)
            ytmp = ph.alloc("ytmp", [128, TT], F32)
            gtmp = ph.alloc("gtmp", [128, TT], F32)
            r1 = [ph.alloc("r1_%d" % i, [128, 64], F32) for i in range(2)]
            r2 = [ph.alloc("r2_%d" % i, [128, 64], F32) for i in range(2)]

            def cons_u(j, ps, cw):
                P.op("act", lambda e: e.activation(out=uT[:, j, :], in_=ps[:, :], func=AF.Copy), r=[ps], w=[(uT, j)])
            proj("w_in", 0, 1024, 8, lambda c: (hT[:, c, :], (hT, c)), cons_u)
            for q in range(8):
                ps = next_ps()
                pb = psbf(ps)
                for j in range(8):
                    P.op("pe", lambda e, q=q, j=j, pb=pb: e.transpose(
                        out=pb[0:64, j * 128:(j + 1) * 128], in_=uT[:, q, j:TT:8], identity=identb.ap),
                        r=[(uT, q), identb], w=[ps])
                P.op("dve", lambda e, q=q, pb=pb: e.tensor_copy(
                    out=U2[:, 8 * q:8 * q + 8, :, :],
                    in_=pb[0:64, :].rearrange("p (j g m) -> p g j m", j=8, g=8)), r=[ps], w=[U2])
            for g0 in range(0, 64, 16):
                ps = next_ps()
                pb = psbf(ps)
                for g in range(g0, g0 + 16):
                    P.op("pe", lambda e, g=g, g0=g0, pb=pb: e.transpose(
                        out=pb[:, (g - g0) * 64:(g - g0 + 1) * 64], in_=U2[:, g, :, :].rearrange("p a b -> p (a b)"),
                        identity=identb[0:64, 0:64]), r=[U2, identb], w=[ps])
                P.op("act", lambda e, g0=g0, pb=pb: e.activation(
                    out=Ucol[:, g0:g0 + 16, :], in_=pb[:, :].rearrange("p (g b) -> p g b", g=16), func=AF.Copy),
                    r=[ps], w=[(Ucol, g0, g0 + 16)])
            for q in range(8):
                P.op("sp", lambda e, q=q: e.dma_start(out=s5tab.ap, in_=s5B[q]), r=["s5B"], w=[s5tab], dma="s5tab")
                for which, dst in ((0, cS), (1, cW)):
                    ps = next_ps()
                    for gl in range(8):
                        g = 8 * q + gl
                        P.op("pe", lambda e, which=which, gl=gl, g=g, ps=ps: e.matmul(
                            ps[:, gl * 64:(gl + 1) * 64], lhsT=s5tab[:, which, gl, :], rhs=Ucol[:, g, :],
                            start=True, stop=True), r=[s5tab, (Ucol, g)], w=[ps])
                    P.op("act", lambda e, q=q, dst=dst, ps=ps: e.activation(
                        out=dst[:, 8 * q:8 * q + 8, :], in_=ps[:, :].rearrange("p (g b) -> p g b", g=8), func=AF.Copy),
                        r=[ps], w=[(dst, 8 * q, 8 * q + 8)])
            for b in range(64):
                S0, W0 = Sst[st["s"] % 2], Ssw[st["s"] % 2]
                S1, W1 = Sst[(st["s"] + 1) % 2], Ssw[(st["s"] + 1) % 2]
                st["s"] += 1
                if not light:
                    P.op("act", lambda e, b=b, S0=S0: e.activation(out=Xprev[:, :, b], in_=S0.ap, func=AF.Copy),
                         r=[S0], w=[Xprev])
                A, Bq = r1[b % 2], r2[b % 2]
                P.op("dve", lambda e, A=A, S0=S0: e.tensor_tensor(out=A.ap, in0=S0.ap, in1=a8r.ap, op=ALU.mult),
                     r=[S0, a8r], w=[A])
                P.op("pool", lambda e, Bq=Bq, W0=W0: e.tensor_tensor(out=Bq.ap, in0=W0.ap, in1=a8r.ap, op=ALU.mult),
                     r=[W0, a8r], w=[Bq])
                P.op("dve", lambda e, W0=W0, S1=S1: e.tensor_tensor(out=S1.ap, in0=W0.ap, in1=a8s.ap, op=ALU.mult),
                     r=[W0, a8s], w=[S1])
                P.op("pool", lambda e, S0=S0, W1=W1: e.tensor_tensor(out=W1.ap, in0=S0.ap, in1=a8n.ap, op=ALU.mult),
                     r=[S0, a8n], w=[W1])
                P.op("dve", lambda e, A=A, S1=S1: e.tensor_tensor(out=S1.ap, in0=S1.ap, in1=A.ap, op=ALU.add),
                     r=[S1, A], w=[S1])
                P.op("pool", lambda e, Bq=Bq, W1=W1: e.tensor_tensor(out=W1.ap, in0=W1.ap, in1=Bq.ap, op=ALU.add),
                     r=[W1, Bq], w=[W1])
                P.op("dve", lambda e, b=b, S1=S1: e.tensor_tensor(out=S1.ap, in0=S1.ap, in1=cS[:, :, b], op=ALU.add),
                     r=[S1, cS], w=[S1])
                P.op("pool", lambda e, b=b, W1=W1: e.tensor_tensor(out=W1.ap, in0=W1.ap, in1=cW[:, :, b], op=ALU.add),
                     r=[W1, cW], w=[W1])
            if light:
                return None
            s5T = P.view("s5T", S5T_OFF, [128, 8, TT], BF16)
            gel = ph.alloc("gel", [128, 8, TT], BF16)
            assert ph.top <= S5T_OFF, ph.top
            for q in range(8):
                P.op("sp", lambda e, q=q: e.dma_start(out=s5tab.ap, in_=s5K[q]), r=["s5K"], w=[s5tab], dma="s5tab")
                ps = next_ps()
                for gl in range(8):
                    g = 8 * q + gl
                    P.op("pe", lambda e, gl=gl, g=g, ps=ps: e.matmul(
                        ps[:, gl * 64:(gl + 1) * 64], lhsT=s5tab[:, 0, gl, :], rhs=Ucol[:, g, :],
                        start=True, stop=False), r=[s5tab, (Ucol, g)], w=[ps])
                    P.op("pe", lambda e, gl=gl, g=g, ps=ps: e.matmul(
                        ps[:, gl * 64:(gl + 1) * 64], lhsT=s5tab[:, 1, gl, :], rhs=Xprev[:, g, :],
                        start=False, stop=True), r=[s5tab, Xprev], w=[ps])
                P.op("act", lambda e, q=q, ps=ps: e.activation(
                    out=Ycol[:, 8 * q:8 * q + 8, :], in_=ps[:, :].rearrange("p (g b) -> p g b", g=8), func=AF.Copy),
                    r=[ps], w=[(Ycol, 8 * q, 8 * q + 8)])
            Y2 = P.view("U2", U2.lo, [64, 8, 64, 16], BF16)
            for g0 in range(0, 64, 8):
                ps = next_ps()
                pb = psbf(ps)
                for g in range(g0, g0 + 8):
                    P.op("pe", lambda e, g=g, g0=g0, pb=pb: e.transpose(
                        out=pb[0:64, (g - g0) * 128:(g - g0 + 1) * 128], in_=Ycol[:, g, :], identity=identb.ap),
                        r=[(Ycol, g), identb], w=[ps])
                P.op("dve", lambda e, g0=g0, pb=pb: e.tensor_copy(
                    out=Y2[:, :, g0:g0 + 8, :],
                    in_=pb[0:64, :].rearrange("p (g j m) -> p j g m", g=8, j=8)), r=[ps], w=[Y2])
            for q in range(8):
                ps = next_ps()
                pb = psbf(ps)
                for j in range(8):
                    P.op("pe", lambda e, q=q, j=j, pb=pb: e.transpose(
                        out=pb[:, j * 64:(j + 1) * 64],
                        in_=Y2[:, j, 8 * q:8 * q + 8, :].rearrange("p a b -> p (a b)"),
                        identity=identb[0:64, 0:64]), r=[Y2, identb], w=[ps])
                P.op("dve", lambda e, q=q, pb=pb: e.scalar_tensor_tensor(
                    out=ytmp.ap.rearrange("p (b j) -> p j b", j=8),
                    in0=uT[:, q, :].rearrange("p (b j) -> p j b", j=8),
                    scalar=pc8[:, V_S5D, q:q + 1],
                    in1=pb[:, 0:512].rearrange("p (j b) -> p j b", j=8),
                    op0=ALU.mult, op1=ALU.add), r=[ps, (uT, q), pc8], w=[ytmp])
                P.op("pool", lambda e: e.tensor_tensor(out=gtmp.ap, in0=ytmp.ap, in1=ytmp.ap, op=ALU.mult),
                     r=[ytmp], w=[gtmp])
                P.op("pool", lambda e: e.tensor_scalar(out=gtmp.ap, in0=gtmp.ap, scalar1=0.044715, scalar2=1.0,
                                                       op0=ALU.mult, op1=ALU.add), r=[gtmp], w=[gtmp])
                P.op("pool", lambda e: e.tensor_tensor(out=gtmp.ap, in0=gtmp.ap, in1=ytmp.ap, op=ALU.mult),
                     r=[gtmp, ytmp], w=[gtmp])
                P.op("act", lambda e: e.activation(out=gtmp.ap, in_=gtmp.ap, func=AF.Sigmoid, scale=1.5957691216),
                     r=[gtmp], w=[gtmp])
                P.op("dve", lambda e, q=q: e.tensor_tensor(out=gel[:, q, :], in0=gtmp.ap, in1=ytmp.ap, op=ALU.mult),
                     r=[gtmp, ytmp], w=[(gel, q)])

            def cons_glu(j, ps, cw):
                g = gsb[st["g"] % 2]
                st["g"] += 1
                P.op("act", lambda e: e.activation(out=g.ap, in_=ps[:, :], func=AF.Sigmoid,
                                                   bias=pc8[:, V_BGLU, j:j + 1]), r=[ps, pc8], w=[g])
                P.op("dve", lambda e: e.tensor_tensor(out=s5T[:, j, :], in0=gel[:, j, :], in1=g.ap, op=ALU.mult),
                     r=[g, (gel, j)], w=[(s5T, j)])
            proj("s5_w_glu", 0, 1024, 8, lambda c: (gel[:, c, :], (gel, c)), cons_glu)
            return s5T

        def ssd_tile(light):
            ph = Phase()
            xbcp = ph.alloc("xbcp", [128, 32, TT + 3], BF16)
            ySS = P.view("ySS", xbcp.lo, [128, 16, TT], BF16)
            xsT = ph.alloc("xsT", [128, 16, TT], BF16)
            Bfm = ph.alloc("Bfm", [128, 8, TT], BF16)
            Cfm = ph.alloc("Cfm", [128, 8, TT], BF16)
            xtok = ph.alloc("xtok", [128, 2048], BF16)
            Btok = ph.alloc("Btok", [128, 1024], BF16)
            dtT = ph.alloc("dtT", [32, TT], F32)
            dAT = ph.alloc("dAT", [32, TT], F32)
            dtok = ph.alloc("dtok", [128, 4, 64], F32)
            dAh = ph.alloc("dAh", [128, 4, 32], BF16)
            dAl = ph.alloc("dAl", [128, 4, 32], BF16)
            dAr = ph.alloc("dAr", [128, 4, 32], F32)
            acs = ph.alloc("acs", [128, 64], F32)
            wgt = ph.alloc("wgt", [128, 32], F32)
            cdec = ph.alloc("cdec", [128, 32], F32)
            dg = [ph.alloc("dg%d" % i, [128, 4, 128], BF16) for i in range(4)]
            cbm = [ph.alloc("cbm%d" % i, [128, 128], F32) for i in range(2)]
            rhsM = [ph.alloc("rhsM%d" % i, [128, 2, 128], BF16) for i in range(2)]
            ed = [ph.alloc("ed%d" % i, [128, 256], F32) for i in range(2)]
            wTb = [ph.alloc("wT%d" % i, [128, 128], BF16) for i in range(2)]
            Csb = [ph.alloc("Cs%d" % i, [128, 128], BF16) for i in range(2)]
            xwb = [ph.alloc("xw%d" % i, [128, 256], BF16) for i in range(2)]
            htmp = [ph.alloc("htmp%d" % i, [128, 4, 64], F32) for i in range(2)]
            sqz = [ph.alloc("sqz%d" % i, [128, TT], BF16) for i in range(2)]
            assert ph.top <= S5T_OFF, ph.top
            cnt = {"h": 0, "g": 0}

            P.op("pool", lambda e: e.tensor_copy(out=xbcp[:, :, 0:3], in_=tail.ap), r=[tail], w=[xbcp])

            def cons_xbc(j, ps, cw):
                P.op("act", lambda e: e.activation(out=xbcp[:, j, 3:TT + 3], in_=ps[:, :], func=AF.Copy),
                     r=[ps], w=[(xbcp, j)])
            proj("w_in", 3072, 4096, 8, lambda c: (hT[:, c, :], (hT, c)), cons_xbc)
            P.op("pool", lambda e: e.tensor_copy(out=tail.ap, in_=xbcp[:, :, TT:TT + 3]), r=[xbcp], w=[tail])
            for oc in range(32):
                d = dg[oc % 4]
                for k in range(4):
                    P.op("act", lambda e, d=d, k=k, oc=oc: e.activation(
                        out=d[:, k, :], in_=identb.ap, func=AF.Copy, scale=convp[:, oc, k:k + 1]),
                        r=[identb, convp], w=[(d, k)])
                ps = next_ps()
                for k in range(4):
                    P.op("pe", lambda e, d=d, k=k, oc=oc, ps=ps: e.matmul(
                        ps[:, :], lhsT=d[:, k, :], rhs=xbcp[:, oc, k:k + TT], start=(k == 0), stop=(k == 3)),
                        r=[(d, k), (xbcp, oc)], w=[ps])
                if oc < 16:
                    dst, dk = xsT[:, oc, :], (xsT, oc)
                elif oc < 24:
                    dst, dk = Bfm[:, oc - 16, :], (Bfm, oc - 16)
                else:
                    dst, dk = Cfm[:, oc - 24, :], (Cfm, oc - 24)
                P.op("act", lambda e, dst=dst, ps=ps, oc=oc: e.activation(
                    out=dst, in_=ps[:, :], func=AF.Silu, bias=convp[:, oc, 4:5]), r=[ps, convp], w=[dk])

            def cons_dt(j, ps, cw):
                P.op("act", lambda e: e.activation(out=dtT.ap, in_=ps[0:32, :], func=AF.Exp, bias=hp[:, 1:2]),
                     r=[ps, hp], w=[dtT])
                P.op("act", lambda e: e.activation(out=dtT.ap, in_=dtT.ap, func=AF.Ln, bias=1.0), r=[dtT], w=[dtT])
                P.op("dve", lambda e: e.tensor_scalar(out=dAT.ap, in0=dtT.ap, scalar1=aneg[:, 0:1], scalar2=None,
                                                      op0=ALU.mult), r=[dtT, aneg], w=[dAT])
            proj("w_in", 7168, 32, 8, lambda c: (hT[:, c, :], (hT, c)), cons_dt, pw=128)
            ps = next_ps()
            for s in range(4):
                P.op("pe", lambda e, s=s, ps=ps: e.transpose(
                    out=ps[:, s * 64:s * 64 + 32], in_=dtT[:, s * 128:(s + 1) * 128], identity=cst[0:32, IDN, 0:32]),
                    r=[dtT, cst], w=[ps])
                P.op("pe", lambda e, s=s, ps=ps: e.transpose(
                    out=ps[:, s * 64 + 32:s * 64 + 64], in_=dAT[:, s * 128:(s + 1) * 128], identity=cst[0:32, IDN, 0:32]),
                    r=[dAT, cst], w=[ps])
            P.op("dve", lambda e, ps=ps: e.tensor_copy(out=dtok.ap, in_=ps[:, 0:256].rearrange("p (s c) -> p s c", s=4)),
                 r=[ps], w=[dtok])
            P.op("dve", lambda e: e.tensor_copy(out=dAh.ap, in_=dtok[:, :, 32:64]), r=[dtok], w=[dAh])
            P.op("dve", lambda e: e.tensor_tensor(out=dAr.ap, in0=dtok[:, :, 32:64], in1=dAh.ap, op=ALU.subtract),
                 r=[dtok, dAh], w=[dAr])
            P.op("dve", lambda e: e.tensor_copy(out=dAl.ap, in_=dAr.ap), r=[dAr], w=[dAl])

            for s in range(4):
                tk = slice(s * 128, (s + 1) * 128)
                for c0 in range(0, 24, 8):
                    ps = next_ps()
                    pb = psbf(ps)
                    for c in range(c0, c0 + 8):
                        src, sk = (xsT[:, c, tk], (xsT, c)) if c < 16 else (Bfm[:, c - 16, tk], (Bfm, c - 16))
                        P.op("pe", lambda e, c=c, c0=c0, pb=pb, src=src: e.transpose(
                            out=pb[:, (c - c0) * 128:(c - c0 + 1) * 128], in_=src, identity=identb.ap),
                            r=[sk, identb], w=[ps])
                    if c0 < 16:
                        P.op("act", lambda e, c0=c0, pb=pb: e.activation(
                            out=xtok[:, c0 * 128:(c0 + 8) * 128], in_=pb[:, :], func=AF.Copy), r=[ps], w=[xtok])
                    else:
                        P.op("act", lambda e, pb=pb: e.activation(out=Btok.ap, in_=pb[:, :], func=AF.Copy),
                             r=[ps], w=[Btok])
                ps = next_ps()
                for ii, dd in enumerate((dAh, dAl)):
                    P.op("pe", lambda e, s=s, ps=ps, ii=ii, dd=dd: e.matmul(ps[:, 0:32], lhsT=mub.ap, rhs=dd[:, s, :],
                                                                            start=(ii == 0), stop=(ii == 1)), r=[mub, dd], w=[ps])
                for ii, dd in enumerate((dAh, dAl)):
                    P.op("pe", lambda e, s=s, ps=ps, ii=ii, dd=dd: e.matmul(ps[:, 32:64], lhsT=onesb.ap, rhs=dd[:, s, :],
                                                                            start=(ii == 0), stop=(ii == 1)), r=[onesb, dd], w=[ps])
                P.op("act", lambda e, ps=ps: e.activation(out=acs.ap, in_=ps[:, 0:64], func=AF.Copy), r=[ps], w=[acs])
                P.op("dve", lambda e: e.tensor_tensor(out=wgt.ap, in0=acs[:, 32:64], in1=acs[:, 0:32], op=ALU.subtract),
                     r=[acs], w=[wgt])
                P.op("act", lambda e: e.activation(out=wgt.ap, in_=wgt.ap, func=AF.Exp), r=[wgt], w=[wgt])
                P.op("dve", lambda e, s=s: e.tensor_tensor(out=wgt.ap, in0=wgt.ap, in1=dtok[:, s, 0:32], op=ALU.mult),
                     r=[wgt, dtok], w=[wgt])
                P.op("act", lambda e: e.activation(out=cdec.ap, in_=acs[:, 32:64], func=AF.Exp), r=[acs], w=[cdec])
                for g in range(8):
                    if not light:
                        cb = cbm[cnt["g"] % 2]
                        ps = next_ps()
                        P.op("pe", lambda e, g=g, ps=ps, tk=tk: e.matmul(ps[:, 0:128], lhsT=Bfm[:, g, tk], rhs=Cfm[:, g, tk],
                                                                    start=True, stop=True),
                             r=[(Bfm, g), (Cfm, g)], w=[ps])
                        P.op("dve", lambda e, cb=cb, ps=ps: e.tensor_tensor(out=cb.ap, in0=ps[:, 0:128], in1=cst[:, MU, :],
                                                                            op=ALU.mult), r=[ps, cst], w=[cb])
                        psy = next_ps()
                        for k in range(4):
                            h = 4 * g + k
                            i = cnt["h"] % 2
                            cnt["h"] += 1
                            rm, e_, wt, cs = rhsM[i], ed[i], wTb[i], Csb[i]
                            for ii, dd in enumerate((dAh, dAl)):
                                P.op("dve", lambda e, rm=rm, s=s, h=h, ii=ii, dd=dd: e.tensor_scalar(
                                    out=rm[:, ii, :], in0=mub.ap, scalar1=dd[:, s, h:h + 1], scalar2=None,
                                    op0=ALU.mult), r=[mub, dd], w=[(rm, ii)])
                            ps2 = next_ps()
                            for ii in range(2):
                                P.op("pe", lambda e, rm=rm, ps2=ps2, ii=ii: e.matmul(
                                    ps2[:, 0:128], lhsT=m1b.ap, rhs=rm[:, ii, :], start=(ii == 0), stop=(ii == 1)),
                                    r=[m1b, (rm, ii)], w=[ps2])
                            for ii in range(2):
                                P.op("pe", lambda e, rm=rm, ps2=ps2, ii=ii: e.matmul(
                                    ps2[:, 128:256], lhsT=onesb.ap, rhs=rm[:, ii, :], start=(ii == 0), stop=(ii == 1)),
                                    r=[onesb, (rm, ii)], w=[ps2])
                            P.op("act", lambda e, e_=e_, ps2=ps2: e.activation(out=e_.ap, in_=ps2[:, 0:256], func=AF.Exp),
                                 r=[ps2], w=[e_])
                            P.op("dve", lambda e, wt=wt, e_=e_, cb=cb, s=s, h=h: e.scalar_tensor_tensor(
                                out=wt.ap, in0=e_[:, 0:128], scalar=dtok[:, s, h:h + 1], in1=cb.ap,
                                op0=ALU.mult, op1=ALU.mult), r=[e_, cb, dtok], w=[wt])
                            P.op("pool", lambda e, cs=cs, e_=e_, g=g, tk=tk: e.tensor_tensor(
                                out=cs.ap, in0=Cfm[:, g, tk], in1=e_[:, 128:256], op=ALU.mult),
                                r=[(Cfm, g), e_], w=[cs])
                            po, co = 64 * (k % 2), 128 * (k // 2)
                            P.op("pe", lambda e, psy=psy, po=po, co=co, h=h, wt=wt: e.matmul(
                                psy[po:po + 64, co:co + 128], lhsT=xtok[:, h * 64:(h + 1) * 64], rhs=wt.ap,
                                start=True, stop=False), r=[xtok, wt], w=[psy])
                            P.op("pe", lambda e, psy=psy, po=po, co=co, h=h, cs=cs: e.matmul(
                                psy[po:po + 64, co:co + 128], lhsT=hSb[:, h, :], rhs=cs.ap,
                                start=False, stop=True), r=[(hSb, h), cs], w=[psy])
                        for cl in range(2):
                            cc = 2 * g + cl
                            P.op("dve", lambda e, psy=psy, cl=cl, cc=cc, tk=tk: e.scalar_tensor_tensor(
                                out=ySS[:, cc, tk], in0=xsT[:, cc, tk], scalar=pc16[:, V_SSDD, cc:cc + 1],
                                in1=psy[:, cl * 128:(cl + 1) * 128], op0=ALU.mult, op1=ALU.add),
                                r=[psy, (xsT, cc), pc16], w=[(ySS, cc)])
                    xw = xwb[cnt["g"] % 2]
                    ht = htmp[cnt["g"] % 2]
                    cnt["g"] += 1
                    P.op("dve", lambda e, xw=xw, g=g: e.tensor_tensor(
                        out=xw.ap.rearrange("p (k d) -> p k d", k=4),
                        in0=xtok[:, g * 256:(g + 1) * 256].rearrange("p (k d) -> p k d", k=4),
                        in1=wgt[:, 4 * g:4 * g + 4].unsqueeze(2).to_broadcast([128, 4, 64]), op=ALU.mult),
                        r=[xtok, wgt], w=[xw])
                    ps3 = next_ps()
                    P.op("pe", lambda e, ps3=ps3, g=g, xw=xw: e.matmul(
                        ps3[:, 0:256], lhsT=Btok[:, g * 128:(g + 1) * 128], rhs=xw.ap, start=True, stop=True),
                        r=[Btok, xw], w=[ps3])
                    P.op("pool", lambda e, ht=ht, g=g: e.tensor_tensor(
                        out=ht.ap, in0=hS[:, 4 * g:4 * g + 4, :],
                        in1=cdec[:, 4 * g:4 * g + 4].unsqueeze(2).to_broadcast([128, 4, 64]), op=ALU.mult),
                        r=[(hS, 4 * g, 4 * g + 4), cdec], w=[ht])
                    P.op("dve", lambda e, ht=ht, g=g, ps3=ps3: e.tensor_tensor(
                        out=hS[:, 4 * g:4 * g + 4, :], in0=ps3[:, 0:256].rearrange("p (k d) -> p k d", k=4),
                        in1=ht.ap, op=ALU.add), r=[ps3, ht], w=[(hS, 4 * g, 4 * g + 4)])
                    P.op("act", lambda e, g=g: e.activation(out=hSb[:, 4 * g:4 * g + 4, :], in_=hS[:, 4 * g:4 * g + 4, :],
                                                            func=AF.Copy), r=[(hS, 4 * g, 4 * g + 4)], w=[(hSb, 4 * g, 4 * g + 4)])
            if light:
                return None
            if DEBUG and st.get("dbg_ssd_done") is None:
                st["dbg_ssd_done"] = 1
                dump("dbg_xs", xsT, 16); dump("dbg_B", Bfm, 8); dump("dbg_C", Cfm, 8); dump("dbg_y", ySS, 16)
            pss = psb[7]

            def cons_z(j, ps, cw):
                g = gsb[st["g"] % 2]
                st["g"] += 1
                sq_ = sqz[j % 2]
                P.op("act", lambda e: e.activation(out=g.ap, in_=ps[:, :], func=AF.Silu), r=[ps], w=[g])
                P.op("dve", lambda e: e.tensor_tensor(out=ySS[:, j, :], in0=ySS[:, j, :], in1=g.ap, op=ALU.mult),
                     r=[(ySS, j), g], w=[(ySS, j)])
                P.op("act", lambda e: e.activation(out=sq_.ap, in_=ySS[:, j, :], func=AF.Square), r=[(ySS, j)], w=[sq_])
                P.op("pe", lambda e: e.matmul(pss[:, :], lhsT=onesb.ap, rhs=sq_.ap, start=(j == 0), stop=(j == 15)),
                     r=[onesb, sq_], w=[pss])
            proj("w_in", 1024, 2048, 8, lambda c: (hT[:, c, :], (hT, c)), cons_z)
            rstd_from(pss, 1.0 / 2048)
            for c in range(16):
                P.op("dve", lambda e, c=c: e.scalar_tensor_tensor(
                    out=ySS[:, c, :], in0=ySS[:, c, :], scalar=pc16[:, V_SSDN, c:c + 1], in1=rstd.ap,
                    op0=ALU.mult, op1=ALU.mult), r=[(ySS, c), pc16, rstd], w=[(ySS, c)])
            return ySS

        def merge_tile(s5T, ynT):
            G = P.view("G", MRG_OFF, [128, 16, TT], BF16)
            Q = P.view("Q", MRG_OFF + 16 * TT * 2, [128, 8, TT], BF16)
            mT = P.view("mT", MRG_OFF + 24 * TT * 2, [128, 8, TT], BF16)
            assert MRG_OFF + 32 * TT * 2 <= S5T_OFF and MRG_OFF >= ynT.hi, (MRG_OFF, ynT.hi, S5T_OFF)

            def cons_g(j, ps, cw):
                P.op("act", lambda e: e.activation(out=G[:, j, :], in_=ps[:, :], func=AF.Sigmoid,
                                                   bias=pc16[:, V_BGATE, j:j + 1]), r=[ps, pc16], w=[(G, j)])
            proj("w_in", 7200, 2048, 8, lambda c: (hT[:, c, :], (hT, c)), cons_g)

            def cons_q(j, ps, cw):
                P.op("dve", lambda e: e.tensor_tensor(out=Q[:, j, :], in0=ps[:, :], in1=G[:, j, :], op=ALU.mult),
                     r=[ps, (G, j)], w=[(Q, j)])
            proj("w_proj_s5", 0, 1024, 8, lambda c: (s5T[:, c, :], (s5T, c)), cons_q)

            def cons_m(j, ps, cw):
                g = gsb[st["g"] % 2]
                st["g"] += 1
                P.op("dve", lambda e: e.tensor_tensor(out=g.ap, in0=ps[:, :], in1=G[:, 8 + j, :], op=ALU.mult),
                     r=[ps, (G, 8 + j)], w=[g])
                P.op("pool", lambda e: e.tensor_tensor(out=mT[:, j, :], in0=g.ap, in1=Q[:, j, :], op=ALU.add),
                     r=[g, (Q, j)], w=[(mT, j)])
            proj("w_proj_ssd", 0, 1024, 16, lambda c: (ynT[:, c, :], (ynT, c)), cons_m)

            def cons_o(j, ps, cw):
                P.op("dve", lambda e: e.tensor_tensor(out=xres[:, j, :], in0=ps[:, :], in1=xres[:, j, :], op=ALU.add),
                     r=[ps, (xres, j)], w=[(xres, j)])
            proj("w_out", 0, 1024, 8, lambda c: (mT[:, c, :], (mT, c)), cons_o)
            return mT

        def store_out(t_own):
            ph = Phase()
            sqb = ph.alloc("sqb", [128, 8, TT], BF16)
            rmsnorm_to(xres, G_FIN, sqb)
            for s in range(4):
                ob = osb[0]
                for c0 in range(0, 8, 4):
                    ps = next_ps()
                    for c in range(c0, c0 + 4):
                        P.op("pe", lambda e, ps=ps, c=c, c0=c0, s=s: e.transpose(
                            out=ps[:, (c - c0) * 128:(c - c0 + 1) * 128], in_=xres[:, c, s * 128:(s + 1) * 128],
                            identity=cst[:, IDN, :]), r=[(xres, c), cst], w=[ps])
                    P.op("act", lambda e, ps=ps, c0=c0, ob=ob: e.activation(
                        out=ob[:, c0 * 128:(c0 + 4) * 128], in_=ps[:, :], func=AF.Copy), r=[ps], w=[ob])
                r0 = t_own * TT + s * 128
                tok = P.op("sp", lambda e, ob=ob, r0=r0: e.dma_start(out=out[r0:r0 + 128, :], in_=ob.ap),
                           r=[ob], w=["outdram"], dma=ob.key)
                P.final[ob.key] = (tok[0], tok[1])

        import os as _os
        STOP = int(_os.environ.get("KSTOP", "9"))
        if STOP >= 1 or STOP == -1:
            s5_setup()
        for tile in range(NT_PRE + NT_OWN if STOP >= 0 else 0):
            own = tile >= NT_PRE
            light = LIGHT and not own
            if tile == NT_PRE:
                fl_ = cstv[:, FLAG:FLAG + 1]
                for b_ in (Sst[st["s"] % 2], Ssw[st["s"] % 2], hS, hSb, tail):
                    P.op("dve", lambda e, b_=b_: e.tensor_scalar(out=b_.ap, in0=b_.ap, scalar1=fl_, scalar2=None,
                                                                 op0=ALU.mult), r=[b_, cstv], w=[b_])
            load_x(tile)
            ffn("ffn1", G_FFN1)
            if tile == NT_PRE:
                dump("dbg_x1", xres, 8)
            s5T = ynT = None
            if STOP >= 2:
                ph = Phase()
                sqb = ph.alloc("sqb", [128, 8, TT], BF16)
                rmsnorm_to(hT, G_MIX, sqb)
                s5T = s5_tile(light)
            if STOP >= 3:
                ynT = ssd_tile(light)
            if own:
                if STOP >= 4:
                    if tile == NT_PRE:
                        dump("dbg_s5", s5T, 8)
                        dump("dbg_ssd", ynT, 16)
                    mT = merge_tile(s5T, ynT)
                    if tile == NT_PRE:
                        dump("dbg_m", mT, 8)
                    ffn("ffn2", G_FFN2)
                store_out(tile - NT_PRE)
        P.emit()
    return nc


_CACHE = {}


def _pc(v, n):
    return np.ascontiguousarray(np.asarray(v, np.float32).reshape(n, 128).T)


def prep_inputs(inputs):
    f = lambda k: np.asarray(inputs[k], dtype=np.float32)
    common = {}
    for n in WSHAPES:
        common[n] = np.ascontiguousarray(f(n)[0])
    pc8 = np.stack([_pc(f("ffn1_norm")[0], 8), _pc(f("mix_norm")[0], 8), _pc(f("ffn2_norm")[0], 8),
                    _pc(f("final_norm"), 8), _pc(f("s5_D")[0], 8), _pc(f("s5_b_glu")[0], 8)], axis=1)
    common["pc8"] = np.ascontiguousarray(pc8)
    pc16 = np.stack([_pc(f("ssd_norm")[0], 16), _pc(f("b_gate")[0], 16),
                     _pc(np.repeat(f("ssd_D")[0], 64), 16)], axis=1)
    common["pc16"] = np.ascontiguousarray(pc16)
    cw = f("conv_w")[0]
    convp = np.zeros((128, 32, 5), np.float32)
    for k in range(4):
        convp[:, :, k] = _pc(cw[k], 32)
    convp[:, :, 4] = _pc(f("conv_b")[0], 32)
    common["convp"] = convp
    common["hp"] = np.ascontiguousarray(np.stack([f("ssd_A_log")[0], f("ssd_dt_bias")[0]], axis=1))
    dup = lambda a: np.concatenate([a, a], axis=0)
    s5a = np.stack([dup(f("s5_A_re")[0].T), dup(f("s5_A_im")[0].T),
                    np.broadcast_to(f("s5_log_dt")[0][None, :], (128, 64))], axis=1)
    common["s5a"] = np.ascontiguousarray(s5a)
    common["s5b"] = np.ascontiguousarray(np.stack([dup(f("s5_B_re")[0].transpose(1, 0, 2)),
                                                   dup(f("s5_B_im")[0].transpose(1, 0, 2))], axis=1))
    common["s5c"] = np.ascontiguousarray(np.stack([dup(f("s5_C_re")[0].transpose(2, 0, 1)),
                                                   dup(f("s5_C_im")[0].transpose(2, 0, 1))], axis=1))
    i = np.arange(128)
    cst = np.zeros((128, 5, 128), np.float32)
    cst[:, 0, :] = np.eye(128)
    cst[:, 1, :] = (i[:, None] <= i[None, :])
    cst[:, 2, :] = (i[:, None] > i[None, :])
    cst[:, 3, :] = 1.0
    cst[:, 4, :] = ((i[None, :] // 16) >= (i[:, None] // 16))
    common["cst"] = cst
    return common


def kernel(**inputs):
    x = np.asarray(inputs["x"], dtype=np.float32)
    B, L, _ = x.shape
    half = L // 2
    if "nc" not in _CACHE:
        _CACHE["nc"] = build_program()
    nc = _CACHE["nc"]
    common = prep_inputs(inputs)
    in_maps = []
    for c in range(8):
        b, h = c // 2, c % 2
        xs = np.zeros((L, D), np.float32)
        if h == 0:
            xs[half:] = x[b, :half]
        else:
            xs[:] = x[b]
        m = dict(common)
        m["xs"] = xs
        cv = np.zeros((128, 4), np.float32)
        cv[:64, 0], cv[64:, 0] = -1.0, 1.0
        cv[:64, 1] = 1.0
        cv[64:, 2] = 1.0
        cv[:, 3] = float(h)
        m["cstv"] = cv
        in_maps.append(m)
    res = run_bass_kernel_spmd(nc, in_maps, core_ids=list(range(8)))
    _CACHE["res"] = res
    outp = np.zeros((B, L, D), np.float32)
    for c in range(8):
        b, h = c // 2, c % 2
        outp[b, h * half:(h + 1) * half] = np.asarray(res.results[c]["out"], dtype=np.float32)
    return outp
```

```python
import numpy as np
import ml_dtypes
from contextlib import ExitStack
import concourse.bass as bass
import concourse.mybir as mybir
from concourse.bass_utils import run_bass_kernel_spmd

F32 = mybir.dt.float32
BF16 = mybir.dt.bfloat16
I32 = mybir.dt.int32
U8 = mybir.dt.uint8
ALU = mybir.AluOpType
AF = mybir.ActivationFunctionType
DTSZ = {F32: 4, BF16: 2, I32: 4, U8: 1}

D = 1024
DFF = 2816
DIN = 9248
TT = 512
NT_PRE = 4
NT_OWN = 4
EPS = 1e-6
ARENA = 206 * 1024
LIGHT = True
DEBUG = False


def _prod(s):
    r = 1
    for v in s:
        r *= v
    return r


class Buf:
    def __init__(self, ap, key, space, lo, hi, shape, dt):
        self.ap, self.key, self.space, self.lo, self.hi, self.shape, self.dt = ap, key, space, lo, hi, shape, dt
        self.sub_bytes = (_prod(shape[2:]) * DTSZ[dt]) if len(shape) > 2 else None

    def __getitem__(self, idx):
        return self.ap[idx]


class Prog:
    ENG = ("pe", "act", "dve", "pool", "sp")
    EPOCH = 12000

    def __init__(self, nc, es):
        self.nc, self.es = nc, es
        self.ops = {e: [] for e in self.ENG}
        self.count = {e: 0 for e in self.ENG}
        self.esems = {e: [] for e in self.ENG}
        self.dsems = {}
        self.lastw = {}
        self.readers = {}
        self.spaces = {}
        self.waited = {e: {} for e in self.ENG}
        self.final = {}
        self.arena = es.enter_context(nc.sbuf_tensor("arena", [128, ARENA], U8))
        self.top = 0
        self.nps = 0

    def view(self, name, off, shape, dt):
        nb = _prod(shape[1:]) * DTSZ[dt]
        assert off + nb <= ARENA, (name, off, nb)
        ap = self.arena[0:shape[0], off:off + nb].bitcast(dt)
        if len(shape) == 3:
            ap = ap.rearrange("p (a b) -> p a b", a=shape[1])
        elif len(shape) == 4:
            ap = ap.rearrange("p (a b c) -> p a b c", a=shape[1], b=shape[2])
        return Buf(ap, "%s@%d" % (name, off), "arena", off, off + nb, shape, dt)

    def alloc(self, name, shape, dt):
        b = self.view(name, self.top, shape, dt)
        self.top = (b.hi + 31) // 32 * 32
        return b

    def psum(self, name):
        t = self.es.enter_context(self.nc.psum_tensor(name, [128, 512], F32))
        return Buf(t[:, :], name, name, 0, 2048, [128, 512], F32)

    def sem(self, name):
        return self.es.enter_context(self.nc.semaphore(name))

    def _norm(self, it):
        if isinstance(it, Buf):
            return (it.key, it.space, it.lo, it.hi)
        if isinstance(it, tuple):
            b = it[0]
            i0 = it[1]
            i1 = it[2] if len(it) > 2 else i0 + 1
            return ((b.key, i0, i1), b.space, b.lo + i0 * b.sub_bytes, b.lo + i1 * b.sub_bytes)
        return (it, it, 0, 1)

    def _related(self, n):
        key, space, lo, hi = n
        sp = self.spaces.setdefault(space, {})
        if key not in sp:
            sp[key] = (lo, hi)
        return [k for k, (l, h) in sp.items() if l < hi and lo < h]

    def op(self, eng, fn, r=(), w=(), dma=None):
        rn = [self._norm(x) for x in r]
        wn = [self._norm(x) for x in w]
        deps = []
        for n in rn:
            for k in self._related(n):
                if k in self.lastw:
                    deps.append(self.lastw[k])
        for n in wn:
            for k in self._related(n):
                if k in self.lastw:
                    deps.append(self.lastw[k])
                deps.extend(self.readers.get(k, {}).values())
        waits = []
        for (sem, val, seng) in deps:
            if seng == "pe" and eng == "pe" and dma is None:
                continue
            cur = self.waited[eng].get(id(sem), 0)
            if val > cur:
                self.waited[eng][id(sem)] = val
                waits.append((sem, val))
        if dma is not None:
            if dma not in self.dsems:
                self.dsems[dma] = [self.sem("d_" + str(dma)), 0]
            ds = self.dsems[dma]
            ds[1] += 16
            tok = (ds[0], ds[1], "dma")
            inc = (ds[0], 16)
        else:
            c = self.count[eng]
            ep = c // self.EPOCH
            while len(self.esems[eng]) <= ep:
                self.esems[eng].append(self.sem("e_%s_%d" % (eng, len(self.esems[eng]))))
            s = self.esems[eng][ep]
            self.count[eng] = c + 1
            tok = (s, c - ep * self.EPOCH + 1, eng)
            inc = (s, 1)
        self.ops[eng].append((waits, fn, inc))
        for n in wn:
            self.lastw[n[0]] = tok
            self.readers[n[0]] = {}
        for n in rn:
            self.readers.setdefault(n[0], {})[id(tok[0])] = tok
        return tok

    def emit(self):
        with self.nc.Block() as block:
            def mk(engname):
                def body(e):
                    for waits, fn, inc in self.ops[engname]:
                        for (s, v) in waits:
                            e.wait_ge(s, v)
                        fn(e).then_inc(inc[0], inc[1])
                    if engname == "sp":
                        for (s, v) in self.final.values():
                            e.wait_ge(s, v)
                return body
            block.tensor(mk("pe"))
            block.scalar(mk("act"))
            block.vector(mk("dve"))
            block.gpsimd(mk("pool"))
            block.sync(mk("sp"))


WSHAPES = {"ffn1_w_gate": (D, DFF), "ffn1_w_up": (D, DFF), "ffn1_w_down": (DFF, D),
           "w_in": (D, DIN), "s5_w_glu": (D, D), "w_proj_s5": (D, D), "w_proj_ssd": (2 * D, D),
           "w_out": (D, D),
           "ffn2_w_gate": (D, DFF), "ffn2_w_up": (D, DFF), "ffn2_w_down": (DFF, D)}
PSHAPES = {"pc8": [128, 6, 8], "pc16": [128, 3, 16], "convp": [128, 32, 5], "hp": [32, 2],
           "s5a": [128, 3, 64], "s5b": [128, 2, 64, 16], "s5c": [128, 2, 64, 16],
           "cst": [128, 5, 128], "cstv": [128, 4]}


def build_program():
    nc = bass.Bass("TRN2", target_bir_lowering=False)
    NTOK = (NT_PRE + NT_OWN) * TT
    din = {}
    for n, s in WSHAPES.items():
        din[n] = nc.dram_tensor(n, list(s), F32, kind="ExternalInput").ap()
    for n, s in PSHAPES.items():
        din[n] = nc.dram_tensor(n, list(s), F32, kind="ExternalInput").ap()
    xs = nc.dram_tensor("xs", [NTOK, D], F32, kind="ExternalInput").ap()
    out = nc.dram_tensor("out", [NT_OWN * TT, D], F32, kind="ExternalOutput").ap()
    wbf = {n: nc.dram_tensor(n + "_bf", list(s), BF16, kind="Internal").ap() for n, s in WSHAPES.items()}
    s5B = nc.dram_tensor("s5B_tab", [8, 128, 2, 8, 128], BF16, kind="Internal").ap()
    s5K = nc.dram_tensor("s5K_tab", [8, 128, 2, 8, 128], BF16, kind="Internal").ap()
    dbg = {}
    if DEBUG:
        for n, c in (("dbg_s5", 8), ("dbg_ssd", 16), ("dbg_m", 8), ("dbg_x1", 8), ("dbg_xs", 16), ("dbg_B", 8), ("dbg_C", 8), ("dbg_y", 16), ("dbg_dt", 1)):
            dbg[n] = nc.dram_tensor(n, [128, c, TT], F32, kind="ExternalOutput").ap()

    with ExitStack() as es:
        P = Prog(nc, es)
        psb = [P.psum("ps%d" % i) for i in range(8)]
        NROT = 7

        rot = {"n": NROT}

        def next_ps():
            P.nps = (P.nps + 1) % rot["n"]
            return psb[P.nps]

        def psbf(ps):
            return ps.ap.bitcast(BF16)

        cst = P.alloc("cst", [128, 5, 128], F32)
        IDN, MU, M1, ONE, KMASK = 0, 1, 2, 3, 4
        cstv = P.alloc("cstv", [128, 4], F32)
        SGN, SELT, SELB, FLAG = 0, 1, 2, 3
        pc8 = P.alloc("pc8", [128, 6, 8], F32)
        G_FFN1, G_MIX, G_FFN2, G_FIN, V_S5D, V_BGLU = range(6)
        pc16 = P.alloc("pc16", [128, 3, 16], F32)
        V_SSDN, V_BGATE, V_SSDD = range(3)
        convp = P.alloc("convp", [128, 32, 5], F32)
        hp = P.alloc("hp", [32, 2], F32)
        aneg = P.alloc("aneg", [32, 1], F32)
        identb = P.alloc("identb", [128, 128], BF16)
        onesb = P.alloc("onesb", [128, 128], BF16)
        mub = P.alloc("mub", [128, 128], BF16)
        m1b = P.alloc("m1b", [128, 128], BF16)
        hSb = P.alloc("hSb", [128, 32, 64], BF16)
        a8r = P.alloc("a8r", [128, 64], F32)
        a8s = P.alloc("a8s", [128, 64], F32)
        a8n = P.alloc("a8n", [128, 64], F32)
        Sst = [P.alloc("Sst%d" % i, [128, 64], F32) for i in range(2)]
        Ssw = [P.alloc("Ssw%d" % i, [128, 64], F32) for i in range(2)]
        hS = P.alloc("hS", [128, 32, 64], F32)
        tail = P.alloc("tail", [128, 32, 3], BF16)
        xres = P.alloc("xres", [128, 8, TT], F32)
        hT = P.alloc("hT", [128, 8, TT], BF16)
        rstd = P.alloc("rstd", [128, TT], F32)
        gsb = [P.alloc("gsb%d" % i, [128, TT], F32) for i in range(2)]
        NPAN = 3
        pans = [P.alloc("pan%d" % i, [128, 4096], BF16) for i in range(NPAN)]
        s5tab = P.alloc("s5tab", [128, 2, 8, 128], BF16)
        xin = P.alloc("xin", [128, D], F32)
        osb = [P.alloc("osb%d" % i, [128, D], F32) for i in range(1)]
        BASE = P.top
        st = {"pan": 0, "g": 0, "o": 0, "s": 0}
        S5T_OFF = ARENA - 8 * TT * 2
        MRG_OFF = BASE + 16 * TT * 2 + 64

        class Phase:
            def __init__(self, base=BASE):
                self.top = base

            def alloc(self, name, shape, dt):
                b = P.view(name, self.top, shape, dt)
                self.top = (b.hi + 31) // 32 * 32
                return b

        def load_params(items, semkey):
            for b, src in items:
                P.op("sp", lambda e, b=b, src=src: e.dma_start(out=b.ap, in_=src), w=[b], dma=semkey)
            sem, total = P.dsems[semkey]
            for b, _ in items:
                P.lastw[b.key] = (sem, total, "dma")

        load_params([(cst, din["cst"]), (cstv, din["cstv"]), (pc8, din["pc8"]), (pc16, din["pc16"]),
                     (convp, din["convp"]), (hp, din["hp"])], "par")
        P.op("dve", lambda e: e.tensor_copy(out=identb.ap, in_=cst[:, IDN, :]), r=[cst], w=[identb])
        P.op("pool", lambda e: e.memset(onesb.ap, 1.0), w=[onesb])
        P.op("dve", lambda e: e.tensor_copy(out=mub.ap, in_=cst[:, MU, :]), r=[cst], w=[mub])
        P.op("dve", lambda e: e.tensor_copy(out=m1b.ap, in_=cst[:, M1, :]), r=[cst], w=[m1b])
        P.op("pool", lambda e: e.memset(hSb.ap, 0.0), w=[hSb])
        P.op("pool", lambda e: e.memset(hS.ap, 0.0), w=[hS])
        P.op("pool", lambda e: e.memset(tail.ap, 0.0), w=[tail])
        P.op("pool", lambda e: e.memset(Sst[0].ap, 0.0), w=[Sst[0]])
        P.op("pool", lambda e: e.memset(Ssw[0].ap, 0.0), w=[Ssw[0]])
        P.op("act", lambda e: e.activation(out=aneg.ap, in_=hp[:, 0:1], func=AF.Exp), r=[hp], w=[aneg])
        P.op("dve", lambda e: e.tensor_scalar(out=aneg.ap, in0=aneg.ap, scalar1=-1.0, scalar2=None, op0=ALU.mult),
             r=[aneg], w=[aneg])

        import os as _os2
        for n, (k, m) in (WSHAPES.items() if int(_os2.environ.get("KSTOP", "9")) >= 0 else []):
            rows = 256
            for r0 in range(0, k, rows):
                rr = min(rows, k - r0)
                P.op("pool", lambda e, n=n, r0=r0, rr=rr: e.dma_start(
                    out=wbf[n][r0:r0 + rr, :], in_=din[n][r0:r0 + rr, :]), w=[n + "_bf"], dma="cast_" + n)

        def get_pan():
            p = pans[st["pan"] % NPAN]
            st["pan"] += 1
            return p

        def proj(wname, col0, ncols, KC, rhs, consume, pw=None):
            W = wbf[wname]
            if pw is None:
                pw = min(512, (4096 // KC) // 128 * 128)
            j = 0
            for p0 in range(0, ncols, pw):
                w_ = min(pw, ncols - p0)
                pan = get_pan()
                pv = pan.ap[:, 0:KC * w_].rearrange("p (c n) -> p c n", c=KC)
                P.op("sp", lambda e, pv=pv, p0=p0, w_=w_: e.dma_start(
                    out=pv, in_=W[:, col0 + p0:col0 + p0 + w_].rearrange("(c p) n -> p c n", p=128)),
                    r=[wname + "_bf"], w=[pan], dma=pan.key)
                for q0 in range(0, w_, 128):
                    cw = min(128, w_ - q0)
                    ps = next_ps()
                    for c in range(KC):
                        rap, rdep = rhs(c)
                        P.op("pe", lambda e, pv=pv, ps=ps, c=c, q0=q0, cw=cw, rap=rap: e.matmul(
                            ps[0:cw, :], lhsT=pv[:, c, q0:q0 + cw], rhs=rap, start=(c == 0), stop=(c == KC - 1)),
                            r=[pan, rdep], w=[ps])
                    consume(j, ps, cw)
                    j += 1

        def rstd_from(ps, scale):
            P.op("dve", lambda e, ps=ps: e.tensor_scalar(out=rstd.ap, in0=ps[:, :], scalar1=scale, scalar2=EPS,
                                                         op0=ALU.mult, op1=ALU.add), r=[ps], w=[rstd])
            P.op("act", lambda e: e.activation(out=rstd.ap, in_=rstd.ap, func=AF.Ln), r=[rstd], w=[rstd])
            P.op("act", lambda e: e.activation(out=rstd.ap, in_=rstd.ap, func=AF.Exp, scale=-0.5), r=[rstd], w=[rstd])

        def rmsnorm_to(dst, gcol, sqbuf):
            for c in range(8):
                P.op("act", lambda e, c=c: e.activation(out=sqbuf[:, c, :], in_=xres[:, c, :], func=AF.Square),
                     r=[(xres, c)], w=[(sqbuf, c)])
            ps = next_ps()
            for c in range(8):
                P.op("pe", lambda e, c=c, ps=ps: e.matmul(ps[:, :], lhsT=onesb.ap, rhs=sqbuf[:, c, :],
                                                            start=(c == 0), stop=(c == 7)),
                     r=[onesb, (sqbuf, c)], w=[ps])
            rstd_from(ps, 1.0 / D)
            for c in range(8):
                P.op("dve", lambda e, c=c: e.scalar_tensor_tensor(
                    out=dst[:, c, :], in0=xres[:, c, :], scalar=pc8[:, gcol, c:c + 1], in1=rstd.ap,
                    op0=ALU.mult, op1=ALU.mult), r=[(xres, c), pc8, rstd], w=[(dst, c)])

        def load_x(tile):
            for s in range(4):
                src = xs[tile * TT + s * 128: tile * TT + (s + 1) * 128, :]
                P.op("sp", lambda e, src=src: e.dma_start(out=xin.ap, in_=src), w=[xin], dma="xin")
                for c0 in range(0, 8, 4):
                    ps = next_ps()
                    for c in range(c0, c0 + 4):
                        P.op("pe", lambda e, c=c, c0=c0, ps=ps: e.transpose(
                            out=ps[:, (c - c0) * 128:(c - c0 + 1) * 128], in_=xin[:, c * 128:(c + 1) * 128],
                            identity=cst[:, IDN, :]), r=[xin, cst], w=[ps])
                    P.op("act", lambda e, c0=c0, s=s, ps=ps: e.activation(
                        out=xres[:, c0:c0 + 4, s * 128:(s + 1) * 128],
                        in_=ps[:, :].rearrange("p (c t) -> p c t", c=4), func=AF.Copy),
                        r=[ps], w=[(xres, c0, c0 + 4)])

        def dump(name, buf, nch):
            if not DEBUG:
                return
            for c in range(nch):
                g = gsb[st["g"] % 2]
                st["g"] += 1
                P.op("dve", lambda e, g=g, c=c: e.tensor_copy(out=g.ap, in_=buf[:, c, :]), r=[(buf, c)], w=[g])
                tok = P.op("sp", lambda e, g=g, c=c: e.dma_start(out=dbg[name][:, c, :], in_=g.ap),
                           r=[g], w=[name], dma=g.key)
                P.final[g.key] = (tok[0], tok[1])

        def ffn(pref, gcol):
            ph = Phase()
            act = ph.alloc("act", [128, 22, TT], BF16)
            sqb = ph.alloc("sqb", [128, 8, TT], BF16)
            rmsnorm_to(hT, gcol, sqb)
            for p0 in range(0, DFF, 512):
                pw = min(512, DFF - p0)
                pp = []
                for wn in (pref + "_w_gate", pref + "_w_up"):
                    pan = get_pan()
                    pv = pan.ap[:, 0:8 * pw].rearrange("p (c n) -> p c n", c=8)
                    P.op("sp", lambda e, pv=pv, wn=wn, p0=p0, pw=pw: e.dma_start(
                        out=pv, in_=wbf[wn][:, p0:p0 + pw].rearrange("(c p) n -> p c n", p=128)),
                        r=[wn + "_bf"], w=[pan], dma=pan.key)
                    pp.append((pan, pv))
                for j in range(pw // 128):
                    oc = (p0 // 128) + j
                    psg, psu = next_ps(), next_ps()
                    for (pan, pv), ps in ((pp[0], psg), (pp[1], psu)):
                        for c in range(8):
                            P.op("pe", lambda e, pv=pv, ps=ps, c=c, j=j: e.matmul(
                                ps[:, :], lhsT=pv[:, c, j * 128:(j + 1) * 128], rhs=hT[:, c, :],
                                start=(c == 0), stop=(c == 7)), r=[pan, (hT, c)], w=[ps])
                    g = gsb[st["g"] % 2]
                    st["g"] += 1
                    P.op("act", lambda e, g=g, psg=psg: e.activation(out=g.ap, in_=psg[:, :], func=AF.Silu),
                         r=[psg], w=[g])
                    P.op("dve", lambda e, g=g, psu=psu, oc=oc: e.tensor_tensor(
                        out=act[:, oc, :], in0=psu[:, :], in1=g.ap, op=ALU.mult), r=[psu, g], w=[(act, oc)])

            def fin(j, ps, cw):
                P.op("dve", lambda e, ps=ps, j=j: e.scalar_tensor_tensor(
                    out=xres[:, j, :], in0=ps[:, :], scalar=0.5, in1=xres[:, j, :],
                    op0=ALU.mult, op1=ALU.add), r=[ps, (xres, j)], w=[(xres, j)])
            proj(pref + "_w_down", 0, D, 22, lambda c: (act[:, c, :], (act, c)), fin, pw=128)

        def s5_setup():
            ph = Phase()
            A = ph.alloc("s5a", [128, 3, 64], F32)
            Bi = ph.alloc("s5b", [128, 2, 64, 16], F32)
            Ci = ph.alloc("s5c", [128, 2, 64, 16], F32)
            load_params([(A, din["s5a"]), (Bi, din["s5b"]), (Ci, din["s5c"])], "par2")
            T = {}
            TMP = Buf(None, "s5tmp", "arena", BASE, ARENA, [128, ARENA - BASE], U8)
            RD = [A, Bi, Ci, TMP, cstv]

            def t(name):
                if name not in T:
                    T[name] = ph.alloc("t_" + name, [128, 64], F32).ap
                return T[name]

            def tt(o, a, b, op):
                P.op("dve", lambda e: e.tensor_tensor(out=o, in0=a, in1=b, op=op), r=RD, w=[TMP])

            def ts(o, a, s1, s2, op0, op1=None):
                if op1 is None:
                    P.op("dve", lambda e: e.tensor_scalar(out=o, in0=a, scalar1=s1, scalar2=None, op0=op0), r=RD, w=[TMP])
                else:
                    P.op("dve", lambda e: e.tensor_scalar(out=o, in0=a, scalar1=s1, scalar2=s2, op0=op0, op1=op1),
                         r=RD, w=[TMP])

            def stt(o, a, s, b, op0, op1):
                P.op("dve", lambda e: e.scalar_tensor_tensor(out=o, in0=a, scalar=s, in1=b, op0=op0, op1=op1),
                     r=RD, w=[TMP])

            def ac(o, a, func, scale=1.0):
                P.op("act", lambda e: e.activation(out=o, in_=a, func=func, scale=scale), r=RD, w=[TMP])

            def rcp(o, a):
                P.op("dve", lambda e: e.reciprocal(out=o, in_=a), r=RD, w=[TMP])

            import os as _os
            SL = int(_os.environ.get("S5STOP", "99"))
            lr, li, ldt = A[:, 0, :], A[:, 1, :], A[:, 2, :]
            ac(t("dt"), ldt, AF.Exp)
            tt(t("lrdt"), lr, t("dt"), ALU.mult)
            tt(t("th"), li, t("dt"), ALU.mult)
            ac(t("mag"), t("lrdt"), AF.Exp)
            ts(t("v"), t("th"), 1.0 / (2 * np.pi), None, ALU.mult)
            if SL <= 1:
                return
            vi = ph.alloc("t_vi", [128, 64], I32).ap
            P.op("dve", lambda e: e.tensor_copy(out=vi, in_=t("v")), r=RD, w=[TMP])
            P.op("dve", lambda e: e.tensor_copy(out=t("vf"), in_=vi), r=RD, w=[TMP])
            tt(t("fr"), t("v"), t("vf"), ALU.subtract)
            ts(t("g"), t("fr"), 0.5, None, ALU.is_gt)
            tt(t("fr"), t("fr"), t("g"), ALU.subtract)
            ts(t("g"), t("fr"), -0.5, None, ALU.is_lt)
            tt(t("fr"), t("fr"), t("g"), ALU.add)
            if SL <= 2:
                return
            ac(t("sin"), t("fr"), AF.Sin, scale=6.28318)
            ts(t("frc"), t("fr"), 0.25, None, ALU.add)
            ts(t("g"), t("frc"), 0.5, None, ALU.is_gt)
            tt(t("frc"), t("frc"), t("g"), ALU.subtract)
            ac(t("cos"), t("frc"), AF.Sin, scale=6.28318)
            if SL <= 3:
                return
            ar, ai = t("ar"), t("ai")
            tt(ar, t("mag"), t("cos"), ALU.mult)
            tt(ai, t("mag"), t("sin"), ALU.mult)
            tt(t("den"), lr, lr, ALU.mult)
            tt(t("d2"), li, li, ALU.mult)
            tt(t("den"), t("den"), t("d2"), ALU.add)
            rcp(t("rden"), t("den"))
            ts(t("am1"), ar, -1.0, None, ALU.add)
            tt(t("x1"), t("am1"), lr, ALU.mult)
            tt(t("x2"), ai, li, ALU.mult)
            tt(t("x1"), t("x1"), t("x2"), ALU.add)
            tt(t("cr"), t("x1"), t("rden"), ALU.mult)
            tt(t("x1"), ai, lr, ALU.mult)
            tt(t("x2"), t("am1"), li, ALU.mult)
            tt(t("x1"), t("x1"), t("x2"), ALU.subtract)
            tt(t("ci"), t("x1"), t("rden"), ALU.mult)
            pr = ph.alloc("t_pr", [128, 9, 64], F32)
            pi = ph.alloc("t_pi", [128, 9, 64], F32)
            P.op("dve", lambda e: e.memset(pr[:, 0, :], 1.0), r=RD, w=[TMP])
            P.op("dve", lambda e: e.memset(pi[:, 0, :], 0.0), r=RD, w=[TMP])

            def cmul(orr, oi, xr, xi, yr, yi):
                tt(t("m1"), xr, yr, ALU.mult)
                tt(t("m2"), xi, yi, ALU.mult)
                tt(t("m3"), xr, yi, ALU.mult)
                tt(t("m4"), xi, yr, ALU.mult)
                tt(orr, t("m1"), t("m2"), ALU.subtract)
                tt(oi, t("m3"), t("m4"), ALU.add)
            for k in range(8):
                cmul(pr[:, k + 1, :], pi[:, k + 1, :], pr[:, k, :], pi[:, k, :], ar, ai)
            tt(t("m1"), pr[:, 8, :], pr[:, 8, :], ALU.mult)
            tt(t("m2"), pi[:, 8, :], pi[:, 8, :], ALU.mult)
            tt(t("m1"), t("m1"), t("m2"), ALU.add)
            rcp(t("rm"), t("m1"))
            tt(t("ir"), pr[:, 8, :], t("rm"), ALU.mult)
            tt(t("ii"), pi[:, 8, :], t("rm"), ALU.mult)
            ts(t("ii"), t("ii"), -1.0, None, ALU.mult)
            P.op("dve", lambda e: e.tensor_copy(out=a8r.ap, in_=pr[:, 8, :]), r=RD, w=[a8r])
            P.op("dve", lambda e: e.tensor_scalar(out=a8s.ap, in0=pi[:, 8, :], scalar1=cstv[:, SGN:SGN + 1], scalar2=None,
                                                  op0=ALU.mult), r=RD, w=[a8s])
            P.op("dve", lambda e: e.tensor_scalar(out=a8n.ap, in0=a8s.ap, scalar1=-1.0, scalar2=None, op0=ALU.mult),
                 r=[a8s], w=[a8n])
            if SL <= 4:
                return
            big = lambda name: ph.alloc(name, [128, 64, 16], F32).ap
            bbr, bbi, BB, BBs, CC1, CC2, w1, w2 = [big(n) for n in ("bbr", "bbi", "BB", "BBs", "CC1", "CC2", "w1", "w2")]

            def bc(ap2d):
                return ap2d.unsqueeze(2).to_broadcast([128, 64, 16])
            Br, Bim, Cr, Cim = Bi[:, 0, :, :], Bi[:, 1, :, :], Ci[:, 0, :, :], Ci[:, 1, :, :]
            crb, cib = bc(t("cr")), bc(t("ci"))
            tt(w1, Br, crb, ALU.mult); tt(w2, Bim, cib, ALU.mult); tt(bbr, w1, w2, ALU.subtract)
            tt(w1, Bim, crb, ALU.mult); tt(w2, Br, cib, ALU.mult); tt(bbi, w1, w2, ALU.add)
            sT, sB = cstv[:, SELT:SELT + 1], cstv[:, SELB:SELB + 1]
            ts(BB, bbr, sT, None, ALU.mult); stt(BB, bbi, sB, BB, ALU.mult, ALU.add)
            ts(BBs, bbi, sT, -1.0, ALU.mult, ALU.mult); stt(BBs, bbr, sB, BBs, ALU.mult, ALU.add)
            ts(CC1, Cim, sB, -1.0, ALU.mult, ALU.mult); stt(CC1, Cr, sT, CC1, ALU.mult, ALU.add)
            ts(CC2, Cim, sT, -1.0, ALU.mult, ALU.mult); ts(w1, Cr, sB, -1.0, ALU.mult, ALU.mult)
            tt(CC2, CC2, w1, ALU.add)
            if SL <= 5:
                return
            NH = 16
            Bpw = ph.alloc("Bpw", [128, NH, 8, 16], F32)
            BJ = ph.alloc("BJ", [128, NH, 8, 16], F32)
            Cpw = ph.alloc("Cpw", [128, NH, 8, 16], F32)
            stB = ph.alloc("stB", [128, 2, 8, 128], BF16)
            stK = ph.alloc("stK", [128, 2, 8, 128], BF16)
            BJh = ph.alloc("BJh", [128, NH, 8, 16], BF16)
            BJl = ph.alloc("BJl", [128, NH, 8, 16], BF16)
            Cph = ph.alloc("Cph", [128, NH, 8, 16], BF16)
            Cpl = ph.alloc("Cpl", [128, NH, 8, 16], BF16)
            fl = lambda b, g: b[:, g, :, :].rearrange("p a b -> p (a b)")

            def bch(ap2d, gs):
                return ap2d[:, gs].unsqueeze(2).to_broadcast([128, NH, 16])
            for hf in range(64 // NH):
                gs = slice(hf * NH, (hf + 1) * NH)
                w1h, w2h = w1[:, 0:NH, :], w2[:, 0:NH, :]
                for j in range(8):
                    zr, zi = bch(pr[:, 7 - j, :], gs), bch(pi[:, 7 - j, :], gs)
                    tt(w1h, BB[:, gs, :], zr, ALU.mult); tt(w2h, BBs[:, gs, :], zi, ALU.mult)
                    tt(Bpw[:, :, j, :], w1h, w2h, ALU.add)
                    cmul(t("qr"), t("qi"), pr[:, 7 - j, :], pi[:, 7 - j, :], t("ir"), t("ii"))
                    zr, zi = bch(t("qr"), gs), bch(t("qi"), gs)
                    tt(w1h, BB[:, gs, :], zr, ALU.mult); tt(w2h, BBs[:, gs, :], zi, ALU.mult)
                    tt(BJ[:, :, j, :], w1h, w2h, ALU.add)
                    zr, zi = bch(pr[:, j + 1, :], gs), bch(pi[:, j + 1, :], gs)
                    tt(w1h, CC1[:, gs, :], zr, ALU.mult); tt(w2h, CC2[:, gs, :], zi, ALU.mult)
                    tt(Cpw[:, :, j, :], w1h, w2h, ALU.add)
                if SL <= 6:
                    continue
                for src_, hi_, lo_ in ((BJ, BJh, BJl), (Cpw, Cph, Cpl)):
                    P.op("dve", lambda e, src_=src_, hi_=hi_: e.tensor_copy(out=hi_.ap, in_=src_.ap), r=RD, w=[TMP])
                    P.op("dve", lambda e, src_=src_, hi_=hi_: e.tensor_tensor(out=src_.ap, in0=src_.ap, in1=hi_.ap,
                                                                           op=ALU.subtract), r=RD, w=[TMP])
                    P.op("dve", lambda e, src_=src_, lo_=lo_: e.tensor_copy(out=lo_.ap, in_=src_.ap), r=RD, w=[TMP])
                for q in range(hf * NH // 8, (hf + 1) * NH // 8):
                    for gl in range(8):
                        g = 8 * q + gl - hf * NH
                        ps = next_ps()
                        psk = next_ps()
                        P.op("pe", lambda e, g=g, ps=ps: e.transpose(out=ps[:, 0:128], in_=fl(Bpw, g), identity=cst[:, IDN, :]),
                             r=[TMP, cst], w=[ps])
                        for ii, (l_, r_) in enumerate(((BJh, Cph), (BJh, Cpl), (BJl, Cph))):
                            P.op("pe", lambda e, g=g, psk=psk, ii=ii, l_=l_, r_=r_: e.matmul(
                                psk[:, 0:128], lhsT=fl(l_, g), rhs=fl(r_, g), start=(ii == 0), stop=(ii == 2)),
                                r=[TMP], w=[psk])
                        P.op("act", lambda e, gl=gl, ps=ps: e.activation(out=stB[:, 0, gl, :], in_=ps[:, 0:128], func=AF.Copy),
                             r=[ps], w=[stB])
                        P.op("act", lambda e, gl=gl, ps=ps: e.activation(out=stB[:, 1, gl, 0:64], in_=ps[:, 64:128], func=AF.Copy),
                             r=[ps], w=[stB])
                        P.op("act", lambda e, gl=gl, ps=ps: e.activation(out=stB[:, 1, gl, 64:128], in_=ps[:, 0:64], func=AF.Copy),
                             r=[ps], w=[stB])
                        P.op("dve", lambda e, gl=gl, psk=psk: e.tensor_tensor(
                            out=stK[:, 0, gl, :], in0=psk[:, 0:128], in1=cst[:, KMASK, :], op=ALU.mult),
                            r=[psk, cst], w=[stK])
                        P.op("dve", lambda e, gl=gl, g=g: e.tensor_copy(out=stK[:, 1, gl, :], in_=fl(Cph, g)),
                             r=[TMP], w=[stK])
                    if int(_os.environ.get("S7", "9")) >= 3:
                        P.op("sp", lambda e, q=q: e.dma_start(out=s5B[q], in_=stB.ap), r=[stB], w=["s5B"], dma="s5B")
                        P.op("sp", lambda e, q=q: e.dma_start(out=s5K[q], in_=stK.ap), r=[stK], w=["s5K"], dma="s5K")
            print("setup phase top", ph.top, "BASE", BASE)

        def s5_tile(light):
            ph = Phase()
            uT = ph.alloc("uT", [128, 8, TT], BF16)
            U2 = ph.alloc("U2", [64, 64, 8, 16], BF16)
            Ucol = ph.alloc("Ucol", [128, 64, 64], BF16)
            cS = ph.alloc("cS", [128, 64, 64], F32)
            cW = ph.alloc("cW", [128, 64, 64], F32)
            Xprev = ph.alloc("Xprev", [128, 64, 64], BF16)
            Ycol = ph.alloc("Ycol", [128, 64, 64], BF16)
            ytmp = ph.alloc("ytmp", [128, TT], F32)
            gtmp = ph.alloc("gtmp", [128, TT], F32)
            r1 = [ph.alloc("r1_%d" % i, [128, 64], F32) for i in range(2)]
            r2 = [ph.alloc("r2_%d" % i, [128, 64], F32) for i in range(2)]

            def cons_u(j, ps, cw):
                P.op("act", lambda e: e.activation(out=uT[:, j, :], in_=ps[:, :], func=AF.Copy), r=[ps], w=[(uT, j)])
            proj("w_in", 0, 1024, 8, lambda c: (hT[:, c, :], (hT, c)), cons_u)
            for q in range(8):
                ps = next_ps()
                pb = psbf(ps)
                for j in range(8):
                    P.op("pe", lambda e, q=q, j=j, pb=pb: e.transpose(
                        out=pb[0:64, j * 128:(j + 1) * 128], in_=uT[:, q, j:TT:8], identity=identb.ap),
                        r=[(uT, q), identb], w=[ps])
                P.op("dve", lambda e, q=q, pb=pb: e.tensor_copy(
                    out=U2[:, 8 * q:8 * q + 8, :, :],
                    in_=pb[0:64, :].rearrange("p (j g m) -> p g j m", j=8, g=8)), r=[ps], w=[U2])
            for g0 in range(0, 64, 16):
                ps = next_ps()
                pb = psbf(ps)
                for g in range(g0, g0 + 16):
                    P.op("pe", lambda e, g=g, g0=g0, pb=pb: e.transpose(
                        out=pb[:, (g - g0) * 64:(g - g0 + 1) * 64], in_=U2[:, g, :, :].rearrange("p a b -> p (a b)"),
                        identity=identb[0:64, 0:64]), r=[U2, identb], w=[ps])
                P.op("act", lambda e, g0=g0, pb=pb: e.activation(
                    out=Ucol[:, g0:g0 + 16, :], in_=pb[:, :].rearrange("p (g b) -> p g b", g=16), func=AF.Copy),
                    r=[ps], w=[(Ucol, g0, g0 + 16)])
            for q in range(8):
                P.op("sp", lambda e, q=q: e.dma_start(out=s5tab.ap, in_=s5B[q]), r=["s5B"], w=[s5tab], dma="s5tab")
                for which, dst in ((0, cS), (1, cW)):
                    ps = next_ps()
                    for gl in range(8):
                        g = 8 * q + gl
                        P.op("pe", lambda e, which=which, gl=gl, g=g, ps=ps: e.matmul(
                            ps[:, gl * 64:(gl + 1) * 64], lhsT=s5tab[:, which, gl, :], rhs=Ucol[:, g, :],
                            start=True, stop=True), r=[s5tab, (Ucol, g)], w=[ps])
                    P.op("act", lambda e, q=q, dst=dst, ps=ps: e.activation(
                        out=dst[:, 8 * q:8 * q + 8, :], in_=ps[:, :].rearrange("p (g b) -> p g b", g=8), func=AF.Copy),
                        r=[ps], w=[(dst, 8 * q, 8 * q + 8)])
            for b in range(64):
                S0, W0 = Sst[st["s"] % 2], Ssw[st["s"] % 2]
                S1, W1 = Sst[(st["s"] + 1) % 2], Ssw[(st["s"] + 1) % 2]
                st["s"] += 1
                if not light:
                    P.op("act", lambda e, b=b, S0=S0: e.activation(out=Xprev[:, :, b], in_=S0.ap, func=AF.Copy),
                         r=[S0], w=[Xprev])
                A, Bq = r1[b % 2], r2[b % 2]
                P.op("dve", lambda e, A=A, S0=S0: e.tensor_tensor(out=A.ap, in0=S0.ap, in1=a8r.ap, op=ALU.mult),
                     r=[S0, a8r], w=[A])
                P.op("pool", lambda e, Bq=Bq, W0=W0: e.tensor_tensor(out=Bq.ap, in0=W0.ap, in1=a8r.ap, op=ALU.mult),
                     r=[W0, a8r], w=[Bq])
                P.op("dve", lambda e, W0=W0, S1=S1: e.tensor_tensor(out=S1.ap, in0=W0.ap, in1=a8s.ap, op=ALU.mult),
                     r=[W0, a8s], w=[S1])
                P.op("pool", lambda e, S0=S0, W1=W1: e.tensor_tensor(out=W1.ap, in0=S0.ap, in1=a8n.ap, op=ALU.mult),
                     r=[S0, a8n], w=[W1])
                P.op("dve", lambda e, A=A, S1=S1: e.tensor_tensor(out=S1.ap, in0=S1.ap, in1=A.ap, op=ALU.add),
                     r=[S1, A], w=[S1])
                P.op("pool", lambda e, Bq=Bq, W1=W1: e.tensor_tensor(out=W1.ap, in0=W1.ap, in1=Bq.ap, op=ALU.add),
                     r=[W1, Bq], w=[W1])
                P.op("dve", lambda e, b=b, S1=S1: e.tensor_tensor(out=S1.ap, in0=S1.ap, in1=cS[:, :, b], op=ALU.add),
                     r=[S1, cS], w=[S1])
                P.op("pool", lambda e, b=b, W1=W1: e.tensor_tensor(out=W1.ap, in0=W1.ap, in1=cW[:, :, b], op=ALU.add),
                     r=[W1, cW], w=[W1])
            if light:
                return None
            s5T = P.view("s5T", S5T_OFF, [128, 8, TT], BF16)
            gel = ph.alloc("gel", [128, 8, TT], BF16)
            assert ph.top <= S5T_OFF, ph.top
            for q in range(8):
                P.op("sp", lambda e, q=q: e.dma_start(out=s5tab.ap, in_=s5K[q]), r=["s5K"], w=[s5tab], dma="s5tab")
                ps = next_ps()
                for gl in range(8):
                    g = 8 * q + gl
                    P.op("pe", lambda e, gl=gl, g=g, ps=ps: e.matmul(
                        ps[:, gl * 64:(gl + 1) * 64], lhsT=s5tab[:, 0, gl, :], rhs=Ucol[:, g, :],
                        start=True, stop=False), r=[s5tab, (Ucol, g)], w=[ps])
                    P.op("pe", lambda e, gl=gl, g=g, ps=ps: e.matmul(
                        ps[:, gl * 64:(gl + 1) * 64], lhsT=s5tab[:, 1, gl, :], rhs=Xprev[:, g, :],
                        start=False, stop=True), r=[s5tab, Xprev], w=[ps])
                P.op("act", lambda e, q=q, ps=ps: e.activation(
                    out=Ycol[:, 8 * q:8 * q + 8, :], in_=ps[:, :].rearrange("p (g b) -> p g b", g=8), func=AF.Copy),
                    r=[ps], w=[(Ycol, 8 * q, 8 * q + 8)])
            Y2 = P.view("U2", U2.lo, [64, 8, 64, 16], BF16)
            for g0 in range(0, 64, 8):
                ps = next_ps()
                pb = psbf(ps)
                for g in range(g0, g0 + 8):
                    P.op("pe", lambda e, g=g, g0=g0, pb=pb: e.transpose(
                        out=pb[0:64, (g - g0) * 128:(g - g0 + 1) * 128], in_=Ycol[:, g, :], identity=identb.ap),
                        r=[(Ycol, g), identb], w=[ps])
                P.op("dve", lambda e, g0=g0, pb=pb: e.tensor_copy(
                    out=Y2[:, :, g0:g0 + 8, :],
                    in_=pb[0:64, :].rearrange("p (g j m) -> p j g m", g=8, j=8)), r=[ps], w=[Y2])
            for q in range(8):
                ps = next_ps()
                pb = psbf(ps)
                for j in range(8):
                    P.op("pe", lambda e, q=q, j=j, pb=pb: e.transpose(
                        out=pb[:, j * 64:(j + 1) * 64],
                        in_=Y2[:, j, 8 * q:8 * q + 8, :].rearrange("p a b -> p (a b)"),
                        identity=identb[0:64, 0:64]), r=[Y2, identb], w=[ps])
                P.op("dve", lambda e, q=q, pb=pb: e.scalar_tensor_tensor(
                    out=ytmp.ap.rearrange("p (b j) -> p j b", j=8),
                    in0=uT[:, q, :].rearrange("p (b j) -> p j b", j=8),
                    scalar=pc8[:, V_S5D, q:q + 1],
                    in1=pb[:, 0:512].rearrange("p (j b) -> p j b", j=8),
                    op0=ALU.mult, op1=ALU.add), r=[ps, (uT, q), pc8], w=[ytmp])
                P.op("pool", lambda e: e.tensor_tensor(out=gtmp.ap, in0=ytmp.ap, in1=ytmp.ap, op=ALU.mult),
                     r=[ytmp], w=[gtmp])
                P.op("pool", lambda e: e.tensor_scalar(out=gtmp.ap, in0=gtmp.ap, scalar1=0.044715, scalar2=1.0,
                                                       op0=ALU.mult, op1=ALU.add), r=[gtmp], w=[gtmp])
                P.op("pool", lambda e: e.tensor_tensor(out=gtmp.ap, in0=gtmp.ap, in1=ytmp.ap, op=ALU.mult),
                     r=[gtmp, ytmp], w=[gtmp])
                P.op("act", lambda e: e.activation(out=gtmp.ap, in_=gtmp.ap, func=AF.Sigmoid, scale=1.5957691216),
                     r=[gtmp], w=[gtmp])
                P.op("dve", lambda e, q=q: e.tensor_tensor(out=gel[:, q, :], in0=gtmp.ap, in1=ytmp.ap, op=ALU.mult),
                     r=[gtmp, ytmp], w=[(gel, q)])

            def cons_glu(j, ps, cw):
                g = gsb[st["g"] % 2]
                st["g"] += 1
                P.op("act", lambda e: e.activation(out=g.ap, in_=ps[:, :], func=AF.Sigmoid,
                                                   bias=pc8[:, V_BGLU, j:j + 1]), r=[ps, pc8], w=[g])
                P.op("dve", lambda e: e.tensor_tensor(out=s5T[:, j, :], in0=gel[:, j, :], in1=g.ap, op=ALU.mult),
                     r=[g, (gel, j)], w=[(s5T, j)])
            proj("s5_w_glu", 0, 1024, 8, lambda c: (gel[:, c, :], (gel, c)), cons_glu)
            return s5T

        def ssd_tile(light):
            ph = Phase()
            xbcp = ph.alloc("xbcp", [128, 32, TT + 3], BF16)
            ySS = P.view("ySS", xbcp.lo, [128, 16, TT], BF16)
            xsT = ph.alloc("xsT", [128, 16, TT], BF16)
            Bfm = ph.alloc("Bfm", [128, 8, TT], BF16)
            Cfm = ph.alloc("Cfm", [128, 8, TT], BF16)
            xtok = ph.alloc("xtok", [128, 2048], BF16)
            Btok = ph.alloc("Btok", [128, 1024], BF16)
            dtT = ph.alloc("dtT", [32, TT], F32)
            dAT = ph.alloc("dAT", [32, TT], F32)
            dtok = ph.alloc("dtok", [128, 4, 64], F32)
            dAh = ph.alloc("dAh", [128, 4, 32], BF16)
            dAl = ph.alloc("dAl", [128, 4, 32], BF16)
            dAr = ph.alloc("dAr", [128, 4, 32], F32)
            acs = ph.alloc("acs", [128, 64], F32)
            wgt = ph.alloc("wgt", [128, 32], F32)
            cdec = ph.alloc("cdec", [128, 32], F32)
            dg = [ph.alloc("dg%d" % i, [128, 4, 128], BF16) for i in range(4)]
            cbm = [ph.alloc("cbm%d" % i, [128, 128], F32) for i in range(2)]
            rhsM = [ph.alloc("rhsM%d" % i, [128, 2, 128], BF16) for i in range(4)]
            ed = [ph.alloc("ed%d" % i, [128, 256], F32) for i in range(4)]
            wTb = [ph.alloc("wT%d" % i, [128, 128], BF16) for i in range(4)]
            Csb = [ph.alloc("Cs%d" % i, [128, 128], BF16) for i in range(4)]
            xwb = [ph.alloc("xw%d" % i, [128, 256], BF16) for i in range(2)]
            htmp = [ph.alloc("htmp%d" % i, [128, 4, 64], F32) for i in range(2)]
            sqz = [ph.alloc("sqz%d" % i, [128, TT], BF16) for i in range(2)]
            assert ph.top <= S5T_OFF, ph.top
            cnt = {"h": 0, "g": 0}

            P.op("pool", lambda e: e.tensor_copy(out=xbcp[:, :, 0:3], in_=tail.ap), r=[tail], w=[xbcp])

            def cons_xbc(j, ps, cw):
                P.op("act", lambda e: e.activation(out=xbcp[:, j, 3:TT + 3], in_=ps[:, :], func=AF.Copy),
                     r=[ps], w=[(xbcp, j)])
            proj("w_in", 3072, 4096, 8, lambda c: (hT[:, c, :], (hT, c)), cons_xbc)
            P.op("pool", lambda e: e.tensor_copy(out=tail.ap, in_=xbcp[:, :, TT:TT + 3]), r=[xbcp], w=[tail])
            for oc in range(32):
                d = dg[oc % 4]
                for k in range(4):
                    P.op("act", lambda e, d=d, k=k, oc=oc: e.activation(
                        out=d[:, k, :], in_=identb.ap, func=AF.Copy, scale=convp[:, oc, k:k + 1]),
                        r=[identb, convp], w=[(d, k)])
                ps = next_ps()
                for k in range(4):
                    P.op("pe", lambda e, d=d, k=k, oc=oc, ps=ps: e.matmul(
                        ps[:, :], lhsT=d[:, k, :], rhs=xbcp[:, oc, k:k + TT], start=(k == 0), stop=(k == 3)),
                        r=[(d, k), (xbcp, oc)], w=[ps])
                if oc < 16:
                    dst, dk = xsT[:, oc, :], (xsT, oc)
                elif oc < 24:
                    dst, dk = Bfm[:, oc - 16, :], (Bfm, oc - 16)
                else:
                    dst, dk = Cfm[:, oc - 24, :], (Cfm, oc - 24)
                P.op("act", lambda e, dst=dst, ps=ps, oc=oc: e.activation(
                    out=dst, in_=ps[:, :], func=AF.Silu, bias=convp[:, oc, 4:5]), r=[ps, convp], w=[dk])

            def cons_dt(j, ps, cw):
                P.op("act", lambda e: e.activation(out=dtT.ap, in_=ps[0:32, :], func=AF.Exp, bias=hp[:, 1:2]),
                     r=[ps, hp], w=[dtT])
                P.op("act", lambda e: e.activation(out=dtT.ap, in_=dtT.ap, func=AF.Ln, bias=1.0), r=[dtT], w=[dtT])
                P.op("dve", lambda e: e.tensor_scalar(out=dAT.ap, in0=dtT.ap, scalar1=aneg[:, 0:1], scalar2=None,
                                                      op0=ALU.mult), r=[dtT, aneg], w=[dAT])
            proj("w_in", 7168, 32, 8, lambda c: (hT[:, c, :], (hT, c)), cons_dt, pw=128)
            ps = next_ps()
            for s in range(4):
                P.op("pe", lambda e, s=s, ps=ps: e.transpose(
                    out=ps[:, s * 64:s * 64 + 32], in_=dtT[:, s * 128:(s + 1) * 128], identity=cst[0:32, IDN, 0:32]),
                    r=[dtT, cst], w=[ps])
                P.op("pe", lambda e, s=s, ps=ps: e.transpose(
                    out=ps[:, s * 64 + 32:s * 64 + 64], in_=dAT[:, s * 128:(s + 1) * 128], identity=cst[0:32, IDN, 0:32]),
                    r=[dAT, cst], w=[ps])
            P.op("dve", lambda e, ps=ps: e.tensor_copy(out=dtok.ap, in_=ps[:, 0:256].rearrange("p (s c) -> p s c", s=4)),
                 r=[ps], w=[dtok])
            P.op("dve", lambda e: e.tensor_copy(out=dAh.ap, in_=dtok[:, :, 32:64]), r=[dtok], w=[dAh])
            P.op("dve", lambda e: e.tensor_tensor(out=dAr.ap, in0=dtok[:, :, 32:64], in1=dAh.ap, op=ALU.subtract),
                 r=[dtok, dAh], w=[dAr])
            P.op("dve", lambda e: e.tensor_copy(out=dAl.ap, in_=dAr.ap), r=[dAr], w=[dAl])

            for s in range(4):
                tk = slice(s * 128, (s + 1) * 128)
                for c0 in range(0, 24, 8):
                    ps = next_ps()
                    pb = psbf(ps)
                    for c in range(c0, c0 + 8):
                        src, sk = (xsT[:, c, tk], (xsT, c)) if c < 16 else (Bfm[:, c - 16, tk], (Bfm, c - 16))
                        P.op("pe", lambda e, c=c, c0=c0, pb=pb, src=src: e.transpose(
                            out=pb[:, (c - c0) * 128:(c - c0 + 1) * 128], in_=src, identity=identb.ap),
                            r=[sk, identb], w=[ps])
                    if c0 < 16:
                        P.op("act", lambda e, c0=c0, pb=pb: e.activation(
                            out=xtok[:, c0 * 128:(c0 + 8) * 128], in_=pb[:, :], func=AF.Copy), r=[ps], w=[xtok])
                    else:
                        P.op("act", lambda e, pb=pb: e.activation(out=Btok.ap, in_=pb[:, :], func=AF.Copy),
                             r=[ps], w=[Btok])
                ps = next_ps()
                for ii, dd in enumerate((dAh, dAl)):
                    P.op("pe", lambda e, s=s, ps=ps, ii=ii, dd=dd: e.matmul(ps[:, 0:32], lhsT=mub.ap, rhs=dd[:, s, :],
                                                                            start=(ii == 0), stop=(ii == 1)), r=[mub, dd], w=[ps])
                for ii, dd in enumerate((dAh, dAl)):
                    P.op("pe", lambda e, s=s, ps=ps, ii=ii, dd=dd: e.matmul(ps[:, 32:64], lhsT=onesb.ap, rhs=dd[:, s, :],
                                                                            start=(ii == 0), stop=(ii == 1)), r=[onesb, dd], w=[ps])
                P.op("act", lambda e, ps=ps: e.activation(out=acs.ap, in_=ps[:, 0:64], func=AF.Copy), r=[ps], w=[acs])
                P.op("dve", lambda e: e.tensor_tensor(out=wgt.ap, in0=acs[:, 32:64], in1=acs[:, 0:32], op=ALU.subtract),
                     r=[acs], w=[wgt])
                P.op("act", lambda e: e.activation(out=wgt.ap, in_=wgt.ap, func=AF.Exp), r=[wgt], w=[wgt])
                P.op("dve", lambda e, s=s: e.tensor_tensor(out=wgt.ap, in0=wgt.ap, in1=dtok[:, s, 0:32], op=ALU.mult),
                     r=[wgt, dtok], w=[wgt])
                P.op("act", lambda e: e.activation(out=cdec.ap, in_=acs[:, 32:64], func=AF.Exp), r=[acs], w=[cdec])
                rot["n"] = 5
                LAG = 2
                hbuf = {}

                def G0(g, tk=tk):
                    cb = cbm[g % 2]
                    ps = next_ps()
                    P.op("pe", lambda e: e.matmul(ps[:, 0:128], lhsT=Bfm[:, g, tk], rhs=Cfm[:, g, tk],
                                                  start=True, stop=True), r=[(Bfm, g), (Cfm, g)], w=[ps])
                    P.op("dve", lambda e: e.tensor_tensor(out=cb.ap, in0=ps[:, 0:128], in1=cst[:, MU, :],
                                                          op=ALU.mult), r=[ps, cst], w=[cb])

                def A(h, s=s):
                    i = h % 4
                    rm, e_ = rhsM[i], ed[i]
                    for ii, dd in enumerate((dAh, dAl)):
                        P.op("dve", lambda e, ii=ii, dd=dd: e.tensor_scalar(
                            out=rm[:, ii, :], in0=mub.ap, scalar1=dd[:, s, h:h + 1], scalar2=None,
                            op0=ALU.mult), r=[mub, dd], w=[(rm, ii)])
                    ps2 = next_ps()
                    for ii in range(2):
                        P.op("pe", lambda e, ii=ii: e.matmul(
                            ps2[:, 0:128], lhsT=m1b.ap, rhs=rm[:, ii, :], start=(ii == 0), stop=(ii == 1)),
                            r=[m1b, (rm, ii)], w=[ps2])
                    for ii in range(2):
                        P.op("pe", lambda e, ii=ii: e.matmul(
                            ps2[:, 128:256], lhsT=onesb.ap, rhs=rm[:, ii, :], start=(ii == 0), stop=(ii == 1)),
                            r=[onesb, (rm, ii)], w=[ps2])
                    P.op("act", lambda e: e.activation(out=e_.ap, in_=ps2[:, 0:256], func=AF.Exp), r=[ps2], w=[e_])

                def B(h, s=s, tk=tk):
                    g, k = h // 4, h % 4
                    i = h % 4
                    e_, wt, cs = ed[i], wTb[i], Csb[i]
                    cb = cbm[g % 2]
                    psy = psb[5 + (g % 2)]
                    P.op("dve", lambda e: e.scalar_tensor_tensor(
                        out=wt.ap, in0=e_[:, 0:128], scalar=dtok[:, s, h:h + 1], in1=cb.ap,
                        op0=ALU.mult, op1=ALU.mult), r=[e_, cb, dtok], w=[wt])
                    P.op("pool", lambda e: e.tensor_tensor(
                        out=cs.ap, in0=Cfm[:, g, tk], in1=e_[:, 128:256], op=ALU.mult),
                        r=[(Cfm, g), e_], w=[cs])
                    po, co = 64 * (k % 2), 128 * (k // 2)
                    P.op("pe", lambda e: e.matmul(
                        psy[po:po + 64, co:co + 128], lhsT=xtok[:, h * 64:(h + 1) * 64], rhs=wt.ap,
                        start=True, stop=False), r=[xtok, wt], w=[psy])
                    P.op("pe", lambda e: e.matmul(
                        psy[po:po + 64, co:co + 128], lhsT=hSb[:, h, :], rhs=cs.ap,
                        start=False, stop=True), r=[(hSb, h), cs], w=[psy])

                def G1(g, tk=tk):
                    if not light:
                        psy = psb[5 + (g % 2)]
                        for cl in range(2):
                            cc = 2 * g + cl
                            P.op("dve", lambda e, cl=cl, cc=cc: e.scalar_tensor_tensor(
                                out=ySS[:, cc, tk], in0=xsT[:, cc, tk], scalar=pc16[:, V_SSDD, cc:cc + 1],
                                in1=psy[:, cl * 128:(cl + 1) * 128], op0=ALU.mult, op1=ALU.add),
                                r=[psy, (xsT, cc), pc16], w=[(ySS, cc)])
                    xw = xwb[g % 2]
                    ht = htmp[g % 2]
                    P.op("dve", lambda e: e.tensor_tensor(
                        out=xw.ap.rearrange("p (k d) -> p k d", k=4),
                        in0=xtok[:, g * 256:(g + 1) * 256].rearrange("p (k d) -> p k d", k=4),
                        in1=wgt[:, 4 * g:4 * g + 4].unsqueeze(2).to_broadcast([128, 4, 64]), op=ALU.mult),
                        r=[xtok, wgt], w=[xw])
                    ps3 = next_ps()
                    P.op("pe", lambda e: e.matmul(
                        ps3[:, 0:256], lhsT=Btok[:, g * 128:(g + 1) * 128], rhs=xw.ap, start=True, stop=True),
                        r=[Btok, xw], w=[ps3])
                    P.op("pool", lambda e: e.tensor_tensor(
                        out=ht.ap, in0=hS[:, 4 * g:4 * g + 4, :],
                        in1=cdec[:, 4 * g:4 * g + 4].unsqueeze(2).to_broadcast([128, 4, 64]), op=ALU.mult),
                        r=[(hS, 4 * g, 4 * g + 4), cdec], w=[ht])
                    P.op("dve", lambda e: e.tensor_tensor(
                        out=hS[:, 4 * g:4 * g + 4, :], in0=ps3[:, 0:256].rearrange("p (k d) -> p k d", k=4),
                        in1=ht.ap, op=ALU.add), r=[ps3, ht], w=[(hS, 4 * g, 4 * g + 4)])
                    P.op("act", lambda e: e.activation(out=hSb[:, 4 * g:4 * g + 4, :], in_=hS[:, 4 * g:4 * g + 4, :],
                                                       func=AF.Copy), r=[(hS, 4 * g, 4 * g + 4)], w=[(hSb, 4 * g, 4 * g + 4)])

                if light:
                    for g in range(8):
                        G1(g)
                else:
                    for idx in range(32 + LAG):
                        if idx < 32:
                            if idx % 4 == 0:
                                G0(idx // 4)
                            A(idx)
                        if idx >= LAG:
                            hb = idx - LAG
                            B(hb)
                            if hb % 4 == 3:
                                G1(hb // 4)
                rot["n"] = NROT
            if light:
                return None
            if DEBUG and st.get("dbg_ssd_done") is None:
                st["dbg_ssd_done"] = 1
                dump("dbg_xs", xsT, 16); dump("dbg_B", Bfm, 8); dump("dbg_C", Cfm, 8); dump("dbg_y", ySS, 16)
            pss = psb[7]

            def cons_z(j, ps, cw):
                g = gsb[st["g"] % 2]
                st["g"] += 1
                sq_ = sqz[j % 2]
                P.op("act", lambda e: e.activation(out=g.ap, in_=ps[:, :], func=AF.Silu), r=[ps], w=[g])
                P.op("dve", lambda e: e.tensor_tensor(out=ySS[:, j, :], in0=ySS[:, j, :], in1=g.ap, op=ALU.mult),
                     r=[(ySS, j), g], w=[(ySS, j)])
                P.op("act", lambda e: e.activation(out=sq_.ap, in_=ySS[:, j, :], func=AF.Square), r=[(ySS, j)], w=[sq_])
                P.op("pe", lambda e: e.matmul(pss[:, :], lhsT=onesb.ap, rhs=sq_.ap, start=(j == 0), stop=(j == 15)),
                     r=[onesb, sq_], w=[pss])
            proj("w_in", 1024, 2048, 8, lambda c: (hT[:, c, :], (hT, c)), cons_z)
            rstd_from(pss, 1.0 / 2048)
            for c in range(16):
                P.op("dve", lambda e, c=c: e.scalar_tensor_tensor(
                    out=ySS[:, c, :], in0=ySS[:, c, :], scalar=pc16[:, V_SSDN, c:c + 1], in1=rstd.ap,
                    op0=ALU.mult, op1=ALU.mult), r=[(ySS, c), pc16, rstd], w=[(ySS, c)])
            return ySS

        def merge_tile(s5T, ynT):
            G = P.view("G", MRG_OFF, [128, 16, TT], BF16)
            Q = P.view("Q", MRG_OFF + 16 * TT * 2, [128, 8, TT], BF16)
            mT = P.view("mT", MRG_OFF + 24 * TT * 2, [128, 8, TT], BF16)
            assert MRG_OFF + 32 * TT * 2 <= S5T_OFF and MRG_OFF >= ynT.hi, (MRG_OFF, ynT.hi, S5T_OFF)

            def cons_g(j, ps, cw):
                P.op("act", lambda e: e.activation(out=G[:, j, :], in_=ps[:, :], func=AF.Sigmoid,
                                                   bias=pc16[:, V_BGATE, j:j + 1]), r=[ps, pc16], w=[(G, j)])
            proj("w_in", 7200, 2048, 8, lambda c: (hT[:, c, :], (hT, c)), cons_g)

            def cons_q(j, ps, cw):
                P.op("dve", lambda e: e.tensor_tensor(out=Q[:, j, :], in0=ps[:, :], in1=G[:, j, :], op=ALU.mult),
                     r=[ps, (G, j)], w=[(Q, j)])
            proj("w_proj_s5", 0, 1024, 8, lambda c: (s5T[:, c, :], (s5T, c)), cons_q)

            def cons_m(j, ps, cw):
                g = gsb[st["g"] % 2]
                st["g"] += 1
                P.op("dve", lambda e: e.tensor_tensor(out=g.ap, in0=ps[:, :], in1=G[:, 8 + j, :], op=ALU.mult),
                     r=[ps, (G, 8 + j)], w=[g])
                P.op("pool", lambda e: e.tensor_tensor(out=mT[:, j, :], in0=g.ap, in1=Q[:, j, :], op=ALU.add),
                     r=[g, (Q, j)], w=[(mT, j)])
            proj("w_proj_ssd", 0, 1024, 16, lambda c: (ynT[:, c, :], (ynT, c)), cons_m)

            def cons_o(j, ps, cw):
                P.op("dve", lambda e: e.tensor_tensor(out=xres[:, j, :], in0=ps[:, :], in1=xres[:, j, :], op=ALU.add),
                     r=[ps, (xres, j)], w=[(xres, j)])
            proj("w_out", 0, 1024, 8, lambda c: (mT[:, c, :], (mT, c)), cons_o)
            return mT

        def store_out(t_own):
            ph = Phase()
            sqb = ph.alloc("sqb", [128, 8, TT], BF16)
            rmsnorm_to(xres, G_FIN, sqb)
            for s in range(4):
                ob = osb[0]
                for c0 in range(0, 8, 4):
                    ps = next_ps()
                    for c in range(c0, c0 + 4):
                        P.op("pe", lambda e, ps=ps, c=c, c0=c0, s=s: e.transpose(
                            out=ps[:, (c - c0) * 128:(c - c0 + 1) * 128], in_=xres[:, c, s * 128:(s + 1) * 128],
                            identity=cst[:, IDN, :]), r=[(xres, c), cst], w=[ps])
                    P.op("act", lambda e, ps=ps, c0=c0, ob=ob: e.activation(
                        out=ob[:, c0 * 128:(c0 + 4) * 128], in_=ps[:, :], func=AF.Copy), r=[ps], w=[ob])
                r0 = t_own * TT + s * 128
                tok = P.op("sp", lambda e, ob=ob, r0=r0: e.dma_start(out=out[r0:r0 + 128, :], in_=ob.ap),
                           r=[ob], w=["outdram"], dma=ob.key)
                P.final[ob.key] = (tok[0], tok[1])

        import os as _os
        STOP = int(_os.environ.get("KSTOP", "9"))
        if STOP >= 1 or STOP == -1:
            s5_setup()
        for tile in range(NT_PRE + NT_OWN if STOP >= 0 else 0):
            own = tile >= NT_PRE
            light = LIGHT and not own
            if tile == NT_PRE:
                fl_ = cstv[:, FLAG:FLAG + 1]
                for b_ in (Sst[st["s"] % 2], Ssw[st["s"] % 2], hS, hSb, tail):
                    P.op("dve", lambda e, b_=b_: e.tensor_scalar(out=b_.ap, in0=b_.ap, scalar1=fl_, scalar2=None,
                                                                 op0=ALU.mult), r=[b_, cstv], w=[b_])
            load_x(tile)
            ffn("ffn1", G_FFN1)
            if tile == NT_PRE:
                dump("dbg_x1", xres, 8)
            s5T = ynT = None
            if STOP >= 2:
                ph = Phase()
                sqb = ph.alloc("sqb", [128, 8, TT], BF16)
                rmsnorm_to(hT, G_MIX, sqb)
                s5T = s5_tile(light)
            if STOP >= 3:
                ynT = ssd_tile(light)
            if own:
                if STOP >= 4:
                    if tile == NT_PRE:
                        dump("dbg_s5", s5T, 8)
                        dump("dbg_ssd", ynT, 16)
                    mT = merge_tile(s5T, ynT)
                    if tile == NT_PRE:
                        dump("dbg_m", mT, 8)
                    ffn("ffn2", G_FFN2)
                store_out(tile - NT_PRE)
        P.emit()
    return nc


_CACHE = {}


def _pc(v, n):
    return np.ascontiguousarray(np.asarray(v, np.float32).reshape(n, 128).T)


def prep_inputs(inputs):
    f = lambda k: np.asarray(inputs[k], dtype=np.float32)
    common = {}
    for n in WSHAPES:
        common[n] = np.ascontiguousarray(f(n)[0])
    pc8 = np.stack([_pc(f("ffn1_norm")[0], 8), _pc(f("mix_norm")[0], 8), _pc(f("ffn2_norm")[0], 8),
                    _pc(f("final_norm"), 8), _pc(f("s5_D")[0], 8), _pc(f("s5_b_glu")[0], 8)], axis=1)
    common["pc8"] = np.ascontiguousarray(pc8)
    pc16 = np.stack([_pc(f("ssd_norm")[0], 16), _pc(f("b_gate")[0], 16),
                     _pc(np.repeat(f("ssd_D")[0], 64), 16)], axis=1)
    common["pc16"] = np.ascontiguousarray(pc16)
    cw = f("conv_w")[0]
    convp = np.zeros((128, 32, 5), np.float32)
    for k in range(4):
        convp[:, :, k] = _pc(cw[k], 32)
    convp[:, :, 4] = _pc(f("conv_b")[0], 32)
    common["convp"] = convp
    common["hp"] = np.ascontiguousarray(np.stack([f("ssd_A_log")[0], f("ssd_dt_bias")[0]], axis=1))
    dup = lambda a: np.concatenate([a, a], axis=0)
    s5a = np.stack([dup(f("s5_A_re")[0].T), dup(f("s5_A_im")[0].T),
                    np.broadcast_to(f("s5_log_dt")[0][None, :], (128, 64))], axis=1)
    common["s5a"] = np.ascontiguousarray(s5a)
    common["s5b"] = np.ascontiguousarray(np.stack([dup(f("s5_B_re")[0].transpose(1, 0, 2)),
                                                   dup(f("s5_B_im")[0].transpose(1, 0, 2))], axis=1))
    common["s5c"] = np.ascontiguousarray(np.stack([dup(f("s5_C_re")[0].transpose(2, 0, 1)),
                                                   dup(f("s5_C_im")[0].transpose(2, 0, 1))], axis=1))
    i = np.arange(128)
    cst = np.zeros((128, 5, 128), np.float32)
    cst[:, 0, :] = np.eye(128)
    cst[:, 1, :] = (i[:, None] <= i[None, :])
    cst[:, 2, :] = (i[:, None] > i[None, :])
    cst[:, 3, :] = 1.0
    cst[:, 4, :] = ((i[None, :] // 16) >= (i[:, None] // 16))
    common["cst"] = cst
    return common


def kernel(**inputs):
    x = np.asarray(inputs["x"], dtype=np.float32)
    B, L, _ = x.shape
    half = L // 2
    if "nc" not in _CACHE:
        _CACHE["nc"] = build_program()
    nc = _CACHE["nc"]
    common = prep_inputs(inputs)
    in_maps = []
    for c in range(8):
        b, h = c // 2, c % 2
        xs = np.zeros((L, D), np.float32)
        if h == 0:
            xs[half:] = x[b, :half]
        else:
            xs[:] = x[b]
        m = dict(common)
        m["xs"] = xs
        cv = np.zeros((128, 4), np.float32)
        cv[:64, 0], cv[64:, 0] = -1.0, 1.0
        cv[:64, 1] = 1.0
        cv[64:, 2] = 1.0
        cv[:, 3] = float(h)
        m["cstv"] = cv
        in_maps.append(m)
    res = run_bass_kernel_spmd(nc, in_maps, core_ids=list(range(8)))
    _CACHE["res"] = res
    outp = np.zeros((B, L, D), np.float32)
    for c in range(8):
        b, h = c // 2, c % 2
        outp[b, h * half:(h + 1) * half] = np.asarray(res.results[c]["out"], dtype=np.float32)
    return outp
```

```python
import numpy as np
import ml_dtypes
from contextlib import ExitStack
import concourse.bass as bass
import concourse.mybir as mybir
from concourse.bass_utils import run_bass_kernel_spmd

F32 = mybir.dt.float32
BF16 = mybir.dt.bfloat16
I32 = mybir.dt.int32
U8 = mybir.dt.uint8
ALU = mybir.AluOpType
AF = mybir.ActivationFunctionType
DTSZ = {F32: 4, BF16: 2, I32: 4, U8: 1}

D = 1024
DFF = 2816
DIN = 9248
TT = 512
NT_PRE = 4
NT_OWN = 4
EPS = 1e-6
ARENA = 206 * 1024
LIGHT = True
DEBUG = False


def _prod(s):
    r = 1
    for v in s:
        r *= v
    return r


class Buf:
    def __init__(self, ap, key, space, lo, hi, shape, dt):
        self.ap, self.key, self.space, self.lo, self.hi, self.shape, self.dt = ap, key, space, lo, hi, shape, dt
        self.sub_bytes = (_prod(shape[2:]) * DTSZ[dt]) if len(shape) > 2 else None

    def __getitem__(self, idx):
        return self.ap[idx]


class Prog:
    ENG = ("pe", "act", "dve", "pool", "sp")
    EPOCH = 12000

    def __init__(self, nc, es):
        self.nc, self.es = nc, es
        self.ops = {e: [] for e in self.ENG}
        self.count = {e: 0 for e in self.ENG}
        self.esems = {e: [] for e in self.ENG}
        self.dsems = {}
        self.lastw = {}
        self.readers = {}
        self.spaces = {}
        self.waited = {e: {} for e in self.ENG}
        self.final = {}
        self.sig = {e: set() for e in self.ENG}
        self.arena = es.enter_context(nc.sbuf_tensor("arena", [128, ARENA], U8))
        self.top = 0
        self.nps = 0

    def view(self, name, off, shape, dt):
        nb = _prod(shape[1:]) * DTSZ[dt]
        assert off + nb <= ARENA, (name, off, nb)
        ap = self.arena[0:shape[0], off:off + nb].bitcast(dt)
        if len(shape) == 3:
            ap = ap.rearrange("p (a b) -> p a b", a=shape[1])
        elif len(shape) == 4:
            ap = ap.rearrange("p (a b c) -> p a b c", a=shape[1], b=shape[2])
        return Buf(ap, "%s@%d" % (name, off), "arena", off, off + nb, shape, dt)

    def alloc(self, name, shape, dt):
        b = self.view(name, self.top, shape, dt)
        self.top = (b.hi + 31) // 32 * 32
        return b

    def psum(self, name):
        t = self.es.enter_context(self.nc.psum_tensor(name, [128, 512], F32))
        return Buf(t[:, :], name, name, 0, 2048, [128, 512], F32)

    def sem(self, name):
        return self.es.enter_context(self.nc.semaphore(name))

    def _norm(self, it):
        if isinstance(it, Buf):
            return (it.key, it.space, it.lo, it.hi)
        if isinstance(it, tuple):
            b = it[0]
            i0 = it[1]
            i1 = it[2] if len(it) > 2 else i0 + 1
            return ((b.key, i0, i1), b.space, b.lo + i0 * b.sub_bytes, b.lo + i1 * b.sub_bytes)
        return (it, it, 0, 1)

    def _related(self, n):
        key, space, lo, hi = n
        sp = self.spaces.setdefault(space, {})
        if key not in sp:
            sp[key] = (lo, hi)
        return [k for k, (l, h) in sp.items() if l < hi and lo < h]

    def op(self, eng, fn, r=(), w=(), dma=None):
        rn = [self._norm(x) for x in r]
        wn = [self._norm(x) for x in w]
        deps = []
        for n in rn:
            for k in self._related(n):
                if k in self.lastw:
                    deps.append(self.lastw[k])
        for n in wn:
            for k in self._related(n):
                if k in self.lastw:
                    deps.append(self.lastw[k])
                deps.extend(self.readers.get(k, {}).values())
        waits = []
        for tok in deps:
            if tok[0] == "dma":
                _, sem, val = tok
                cur = self.waited[eng].get(id(sem), 0)
                if val > cur:
                    self.waited[eng][id(sem)] = val
                    waits.append(tok)
            else:
                _, seng, cidx = tok
                if seng == "pe" and eng == "pe" and dma is None:
                    continue
                cur = self.waited[eng].get(seng, -1)
                if cidx > cur:
                    self.waited[eng][seng] = cidx
                    waits.append(tok)
                    self.sig[seng].add(cidx)
        if dma is not None:
            if dma not in self.dsems:
                self.dsems[dma] = [self.sem("d_" + str(dma)), 0]
            ds = self.dsems[dma]
            ds[1] += 16
            tok = ("dma", ds[0], ds[1])
            self.ops[eng].append((waits, fn, ("dma", ds[0])))
            rkey = id(ds[0])
        else:
            c = self.count[eng]
            self.count[eng] = c + 1
            tok = ("c", eng, c)
            self.ops[eng].append((waits, fn, ("c", c)))
            rkey = eng
        for n in wn:
            self.lastw[n[0]] = tok
            self.readers[n[0]] = {}
        for n in rn:
            self.readers.setdefault(n[0], {})[rkey] = tok
        return tok

    def emit(self):
        semval = {}
        for eng in self.ENG:
            k = 0
            for c in range(self.count[eng]):
                if c in self.sig[eng]:
                    ep = k // self.EPOCH
                    while len(self.esems[eng]) <= ep:
                        self.esems[eng].append(self.sem("e_%s_%d" % (eng, len(self.esems[eng]))))
                    semval[(eng, c)] = (self.esems[eng][ep], k - ep * self.EPOCH + 1)
                    k += 1
        with self.nc.Block() as block:
            def mk(engname):
                def body(e):
                    for waits, fn, inc in self.ops[engname]:
                        for tok in waits:
                            if tok[0] == "dma":
                                e.wait_ge(tok[1], tok[2])
                            else:
                                s_, v_ = semval[(tok[1], tok[2])]
                                e.wait_ge(s_, v_)
                        ins = fn(e)
                        if inc[0] == "dma":
                            ins.then_inc(inc[1], 16)
                        elif (engname, inc[1]) in semval:
                            ins.then_inc(semval[(engname, inc[1])][0], 1)
                    if engname == "sp":
                        for tok in self.final.values():
                            e.wait_ge(tok[1], tok[2])
                return body
            block.tensor(mk("pe"))
            block.scalar(mk("act"))
            block.vector(mk("dve"))
            block.gpsimd(mk("pool"))
            block.sync(mk("sp"))


WSHAPES = {"ffn1_w_gate": (D, DFF), "ffn1_w_up": (D, DFF), "ffn1_w_down": (DFF, D),
           "w_in": (D, DIN), "s5_w_glu": (D, D), "w_proj_s5": (D, D), "w_proj_ssd": (2 * D, D),
           "w_out": (D, D),
           "ffn2_w_gate": (D, DFF), "ffn2_w_up": (D, DFF), "ffn2_w_down": (DFF, D)}
PSHAPES = {"pc8": [128, 6, 8], "pc16": [128, 3, 16], "convp": [128, 32, 5], "hp": [32, 2],
           "s5a": [128, 3, 64], "s5b": [128, 2, 64, 16], "s5c": [128, 2, 64, 16],
           "cst": [128, 5, 128], "cstv": [128, 4]}


def build_program():
    nc = bass.Bass("TRN2", target_bir_lowering=False)
    NTOK = (NT_PRE + NT_OWN) * TT
    din = {}
    for n, s in WSHAPES.items():
        din[n] = nc.dram_tensor(n, list(s), F32, kind="ExternalInput").ap()
    for n, s in PSHAPES.items():
        din[n] = nc.dram_tensor(n, list(s), F32, kind="ExternalInput").ap()
    xs = nc.dram_tensor("xs", [NTOK, D], F32, kind="ExternalInput").ap()
    out = nc.dram_tensor("out", [NT_OWN * TT, D], F32, kind="ExternalOutput").ap()
    wbf = {n: nc.dram_tensor(n + "_bf", list(s), BF16, kind="Internal").ap() for n, s in WSHAPES.items()}
    s5B = nc.dram_tensor("s5B_tab", [8, 128, 2, 8, 128], BF16, kind="Internal").ap()
    s5K = nc.dram_tensor("s5K_tab", [8, 128, 2, 8, 128], BF16, kind="Internal").ap()
    dbg = {}
    if DEBUG:
        for n, c in (("dbg_s5", 8), ("dbg_ssd", 16), ("dbg_m", 8), ("dbg_x1", 8), ("dbg_xs", 16), ("dbg_B", 8), ("dbg_C", 8), ("dbg_y", 16), ("dbg_dt", 1)):
            dbg[n] = nc.dram_tensor(n, [128, c, TT], F32, kind="ExternalOutput").ap()

    with ExitStack() as es:
        P = Prog(nc, es)
        psb = [P.psum("ps%d" % i) for i in range(8)]
        NROT = 7

        rot = {"n": NROT}

        def next_ps():
            P.nps = (P.nps + 1) % rot["n"]
            return psb[P.nps]

        def psbf(ps):
            return ps.ap.bitcast(BF16)

        cst = P.alloc("cst", [128, 5, 128], F32)
        IDN, MU, M1, ONE, KMASK = 0, 1, 2, 3, 4
        cstv = P.alloc("cstv", [128, 4], F32)
        SGN, SELT, SELB, FLAG = 0, 1, 2, 3
        pc8 = P.alloc("pc8", [128, 6, 8], F32)
        G_FFN1, G_MIX, G_FFN2, G_FIN, V_S5D, V_BGLU = range(6)
        pc16 = P.alloc("pc16", [128, 3, 16], F32)
        V_SSDN, V_BGATE, V_SSDD = range(3)
        convp = P.alloc("convp", [128, 32, 5], F32)
        hp = P.alloc("hp", [32, 2], F32)
        aneg = P.alloc("aneg", [32, 1], F32)
        identb = P.alloc("identb", [128, 128], BF16)
        onesb = P.alloc("onesb", [128, 128], BF16)
        mub = P.alloc("mub", [128, 128], BF16)
        m1b = P.alloc("m1b", [128, 128], BF16)
        hSb = P.alloc("hSb", [128, 32, 64], BF16)
        a8r = P.alloc("a8r", [128, 64], F32)
        a8s = P.alloc("a8s", [128, 64], F32)
        a8n = P.alloc("a8n", [128, 64], F32)
        Sst = [P.alloc("Sst%d" % i, [128, 64], F32) for i in range(2)]
        Ssw = [P.alloc("Ssw%d" % i, [128, 64], F32) for i in range(2)]
        hS = P.alloc("hS", [128, 32, 64], F32)
        tail = P.alloc("tail", [128, 32, 3], BF16)
        xres = P.alloc("xres", [128, 8, TT], F32)
        hT = P.alloc("hT", [128, 8, TT], BF16)
        rstd = P.alloc("rstd", [128, TT], F32)
        gsb = [P.alloc("gsb%d" % i, [128, TT], F32) for i in range(2)]
        NPAN = 3
        pans = [P.alloc("pan%d" % i, [128, 4096], BF16) for i in range(NPAN)]
        s5tab = P.alloc("s5tab", [128, 2, 8, 128], BF16)
        xin = P.alloc("xin", [128, D], F32)
        osb = [P.alloc("osb%d" % i, [128, D], F32) for i in range(1)]
        BASE = P.top
        st = {"pan": 0, "g": 0, "o": 0, "s": 0}
        S5T_OFF = ARENA - 8 * TT * 2
        MRG_OFF = BASE + 16 * TT * 2 + 64

        class Phase:
            def __init__(self, base=BASE):
                self.top = base

            def alloc(self, name, shape, dt):
                b = P.view(name, self.top, shape, dt)
                self.top = (b.hi + 31) // 32 * 32
                return b

        def load_params(items, semkey):
            for b, src in items:
                P.op("sp", lambda e, b=b, src=src: e.dma_start(out=b.ap, in_=src), w=[b], dma=semkey)
            sem, total = P.dsems[semkey]
            for b, _ in items:
                P.lastw[b.key] = ("dma", sem, total)

        load_params([(cst, din["cst"]), (cstv, din["cstv"]), (pc8, din["pc8"]), (pc16, din["pc16"]),
                     (convp, din["convp"]), (hp, din["hp"])], "par")
        P.op("dve", lambda e: e.tensor_copy(out=identb.ap, in_=cst[:, IDN, :]), r=[cst], w=[identb])
        P.op("pool", lambda e: e.memset(onesb.ap, 1.0), w=[onesb])
        P.op("dve", lambda e: e.tensor_copy(out=mub.ap, in_=cst[:, MU, :]), r=[cst], w=[mub])
        P.op("dve", lambda e: e.tensor_copy(out=m1b.ap, in_=cst[:, M1, :]), r=[cst], w=[m1b])
        P.op("pool", lambda e: e.memset(hSb.ap, 0.0), w=[hSb])
        P.op("pool", lambda e: e.memset(hS.ap, 0.0), w=[hS])
        P.op("pool", lambda e: e.memset(tail.ap, 0.0), w=[tail])
        P.op("pool", lambda e: e.memset(Sst[0].ap, 0.0), w=[Sst[0]])
        P.op("pool", lambda e: e.memset(Ssw[0].ap, 0.0), w=[Ssw[0]])
        P.op("act", lambda e: e.activation(out=aneg.ap, in_=hp[:, 0:1], func=AF.Exp), r=[hp], w=[aneg])
        P.op("dve", lambda e: e.tensor_scalar(out=aneg.ap, in0=aneg.ap, scalar1=-1.0, scalar2=None, op0=ALU.mult),
             r=[aneg], w=[aneg])

        import os as _os2
        for n, (k, m) in (WSHAPES.items() if int(_os2.environ.get("KSTOP", "9")) >= 0 else []):
            rows = 256
            for r0 in range(0, k, rows):
                rr = min(rows, k - r0)
                P.op("pool", lambda e, n=n, r0=r0, rr=rr: e.dma_start(
                    out=wbf[n][r0:r0 + rr, :], in_=din[n][r0:r0 + rr, :]), w=[n + "_bf"], dma="cast_" + n)

        def get_pan():
            p = pans[st["pan"] % NPAN]
            st["pan"] += 1
            return p

        def proj(wname, col0, ncols, KC, rhs, consume, pw=None):
            W = wbf[wname]
            if pw is None:
                pw = min(512, (4096 // KC) // 128 * 128)
            j = 0
            for p0 in range(0, ncols, pw):
                w_ = min(pw, ncols - p0)
                pan = get_pan()
                pv = pan.ap[:, 0:KC * w_].rearrange("p (c n) -> p c n", c=KC)
                P.op("sp", lambda e, pv=pv, p0=p0, w_=w_: e.dma_start(
                    out=pv, in_=W[:, col0 + p0:col0 + p0 + w_].rearrange("(c p) n -> p c n", p=128)),
                    r=[wname + "_bf"], w=[pan], dma=pan.key)
                for q0 in range(0, w_, 128):
                    cw = min(128, w_ - q0)
                    ps = next_ps()
                    for c in range(KC):
                        rap, rdep = rhs(c)
                        P.op("pe", lambda e, pv=pv, ps=ps, c=c, q0=q0, cw=cw, rap=rap: e.matmul(
                            ps[0:cw, :], lhsT=pv[:, c, q0:q0 + cw], rhs=rap, start=(c == 0), stop=(c == KC - 1)),
                            r=[pan, rdep], w=[ps])
                    consume(j, ps, cw)
                    j += 1

        def rstd_from(ps, scale):
            P.op("dve", lambda e, ps=ps: e.tensor_scalar(out=rstd.ap, in0=ps[:, :], scalar1=scale, scalar2=EPS,
                                                         op0=ALU.mult, op1=ALU.add), r=[ps], w=[rstd])
            P.op("act", lambda e: e.activation(out=rstd.ap, in_=rstd.ap, func=AF.Ln), r=[rstd], w=[rstd])
            P.op("act", lambda e: e.activation(out=rstd.ap, in_=rstd.ap, func=AF.Exp, scale=-0.5), r=[rstd], w=[rstd])

        def rmsnorm_to(dst, gcol, sqbuf):
            for c in range(8):
                P.op("act", lambda e, c=c: e.activation(out=sqbuf[:, c, :], in_=xres[:, c, :], func=AF.Square),
                     r=[(xres, c)], w=[(sqbuf, c)])
            ps = next_ps()
            for c in range(8):
                P.op("pe", lambda e, c=c, ps=ps: e.matmul(ps[:, :], lhsT=onesb.ap, rhs=sqbuf[:, c, :],
                                                            start=(c == 0), stop=(c == 7)),
                     r=[onesb, (sqbuf, c)], w=[ps])
            rstd_from(ps, 1.0 / D)
            for c in range(8):
                P.op("dve", lambda e, c=c: e.scalar_tensor_tensor(
                    out=dst[:, c, :], in0=xres[:, c, :], scalar=pc8[:, gcol, c:c + 1], in1=rstd.ap,
                    op0=ALU.mult, op1=ALU.mult), r=[(xres, c), pc8, rstd], w=[(dst, c)])

        def load_x(tile):
            for s in range(4):
                src = xs[tile * TT + s * 128: tile * TT + (s + 1) * 128, :]
                P.op("sp", lambda e, src=src: e.dma_start(out=xin.ap, in_=src), w=[xin], dma="xin")
                for c0 in range(0, 8, 4):
                    ps = next_ps()
                    for c in range(c0, c0 + 4):
                        P.op("pe", lambda e, c=c, c0=c0, ps=ps: e.transpose(
                            out=ps[:, (c - c0) * 128:(c - c0 + 1) * 128], in_=xin[:, c * 128:(c + 1) * 128],
                            identity=cst[:, IDN, :]), r=[xin, cst], w=[ps])
                    P.op("act", lambda e, c0=c0, s=s, ps=ps: e.activation(
                        out=xres[:, c0:c0 + 4, s * 128:(s + 1) * 128],
                        in_=ps[:, :].rearrange("p (c t) -> p c t", c=4), func=AF.Copy),
                        r=[ps], w=[(xres, c0, c0 + 4)])

        def dump(name, buf, nch):
            if not DEBUG:
                return
            for c in range(nch):
                g = gsb[st["g"] % 2]
                st["g"] += 1
                P.op("dve", lambda e, g=g, c=c: e.tensor_copy(out=g.ap, in_=buf[:, c, :]), r=[(buf, c)], w=[g])
                tok = P.op("sp", lambda e, g=g, c=c: e.dma_start(out=dbg[name][:, c, :], in_=g.ap),
                           r=[g], w=[name], dma=g.key)
                P.final[g.key] = tok

        def ffn(pref, gcol):
            ph = Phase()
            act = ph.alloc("act", [128, 22, TT], BF16)
            sqb = ph.alloc("sqb", [128, 8, TT], BF16)
            rmsnorm_to(hT, gcol, sqb)
            for p0 in range(0, DFF, 512):
                pw = min(512, DFF - p0)
                pp = []
                for wn in (pref + "_w_gate", pref + "_w_up"):
                    pan = get_pan()
                    pv = pan.ap[:, 0:8 * pw].rearrange("p (c n) -> p c n", c=8)
                    P.op("sp", lambda e, pv=pv, wn=wn, p0=p0, pw=pw: e.dma_start(
                        out=pv, in_=wbf[wn][:, p0:p0 + pw].rearrange("(c p) n -> p c n", p=128)),
                        r=[wn + "_bf"], w=[pan], dma=pan.key)
                    pp.append((pan, pv))
                for j in range(pw // 128):
                    oc = (p0 // 128) + j
                    psg, psu = next_ps(), next_ps()
                    for (pan, pv), ps in ((pp[0], psg), (pp[1], psu)):
                        for c in range(8):
                            P.op("pe", lambda e, pv=pv, ps=ps, c=c, j=j: e.matmul(
                                ps[:, :], lhsT=pv[:, c, j * 128:(j + 1) * 128], rhs=hT[:, c, :],
                                start=(c == 0), stop=(c == 7)), r=[pan, (hT, c)], w=[ps])
                    g = gsb[st["g"] % 2]
                    st["g"] += 1
                    P.op("act", lambda e, g=g, psg=psg: e.activation(out=g.ap, in_=psg[:, :], func=AF.Silu),
                         r=[psg], w=[g])
                    P.op("dve", lambda e, g=g, psu=psu, oc=oc: e.tensor_tensor(
                        out=act[:, oc, :], in0=psu[:, :], in1=g.ap, op=ALU.mult), r=[psu, g], w=[(act, oc)])

            def fin(j, ps, cw):
                P.op("dve", lambda e, ps=ps, j=j: e.scalar_tensor_tensor(
                    out=xres[:, j, :], in0=ps[:, :], scalar=0.5, in1=xres[:, j, :],
                    op0=ALU.mult, op1=ALU.add), r=[ps, (xres, j)], w=[(xres, j)])
            proj(pref + "_w_down", 0, D, 22, lambda c: (act[:, c, :], (act, c)), fin, pw=128)

        def s5_setup():
            ph = Phase()
            A = ph.alloc("s5a", [128, 3, 64], F32)
            Bi = ph.alloc("s5b", [128, 2, 64, 16], F32)
            Ci = ph.alloc("s5c", [128, 2, 64, 16], F32)
            load_params([(A, din["s5a"]), (Bi, din["s5b"]), (Ci, din["s5c"])], "par2")
            T = {}
            TMP = Buf(None, "s5tmp", "arena", BASE, ARENA, [128, ARENA - BASE], U8)
            RD = [A, Bi, Ci, TMP, cstv]

            def t(name):
                if name not in T:
                    T[name] = ph.alloc("t_" + name, [128, 64], F32).ap
                return T[name]

            def tt(o, a, b, op):
                P.op("dve", lambda e: e.tensor_tensor(out=o, in0=a, in1=b, op=op), r=RD, w=[TMP])

            def ts(o, a, s1, s2, op0, op1=None):
                if op1 is None:
                    P.op("dve", lambda e: e.tensor_scalar(out=o, in0=a, scalar1=s1, scalar2=None, op0=op0), r=RD, w=[TMP])
                else:
                    P.op("dve", lambda e: e.tensor_scalar(out=o, in0=a, scalar1=s1, scalar2=s2, op0=op0, op1=op1),
                         r=RD, w=[TMP])

            def stt(o, a, s, b, op0, op1):
                P.op("dve", lambda e: e.scalar_tensor_tensor(out=o, in0=a, scalar=s, in1=b, op0=op0, op1=op1),
                     r=RD, w=[TMP])

            def ac(o, a, func, scale=1.0):
                P.op("act", lambda e: e.activation(out=o, in_=a, func=func, scale=scale), r=RD, w=[TMP])

            def rcp(o, a):
                P.op("dve", lambda e: e.reciprocal(out=o, in_=a), r=RD, w=[TMP])

            import os as _os
            SL = int(_os.environ.get("S5STOP", "99"))
            lr, li, ldt = A[:, 0, :], A[:, 1, :], A[:, 2, :]
            ac(t("dt"), ldt, AF.Exp)
            tt(t("lrdt"), lr, t("dt"), ALU.mult)
            tt(t("th"), li, t("dt"), ALU.mult)
            ac(t("mag"), t("lrdt"), AF.Exp)
            ts(t("v"), t("th"), 1.0 / (2 * np.pi), None, ALU.mult)
            if SL <= 1:
                return
            vi = ph.alloc("t_vi", [128, 64], I32).ap
            P.op("dve", lambda e: e.tensor_copy(out=vi, in_=t("v")), r=RD, w=[TMP])
            P.op("dve", lambda e: e.tensor_copy(out=t("vf"), in_=vi), r=RD, w=[TMP])
            tt(t("fr"), t("v"), t("vf"), ALU.subtract)
            ts(t("g"), t("fr"), 0.5, None, ALU.is_gt)
            tt(t("fr"), t("fr"), t("g"), ALU.subtract)
            ts(t("g"), t("fr"), -0.5, None, ALU.is_lt)
            tt(t("fr"), t("fr"), t("g"), ALU.add)
            if SL <= 2:
                return
            ac(t("sin"), t("fr"), AF.Sin, scale=6.28318)
            ts(t("frc"), t("fr"), 0.25, None, ALU.add)
            ts(t("g"), t("frc"), 0.5, None, ALU.is_gt)
            tt(t("frc"), t("frc"), t("g"), ALU.subtract)
            ac(t("cos"), t("frc"), AF.Sin, scale=6.28318)
            if SL <= 3:
                return
            ar, ai = t("ar"), t("ai")
            tt(ar, t("mag"), t("cos"), ALU.mult)
            tt(ai, t("mag"), t("sin"), ALU.mult)
            tt(t("den"), lr, lr, ALU.mult)
            tt(t("d2"), li, li, ALU.mult)
            tt(t("den"), t("den"), t("d2"), ALU.add)
            rcp(t("rden"), t("den"))
            ts(t("am1"), ar, -1.0, None, ALU.add)
            tt(t("x1"), t("am1"), lr, ALU.mult)
            tt(t("x2"), ai, li, ALU.mult)
            tt(t("x1"), t("x1"), t("x2"), ALU.add)
            tt(t("cr"), t("x1"), t("rden"), ALU.mult)
            tt(t("x1"), ai, lr, ALU.mult)
            tt(t("x2"), t("am1"), li, ALU.mult)
            tt(t("x1"), t("x1"), t("x2"), ALU.subtract)
            tt(t("ci"), t("x1"), t("rden"), ALU.mult)
            pr = ph.alloc("t_pr", [128, 9, 64], F32)
            pi = ph.alloc("t_pi", [128, 9, 64], F32)
            P.op("dve", lambda e: e.memset(pr[:, 0, :], 1.0), r=RD, w=[TMP])
            P.op("dve", lambda e: e.memset(pi[:, 0, :], 0.0), r=RD, w=[TMP])

            def cmul(orr, oi, xr, xi, yr, yi):
                tt(t("m1"), xr, yr, ALU.mult)
                tt(t("m2"), xi, yi, ALU.mult)
                tt(t("m3"), xr, yi, ALU.mult)
                tt(t("m4"), xi, yr, ALU.mult)
                tt(orr, t("m1"), t("m2"), ALU.subtract)
                tt(oi, t("m3"), t("m4"), ALU.add)
            for k in range(8):
                cmul(pr[:, k + 1, :], pi[:, k + 1, :], pr[:, k, :], pi[:, k, :], ar, ai)
            tt(t("m1"), pr[:, 8, :], pr[:, 8, :], ALU.mult)
            tt(t("m2"), pi[:, 8, :], pi[:, 8, :], ALU.mult)
            tt(t("m1"), t("m1"), t("m2"), ALU.add)
            rcp(t("rm"), t("m1"))
            tt(t("ir"), pr[:, 8, :], t("rm"), ALU.mult)
            tt(t("ii"), pi[:, 8, :], t("rm"), ALU.mult)
            ts(t("ii"), t("ii"), -1.0, None, ALU.mult)
            P.op("dve", lambda e: e.tensor_copy(out=a8r.ap, in_=pr[:, 8, :]), r=RD, w=[a8r])
            P.op("dve", lambda e: e.tensor_scalar(out=a8s.ap, in0=pi[:, 8, :], scalar1=cstv[:, SGN:SGN + 1], scalar2=None,
                                                  op0=ALU.mult), r=RD, w=[a8s])
            P.op("dve", lambda e: e.tensor_scalar(out=a8n.ap, in0=a8s.ap, scalar1=-1.0, scalar2=None, op0=ALU.mult),
                 r=[a8s], w=[a8n])
            if SL <= 4:
                return
            big = lambda name: ph.alloc(name, [128, 64, 16], F32).ap
            bbr, bbi, BB, BBs, CC1, CC2, w1, w2 = [big(n) for n in ("bbr", "bbi", "BB", "BBs", "CC1", "CC2", "w1", "w2")]

            def bc(ap2d):
                return ap2d.unsqueeze(2).to_broadcast([128, 64, 16])
            Br, Bim, Cr, Cim = Bi[:, 0, :, :], Bi[:, 1, :, :], Ci[:, 0, :, :], Ci[:, 1, :, :]
            crb, cib = bc(t("cr")), bc(t("ci"))
            tt(w1, Br, crb, ALU.mult); tt(w2, Bim, cib, ALU.mult); tt(bbr, w1, w2, ALU.subtract)
            tt(w1, Bim, crb, ALU.mult); tt(w2, Br, cib, ALU.mult); tt(bbi, w1, w2, ALU.add)
            sT, sB = cstv[:, SELT:SELT + 1], cstv[:, SELB:SELB + 1]
            ts(BB, bbr, sT, None, ALU.mult); stt(BB, bbi, sB, BB, ALU.mult, ALU.add)
            ts(BBs, bbi, sT, -1.0, ALU.mult, ALU.mult); stt(BBs, bbr, sB, BBs, ALU.mult, ALU.add)
            ts(CC1, Cim, sB, -1.0, ALU.mult, ALU.mult); stt(CC1, Cr, sT, CC1, ALU.mult, ALU.add)
            ts(CC2, Cim, sT, -1.0, ALU.mult, ALU.mult); ts(w1, Cr, sB, -1.0, ALU.mult, ALU.mult)
            tt(CC2, CC2, w1, ALU.add)
            if SL <= 5:
                return
            NH = 16
            Bpw = ph.alloc("Bpw", [128, NH, 8, 16], F32)
            BJ = ph.alloc("BJ", [128, NH, 8, 16], F32)
            Cpw = ph.alloc("Cpw", [128, NH, 8, 16], F32)
            stB = ph.alloc("stB", [128, 2, 8, 128], BF16)
            stK = ph.alloc("stK", [128, 2, 8, 128], BF16)
            BJh = ph.alloc("BJh", [128, NH, 8, 16], BF16)
            BJl = ph.alloc("BJl", [128, NH, 8, 16], BF16)
            Cph = ph.alloc("Cph", [128, NH, 8, 16], BF16)
            Cpl = ph.alloc("Cpl", [128, NH, 8, 16], BF16)
            fl = lambda b, g: b[:, g, :, :].rearrange("p a b -> p (a b)")

            def bch(ap2d, gs):
                return ap2d[:, gs].unsqueeze(2).to_broadcast([128, NH, 16])
            for hf in range(64 // NH):
                gs = slice(hf * NH, (hf + 1) * NH)
                w1h, w2h = w1[:, 0:NH, :], w2[:, 0:NH, :]
                for j in range(8):
                    zr, zi = bch(pr[:, 7 - j, :], gs), bch(pi[:, 7 - j, :], gs)
                    tt(w1h, BB[:, gs, :], zr, ALU.mult); tt(w2h, BBs[:, gs, :], zi, ALU.mult)
                    tt(Bpw[:, :, j, :], w1h, w2h, ALU.add)
                    cmul(t("qr"), t("qi"), pr[:, 7 - j, :], pi[:, 7 - j, :], t("ir"), t("ii"))
                    zr, zi = bch(t("qr"), gs), bch(t("qi"), gs)
                    tt(w1h, BB[:, gs, :], zr, ALU.mult); tt(w2h, BBs[:, gs, :], zi, ALU.mult)
                    tt(BJ[:, :, j, :], w1h, w2h, ALU.add)
                    zr, zi = bch(pr[:, j + 1, :], gs), bch(pi[:, j + 1, :], gs)
                    tt(w1h, CC1[:, gs, :], zr, ALU.mult); tt(w2h, CC2[:, gs, :], zi, ALU.mult)
                    tt(Cpw[:, :, j, :], w1h, w2h, ALU.add)
                if SL <= 6:
                    continue
                for src_, hi_, lo_ in ((BJ, BJh, BJl), (Cpw, Cph, Cpl)):
                    P.op("dve", lambda e, src_=src_, hi_=hi_: e.tensor_copy(out=hi_.ap, in_=src_.ap), r=RD, w=[TMP])
                    P.op("dve", lambda e, src_=src_, hi_=hi_: e.tensor_tensor(out=src_.ap, in0=src_.ap, in1=hi_.ap,
                                                                           op=ALU.subtract), r=RD, w=[TMP])
                    P.op("dve", lambda e, src_=src_, lo_=lo_: e.tensor_copy(out=lo_.ap, in_=src_.ap), r=RD, w=[TMP])
                for q in range(hf * NH // 8, (hf + 1) * NH // 8):
                    for gl in range(8):
                        g = 8 * q + gl - hf * NH
                        ps = next_ps()
                        psk = next_ps()
                        P.op("pe", lambda e, g=g, ps=ps: e.transpose(out=ps[:, 0:128], in_=fl(Bpw, g), identity=cst[:, IDN, :]),
                             r=[TMP, cst], w=[ps])
                        for ii, (l_, r_) in enumerate(((BJh, Cph), (BJh, Cpl), (BJl, Cph))):
                            P.op("pe", lambda e, g=g, psk=psk, ii=ii, l_=l_, r_=r_: e.matmul(
                                psk[:, 0:128], lhsT=fl(l_, g), rhs=fl(r_, g), start=(ii == 0), stop=(ii == 2)),
                                r=[TMP], w=[psk])
                        P.op("act", lambda e, gl=gl, ps=ps: e.activation(out=stB[:, 0, gl, :], in_=ps[:, 0:128], func=AF.Copy),
                             r=[ps], w=[stB])
                        P.op("act", lambda e, gl=gl, ps=ps: e.activation(out=stB[:, 1, gl, 0:64], in_=ps[:, 64:128], func=AF.Copy),
                             r=[ps], w=[stB])
                        P.op("act", lambda e, gl=gl, ps=ps: e.activation(out=stB[:, 1, gl, 64:128], in_=ps[:, 0:64], func=AF.Copy),
                             r=[ps], w=[stB])
                        P.op("dve", lambda e, gl=gl, psk=psk: e.tensor_tensor(
                            out=stK[:, 0, gl, :], in0=psk[:, 0:128], in1=cst[:, KMASK, :], op=ALU.mult),
                            r=[psk, cst], w=[stK])
                        P.op("dve", lambda e, gl=gl, g=g: e.tensor_copy(out=stK[:, 1, gl, :], in_=fl(Cph, g)),
                             r=[TMP], w=[stK])
                    if int(_os.environ.get("S7", "9")) >= 3:
                        P.op("sp", lambda e, q=q: e.dma_start(out=s5B[q], in_=stB.ap), r=[stB], w=["s5B"], dma="s5B")
                        P.op("sp", lambda e, q=q: e.dma_start(out=s5K[q], in_=stK.ap), r=[stK], w=["s5K"], dma="s5K")
            print("setup phase top", ph.top, "BASE", BASE)

        def s5_tile(light):
            ph = Phase()
            uT = ph.alloc("uT", [128, 8, TT], BF16)
            U2 = ph.alloc("U2", [64, 64, 8, 16], BF16)
            Ucol = ph.alloc("Ucol", [128, 64, 64], BF16)
            cS = ph.alloc("cS", [128, 64, 64], F32)
            cW = ph.alloc("cW", [128, 64, 64], F32)
            Xprev = ph.alloc("Xprev", [128, 64, 64], BF16)
            Ycol = ph.alloc("Ycol", [128, 64, 64], BF16)
            ytmp = ph.alloc("ytmp", [128, TT], F32)
            gtmp = ph.alloc("gtmp", [128, TT], F32)
            r1 = [ph.alloc("r1_%d" % i, [128, 64], F32) for i in range(2)]
            r2 = [ph.alloc("r2_%d" % i, [128, 64], F32) for i in range(2)]

            def cons_u(j, ps, cw):
                P.op("act", lambda e: e.activation(out=uT[:, j, :], in_=ps[:, :], func=AF.Copy), r=[ps], w=[(uT, j)])
            proj("w_in", 0, 1024, 8, lambda c: (hT[:, c, :], (hT, c)), cons_u)
            for q in range(8):
                ps = next_ps()
                pb = psbf(ps)
                for j in range(8):
                    P.op("pe", lambda e, q=q, j=j, pb=pb: e.transpose(
                        out=pb[0:64, j * 128:(j + 1) * 128], in_=uT[:, q, j:TT:8], identity=identb.ap),
                        r=[(uT, q), identb], w=[ps])
                P.op("dve", lambda e, q=q, pb=pb: e.tensor_copy(
                    out=U2[:, 8 * q:8 * q + 8, :, :],
                    in_=pb[0:64, :].rearrange("p (j g m) -> p g j m", j=8, g=8)), r=[ps], w=[U2])
            for g0 in range(0, 64, 16):
                ps = next_ps()
                pb = psbf(ps)
                for g in range(g0, g0 + 16):
                    P.op("pe", lambda e, g=g, g0=g0, pb=pb: e.transpose(
                        out=pb[:, (g - g0) * 64:(g - g0 + 1) * 64], in_=U2[:, g, :, :].rearrange("p a b -> p (a b)"),
                        identity=identb[0:64, 0:64]), r=[U2, identb], w=[ps])
                P.op("act", lambda e, g0=g0, pb=pb: e.activation(
                    out=Ucol[:, g0:g0 + 16, :], in_=pb[:, :].rearrange("p (g b) -> p g b", g=16), func=AF.Copy),
                    r=[ps], w=[(Ucol, g0, g0 + 16)])
            for q in range(8):
                P.op("sp", lambda e, q=q: e.dma_start(out=s5tab.ap, in_=s5B[q]), r=["s5B"], w=[s5tab], dma="s5tab")
                for which, dst in ((0, cS), (1, cW)):
                    ps = next_ps()
                    for gl in range(8):
                        g = 8 * q + gl
                        P.op("pe", lambda e, which=which, gl=gl, g=g, ps=ps: e.matmul(
                            ps[:, gl * 64:(gl + 1) * 64], lhsT=s5tab[:, which, gl, :], rhs=Ucol[:, g, :],
                            start=True, stop=True), r=[s5tab, (Ucol, g)], w=[ps])
                    P.op("act", lambda e, q=q, dst=dst, ps=ps: e.activation(
                        out=dst[:, 8 * q:8 * q + 8, :], in_=ps[:, :].rearrange("p (g b) -> p g b", g=8), func=AF.Copy),
                        r=[ps], w=[(dst, 8 * q, 8 * q + 8)])
            for b in range(64):
                S0, W0 = Sst[st["s"] % 2], Ssw[st["s"] % 2]
                S1, W1 = Sst[(st["s"] + 1) % 2], Ssw[(st["s"] + 1) % 2]
                st["s"] += 1
                if not light:
                    P.op("act", lambda e, b=b, S0=S0: e.activation(out=Xprev[:, :, b], in_=S0.ap, func=AF.Copy),
                         r=[S0], w=[Xprev])
                A, Bq = r1[b % 2], r2[b % 2]
                P.op("dve", lambda e, A=A, S0=S0: e.tensor_tensor(out=A.ap, in0=S0.ap, in1=a8r.ap, op=ALU.mult),
                     r=[S0, a8r], w=[A])
                P.op("pool", lambda e, Bq=Bq, W0=W0: e.tensor_tensor(out=Bq.ap, in0=W0.ap, in1=a8r.ap, op=ALU.mult),
                     r=[W0, a8r], w=[Bq])
                P.op("dve", lambda e, W0=W0, S1=S1: e.tensor_tensor(out=S1.ap, in0=W0.ap, in1=a8s.ap, op=ALU.mult),
                     r=[W0, a8s], w=[S1])
                P.op("pool", lambda e, S0=S0, W1=W1: e.tensor_tensor(out=W1.ap, in0=S0.ap, in1=a8n.ap, op=ALU.mult),
                     r=[S0, a8n], w=[W1])
                P.op("dve", lambda e, A=A, S1=S1: e.tensor_tensor(out=S1.ap, in0=S1.ap, in1=A.ap, op=ALU.add),
                     r=[S1, A], w=[S1])
                P.op("pool", lambda e, Bq=Bq, W1=W1: e.tensor_tensor(out=W1.ap, in0=W1.ap, in1=Bq.ap, op=ALU.add),
                     r=[W1, Bq], w=[W1])
                P.op("dve", lambda e, b=b, S1=S1: e.tensor_tensor(out=S1.ap, in0=S1.ap, in1=cS[:, :, b], op=ALU.add),
                     r=[S1, cS], w=[S1])
                P.op("pool", lambda e, b=b, W1=W1: e.tensor_tensor(out=W1.ap, in0=W1.ap, in1=cW[:, :, b], op=ALU.add),
                     r=[W1, cW], w=[W1])
            if light:
                return None
            s5T = P.view("s5T", S5T_OFF, [128, 8, TT], BF16)
            gel = ph.alloc("gel", [128, 8, TT], BF16)
            assert ph.top <= S5T_OFF, ph.top
            for q in range(8):
                P.op("sp", lambda e, q=q: e.dma_start(out=s5tab.ap, in_=s5K[q]), r=["s5K"], w=[s5tab], dma="s5tab")
                ps = next_ps()
                for gl in range(8):
                    g = 8 * q + gl
                    P.op("pe", lambda e, gl=gl, g=g, ps=ps: e.matmul(
                        ps[:, gl * 64:(gl + 1) * 64], lhsT=s5tab[:, 0, gl, :], rhs=Ucol[:, g, :],
                        start=True, stop=False), r=[s5tab, (Ucol, g)], w=[ps])
                    P.op("pe", lambda e, gl=gl, g=g, ps=ps: e.matmul(
                        ps[:, gl * 64:(gl + 1) * 64], lhsT=s5tab[:, 1, gl, :], rhs=Xprev[:, g, :],
                        start=False, stop=True), r=[s5tab, Xprev], w=[ps])
                P.op("act", lambda e, q=q, ps=ps: e.activation(
                    out=Ycol[:, 8 * q:8 * q + 8, :], in_=ps[:, :].rearrange("p (g b) -> p g b", g=8), func=AF.Copy),
                    r=[ps], w=[(Ycol, 8 * q, 8 * q + 8)])
            Y2 = P.view("U2", U2.lo, [64, 8, 64, 16], BF16)
            for g0 in range(0, 64, 8):
                ps = next_ps()
                pb = psbf(ps)
                for g in range(g0, g0 + 8):
                    P.op("pe", lambda e, g=g, g0=g0, pb=pb: e.transpose(
                        out=pb[0:64, (g - g0) * 128:(g - g0 + 1) * 128], in_=Ycol[:, g, :], identity=identb.ap),
                        r=[(Ycol, g), identb], w=[ps])
                P.op("dve", lambda e, g0=g0, pb=pb: e.tensor_copy(
                    out=Y2[:, :, g0:g0 + 8, :],
                    in_=pb[0:64, :].rearrange("p (g j m) -> p j g m", g=8, j=8)), r=[ps], w=[Y2])
            for q in range(8):
                ps = next_ps()
                pb = psbf(ps)
                for j in range(8):
                    P.op("pe", lambda e, q=q, j=j, pb=pb: e.transpose(
                        out=pb[:, j * 64:(j + 1) * 64],
                        in_=Y2[:, j, 8 * q:8 * q + 8, :].rearrange("p a b -> p (a b)"),
                        identity=identb[0:64, 0:64]), r=[Y2, identb], w=[ps])
                P.op("dve", lambda e, q=q, pb=pb: e.scalar_tensor_tensor(
                    out=ytmp.ap.rearrange("p (b j) -> p j b", j=8),
                    in0=uT[:, q, :].rearrange("p (b j) -> p j b", j=8),
                    scalar=pc8[:, V_S5D, q:q + 1],
                    in1=pb[:, 0:512].rearrange("p (j b) -> p j b", j=8),
                    op0=ALU.mult, op1=ALU.add), r=[ps, (uT, q), pc8], w=[ytmp])
                P.op("pool", lambda e: e.tensor_tensor(out=gtmp.ap, in0=ytmp.ap, in1=ytmp.ap, op=ALU.mult),
                     r=[ytmp], w=[gtmp])
                P.op("pool", lambda e: e.tensor_scalar(out=gtmp.ap, in0=gtmp.ap, scalar1=0.044715, scalar2=1.0,
                                                       op0=ALU.mult, op1=ALU.add), r=[gtmp], w=[gtmp])
                P.op("pool", lambda e: e.tensor_tensor(out=gtmp.ap, in0=gtmp.ap, in1=ytmp.ap, op=ALU.mult),
                     r=[gtmp, ytmp], w=[gtmp])
                P.op("act", lambda e: e.activation(out=gtmp.ap, in_=gtmp.ap, func=AF.Sigmoid, scale=1.5957691216),
                     r=[gtmp], w=[gtmp])
                P.op("dve", lambda e, q=q: e.tensor_tensor(out=gel[:, q, :], in0=gtmp.ap, in1=ytmp.ap, op=ALU.mult),
                     r=[gtmp, ytmp], w=[(gel, q)])

            def cons_glu(j, ps, cw):
                g = gsb[st["g"] % 2]
                st["g"] += 1
                P.op("act", lambda e: e.activation(out=g.ap, in_=ps[:, :], func=AF.Sigmoid,
                                                   bias=pc8[:, V_BGLU, j:j + 1]), r=[ps, pc8], w=[g])
                P.op("dve", lambda e: e.tensor_tensor(out=s5T[:, j, :], in0=gel[:, j, :], in1=g.ap, op=ALU.mult),
                     r=[g, (gel, j)], w=[(s5T, j)])
            proj("s5_w_glu", 0, 1024, 8, lambda c: (gel[:, c, :], (gel, c)), cons_glu)
            return s5T

        def ssd_tile(light):
            ph = Phase()
            xbcp = ph.alloc("xbcp", [128, 32, TT + 3], BF16)
            ySS = P.view("ySS", xbcp.lo, [128, 16, TT], BF16)
            xsT = ph.alloc("xsT", [128, 16, TT], BF16)
            Bfm = ph.alloc("Bfm", [128, 8, TT], BF16)
            Cfm = ph.alloc("Cfm", [128, 8, TT], BF16)
            xtok = ph.alloc("xtok", [128, 2048], BF16)
            Btok = ph.alloc("Btok", [128, 1024], BF16)
            dtT = ph.alloc("dtT", [32, TT], F32)
            dAT = ph.alloc("dAT", [32, TT], F32)
            dtok = ph.alloc("dtok", [128, 4, 64], F32)
            dAh = ph.alloc("dAh", [128, 4, 32], BF16)
            dAl = ph.alloc("dAl", [128, 4, 32], BF16)
            dAr = ph.alloc("dAr", [128, 4, 32], F32)
            acs = ph.alloc("acs", [128, 64], F32)
            wgt = ph.alloc("wgt", [128, 32], F32)
            cdec = ph.alloc("cdec", [128, 32], F32)
            dg = [ph.alloc("dg%d" % i, [128, 4, 128], BF16) for i in range(4)]
            cbm = [ph.alloc("cbm%d" % i, [128, 128], F32) for i in range(2)]
            rhsM = [ph.alloc("rhsM%d" % i, [128, 2, 128], BF16) for i in range(4)]
            ed = [ph.alloc("ed%d" % i, [128, 256], F32) for i in range(4)]
            wTb = [ph.alloc("wT%d" % i, [128, 128], BF16) for i in range(4)]
            Csb = [ph.alloc("Cs%d" % i, [128, 128], BF16) for i in range(4)]
            xwb = [ph.alloc("xw%d" % i, [128, 256], BF16) for i in range(2)]
            htmp = [ph.alloc("htmp%d" % i, [128, 4, 64], F32) for i in range(2)]
            sqz = [ph.alloc("sqz%d" % i, [128, TT], BF16) for i in range(2)]
            assert ph.top <= S5T_OFF, ph.top
            cnt = {"h": 0, "g": 0}

            P.op("pool", lambda e: e.tensor_copy(out=xbcp[:, :, 0:3], in_=tail.ap), r=[tail], w=[xbcp])

            def cons_xbc(j, ps, cw):
                P.op("act", lambda e: e.activation(out=xbcp[:, j, 3:TT + 3], in_=ps[:, :], func=AF.Copy),
                     r=[ps], w=[(xbcp, j)])
            proj("w_in", 3072, 4096, 8, lambda c: (hT[:, c, :], (hT, c)), cons_xbc)
            P.op("pool", lambda e: e.tensor_copy(out=tail.ap, in_=xbcp[:, :, TT:TT + 3]), r=[xbcp], w=[tail])
            for oc in range(32):
                d = dg[oc % 4]
                for k in range(4):
                    P.op("act", lambda e, d=d, k=k, oc=oc: e.activation(
                        out=d[:, k, :], in_=identb.ap, func=AF.Copy, scale=convp[:, oc, k:k + 1]),
                        r=[identb, convp], w=[(d, k)])
                ps = next_ps()
                for k in range(4):
                    P.op("pe", lambda e, d=d, k=k, oc=oc, ps=ps: e.matmul(
                        ps[:, :], lhsT=d[:, k, :], rhs=xbcp[:, oc, k:k + TT], start=(k == 0), stop=(k == 3)),
                        r=[(d, k), (xbcp, oc)], w=[ps])
                if oc < 16:
                    dst, dk = xsT[:, oc, :], (xsT, oc)
                elif oc < 24:
                    dst, dk = Bfm[:, oc - 16, :], (Bfm, oc - 16)
                else:
                    dst, dk = Cfm[:, oc - 24, :], (Cfm, oc - 24)
                P.op("act", lambda e, dst=dst, ps=ps, oc=oc: e.activation(
                    out=dst, in_=ps[:, :], func=AF.Silu, bias=convp[:, oc, 4:5]), r=[ps, convp], w=[dk])

            def cons_dt(j, ps, cw):
                P.op("act", lambda e: e.activation(out=dtT.ap, in_=ps[0:32, :], func=AF.Exp, bias=hp[:, 1:2]),
                     r=[ps, hp], w=[dtT])
                P.op("act", lambda e: e.activation(out=dtT.ap, in_=dtT.ap, func=AF.Ln, bias=1.0), r=[dtT], w=[dtT])
                P.op("dve", lambda e: e.tensor_scalar(out=dAT.ap, in0=dtT.ap, scalar1=aneg[:, 0:1], scalar2=None,
                                                      op0=ALU.mult), r=[dtT, aneg], w=[dAT])
            proj("w_in", 7168, 32, 8, lambda c: (hT[:, c, :], (hT, c)), cons_dt, pw=128)
            ps = next_ps()
            for s in range(4):
                P.op("pe", lambda e, s=s, ps=ps: e.transpose(
                    out=ps[:, s * 64:s * 64 + 32], in_=dtT[:, s * 128:(s + 1) * 128], identity=cst[0:32, IDN, 0:32]),
                    r=[dtT, cst], w=[ps])
                P.op("pe", lambda e, s=s, ps=ps: e.transpose(
                    out=ps[:, s * 64 + 32:s * 64 + 64], in_=dAT[:, s * 128:(s + 1) * 128], identity=cst[0:32, IDN, 0:32]),
                    r=[dAT, cst], w=[ps])
            P.op("dve", lambda e, ps=ps: e.tensor_copy(out=dtok.ap, in_=ps[:, 0:256].rearrange("p (s c) -> p s c", s=4)),
                 r=[ps], w=[dtok])
            P.op("dve", lambda e: e.tensor_copy(out=dAh.ap, in_=dtok[:, :, 32:64]), r=[dtok], w=[dAh])
            P.op("dve", lambda e: e.tensor_tensor(out=dAr.ap, in0=dtok[:, :, 32:64], in1=dAh.ap, op=ALU.subtract),
                 r=[dtok, dAh], w=[dAr])
            P.op("dve", lambda e: e.tensor_copy(out=dAl.ap, in_=dAr.ap), r=[dAr], w=[dAl])

            for s in range(4):
                tk = slice(s * 128, (s + 1) * 128)
                for c0 in range(0, 24, 8):
                    ps = next_ps()
                    pb = psbf(ps)
                    for c in range(c0, c0 + 8):
                        src, sk = (xsT[:, c, tk], (xsT, c)) if c < 16 else (Bfm[:, c - 16, tk], (Bfm, c - 16))
                        P.op("pe", lambda e, c=c, c0=c0, pb=pb, src=src: e.transpose(
                            out=pb[:, (c - c0) * 128:(c - c0 + 1) * 128], in_=src, identity=identb.ap),
                            r=[sk, identb], w=[ps])
                    if c0 < 16:
                        P.op("act", lambda e, c0=c0, pb=pb: e.activation(
                            out=xtok[:, c0 * 128:(c0 + 8) * 128], in_=pb[:, :], func=AF.Copy), r=[ps], w=[xtok])
                    else:
                        P.op("act", lambda e, pb=pb: e.activation(out=Btok.ap, in_=pb[:, :], func=AF.Copy),
                             r=[ps], w=[Btok])
                ps = next_ps()
                for ii, dd in enumerate((dAh, dAl)):
                    P.op("pe", lambda e, s=s, ps=ps, ii=ii, dd=dd: e.matmul(ps[:, 0:32], lhsT=mub.ap, rhs=dd[:, s, :],
                                                                            start=(ii == 0), stop=(ii == 1)), r=[mub, dd], w=[ps])
                for ii, dd in enumerate((dAh, dAl)):
                    P.op("pe", lambda e, s=s, ps=ps, ii=ii, dd=dd: e.matmul(ps[:, 32:64], lhsT=onesb.ap, rhs=dd[:, s, :],
                                                                            start=(ii == 0), stop=(ii == 1)), r=[onesb, dd], w=[ps])
                P.op("act", lambda e, ps=ps: e.activation(out=acs.ap, in_=ps[:, 0:64], func=AF.Copy), r=[ps], w=[acs])
                P.op("dve", lambda e: e.tensor_tensor(out=wgt.ap, in0=acs[:, 32:64], in1=acs[:, 0:32], op=ALU.subtract),
                     r=[acs], w=[wgt])
                P.op("act", lambda e: e.activation(out=wgt.ap, in_=wgt.ap, func=AF.Exp), r=[wgt], w=[wgt])
                P.op("dve", lambda e, s=s: e.tensor_tensor(out=wgt.ap, in0=wgt.ap, in1=dtok[:, s, 0:32], op=ALU.mult),
                     r=[wgt, dtok], w=[wgt])
                P.op("act", lambda e: e.activation(out=cdec.ap, in_=acs[:, 32:64], func=AF.Exp), r=[acs], w=[cdec])
                rot["n"] = 5
                LAG = 2
                hbuf = {}

                def G0(g, tk=tk):
                    cb = cbm[g % 2]
                    ps = next_ps()
                    P.op("pe", lambda e: e.matmul(ps[:, 0:128], lhsT=Bfm[:, g, tk], rhs=Cfm[:, g, tk],
                                                  start=True, stop=True), r=[(Bfm, g), (Cfm, g)], w=[ps])
                    P.op("dve", lambda e: e.tensor_tensor(out=cb.ap, in0=ps[:, 0:128], in1=cst[:, MU, :],
                                                          op=ALU.mult), r=[ps, cst], w=[cb])

                def A(h, s=s):
                    i = h % 4
                    rm, e_ = rhsM[i], ed[i]
                    for ii, dd in enumerate((dAh, dAl)):
                        P.op("dve", lambda e, ii=ii, dd=dd: e.tensor_scalar(
                            out=rm[:, ii, :], in0=mub.ap, scalar1=dd[:, s, h:h + 1], scalar2=None,
                            op0=ALU.mult), r=[mub, dd], w=[(rm, ii)])
                    ps2 = next_ps()
                    for ii in range(2):
                        P.op("pe", lambda e, ii=ii: e.matmul(
                            ps2[:, 0:128], lhsT=m1b.ap, rhs=rm[:, ii, :], start=(ii == 0), stop=(ii == 1)),
                            r=[m1b, (rm, ii)], w=[ps2])
                    for ii in range(2):
                        P.op("pe", lambda e, ii=ii: e.matmul(
                            ps2[:, 128:256], lhsT=onesb.ap, rhs=rm[:, ii, :], start=(ii == 0), stop=(ii == 1)),
                            r=[onesb, (rm, ii)], w=[ps2])
                    P.op("act", lambda e: e.activation(out=e_.ap, in_=ps2[:, 0:256], func=AF.Exp), r=[ps2], w=[e_])

                def B(h, s=s, tk=tk):
                    g, k = h // 4, h % 4
                    i = h % 4
                    e_, wt, cs = ed[i], wTb[i], Csb[i]
                    cb = cbm[g % 2]
                    psy = psb[5 + (g % 2)]
                    P.op("dve", lambda e: e.scalar_tensor_tensor(
                        out=wt.ap, in0=e_[:, 0:128], scalar=dtok[:, s, h:h + 1], in1=cb.ap,
                        op0=ALU.mult, op1=ALU.mult), r=[e_, cb, dtok], w=[wt])
                    P.op("pool", lambda e: e.tensor_tensor(
                        out=cs.ap, in0=Cfm[:, g, tk], in1=e_[:, 128:256], op=ALU.mult),
                        r=[(Cfm, g), e_], w=[cs])
                    po, co = 64 * (k % 2), 128 * (k // 2)
                    P.op("pe", lambda e: e.matmul(
                        psy[po:po + 64, co:co + 128], lhsT=xtok[:, h * 64:(h + 1) * 64], rhs=wt.ap,
                        start=True, stop=False), r=[xtok, wt], w=[psy])
                    P.op("pe", lambda e: e.matmul(
                        psy[po:po + 64, co:co + 128], lhsT=hSb[:, h, :], rhs=cs.ap,
                        start=False, stop=True), r=[(hSb, h), cs], w=[psy])

                def G1(g, tk=tk):
                    if not light:
                        psy = psb[5 + (g % 2)]
                        for cl in range(2):
                            cc = 2 * g + cl
                            P.op("dve", lambda e, cl=cl, cc=cc: e.scalar_tensor_tensor(
                                out=ySS[:, cc, tk], in0=xsT[:, cc, tk], scalar=pc16[:, V_SSDD, cc:cc + 1],
                                in1=psy[:, cl * 128:(cl + 1) * 128], op0=ALU.mult, op1=ALU.add),
                                r=[psy, (xsT, cc), pc16], w=[(ySS, cc)])
                    xw = xwb[g % 2]
                    ht = htmp[g % 2]
                    P.op("dve", lambda e: e.tensor_tensor(
                        out=xw.ap.rearrange("p (k d) -> p k d", k=4),
                        in0=xtok[:, g * 256:(g + 1) * 256].rearrange("p (k d) -> p k d", k=4),
                        in1=wgt[:, 4 * g:4 * g + 4].unsqueeze(2).to_broadcast([128, 4, 64]), op=ALU.mult),
                        r=[xtok, wgt], w=[xw])
                    ps3 = next_ps()
                    P.op("pe", lambda e: e.matmul(
                        ps3[:, 0:256], lhsT=Btok[:, g * 128:(g + 1) * 128], rhs=xw.ap, start=True, stop=True),
                        r=[Btok, xw], w=[ps3])
                    P.op("pool", lambda e: e.tensor_tensor(
                        out=ht.ap, in0=hS[:, 4 * g:4 * g + 4, :],
                        in1=cdec[:, 4 * g:4 * g + 4].unsqueeze(2).to_broadcast([128, 4, 64]), op=ALU.mult),
                        r=[(hS, 4 * g, 4 * g + 4), cdec], w=[ht])
                    P.op("dve", lambda e: e.tensor_tensor(
                        out=hS[:, 4 * g:4 * g + 4, :], in0=ps3[:, 0:256].rearrange("p (k d) -> p k d", k=4),
                        in1=ht.ap, op=ALU.add), r=[ps3, ht], w=[(hS, 4 * g, 4 * g + 4)])
                    P.op("act", lambda e: e.activation(out=hSb[:, 4 * g:4 * g + 4, :], in_=hS[:, 4 * g:4 * g + 4, :],
                                                       func=AF.Copy), r=[(hS, 4 * g, 4 * g + 4)], w=[(hSb, 4 * g, 4 * g + 4)])

                if light:
                    for g in range(8):
                        G1(g)
                else:
                    for idx in range(32 + LAG):
                        if idx < 32:
                            if idx % 4 == 0:
                                G0(idx // 4)
                            A(idx)
                        if idx >= LAG:
                            hb = idx - LAG
                            B(hb)
                            if hb % 4 == 3:
                                G1(hb // 4)
                rot["n"] = NROT
            if light:
                return None
            if DEBUG and st.get("dbg_ssd_done") is None:
                st["dbg_ssd_done"] = 1
                dump("dbg_xs", xsT, 16); dump("dbg_B", Bfm, 8); dump("dbg_C", Cfm, 8); dump("dbg_y", ySS, 16)
            pss = psb[7]

            def cons_z(j, ps, cw):
                g = gsb[st["g"] % 2]
                st["g"] += 1
                sq_ = sqz[j % 2]
                P.op("act", lambda e: e.activation(out=g.ap, in_=ps[:, :], func=AF.Silu), r=[ps], w=[g])
                P.op("dve", lambda e: e.tensor_tensor(out=ySS[:, j, :], in0=ySS[:, j, :], in1=g.ap, op=ALU.mult),
                     r=[(ySS, j), g], w=[(ySS, j)])
                P.op("act", lambda e: e.activation(out=sq_.ap, in_=ySS[:, j, :], func=AF.Square), r=[(ySS, j)], w=[sq_])
                P.op("pe", lambda e: e.matmul(pss[:, :], lhsT=onesb.ap, rhs=sq_.ap, start=(j == 0), stop=(j == 15)),
                     r=[onesb, sq_], w=[pss])
            proj("w_in", 1024, 2048, 8, lambda c: (hT[:, c, :], (hT, c)), cons_z)
            rstd_from(pss, 1.0 / 2048)
            for c in range(16):
                P.op("dve", lambda e, c=c: e.scalar_tensor_tensor(
                    out=ySS[:, c, :], in0=ySS[:, c, :], scalar=pc16[:, V_SSDN, c:c + 1], in1=rstd.ap,
                    op0=ALU.mult, op1=ALU.mult), r=[(ySS, c), pc16, rstd], w=[(ySS, c)])
            return ySS

        def merge_tile(s5T, ynT):
            G = P.view("G", MRG_OFF, [128, 16, TT], BF16)
            Q = P.view("Q", MRG_OFF + 16 * TT * 2, [128, 8, TT], BF16)
            mT = P.view("mT", MRG_OFF + 24 * TT * 2, [128, 8, TT], BF16)
            assert MRG_OFF + 32 * TT * 2 <= S5T_OFF and MRG_OFF >= ynT.hi, (MRG_OFF, ynT.hi, S5T_OFF)

            def cons_g(j, ps, cw):
                P.op("act", lambda e: e.activation(out=G[:, j, :], in_=ps[:, :], func=AF.Sigmoid,
                                                   bias=pc16[:, V_BGATE, j:j + 1]), r=[ps, pc16], w=[(G, j)])
            proj("w_in", 7200, 2048, 8, lambda c: (hT[:, c, :], (hT, c)), cons_g)

            def cons_q(j, ps, cw):
                P.op("dve", lambda e: e.tensor_tensor(out=Q[:, j, :], in0=ps[:, :], in1=G[:, j, :], op=ALU.mult),
                     r=[ps, (G, j)], w=[(Q, j)])
            proj("w_proj_s5", 0, 1024, 8, lambda c: (s5T[:, c, :], (s5T, c)), cons_q)

            def cons_m(j, ps, cw):
                g = gsb[st["g"] % 2]
                st["g"] += 1
                P.op("dve", lambda e: e.tensor_tensor(out=g.ap, in0=ps[:, :], in1=G[:, 8 + j, :], op=ALU.mult),
                     r=[ps, (G, 8 + j)], w=[g])
                P.op("pool", lambda e: e.tensor_tensor(out=mT[:, j, :], in0=g.ap, in1=Q[:, j, :], op=ALU.add),
                     r=[g, (Q, j)], w=[(mT, j)])
            proj("w_proj_ssd", 0, 1024, 16, lambda c: (ynT[:, c, :], (ynT, c)), cons_m)

            def cons_o(j, ps, cw):
                P.op("dve", lambda e: e.tensor_tensor(out=xres[:, j, :], in0=ps[:, :], in1=xres[:, j, :], op=ALU.add),
                     r=[ps, (xres, j)], w=[(xres, j)])
            proj("w_out", 0, 1024, 8, lambda c: (mT[:, c, :], (mT, c)), cons_o)
            return mT

        def store_out(t_own):
            ph = Phase()
            sqb = ph.alloc("sqb", [128, 8, TT], BF16)
            rmsnorm_to(xres, G_FIN, sqb)
            for s in range(4):
                ob = osb[0]
                for c0 in range(0, 8, 4):
                    ps = next_ps()
                    for c in range(c0, c0 + 4):
                        P.op("pe", lambda e, ps=ps, c=c, c0=c0, s=s: e.transpose(
                            out=ps[:, (c - c0) * 128:(c - c0 + 1) * 128], in_=xres[:, c, s * 128:(s + 1) * 128],
                            identity=cst[:, IDN, :]), r=[(xres, c), cst], w=[ps])
                    P.op("act", lambda e, ps=ps, c0=c0, ob=ob: e.activation(
                        out=ob[:, c0 * 128:(c0 + 4) * 128], in_=ps[:, :], func=AF.Copy), r=[ps], w=[ob])
                r0 = t_own * TT + s * 128
                tok = P.op("sp", lambda e, ob=ob, r0=r0: e.dma_start(out=out[r0:r0 + 128, :], in_=ob.ap),
                           r=[ob], w=["outdram"], dma=ob.key)
                P.final[ob.key] = tok

        import os as _os
        STOP = int(_os.environ.get("KSTOP", "9"))
        if STOP >= 1 or STOP == -1:
            s5_setup()
        for tile in range(NT_PRE + NT_OWN if STOP >= 0 else 0):
            own = tile >= NT_PRE
            light = LIGHT and not own
            if tile == NT_PRE:
                fl_ = cstv[:, FLAG:FLAG + 1]
                for b_ in (Sst[st["s"] % 2], Ssw[st["s"] % 2], hS, hSb, tail):
                    P.op("dve", lambda e, b_=b_: e.tensor_scalar(out=b_.ap, in0=b_.ap, scalar1=fl_, scalar2=None,
                                                                 op0=ALU.mult), r=[b_, cstv], w=[b_])
            load_x(tile)
            ffn("ffn1", G_FFN1)
            if tile == NT_PRE:
                dump("dbg_x1", xres, 8)
            s5T = ynT = None
            if STOP >= 2:
                ph = Phase()
                sqb = ph.alloc("sqb", [128, 8, TT], BF16)
                rmsnorm_to(hT, G_MIX, sqb)
                s5T = s5_tile(light)
            if STOP >= 3:
                ynT = ssd_tile(light)
            if own:
                if STOP >= 4:
                    if tile == NT_PRE:
                        dump("dbg_s5", s5T, 8)
                        dump("dbg_ssd", ynT, 16)
                    mT = merge_tile(s5T, ynT)
                    if tile == NT_PRE:
                        dump("dbg_m", mT, 8)
                    ffn("ffn2", G_FFN2)
                store_out(tile - NT_PRE)
        P.emit()
    return nc


_CACHE = {}


def _pc(v, n):
    return np.ascontiguousarray(np.asarray(v, np.float32).reshape(n, 128).T)


def prep_inputs(inputs):
    f = lambda k: np.asarray(inputs[k], dtype=np.float32)
    common = {}
    for n in WSHAPES:
        common[n] = np.ascontiguousarray(f(n)[0])
    pc8 = np.stack([_pc(f("ffn1_norm")[0], 8), _pc(f("mix_norm")[0], 8), _pc(f("ffn2_norm")[0], 8),
                    _pc(f("final_norm"), 8), _pc(f("s5_D")[0], 8), _pc(f("s5_b_glu")[0], 8)], axis=1)
    common["pc8"] = np.ascontiguousarray(pc8)
    pc16 = np.stack([_pc(f("ssd_norm")[0], 16), _pc(f("b_gate")[0], 16),
                     _pc(np.repeat(f("ssd_D")[0], 64), 16)], axis=1)
    common["pc16"] = np.ascontiguousarray(pc16)
    cw = f("conv_w")[0]
    convp = np.zeros((128, 32, 5), np.float32)
    for k in range(4):
        convp[:, :, k] = _pc(cw[k], 32)
    convp[:, :, 4] = _pc(f("conv_b")[0], 32)
    common["convp"] = convp
    common["hp"] = np.ascontiguousarray(np.stack([f("ssd_A_log")[0], f("ssd_dt_bias")[0]], axis=1))
    dup = lambda a: np.concatenate([a, a], axis=0)
    s5a = np.stack([dup(f("s5_A_re")[0].T), dup(f("s5_A_im")[0].T),
                    np.broadcast_to(f("s5_log_dt")[0][None, :], (128, 64))], axis=1)
    common["s5a"] = np.ascontiguousarray(s5a)
    common["s5b"] = np.ascontiguousarray(np.stack([dup(f("s5_B_re")[0].transpose(1, 0, 2)),
                                                   dup(f("s5_B_im")[0].transpose(1, 0, 2))], axis=1))
    common["s5c"] = np.ascontiguousarray(np.stack([dup(f("s5_C_re")[0].transpose(2, 0, 1)),
                                                   dup(f("s5_C_im")[0].transpose(2, 0, 1))], axis=1))
    i = np.arange(128)
    cst = np.zeros((128, 5, 128), np.float32)
    cst[:, 0, :] = np.eye(128)
    cst[:, 1, :] = (i[:, None] <= i[None, :])
    cst[:, 2, :] = (i[:, None] > i[None, :])
    cst[:, 3, :] = 1.0
    cst[:, 4, :] = ((i[None, :] // 16) >= (i[:, None] // 16))
    common["cst"] = cst
    return common


def kernel(**inputs):
    x = np.asarray(inputs["x"], dtype=np.float32)
    B, L, _ = x.shape
    half = L // 2
    if "nc" not in _CACHE:
        _CACHE["nc"] = build_program()
    nc = _CACHE["nc"]
    common = prep_inputs(inputs)
    in_maps = []
    for c in range(8):
        b, h = c // 2, c % 2
        xs = np.zeros((L, D), np.float32)
        if h == 0:
            xs[half:] = x[b, :half]
        else:
            xs[:] = x[b]
        m = dict(common)
        m["xs"] = xs
        cv = np.zeros((128, 4), np.float32)
        cv[:64, 0], cv[64:, 0] = -1.0, 1.0
        cv[:64, 1] = 1.0
        cv[64:, 2] = 1.0
        cv[:, 3] = float(h)
        m["cstv"] = cv
        in_maps.append(m)
    res = run_bass_kernel_spmd(nc, in_maps, core_ids=list(range(8)))
    _CACHE["res"] = res
    outp = np.zeros((B, L, D), np.float32)
    for c in range(8):
        b, h = c // 2, c % 2
        outp[b, h * half:(h + 1) * half] = np.asarray(res.results[c]["out"], dtype=np.float32)
    return outp
```

```python
import numpy as np
import ml_dtypes
from contextlib import ExitStack
import concourse.bass as bass
import concourse.mybir as mybir
from concourse.bass_utils import run_bass_kernel_spmd

F32 = mybir.dt.float32
BF16 = mybir.dt.bfloat16
I32 = mybir.dt.int32
U8 = mybir.dt.uint8
ALU = mybir.AluOpType
AF = mybir.ActivationFunctionType
DTSZ = {F32: 4, BF16: 2, I32: 4, U8: 1}

D = 1024
DFF = 2816
DIN = 9248
TT = 512
NT_PRE = 4
NT_OWN = 4
EPS = 1e-6
ARENA = 206 * 1024
LIGHT = True
DEBUG = False


def _prod(s):
    r = 1
    for v in s:
        r *= v
    return r


class Buf:
    def __init__(self, ap, key, space, lo, hi, shape, dt):
        self.ap, self.key, self.space, self.lo, self.hi, self.shape, self.dt = ap, key, space, lo, hi, shape, dt
        self.sub_bytes = (_prod(shape[2:]) * DTSZ[dt]) if len(shape) > 2 else None

    def __getitem__(self, idx):
        return self.ap[idx]


class Prog:
    ENG = ("pe", "act", "dve", "pool", "sp")
    EPOCH = 12000

    def __init__(self, nc, es):
        self.nc, self.es = nc, es
        self.ops = {e: [] for e in self.ENG}
        self.count = {e: 0 for e in self.ENG}
        self.esems = {e: [] for e in self.ENG}
        self.dsems = {}
        self.lastw = {}
        self.readers = {}
        self.spaces = {}
        self.waited = {e: {} for e in self.ENG}
        self.final = {}
        self.sig = {e: set() for e in self.ENG}
        self.arena = es.enter_context(nc.sbuf_tensor("arena", [128, ARENA], U8))
        self.top = 0
        self.nps = 0

    def view(self, name, off, shape, dt):
        nb = _prod(shape[1:]) * DTSZ[dt]
        assert off + nb <= ARENA, (name, off, nb)
        ap = self.arena[0:shape[0], off:off + nb].bitcast(dt)
        if len(shape) == 3:
            ap = ap.rearrange("p (a b) -> p a b", a=shape[1])
        elif len(shape) == 4:
            ap = ap.rearrange("p (a b c) -> p a b c", a=shape[1], b=shape[2])
        return Buf(ap, "%s@%d" % (name, off), "arena", off, off + nb, shape, dt)

    def alloc(self, name, shape, dt):
        b = self.view(name, self.top, shape, dt)
        self.top = (b.hi + 31) // 32 * 32
        return b

    def psum(self, name):
        t = self.es.enter_context(self.nc.psum_tensor(name, [128, 512], F32))
        return Buf(t[:, :], name, name, 0, 2048, [128, 512], F32)

    def sem(self, name):
        return self.es.enter_context(self.nc.semaphore(name))

    def _norm(self, it):
        if isinstance(it, Buf):
            return (it.key, it.space, it.lo, it.hi)
        if isinstance(it, tuple):
            b = it[0]
            i0 = it[1]
            i1 = it[2] if len(it) > 2 else i0 + 1
            return ((b.key, i0, i1), b.space, b.lo + i0 * b.sub_bytes, b.lo + i1 * b.sub_bytes)
        return (it, it, 0, 1)

    def _related(self, n):
        key, space, lo, hi = n
        sp = self.spaces.setdefault(space, {})
        if key not in sp:
            sp[key] = (lo, hi)
        return [k for k, (l, h) in sp.items() if l < hi and lo < h]

    def op(self, eng, fn, r=(), w=(), dma=None):
        rn = [self._norm(x) for x in r]
        wn = [self._norm(x) for x in w]
        deps = []
        for n in rn:
            for k in self._related(n):
                if k in self.lastw:
                    deps.append(self.lastw[k])
        for n in wn:
            for k in self._related(n):
                if k in self.lastw:
                    deps.append(self.lastw[k])
                deps.extend(self.readers.get(k, {}).values())
        waits = []
        for tok in deps:
            if tok[0] == "dma":
                _, sem, val = tok
                cur = self.waited[eng].get(id(sem), 0)
                if val > cur:
                    self.waited[eng][id(sem)] = val
                    waits.append(tok)
            else:
                _, seng, cidx = tok
                if seng == "pe" and eng == "pe" and dma is None:
                    continue
                cur = self.waited[eng].get(seng, -1)
                if cidx > cur:
                    self.waited[eng][seng] = cidx
                    waits.append(tok)
                    self.sig[seng].add(cidx)
        if dma is not None:
            if dma not in self.dsems:
                self.dsems[dma] = [self.sem("d_" + str(dma)), 0]
            ds = self.dsems[dma]
            ds[1] += 16
            tok = ("dma", ds[0], ds[1])
            self.ops[eng].append((waits, fn, ("dma", ds[0])))
            rkey = id(ds[0])
        else:
            c = self.count[eng]
            self.count[eng] = c + 1
            tok = ("c", eng, c)
            self.ops[eng].append((waits, fn, ("c", c)))
            rkey = eng
        for n in wn:
            self.lastw[n[0]] = tok
            self.readers[n[0]] = {}
        for n in rn:
            self.readers.setdefault(n[0], {})[rkey] = tok
        return tok

    def emit(self):
        semval = {}
        for eng in self.ENG:
            k = 0
            for c in range(self.count[eng]):
                if c in self.sig[eng]:
                    ep = k // self.EPOCH
                    while len(self.esems[eng]) <= ep:
                        self.esems[eng].append(self.sem("e_%s_%d" % (eng, len(self.esems[eng]))))
                    semval[(eng, c)] = (self.esems[eng][ep], k - ep * self.EPOCH + 1)
                    k += 1
        with self.nc.Block() as block:
            def mk(engname):
                def body(e):
                    for waits, fn, inc in self.ops[engname]:
                        for tok in waits:
                            if tok[0] == "dma":
                                e.wait_ge(tok[1], tok[2])
                            else:
                                s_, v_ = semval[(tok[1], tok[2])]
                                e.wait_ge(s_, v_)
                        ins = fn(e)
                        if inc[0] == "dma":
                            ins.then_inc(inc[1], 16)
                        elif (engname, inc[1]) in semval:
                            ins.then_inc(semval[(engname, inc[1])][0], 1)
                    if engname == "sp":
                        for tok in self.final.values():
                            e.wait_ge(tok[1], tok[2])
                return body
            block.tensor(mk("pe"))
            block.scalar(mk("act"))
            block.vector(mk("dve"))
            block.gpsimd(mk("pool"))
            block.sync(mk("sp"))


WSHAPES = {"ffn1_w_gate": (D, DFF), "ffn1_w_up": (D, DFF), "ffn1_w_down": (DFF, D),
           "w_in": (D, DIN), "s5_w_glu": (D, D), "w_proj_s5": (D, D), "w_proj_ssd": (2 * D, D),
           "w_out": (D, D),
           "ffn2_w_gate": (D, DFF), "ffn2_w_up": (D, DFF), "ffn2_w_down": (DFF, D)}
PSHAPES = {"pc8": [128, 6, 8], "pc16": [128, 3, 16], "convp": [128, 32, 5], "hp": [32, 2],
           "s5a": [128, 3, 64], "s5b": [128, 2, 64, 16], "s5c": [128, 2, 64, 16],
           "cst": [128, 5, 128], "cstv": [128, 4]}


def build_program():
    nc = bass.Bass("TRN2", target_bir_lowering=False)
    NTOK = (NT_PRE + NT_OWN) * TT
    din = {}
    for n, s in WSHAPES.items():
        din[n] = nc.dram_tensor(n, list(s), F32, kind="ExternalInput").ap()
    for n, s in PSHAPES.items():
        din[n] = nc.dram_tensor(n, list(s), F32, kind="ExternalInput").ap()
    xs = nc.dram_tensor("xs", [NTOK, D], F32, kind="ExternalInput").ap()
    out = nc.dram_tensor("out", [NT_OWN * TT, D], F32, kind="ExternalOutput").ap()
    wbf = {n: nc.dram_tensor(n + "_bf", list(s), BF16, kind="Internal").ap() for n, s in WSHAPES.items()}
    s5B = nc.dram_tensor("s5B_tab", [8, 128, 2, 8, 128], BF16, kind="Internal").ap()
    s5K = nc.dram_tensor("s5K_tab", [8, 128, 2, 8, 128], BF16, kind="Internal").ap()
    dbg = {}
    if DEBUG:
        for n, c in (("dbg_s5", 8), ("dbg_ssd", 16), ("dbg_m", 8), ("dbg_x1", 8), ("dbg_xs", 16), ("dbg_B", 8), ("dbg_C", 8), ("dbg_y", 16), ("dbg_dt", 1)):
            dbg[n] = nc.dram_tensor(n, [128, c, TT], F32, kind="ExternalOutput").ap()

    with ExitStack() as es:
        P = Prog(nc, es)
        psb = [P.psum("ps%d" % i) for i in range(8)]
        NROT = 7

        rot = {"n": NROT}

        def next_ps():
            P.nps = (P.nps + 1) % rot["n"]
            return psb[P.nps]

        def psbf(ps):
            return ps.ap.bitcast(BF16)

        cst = P.alloc("cst", [128, 5, 128], F32)
        IDN, MU, M1, ONE, KMASK = 0, 1, 2, 3, 4
        cstv = P.alloc("cstv", [128, 4], F32)
        SGN, SELT, SELB, FLAG = 0, 1, 2, 3
        pc8 = P.alloc("pc8", [128, 6, 8], F32)
        G_FFN1, G_MIX, G_FFN2, G_FIN, V_S5D, V_BGLU = range(6)
        pc16 = P.alloc("pc16", [128, 3, 16], F32)
        V_SSDN, V_BGATE, V_SSDD = range(3)
        convp = P.alloc("convp", [128, 32, 5], F32)
        hp = P.alloc("hp", [32, 2], F32)
        aneg = P.alloc("aneg", [32, 1], F32)
        identb = P.alloc("identb", [128, 128], BF16)
        onesb = P.alloc("onesb", [128, 128], BF16)
        mub = P.alloc("mub", [128, 128], BF16)
        m1b = P.alloc("m1b", [128, 128], BF16)
        hSb = P.alloc("hSb", [128, 32, 64], BF16)
        a8r = P.alloc("a8r", [128, 64], F32)
        a8s = P.alloc("a8s", [128, 64], F32)
        a8n = P.alloc("a8n", [128, 64], F32)
        Sst = [P.alloc("Sst%d" % i, [128, 64], F32) for i in range(2)]
        Ssw = [P.alloc("Ssw%d" % i, [128, 64], F32) for i in range(2)]
        hS = P.alloc("hS", [128, 32, 64], F32)
        tail = P.alloc("tail", [128, 32, 3], BF16)
        xres = P.alloc("xres", [128, 8, TT], F32)
        hT = P.alloc("hT", [128, 8, TT], BF16)
        rstd = P.alloc("rstd", [128, TT], F32)
        gsb = [P.alloc("gsb%d" % i, [128, TT], F32) for i in range(2)]
        NPAN = 3
        pans = [P.alloc("pan%d" % i, [128, 4096], BF16) for i in range(NPAN)]
        s5tab = P.alloc("s5tab", [128, 2, 8, 128], BF16)
        xin = P.alloc("xin", [128, D], F32)
        osb = [P.alloc("osb%d" % i, [128, D], F32) for i in range(1)]
        BASE = P.top
        st = {"pan": 0, "g": 0, "o": 0, "s": 0}
        S5T_OFF = ARENA - 8 * TT * 2
        MRG_OFF = BASE + 16 * TT * 2 + 64

        class Phase:
            def __init__(self, base=BASE):
                self.top = base

            def alloc(self, name, shape, dt):
                b = P.view(name, self.top, shape, dt)
                self.top = (b.hi + 31) // 32 * 32
                return b

        def load_params(items, semkey):
            for b, src in items:
                P.op("sp", lambda e, b=b, src=src: e.dma_start(out=b.ap, in_=src), w=[b], dma=semkey)
            sem, total = P.dsems[semkey]
            for b, _ in items:
                P.lastw[b.key] = ("dma", sem, total)

        load_params([(cst, din["cst"]), (cstv, din["cstv"]), (pc8, din["pc8"]), (pc16, din["pc16"]),
                     (convp, din["convp"]), (hp, din["hp"])], "par")
        P.op("dve", lambda e: e.tensor_copy(out=identb.ap, in_=cst[:, IDN, :]), r=[cst], w=[identb])
        P.op("pool", lambda e: e.memset(onesb.ap, 1.0), w=[onesb])
        P.op("dve", lambda e: e.tensor_copy(out=mub.ap, in_=cst[:, MU, :]), r=[cst], w=[mub])
        P.op("dve", lambda e: e.tensor_copy(out=m1b.ap, in_=cst[:, M1, :]), r=[cst], w=[m1b])
        P.op("pool", lambda e: e.memset(hSb.ap, 0.0), w=[hSb])
        P.op("pool", lambda e: e.memset(hS.ap, 0.0), w=[hS])
        P.op("pool", lambda e: e.memset(tail.ap, 0.0), w=[tail])
        P.op("pool", lambda e: e.memset(Sst[0].ap, 0.0), w=[Sst[0]])
        P.op("pool", lambda e: e.memset(Ssw[0].ap, 0.0), w=[Ssw[0]])
        P.op("act", lambda e: e.activation(out=aneg.ap, in_=hp[:, 0:1], func=AF.Exp), r=[hp], w=[aneg])
        P.op("dve", lambda e: e.tensor_scalar(out=aneg.ap, in0=aneg.ap, scalar1=-1.0, scalar2=None, op0=ALU.mult),
             r=[aneg], w=[aneg])

        import os as _os2
        for n, (k, m) in (WSHAPES.items() if int(_os2.environ.get("KSTOP", "9")) >= 0 else []):
            rows = 256
            for r0 in range(0, k, rows):
                rr = min(rows, k - r0)
                P.op("pool", lambda e, n=n, r0=r0, rr=rr: e.dma_start(
                    out=wbf[n][r0:r0 + rr, :], in_=din[n][r0:r0 + rr, :]), w=[n + "_bf"], dma="cast_" + n)

        def get_pan():
            p = pans[st["pan"] % NPAN]
            st["pan"] += 1
            return p

        def proj(wname, col0, ncols, KC, rhs, consume, pw=None):
            W = wbf[wname]
            if pw is None:
                pw = min(512, (4096 // KC) // 128 * 128)
            j = 0
            for p0 in range(0, ncols, pw):
                w_ = min(pw, ncols - p0)
                pan = get_pan()
                pv = pan.ap[:, 0:KC * w_].rearrange("p (c n) -> p c n", c=KC)
                P.op("sp", lambda e, pv=pv, p0=p0, w_=w_: e.dma_start(
                    out=pv, in_=W[:, col0 + p0:col0 + p0 + w_].rearrange("(c p) n -> p c n", p=128)),
                    r=[wname + "_bf"], w=[pan], dma=pan.key)
                for q0 in range(0, w_, 128):
                    cw = min(128, w_ - q0)
                    ps = next_ps()
                    for c in range(KC):
                        rap, rdep = rhs(c)
                        P.op("pe", lambda e, pv=pv, ps=ps, c=c, q0=q0, cw=cw, rap=rap: e.matmul(
                            ps[0:cw, :], lhsT=pv[:, c, q0:q0 + cw], rhs=rap, start=(c == 0), stop=(c == KC - 1)),
                            r=[pan, rdep], w=[ps])
                    consume(j, ps, cw)
                    j += 1

        def rstd_from(ps, scale):
            P.op("dve", lambda e, ps=ps: e.tensor_scalar(out=rstd.ap, in0=ps[:, :], scalar1=scale, scalar2=EPS,
                                                         op0=ALU.mult, op1=ALU.add), r=[ps], w=[rstd])
            P.op("act", lambda e: e.activation(out=rstd.ap, in_=rstd.ap, func=AF.Ln), r=[rstd], w=[rstd])
            P.op("act", lambda e: e.activation(out=rstd.ap, in_=rstd.ap, func=AF.Exp, scale=-0.5), r=[rstd], w=[rstd])

        def rmsnorm_to(dst, gcol, sqbuf):
            for c in range(8):
                P.op("act", lambda e, c=c: e.activation(out=sqbuf[:, c, :], in_=xres[:, c, :], func=AF.Square),
                     r=[(xres, c)], w=[(sqbuf, c)])
            ps = next_ps()
            for c in range(8):
                P.op("pe", lambda e, c=c, ps=ps: e.matmul(ps[:, :], lhsT=onesb.ap, rhs=sqbuf[:, c, :],
                                                            start=(c == 0), stop=(c == 7)),
                     r=[onesb, (sqbuf, c)], w=[ps])
            rstd_from(ps, 1.0 / D)
            for c in range(8):
                P.op("dve", lambda e, c=c: e.scalar_tensor_tensor(
                    out=dst[:, c, :], in0=xres[:, c, :], scalar=pc8[:, gcol, c:c + 1], in1=rstd.ap,
                    op0=ALU.mult, op1=ALU.mult), r=[(xres, c), pc8, rstd], w=[(dst, c)])

        def load_x(tile):
            for s in range(4):
                src = xs[tile * TT + s * 128: tile * TT + (s + 1) * 128, :]
                P.op("sp", lambda e, src=src: e.dma_start(out=xin.ap, in_=src), w=[xin], dma="xin")
                for c0 in range(0, 8, 4):
                    ps = next_ps()
                    for c in range(c0, c0 + 4):
                        P.op("pe", lambda e, c=c, c0=c0, ps=ps: e.transpose(
                            out=ps[:, (c - c0) * 128:(c - c0 + 1) * 128], in_=xin[:, c * 128:(c + 1) * 128],
                            identity=cst[:, IDN, :]), r=[xin, cst], w=[ps])
                    P.op("act", lambda e, c0=c0, s=s, ps=ps: e.activation(
                        out=xres[:, c0:c0 + 4, s * 128:(s + 1) * 128],
                        in_=ps[:, :].rearrange("p (c t) -> p c t", c=4), func=AF.Copy),
                        r=[ps], w=[(xres, c0, c0 + 4)])

        def dump(name, buf, nch):
            if not DEBUG:
                return
            for c in range(nch):
                g = gsb[st["g"] % 2]
                st["g"] += 1
                P.op("dve", lambda e, g=g, c=c: e.tensor_copy(out=g.ap, in_=buf[:, c, :]), r=[(buf, c)], w=[g])
                tok = P.op("sp", lambda e, g=g, c=c: e.dma_start(out=dbg[name][:, c, :], in_=g.ap),
                           r=[g], w=[name], dma=g.key)
                P.final[g.key] = tok

        def ffn(pref, gcol):
            ph = Phase()
            act = ph.alloc("act", [128, 22, TT], BF16)
            sqb = ph.alloc("sqb", [128, 8, TT], BF16)
            rmsnorm_to(hT, gcol, sqb)
            for p0 in range(0, DFF, 512):
                pw = min(512, DFF - p0)
                pp = []
                for wn in (pref + "_w_gate", pref + "_w_up"):
                    pan = get_pan()
                    pv = pan.ap[:, 0:8 * pw].rearrange("p (c n) -> p c n", c=8)
                    P.op("sp", lambda e, pv=pv, wn=wn, p0=p0, pw=pw: e.dma_start(
                        out=pv, in_=wbf[wn][:, p0:p0 + pw].rearrange("(c p) n -> p c n", p=128)),
                        r=[wn + "_bf"], w=[pan], dma=pan.key)
                    pp.append((pan, pv))
                for j in range(pw // 128):
                    oc = (p0 // 128) + j
                    psg, psu = next_ps(), next_ps()
                    for (pan, pv), ps in ((pp[0], psg), (pp[1], psu)):
                        for c in range(8):
                            P.op("pe", lambda e, pv=pv, ps=ps, c=c, j=j: e.matmul(
                                ps[:, :], lhsT=pv[:, c, j * 128:(j + 1) * 128], rhs=hT[:, c, :],
                                start=(c == 0), stop=(c == 7)), r=[pan, (hT, c)], w=[ps])
                    g = gsb[st["g"] % 2]
                    st["g"] += 1
                    P.op("act", lambda e, g=g, psg=psg: e.activation(out=g.ap, in_=psg[:, :], func=AF.Silu),
                         r=[psg], w=[g])
                    P.op("dve", lambda e, g=g, psu=psu, oc=oc: e.tensor_tensor(
                        out=act[:, oc, :], in0=psu[:, :], in1=g.ap, op=ALU.mult), r=[psu, g], w=[(act, oc)])

            def fin(j, ps, cw):
                P.op("dve", lambda e, ps=ps, j=j: e.scalar_tensor_tensor(
                    out=xres[:, j, :], in0=ps[:, :], scalar=0.5, in1=xres[:, j, :],
                    op0=ALU.mult, op1=ALU.add), r=[ps, (xres, j)], w=[(xres, j)])
            proj(pref + "_w_down", 0, D, 22, lambda c: (act[:, c, :], (act, c)), fin, pw=128)

        def s5_setup():
            ph = Phase()
            A = ph.alloc("s5a", [128, 3, 64], F32)
            Bi = ph.alloc("s5b", [128, 2, 64, 16], F32)
            Ci = ph.alloc("s5c", [128, 2, 64, 16], F32)
            load_params([(A, din["s5a"]), (Bi, din["s5b"]), (Ci, din["s5c"])], "par2")
            T = {}
            TMP = Buf(None, "s5tmp", "arena", BASE, ARENA, [128, ARENA - BASE], U8)
            RD = [A, Bi, Ci, TMP, cstv]

            def t(name):
                if name not in T:
                    T[name] = ph.alloc("t_" + name, [128, 64], F32).ap
                return T[name]

            def tt(o, a, b, op):
                P.op("dve", lambda e: e.tensor_tensor(out=o, in0=a, in1=b, op=op), r=RD, w=[TMP])

            def ts(o, a, s1, s2, op0, op1=None):
                if op1 is None:
                    P.op("dve", lambda e: e.tensor_scalar(out=o, in0=a, scalar1=s1, scalar2=None, op0=op0), r=RD, w=[TMP])
                else:
                    P.op("dve", lambda e: e.tensor_scalar(out=o, in0=a, scalar1=s1, scalar2=s2, op0=op0, op1=op1),
                         r=RD, w=[TMP])

            def stt(o, a, s, b, op0, op1):
                P.op("dve", lambda e: e.scalar_tensor_tensor(out=o, in0=a, scalar=s, in1=b, op0=op0, op1=op1),
                     r=RD, w=[TMP])

            def ac(o, a, func, scale=1.0):
                P.op("act", lambda e: e.activation(out=o, in_=a, func=func, scale=scale), r=RD, w=[TMP])

            def rcp(o, a):
                P.op("dve", lambda e: e.reciprocal(out=o, in_=a), r=RD, w=[TMP])

            import os as _os
            SL = int(_os.environ.get("S5STOP", "99"))
            lr, li, ldt = A[:, 0, :], A[:, 1, :], A[:, 2, :]
            ac(t("dt"), ldt, AF.Exp)
            tt(t("lrdt"), lr, t("dt"), ALU.mult)
            tt(t("th"), li, t("dt"), ALU.mult)
            ac(t("mag"), t("lrdt"), AF.Exp)
            ts(t("v"), t("th"), 1.0 / (2 * np.pi), None, ALU.mult)
            if SL <= 1:
                return
            vi = ph.alloc("t_vi", [128, 64], I32).ap
            P.op("dve", lambda e: e.tensor_copy(out=vi, in_=t("v")), r=RD, w=[TMP])
            P.op("dve", lambda e: e.tensor_copy(out=t("vf"), in_=vi), r=RD, w=[TMP])
            tt(t("fr"), t("v"), t("vf"), ALU.subtract)
            ts(t("g"), t("fr"), 0.5, None, ALU.is_gt)
            tt(t("fr"), t("fr"), t("g"), ALU.subtract)
            ts(t("g"), t("fr"), -0.5, None, ALU.is_lt)
            tt(t("fr"), t("fr"), t("g"), ALU.add)
            if SL <= 2:
                return
            ac(t("sin"), t("fr"), AF.Sin, scale=6.28318)
            ts(t("frc"), t("fr"), 0.25, None, ALU.add)
            ts(t("g"), t("frc"), 0.5, None, ALU.is_gt)
            tt(t("frc"), t("frc"), t("g"), ALU.subtract)
            ac(t("cos"), t("frc"), AF.Sin, scale=6.28318)
            if SL <= 3:
                return
            ar, ai = t("ar"), t("ai")
            tt(ar, t("mag"), t("cos"), ALU.mult)
            tt(ai, t("mag"), t("sin"), ALU.mult)
            tt(t("den"), lr, lr, ALU.mult)
            tt(t("d2"), li, li, ALU.mult)
            tt(t("den"), t("den"), t("d2"), ALU.add)
            rcp(t("rden"), t("den"))
            ts(t("am1"), ar, -1.0, None, ALU.add)
            tt(t("x1"), t("am1"), lr, ALU.mult)
            tt(t("x2"), ai, li, ALU.mult)
            tt(t("x1"), t("x1"), t("x2"), ALU.add)
            tt(t("cr"), t("x1"), t("rden"), ALU.mult)
            tt(t("x1"), ai, lr, ALU.mult)
            tt(t("x2"), t("am1"), li, ALU.mult)
            tt(t("x1"), t("x1"), t("x2"), ALU.subtract)
            tt(t("ci"), t("x1"), t("rden"), ALU.mult)
            pr = ph.alloc("t_pr", [128, 9, 64], F32)
            pi = ph.alloc("t_pi", [128, 9, 64], F32)
            P.op("dve", lambda e: e.memset(pr[:, 0, :], 1.0), r=RD, w=[TMP])
            P.op("dve", lambda e: e.memset(pi[:, 0, :], 0.0), r=RD, w=[TMP])

            def cmul(orr, oi, xr, xi, yr, yi):
                tt(t("m1"), xr, yr, ALU.mult)
                tt(t("m2"), xi, yi, ALU.mult)
                tt(t("m3"), xr, yi, ALU.mult)
                tt(t("m4"), xi, yr, ALU.mult)
                tt(orr, t("m1"), t("m2"), ALU.subtract)
                tt(oi, t("m3"), t("m4"), ALU.add)
            for k in range(8):
                cmul(pr[:, k + 1, :], pi[:, k + 1, :], pr[:, k, :], pi[:, k, :], ar, ai)
            tt(t("m1"), pr[:, 8, :], pr[:, 8, :], ALU.mult)
            tt(t("m2"), pi[:, 8, :], pi[:, 8, :], ALU.mult)
            tt(t("m1"), t("m1"), t("m2"), ALU.add)
            rcp(t("rm"), t("m1"))
            tt(t("ir"), pr[:, 8, :], t("rm"), ALU.mult)
            tt(t("ii"), pi[:, 8, :], t("rm"), ALU.mult)
            ts(t("ii"), t("ii"), -1.0, None, ALU.mult)
            P.op("dve", lambda e: e.tensor_copy(out=a8r.ap, in_=pr[:, 8, :]), r=RD, w=[a8r])
            P.op("dve", lambda e: e.tensor_scalar(out=a8s.ap, in0=pi[:, 8, :], scalar1=cstv[:, SGN:SGN + 1], scalar2=None,
                                                  op0=ALU.mult), r=RD, w=[a8s])
            P.op("dve", lambda e: e.tensor_scalar(out=a8n.ap, in0=a8s.ap, scalar1=-1.0, scalar2=None, op0=ALU.mult),
                 r=[a8s], w=[a8n])
            if SL <= 4:
                return
            big = lambda name: ph.alloc(name, [128, 64, 16], F32).ap
            bbr, bbi, BB, BBs, CC1, CC2, w1, w2 = [big(n) for n in ("bbr", "bbi", "BB", "BBs", "CC1", "CC2", "w1", "w2")]

            def bc(ap2d):
                return ap2d.unsqueeze(2).to_broadcast([128, 64, 16])
            Br, Bim, Cr, Cim = Bi[:, 0, :, :], Bi[:, 1, :, :], Ci[:, 0, :, :], Ci[:, 1, :, :]
            crb, cib = bc(t("cr")), bc(t("ci"))
            tt(w1, Br, crb, ALU.mult); tt(w2, Bim, cib, ALU.mult); tt(bbr, w1, w2, ALU.subtract)
            tt(w1, Bim, crb, ALU.mult); tt(w2, Br, cib, ALU.mult); tt(bbi, w1, w2, ALU.add)
            sT, sB = cstv[:, SELT:SELT + 1], cstv[:, SELB:SELB + 1]
            ts(BB, bbr, sT, None, ALU.mult); stt(BB, bbi, sB, BB, ALU.mult, ALU.add)
            ts(BBs, bbi, sT, -1.0, ALU.mult, ALU.mult); stt(BBs, bbr, sB, BBs, ALU.mult, ALU.add)
            ts(CC1, Cim, sB, -1.0, ALU.mult, ALU.mult); stt(CC1, Cr, sT, CC1, ALU.mult, ALU.add)
            ts(CC2, Cim, sT, -1.0, ALU.mult, ALU.mult); ts(w1, Cr, sB, -1.0, ALU.mult, ALU.mult)
            tt(CC2, CC2, w1, ALU.add)
            if SL <= 5:
                return
            NH = 16
            Bpw = ph.alloc("Bpw", [128, NH, 8, 16], F32)
            BJ = ph.alloc("BJ", [128, NH, 8, 16], F32)
            Cpw = ph.alloc("Cpw", [128, NH, 8, 16], F32)
            stB = ph.alloc("stB", [128, 2, 8, 128], BF16)
            stK = ph.alloc("stK", [128, 2, 8, 128], BF16)
            BJh = ph.alloc("BJh", [128, NH, 8, 16], BF16)
            BJl = ph.alloc("BJl", [128, NH, 8, 16], BF16)
            Cph = ph.alloc("Cph", [128, NH, 8, 16], BF16)
            Cpl = ph.alloc("Cpl", [128, NH, 8, 16], BF16)
            fl = lambda b, g: b[:, g, :, :].rearrange("p a b -> p (a b)")

            def bch(ap2d, gs):
                return ap2d[:, gs].unsqueeze(2).to_broadcast([128, NH, 16])
            for hf in range(64 // NH):
                gs = slice(hf * NH, (hf + 1) * NH)
                w1h, w2h = w1[:, 0:NH, :], w2[:, 0:NH, :]
                for j in range(8):
                    zr, zi = bch(pr[:, 7 - j, :], gs), bch(pi[:, 7 - j, :], gs)
                    tt(w1h, BB[:, gs, :], zr, ALU.mult); tt(w2h, BBs[:, gs, :], zi, ALU.mult)
                    tt(Bpw[:, :, j, :], w1h, w2h, ALU.add)
                    cmul(t("qr"), t("qi"), pr[:, 7 - j, :], pi[:, 7 - j, :], t("ir"), t("ii"))
                    zr, zi = bch(t("qr"), gs), bch(t("qi"), gs)
                    tt(w1h, BB[:, gs, :], zr, ALU.mult); tt(w2h, BBs[:, gs, :], zi, ALU.mult)
                    tt(BJ[:, :, j, :], w1h, w2h, ALU.add)
                    zr, zi = bch(pr[:, j + 1, :], gs), bch(pi[:, j + 1, :], gs)
                    tt(w1h, CC1[:, gs, :], zr, ALU.mult); tt(w2h, CC2[:, gs, :], zi, ALU.mult)
                    tt(Cpw[:, :, j, :], w1h, w2h, ALU.add)
                if SL <= 6:
                    continue
                for src_, hi_, lo_ in ((BJ, BJh, BJl), (Cpw, Cph, Cpl)):
                    P.op("dve", lambda e, src_=src_, hi_=hi_: e.tensor_copy(out=hi_.ap, in_=src_.ap), r=RD, w=[TMP])
                    P.op("dve", lambda e, src_=src_, hi_=hi_: e.tensor_tensor(out=src_.ap, in0=src_.ap, in1=hi_.ap,
                                                                           op=ALU.subtract), r=RD, w=[TMP])
                    P.op("dve", lambda e, src_=src_, lo_=lo_: e.tensor_copy(out=lo_.ap, in_=src_.ap), r=RD, w=[TMP])
                for q in range(hf * NH // 8, (hf + 1) * NH // 8):
                    for gl in range(8):
                        g = 8 * q + gl - hf * NH
                        ps = next_ps()
                        psk = next_ps()
                        P.op("pe", lambda e, g=g, ps=ps: e.transpose(out=ps[:, 0:128], in_=fl(Bpw, g), identity=cst[:, IDN, :]),
                             r=[TMP, cst], w=[ps])
                        for ii, (l_, r_) in enumerate(((BJh, Cph), (BJh, Cpl), (BJl, Cph))):
                            P.op("pe", lambda e, g=g, psk=psk, ii=ii, l_=l_, r_=r_: e.matmul(
                                psk[:, 0:128], lhsT=fl(l_, g), rhs=fl(r_, g), start=(ii == 0), stop=(ii == 2)),
                                r=[TMP], w=[psk])
                        P.op("act", lambda e, gl=gl, ps=ps: e.activation(out=stB[:, 0, gl, :], in_=ps[:, 0:128], func=AF.Copy),
                             r=[ps], w=[stB])
                        P.op("act", lambda e, gl=gl, ps=ps: e.activation(out=stB[:, 1, gl, 0:64], in_=ps[:, 64:128], func=AF.Copy),
                             r=[ps], w=[stB])
                        P.op("act", lambda e, gl=gl, ps=ps: e.activation(out=stB[:, 1, gl, 64:128], in_=ps[:, 0:64], func=AF.Copy),
                             r=[ps], w=[stB])
                        P.op("dve", lambda e, gl=gl, psk=psk: e.tensor_tensor(
                            out=stK[:, 0, gl, :], in0=psk[:, 0:128], in1=cst[:, KMASK, :], op=ALU.mult),
                            r=[psk, cst], w=[stK])
                        P.op("dve", lambda e, gl=gl, g=g: e.tensor_copy(out=stK[:, 1, gl, :], in_=fl(Cph, g)),
                             r=[TMP], w=[stK])
                    if int(_os.environ.get("S7", "9")) >= 3:
                        P.op("sp", lambda e, q=q: e.dma_start(out=s5B[q], in_=stB.ap), r=[stB], w=["s5B"], dma="s5B")
                        P.op("sp", lambda e, q=q: e.dma_start(out=s5K[q], in_=stK.ap), r=[stK], w=["s5K"], dma="s5K")
            print("setup phase top", ph.top, "BASE", BASE)

        def s5_tile(light):
            ph = Phase()
            uT = ph.alloc("uT", [128, 8, TT], BF16)
            U2 = ph.alloc("U2", [64, 64, 8, 16], BF16)
            Ucol = ph.alloc("Ucol", [128, 64, 64], BF16)
            cS = ph.alloc("cS", [128, 64, 64], F32)
            cW = ph.alloc("cW", [128, 64, 64], F32)
            Xprev = ph.alloc("Xprev", [128, 64, 64], BF16)
            Ycol = ph.alloc("Ycol", [128, 64, 64], BF16)
            ytmp = ph.alloc("ytmp", [128, TT], F32)
            gtmp = ph.alloc("gtmp", [128, TT], F32)
            r1 = [ph.alloc("r1_%d" % i, [128, 64], F32) for i in range(2)]
            r2 = [ph.alloc("r2_%d" % i, [128, 64], F32) for i in range(2)]

            def cons_u(j, ps, cw):
                P.op("act", lambda e: e.activation(out=uT[:, j, :], in_=ps[:, :], func=AF.Copy), r=[ps], w=[(uT, j)])
            proj("w_in", 0, 1024, 8, lambda c: (hT[:, c, :], (hT, c)), cons_u)
            for q in range(8):
                ps = next_ps()
                pb = psbf(ps)
                for j in range(8):
                    P.op("pe", lambda e, q=q, j=j, pb=pb: e.transpose(
                        out=pb[0:64, j * 128:(j + 1) * 128], in_=uT[:, q, j:TT:8], identity=identb.ap),
                        r=[(uT, q), identb], w=[ps])
                P.op("dve", lambda e, q=q, pb=pb: e.tensor_copy(
                    out=U2[:, 8 * q:8 * q + 8, :, :],
                    in_=pb[0:64, :].rearrange("p (j g m) -> p g j m", j=8, g=8)), r=[ps], w=[U2])
            for g0 in range(0, 64, 16):
                ps = next_ps()
                pb = psbf(ps)
                for g in range(g0, g0 + 16):
                    P.op("pe", lambda e, g=g, g0=g0, pb=pb: e.transpose(
                        out=pb[:, (g - g0) * 64:(g - g0 + 1) * 64], in_=U2[:, g, :, :].rearrange("p a b -> p (a b)"),
                        identity=identb[0:64, 0:64]), r=[U2, identb], w=[ps])
                P.op("act", lambda e, g0=g0, pb=pb: e.activation(
                    out=Ucol[:, g0:g0 + 16, :], in_=pb[:, :].rearrange("p (g b) -> p g b", g=16), func=AF.Copy),
                    r=[ps], w=[(Ucol, g0, g0 + 16)])
            for q in range(8):
                P.op("sp", lambda e, q=q: e.dma_start(out=s5tab.ap, in_=s5B[q]), r=["s5B"], w=[s5tab], dma="s5tab")
                for which, dst in ((0, cS), (1, cW)):
                    ps = next_ps()
                    for gl in range(8):
                        g = 8 * q + gl
                        P.op("pe", lambda e, which=which, gl=gl, g=g, ps=ps: e.matmul(
                            ps[:, gl * 64:(gl + 1) * 64], lhsT=s5tab[:, which, gl, :], rhs=Ucol[:, g, :],
                            start=True, stop=True), r=[s5tab, (Ucol, g)], w=[ps])
                    P.op("act", lambda e, q=q, dst=dst, ps=ps: e.activation(
                        out=dst[:, 8 * q:8 * q + 8, :], in_=ps[:, :].rearrange("p (g b) -> p g b", g=8), func=AF.Copy),
                        r=[ps], w=[(dst, 8 * q, 8 * q + 8)])
            for b in range(64):
                S0, W0 = Sst[st["s"] % 2], Ssw[st["s"] % 2]
                S1, W1 = Sst[(st["s"] + 1) % 2], Ssw[(st["s"] + 1) % 2]
                st["s"] += 1
                if not light:
                    P.op("act", lambda e, b=b, S0=S0: e.activation(out=Xprev[:, :, b], in_=S0.ap, func=AF.Copy),
                         r=[S0], w=[Xprev])
                A, Bq = r1[b % 2], r2[b % 2]
                P.op("dve", lambda e, A=A, S0=S0: e.tensor_tensor(out=A.ap, in0=S0.ap, in1=a8r.ap, op=ALU.mult),
                     r=[S0, a8r], w=[A])
                P.op("pool", lambda e, Bq=Bq, W0=W0: e.tensor_tensor(out=Bq.ap, in0=W0.ap, in1=a8r.ap, op=ALU.mult),
                     r=[W0, a8r], w=[Bq])
                P.op("dve", lambda e, W0=W0, S1=S1: e.tensor_tensor(out=S1.ap, in0=W0.ap, in1=a8s.ap, op=ALU.mult),
                     r=[W0, a8s], w=[S1])
                P.op("pool", lambda e, S0=S0, W1=W1: e.tensor_tensor(out=W1.ap, in0=S0.ap, in1=a8n.ap, op=ALU.mult),
                     r=[S0, a8n], w=[W1])
                P.op("dve", lambda e, A=A, S1=S1: e.tensor_tensor(out=S1.ap, in0=S1.ap, in1=A.ap, op=ALU.add),
                     r=[S1, A], w=[S1])
                P.op("dve", lambda e, Bq=Bq, W1=W1: e.tensor_tensor(out=W1.ap, in0=W1.ap, in1=Bq.ap, op=ALU.add),
                     r=[W1, Bq], w=[W1])
                P.op("dve", lambda e, b=b, S1=S1: e.tensor_tensor(out=S1.ap, in0=S1.ap, in1=cS[:, :, b], op=ALU.add),
                     r=[S1, cS], w=[S1])
                P.op("dve", lambda e, b=b, W1=W1: e.tensor_tensor(out=W1.ap, in0=W1.ap, in1=cW[:, :, b], op=ALU.add),
                     r=[W1, cW], w=[W1])
            if light:
                return None
            s5T = P.view("s5T", S5T_OFF, [128, 8, TT], BF16)
            gel = ph.alloc("gel", [128, 8, TT], BF16)
            assert ph.top <= S5T_OFF, ph.top
            for q in range(8):
                P.op("sp", lambda e, q=q: e.dma_start(out=s5tab.ap, in_=s5K[q]), r=["s5K"], w=[s5tab], dma="s5tab")
                ps = next_ps()
                for gl in range(8):
                    g = 8 * q + gl
                    P.op("pe", lambda e, gl=gl, g=g, ps=ps: e.matmul(
                        ps[:, gl * 64:(gl + 1) * 64], lhsT=s5tab[:, 0, gl, :], rhs=Ucol[:, g, :],
                        start=True, stop=False), r=[s5tab, (Ucol, g)], w=[ps])
                    P.op("pe", lambda e, gl=gl, g=g, ps=ps: e.matmul(
                        ps[:, gl * 64:(gl + 1) * 64], lhsT=s5tab[:, 1, gl, :], rhs=Xprev[:, g, :],
                        start=False, stop=True), r=[s5tab, Xprev], w=[ps])
                P.op("act", lambda e, q=q, ps=ps: e.activation(
                    out=Ycol[:, 8 * q:8 * q + 8, :], in_=ps[:, :].rearrange("p (g b) -> p g b", g=8), func=AF.Copy),
                    r=[ps], w=[(Ycol, 8 * q, 8 * q + 8)])
            Y2 = P.view("U2", U2.lo, [64, 8, 64, 16], BF16)
            for g0 in range(0, 64, 8):
                ps = next_ps()
                pb = psbf(ps)
                for g in range(g0, g0 + 8):
                    P.op("pe", lambda e, g=g, g0=g0, pb=pb: e.transpose(
                        out=pb[0:64, (g - g0) * 128:(g - g0 + 1) * 128], in_=Ycol[:, g, :], identity=identb.ap),
                        r=[(Ycol, g), identb], w=[ps])
                P.op("dve", lambda e, g0=g0, pb=pb: e.tensor_copy(
                    out=Y2[:, :, g0:g0 + 8, :],
                    in_=pb[0:64, :].rearrange("p (g j m) -> p j g m", g=8, j=8)), r=[ps], w=[Y2])
            for q in range(8):
                ps = next_ps()
                pb = psbf(ps)
                for j in range(8):
                    P.op("pe", lambda e, q=q, j=j, pb=pb: e.transpose(
                        out=pb[:, j * 64:(j + 1) * 64],
                        in_=Y2[:, j, 8 * q:8 * q + 8, :].rearrange("p a b -> p (a b)"),
                        identity=identb[0:64, 0:64]), r=[Y2, identb], w=[ps])
                P.op("dve", lambda e, q=q, pb=pb: e.scalar_tensor_tensor(
                    out=ytmp.ap.rearrange("p (b j) -> p j b", j=8),
                    in0=uT[:, q, :].rearrange("p (b j) -> p j b", j=8),
                    scalar=pc8[:, V_S5D, q:q + 1],
                    in1=pb[:, 0:512].rearrange("p (j b) -> p j b", j=8),
                    op0=ALU.mult, op1=ALU.add), r=[ps, (uT, q), pc8], w=[ytmp])
                P.op("pool", lambda e: e.tensor_tensor(out=gtmp.ap, in0=ytmp.ap, in1=ytmp.ap, op=ALU.mult),
                     r=[ytmp], w=[gtmp])
                P.op("pool", lambda e: e.tensor_scalar(out=gtmp.ap, in0=gtmp.ap, scalar1=0.044715, scalar2=1.0,
                                                       op0=ALU.mult, op1=ALU.add), r=[gtmp], w=[gtmp])
                P.op("pool", lambda e: e.tensor_tensor(out=gtmp.ap, in0=gtmp.ap, in1=ytmp.ap, op=ALU.mult),
                     r=[gtmp, ytmp], w=[gtmp])
                P.op("act", lambda e: e.activation(out=gtmp.ap, in_=gtmp.ap, func=AF.Sigmoid, scale=1.5957691216),
                     r=[gtmp], w=[gtmp])
                P.op("dve", lambda e, q=q: e.tensor_tensor(out=gel[:, q, :], in0=gtmp.ap, in1=ytmp.ap, op=ALU.mult),
                     r=[gtmp, ytmp], w=[(gel, q)])

            def cons_glu(j, ps, cw):
                g = gsb[st["g"] % 2]
                st["g"] += 1
                P.op("act", lambda e: e.activation(out=g.ap, in_=ps[:, :], func=AF.Sigmoid,
                                                   bias=pc8[:, V_BGLU, j:j + 1]), r=[ps, pc8], w=[g])
                P.op("dve", lambda e: e.tensor_tensor(out=s5T[:, j, :], in0=gel[:, j, :], in1=g.ap, op=ALU.mult),
                     r=[g, (gel, j)], w=[(s5T, j)])
            proj("s5_w_glu", 0, 1024, 8, lambda c: (gel[:, c, :], (gel, c)), cons_glu)
            return s5T

        def ssd_tile(light):
            ph = Phase()
            xbcp = ph.alloc("xbcp", [128, 32, TT + 3], BF16)
            ySS = P.view("ySS", xbcp.lo, [128, 16, TT], BF16)
            xsT = ph.alloc("xsT", [128, 16, TT], BF16)
            Bfm = ph.alloc("Bfm", [128, 8, TT], BF16)
            Cfm = ph.alloc("Cfm", [128, 8, TT], BF16)
            xtok = ph.alloc("xtok", [128, 2048], BF16)
            Btok = ph.alloc("Btok", [128, 1024], BF16)
            dtT = ph.alloc("dtT", [32, TT], F32)
            dAT = ph.alloc("dAT", [32, TT], F32)
            dtok = ph.alloc("dtok", [128, 4, 64], F32)
            dAh = ph.alloc("dAh", [128, 4, 32], BF16)
            dAl = ph.alloc("dAl", [128, 4, 32], BF16)
            dAr = ph.alloc("dAr", [128, 4, 32], F32)
            acs = ph.alloc("acs", [128, 64], F32)
            wgt = ph.alloc("wgt", [128, 32], F32)
            cdec = ph.alloc("cdec", [128, 32], F32)
            dg = [ph.alloc("dg%d" % i, [128, 4, 128], BF16) for i in range(4)]
            cbm = [ph.alloc("cbm%d" % i, [128, 128], F32) for i in range(2)]
            rhsM = [ph.alloc("rhsM%d" % i, [128, 2, 128], BF16) for i in range(4)]
            ed = [ph.alloc("ed%d" % i, [128, 256], F32) for i in range(4)]
            wTb = [ph.alloc("wT%d" % i, [128, 128], BF16) for i in range(4)]
            Csb = [ph.alloc("Cs%d" % i, [128, 128], BF16) for i in range(4)]
            xwb = [ph.alloc("xw%d" % i, [128, 256], BF16) for i in range(2)]
            htmp = [ph.alloc("htmp%d" % i, [128, 4, 64], F32) for i in range(2)]
            sqz = [ph.alloc("sqz%d" % i, [128, TT], BF16) for i in range(2)]
            assert ph.top <= S5T_OFF, ph.top
            cnt = {"h": 0, "g": 0}

            P.op("pool", lambda e: e.tensor_copy(out=xbcp[:, :, 0:3], in_=tail.ap), r=[tail], w=[xbcp])

            def cons_xbc(j, ps, cw):
                P.op("dve", lambda e: e.tensor_copy(out=xbcp[:, j, 3:TT + 3], in_=ps[:, :]),
                     r=[ps], w=[(xbcp, j)])
            proj("w_in", 3072, 4096, 8, lambda c: (hT[:, c, :], (hT, c)), cons_xbc)
            P.op("pool", lambda e: e.tensor_copy(out=tail.ap, in_=xbcp[:, :, TT:TT + 3]), r=[xbcp], w=[tail])
            for oc in range(32):
                d = dg[oc % 4]
                for k in range(4):
                    P.op("act", lambda e, d=d, k=k, oc=oc: e.activation(
                        out=d[:, k, :], in_=identb.ap, func=AF.Copy, scale=convp[:, oc, k:k + 1]),
                        r=[identb, convp], w=[(d, k)])
                ps = next_ps()
                for k in range(4):
                    P.op("pe", lambda e, d=d, k=k, oc=oc, ps=ps: e.matmul(
                        ps[:, :], lhsT=d[:, k, :], rhs=xbcp[:, oc, k:k + TT], start=(k == 0), stop=(k == 3)),
                        r=[(d, k), (xbcp, oc)], w=[ps])
                if oc < 16:
                    dst, dk = xsT[:, oc, :], (xsT, oc)
                elif oc < 24:
                    dst, dk = Bfm[:, oc - 16, :], (Bfm, oc - 16)
                else:
                    dst, dk = Cfm[:, oc - 24, :], (Cfm, oc - 24)
                P.op("act", lambda e, dst=dst, ps=ps, oc=oc: e.activation(
                    out=dst, in_=ps[:, :], func=AF.Silu, bias=convp[:, oc, 4:5]), r=[ps, convp], w=[dk])

            def cons_dt(j, ps, cw):
                P.op("act", lambda e: e.activation(out=dtT.ap, in_=ps[0:32, :], func=AF.Exp, bias=hp[:, 1:2]),
                     r=[ps, hp], w=[dtT])
                P.op("act", lambda e: e.activation(out=dtT.ap, in_=dtT.ap, func=AF.Ln, bias=1.0), r=[dtT], w=[dtT])
                P.op("dve", lambda e: e.tensor_scalar(out=dAT.ap, in0=dtT.ap, scalar1=aneg[:, 0:1], scalar2=None,
                                                      op0=ALU.mult), r=[dtT, aneg], w=[dAT])
            proj("w_in", 7168, 32, 8, lambda c: (hT[:, c, :], (hT, c)), cons_dt, pw=128)
            ps = next_ps()
            for s in range(4):
                P.op("pe", lambda e, s=s, ps=ps: e.transpose(
                    out=ps[:, s * 64:s * 64 + 32], in_=dtT[:, s * 128:(s + 1) * 128], identity=cst[0:32, IDN, 0:32]),
                    r=[dtT, cst], w=[ps])
                P.op("pe", lambda e, s=s, ps=ps: e.transpose(
                    out=ps[:, s * 64 + 32:s * 64 + 64], in_=dAT[:, s * 128:(s + 1) * 128], identity=cst[0:32, IDN, 0:32]),
                    r=[dAT, cst], w=[ps])
            P.op("dve", lambda e, ps=ps: e.tensor_copy(out=dtok.ap, in_=ps[:, 0:256].rearrange("p (s c) -> p s c", s=4)),
                 r=[ps], w=[dtok])
            P.op("dve", lambda e: e.tensor_copy(out=dAh.ap, in_=dtok[:, :, 32:64]), r=[dtok], w=[dAh])
            P.op("dve", lambda e: e.tensor_tensor(out=dAr.ap, in0=dtok[:, :, 32:64], in1=dAh.ap, op=ALU.subtract),
                 r=[dtok, dAh], w=[dAr])
            P.op("dve", lambda e: e.tensor_copy(out=dAl.ap, in_=dAr.ap), r=[dAr], w=[dAl])

            for s in range(4):
                tk = slice(s * 128, (s + 1) * 128)
                for c0 in range(0, 24, 8):
                    ps = next_ps()
                    pb = psbf(ps)
                    for c in range(c0, c0 + 8):
                        src, sk = (xsT[:, c, tk], (xsT, c)) if c < 16 else (Bfm[:, c - 16, tk], (Bfm, c - 16))
                        P.op("pe", lambda e, c=c, c0=c0, pb=pb, src=src: e.transpose(
                            out=pb[:, (c - c0) * 128:(c - c0 + 1) * 128], in_=src, identity=identb.ap),
                            r=[sk, identb], w=[ps])
                    if c0 < 16:
                        P.op("act", lambda e, c0=c0, pb=pb: e.activation(
                            out=xtok[:, c0 * 128:(c0 + 8) * 128], in_=pb[:, :], func=AF.Copy), r=[ps], w=[xtok])
                    else:
                        P.op("act", lambda e, pb=pb: e.activation(out=Btok.ap, in_=pb[:, :], func=AF.Copy),
                             r=[ps], w=[Btok])
                ps = next_ps()
                for ii, dd in enumerate((dAh, dAl)):
                    P.op("pe", lambda e, s=s, ps=ps, ii=ii, dd=dd: e.matmul(ps[:, 0:32], lhsT=mub.ap, rhs=dd[:, s, :],
                                                                            start=(ii == 0), stop=(ii == 1)), r=[mub, dd], w=[ps])
                for ii, dd in enumerate((dAh, dAl)):
                    P.op("pe", lambda e, s=s, ps=ps, ii=ii, dd=dd: e.matmul(ps[:, 32:64], lhsT=onesb.ap, rhs=dd[:, s, :],
                                                                            start=(ii == 0), stop=(ii == 1)), r=[onesb, dd], w=[ps])
                P.op("act", lambda e, ps=ps: e.activation(out=acs.ap, in_=ps[:, 0:64], func=AF.Copy), r=[ps], w=[acs])
                P.op("dve", lambda e: e.tensor_tensor(out=wgt.ap, in0=acs[:, 32:64], in1=acs[:, 0:32], op=ALU.subtract),
                     r=[acs], w=[wgt])
                P.op("act", lambda e: e.activation(out=wgt.ap, in_=wgt.ap, func=AF.Exp), r=[wgt], w=[wgt])
                P.op("dve", lambda e, s=s: e.tensor_tensor(out=wgt.ap, in0=wgt.ap, in1=dtok[:, s, 0:32], op=ALU.mult),
                     r=[wgt, dtok], w=[wgt])
                P.op("act", lambda e: e.activation(out=cdec.ap, in_=acs[:, 32:64], func=AF.Exp), r=[acs], w=[cdec])
                rot["n"] = 5
                LAG = 2
                hbuf = {}

                def G0(g, tk=tk):
                    cb = cbm[g % 2]
                    ps = next_ps()
                    P.op("pe", lambda e: e.matmul(ps[:, 0:128], lhsT=Bfm[:, g, tk], rhs=Cfm[:, g, tk],
                                                  start=True, stop=True), r=[(Bfm, g), (Cfm, g)], w=[ps])
                    P.op("dve", lambda e: e.tensor_tensor(out=cb.ap, in0=ps[:, 0:128], in1=cst[:, MU, :],
                                                          op=ALU.mult), r=[ps, cst], w=[cb])

                def A(h, s=s):
                    i = h % 4
                    rm, e_ = rhsM[i], ed[i]
                    for ii, dd in enumerate((dAh, dAl)):
                        P.op("dve", lambda e, ii=ii, dd=dd: e.tensor_scalar(
                            out=rm[:, ii, :], in0=mub.ap, scalar1=dd[:, s, h:h + 1], scalar2=None,
                            op0=ALU.mult), r=[mub, dd], w=[(rm, ii)])
                    ps2 = next_ps()
                    for ii in range(2):
                        P.op("pe", lambda e, ii=ii: e.matmul(
                            ps2[:, 0:128], lhsT=m1b.ap, rhs=rm[:, ii, :], start=(ii == 0), stop=(ii == 1)),
                            r=[m1b, (rm, ii)], w=[ps2])
                    for ii in range(2):
                        P.op("pe", lambda e, ii=ii: e.matmul(
                            ps2[:, 128:256], lhsT=onesb.ap, rhs=rm[:, ii, :], start=(ii == 0), stop=(ii == 1)),
                            r=[onesb, (rm, ii)], w=[ps2])
                    P.op("act", lambda e: e.activation(out=e_.ap, in_=ps2[:, 0:256], func=AF.Exp), r=[ps2], w=[e_])

                def B(h, s=s, tk=tk):
                    g, k = h // 4, h % 4
                    i = h % 4
                    e_, wt, cs = ed[i], wTb[i], Csb[i]
                    cb = cbm[g % 2]
                    psy = psb[5 + (g % 2)]
                    P.op("dve", lambda e: e.scalar_tensor_tensor(
                        out=wt.ap, in0=e_[:, 0:128], scalar=dtok[:, s, h:h + 1], in1=cb.ap,
                        op0=ALU.mult, op1=ALU.mult), r=[e_, cb, dtok], w=[wt])
                    P.op("pool", lambda e: e.tensor_tensor(
                        out=cs.ap, in0=Cfm[:, g, tk], in1=e_[:, 128:256], op=ALU.mult),
                        r=[(Cfm, g), e_], w=[cs])
                    po, co = 64 * (k % 2), 128 * (k // 2)
                    P.op("pe", lambda e: e.matmul(
                        psy[po:po + 64, co:co + 128], lhsT=xtok[:, h * 64:(h + 1) * 64], rhs=wt.ap,
                        start=True, stop=False), r=[xtok, wt], w=[psy])
                    P.op("pe", lambda e: e.matmul(
                        psy[po:po + 64, co:co + 128], lhsT=hSb[:, h, :], rhs=cs.ap,
                        start=False, stop=True), r=[(hSb, h), cs], w=[psy])

                def G1(g, tk=tk):
                    if not light:
                        psy = psb[5 + (g % 2)]
                        for cl in range(2):
                            cc = 2 * g + cl
                            P.op("dve", lambda e, cl=cl, cc=cc: e.scalar_tensor_tensor(
                                out=ySS[:, cc, tk], in0=xsT[:, cc, tk], scalar=pc16[:, V_SSDD, cc:cc + 1],
                                in1=psy[:, cl * 128:(cl + 1) * 128], op0=ALU.mult, op1=ALU.add),
                                r=[psy, (xsT, cc), pc16], w=[(ySS, cc)])
                    xw = xwb[g % 2]
                    ht = htmp[g % 2]
                    P.op("dve", lambda e: e.tensor_tensor(
                        out=xw.ap.rearrange("p (k d) -> p k d", k=4),
                        in0=xtok[:, g * 256:(g + 1) * 256].rearrange("p (k d) -> p k d", k=4),
                        in1=wgt[:, 4 * g:4 * g + 4].unsqueeze(2).to_broadcast([128, 4, 64]), op=ALU.mult),
                        r=[xtok, wgt], w=[xw])
                    ps3 = next_ps()
                    P.op("pe", lambda e: e.matmul(
                        ps3[:, 0:256], lhsT=Btok[:, g * 128:(g + 1) * 128], rhs=xw.ap, start=True, stop=True),
                        r=[Btok, xw], w=[ps3])
                    P.op("pool", lambda e: e.tensor_tensor(
                        out=ht.ap, in0=hS[:, 4 * g:4 * g + 4, :],
                        in1=cdec[:, 4 * g:4 * g + 4].unsqueeze(2).to_broadcast([128, 4, 64]), op=ALU.mult),
                        r=[(hS, 4 * g, 4 * g + 4), cdec], w=[ht])
                    P.op("dve", lambda e: e.tensor_tensor(
                        out=hS[:, 4 * g:4 * g + 4, :], in0=ps3[:, 0:256].rearrange("p (k d) -> p k d", k=4),
                        in1=ht.ap, op=ALU.add), r=[ps3, ht], w=[(hS, 4 * g, 4 * g + 4)])
                    P.op("act", lambda e: e.activation(out=hSb[:, 4 * g:4 * g + 4, :], in_=hS[:, 4 * g:4 * g + 4, :],
                                                       func=AF.Copy), r=[(hS, 4 * g, 4 * g + 4)], w=[(hSb, 4 * g, 4 * g + 4)])

                if light:
                    for g in range(8):
                        G1(g)
                else:
                    for idx in range(32 + LAG):
                        if idx < 32:
                            if idx % 4 == 0:
                                G0(idx // 4)
                            A(idx)
                        if idx >= LAG:
                            hb = idx - LAG
                            B(hb)
                            if hb % 4 == 3:
                                G1(hb // 4)
                rot["n"] = NROT
            if light:
                return None
            if DEBUG and st.get("dbg_ssd_done") is None:
                st["dbg_ssd_done"] = 1
                dump("dbg_xs", xsT, 16); dump("dbg_B", Bfm, 8); dump("dbg_C", Cfm, 8); dump("dbg_y", ySS, 16)
            pss = psb[7]

            def cons_z(j, ps, cw):
                g = gsb[st["g"] % 2]
                st["g"] += 1
                sq_ = sqz[j % 2]
                P.op("act", lambda e: e.activation(out=g.ap, in_=ps[:, :], func=AF.Silu), r=[ps], w=[g])
                P.op("dve", lambda e: e.tensor_tensor(out=ySS[:, j, :], in0=ySS[:, j, :], in1=g.ap, op=ALU.mult),
                     r=[(ySS, j), g], w=[(ySS, j)])
                P.op("act", lambda e: e.activation(out=sq_.ap, in_=ySS[:, j, :], func=AF.Square), r=[(ySS, j)], w=[sq_])
                P.op("pe", lambda e: e.matmul(pss[:, :], lhsT=onesb.ap, rhs=sq_.ap, start=(j == 0), stop=(j == 15)),
                     r=[onesb, sq_], w=[pss])
            proj("w_in", 1024, 2048, 8, lambda c: (hT[:, c, :], (hT, c)), cons_z)
            rstd_from(pss, 1.0 / 2048)
            for c in range(16):
                P.op("dve", lambda e, c=c: e.scalar_tensor_tensor(
                    out=ySS[:, c, :], in0=ySS[:, c, :], scalar=pc16[:, V_SSDN, c:c + 1], in1=rstd.ap,
                    op0=ALU.mult, op1=ALU.mult), r=[(ySS, c), pc16, rstd], w=[(ySS, c)])
            return ySS

        def merge_tile(s5T, ynT):
            G = P.view("G", MRG_OFF, [128, 16, TT], BF16)
            Q = P.view("Q", MRG_OFF + 16 * TT * 2, [128, 8, TT], BF16)
            mT = P.view("mT", MRG_OFF + 24 * TT * 2, [128, 8, TT], BF16)
            assert MRG_OFF + 32 * TT * 2 <= S5T_OFF and MRG_OFF >= ynT.hi, (MRG_OFF, ynT.hi, S5T_OFF)

            def cons_g(j, ps, cw):
                P.op("act", lambda e: e.activation(out=G[:, j, :], in_=ps[:, :], func=AF.Sigmoid,
                                                   bias=pc16[:, V_BGATE, j:j + 1]), r=[ps, pc16], w=[(G, j)])
            proj("w_in", 7200, 2048, 8, lambda c: (hT[:, c, :], (hT, c)), cons_g)

            def cons_q(j, ps, cw):
                P.op("dve", lambda e: e.tensor_tensor(out=Q[:, j, :], in0=ps[:, :], in1=G[:, j, :], op=ALU.mult),
                     r=[ps, (G, j)], w=[(Q, j)])
            proj("w_proj_s5", 0, 1024, 8, lambda c: (s5T[:, c, :], (s5T, c)), cons_q)

            def cons_m(j, ps, cw):
                g = gsb[st["g"] % 2]
                st["g"] += 1
                P.op("dve", lambda e: e.tensor_tensor(out=g.ap, in0=ps[:, :], in1=G[:, 8 + j, :], op=ALU.mult),
                     r=[ps, (G, 8 + j)], w=[g])
                P.op("pool", lambda e: e.tensor_tensor(out=mT[:, j, :], in0=g.ap, in1=Q[:, j, :], op=ALU.add),
                     r=[g, (Q, j)], w=[(mT, j)])
            proj("w_proj_ssd", 0, 1024, 16, lambda c: (ynT[:, c, :], (ynT, c)), cons_m)

            def cons_o(j, ps, cw):
                P.op("dve", lambda e: e.tensor_tensor(out=xres[:, j, :], in0=ps[:, :], in1=xres[:, j, :], op=ALU.add),
                     r=[ps, (xres, j)], w=[(xres, j)])
            proj("w_out", 0, 1024, 8, lambda c: (mT[:, c, :], (mT, c)), cons_o)
            return mT

        def store_out(t_own):
            ph = Phase()
            sqb = ph.alloc("sqb", [128, 8, TT], BF16)
            rmsnorm_to(xres, G_FIN, sqb)
            for s in range(4):
                ob = osb[0]
                for c0 in range(0, 8, 4):
                    ps = next_ps()
                    for c in range(c0, c0 + 4):
                        P.op("pe", lambda e, ps=ps, c=c, c0=c0, s=s: e.transpose(
                            out=ps[:, (c - c0) * 128:(c - c0 + 1) * 128], in_=xres[:, c, s * 128:(s + 1) * 128],
                            identity=cst[:, IDN, :]), r=[(xres, c), cst], w=[ps])
                    P.op("act", lambda e, ps=ps, c0=c0, ob=ob: e.activation(
                        out=ob[:, c0 * 128:(c0 + 4) * 128], in_=ps[:, :], func=AF.Copy), r=[ps], w=[ob])
                r0 = t_own * TT + s * 128
                tok = P.op("sp", lambda e, ob=ob, r0=r0: e.dma_start(out=out[r0:r0 + 128, :], in_=ob.ap),
                           r=[ob], w=["outdram"], dma=ob.key)
                P.final[ob.key] = tok

        import os as _os
        STOP = int(_os.environ.get("KSTOP", "9"))
        if STOP >= 1 or STOP == -1:
            s5_setup()
        for tile in range(NT_PRE + NT_OWN if STOP >= 0 else 0):
            own = tile >= NT_PRE
            light = LIGHT and not own
            if tile == NT_PRE:
                fl_ = cstv[:, FLAG:FLAG + 1]
                for b_ in (Sst[st["s"] % 2], Ssw[st["s"] % 2], hS, hSb, tail):
                    P.op("dve", lambda e, b_=b_: e.tensor_scalar(out=b_.ap, in0=b_.ap, scalar1=fl_, scalar2=None,
                                                                 op0=ALU.mult), r=[b_, cstv], w=[b_])
            load_x(tile)
            ffn("ffn1", G_FFN1)
            if tile == NT_PRE:
                dump("dbg_x1", xres, 8)
            s5T = ynT = None
            if STOP >= 2:
                ph = Phase()
                sqb = ph.alloc("sqb", [128, 8, TT], BF16)
                rmsnorm_to(hT, G_MIX, sqb)
                s5T = s5_tile(light)
            if STOP >= 3:
                ynT = ssd_tile(light)
            if own:
                if STOP >= 4:
                    if tile == NT_PRE:
                        dump("dbg_s5", s5T, 8)
                        dump("dbg_ssd", ynT, 16)
                    mT = merge_tile(s5T, ynT)
                    if tile == NT_PRE:
                        dump("dbg_m", mT, 8)
                    ffn("ffn2", G_FFN2)
                store_out(tile - NT_PRE)
        P.emit()
    return nc


_CACHE = {}


def _pc(v, n):
    return np.ascontiguousarray(np.asarray(v, np.float32).reshape(n, 128).T)


def prep_inputs(inputs):
    f = lambda k: np.asarray(inputs[k], dtype=np.float32)
    common = {}
    for n in WSHAPES:
        common[n] = np.ascontiguousarray(f(n)[0])
    pc8 = np.stack([_pc(f("ffn1_norm")[0], 8), _pc(f("mix_norm")[0], 8), _pc(f("ffn2_norm")[0], 8),
                    _pc(f("final_norm"), 8), _pc(f("s5_D")[0], 8), _pc(f("s5_b_glu")[0], 8)], axis=1)
    common["pc8"] = np.ascontiguousarray(pc8)
    pc16 = np.stack([_pc(f("ssd_norm")[0], 16), _pc(f("b_gate")[0], 16),
                     _pc(np.repeat(f("ssd_D")[0], 64), 16)], axis=1)
    common["pc16"] = np.ascontiguousarray(pc16)
    cw = f("conv_w")[0]
    convp = np.zeros((128, 32, 5), np.float32)
    for k in range(4):
        convp[:, :, k] = _pc(cw[k], 32)
    convp[:, :, 4] = _pc(f("conv_b")[0], 32)
    common["convp"] = convp
    common["hp"] = np.ascontiguousarray(np.stack([f("ssd_A_log")[0], f("ssd_dt_bias")[0]], axis=1))
    dup = lambda a: np.concatenate([a, a], axis=0)
    s5a = np.stack([dup(f("s5_A_re")[0].T), dup(f("s5_A_im")[0].T),
                    np.broadcast_to(f("s5_log_dt")[0][None, :], (128, 64))], axis=1)
    common["s5a"] = np.ascontiguousarray(s5a)
    common["s5b"] = np.ascontiguousarray(np.stack([dup(f("s5_B_re")[0].transpose(1, 0, 2)),
                                                   dup(f("s5_B_im")[0].transpose(1, 0, 2))], axis=1))
    common["s5c"] = np.ascontiguousarray(np.stack([dup(f("s5_C_re")[0].transpose(2, 0, 1)),
                                                   dup(f("s5_C_im")[0].transpose(2, 0, 1))], axis=1))
    i = np.arange(128)
    cst = np.zeros((128, 5, 128), np.float32)
    cst[:, 0, :] = np.eye(128)
    cst[:, 1, :] = (i[:, None] <= i[None, :])
    cst[:, 2, :] = (i[:, None] > i[None, :])
    cst[:, 3, :] = 1.0
    cst[:, 4, :] = ((i[None, :] // 16) >= (i[:, None] // 16))
    common["cst"] = cst
    return common


def kernel(**inputs):
    x = np.asarray(inputs["x"], dtype=np.float32)
    B, L, _ = x.shape
    half = L // 2
    if "nc" not in _CACHE:
        _CACHE["nc"] = build_program()
    nc = _CACHE["nc"]
    common = prep_inputs(inputs)
    in_maps = []
    for c in range(8):
        b, h = c // 2, c % 2
        xs = np.zeros((L, D), np.float32)
        if h == 0:
            xs[half:] = x[b, :half]
        else:
            xs[:] = x[b]
        m = dict(common)
        m["xs"] = xs
        cv = np.zeros((128, 4), np.float32)
        cv[:64, 0], cv[64:, 0] = -1.0, 1.0
        cv[:64, 1] = 1.0
        cv[64:, 2] = 1.0
        cv[:, 3] = float(h)
        m["cstv"] = cv
        in_maps.append(m)
    res = run_bass_kernel_spmd(nc, in_maps, core_ids=list(range(8)))
    _CACHE["res"] = res
    outp = np.zeros((B, L, D), np.float32)
    for c in range(8):
        b, h = c // 2, c % 2
        outp[b, h * half:(h + 1) * half] = np.asarray(res.results[c]["out"], dtype=np.float32)
    return outp
```

```python
import numpy as np
import ml_dtypes
from contextlib import ExitStack
import concourse.bass as bass
import concourse.mybir as mybir
from concourse.bass_utils import run_bass_kernel_spmd

F32 = mybir.dt.float32
BF16 = mybir.dt.bfloat16
I32 = mybir.dt.int32
U8 = mybir.dt.uint8
ALU = mybir.AluOpType
AF = mybir.ActivationFunctionType
DTSZ = {F32: 4, BF16: 2, I32: 4, U8: 1}

D = 1024
DFF = 2816
DIN = 9248
TT = 512
NT_PRE = 4
NT_OWN = 4
EPS = 1e-6
ARENA = 206 * 1024
LIGHT = True
DEBUG = False


def _prod(s):
    r = 1
    for v in s:
        r *= v
    return r


class Buf:
    def __init__(self, ap, key, space, lo, hi, shape, dt):
        self.ap, self.key, self.space, self.lo, self.hi, self.shape, self.dt = ap, key, space, lo, hi, shape, dt
        self.sub_bytes = (_prod(shape[2:]) * DTSZ[dt]) if len(shape) > 2 else None

    def __getitem__(self, idx):
        return self.ap[idx]


class Prog:
    ENG = ("pe", "act", "dve", "pool", "sp")
    EPOCH = 12000

    def __init__(self, nc, es):
        self.nc, self.es = nc, es
        self.ops = {e: [] for e in self.ENG}
        self.count = {e: 0 for e in self.ENG}
        self.esems = {e: [] for e in self.ENG}
        self.dsems = {}
        self.lastw = {}
        self.readers = {}
        self.spaces = {}
        self.waited = {e: {} for e in self.ENG}
        self.final = {}
        self.sig = {e: set() for e in self.ENG}
        self.arena = es.enter_context(nc.sbuf_tensor("arena", [128, ARENA], U8))
        self.top = 0
        self.nps = 0

    def view(self, name, off, shape, dt):
        nb = _prod(shape[1:]) * DTSZ[dt]
        assert off + nb <= ARENA, (name, off, nb)
        ap = self.arena[0:shape[0], off:off + nb].bitcast(dt)
        if len(shape) == 3:
            ap = ap.rearrange("p (a b) -> p a b", a=shape[1])
        elif len(shape) == 4:
            ap = ap.rearrange("p (a b c) -> p a b c", a=shape[1], b=shape[2])
        return Buf(ap, "%s@%d" % (name, off), "arena", off, off + nb, shape, dt)

    def alloc(self, name, shape, dt):
        b = self.view(name, self.top, shape, dt)
        self.top = (b.hi + 31) // 32 * 32
        return b

    def psum(self, name):
        t = self.es.enter_context(self.nc.psum_tensor(name, [128, 512], F32))
        return Buf(t[:, :], name, name, 0, 2048, [128, 512], F32)

    def sem(self, name):
        return self.es.enter_context(self.nc.semaphore(name))

    def _norm(self, it):
        if isinstance(it, Buf):
            return (it.key, it.space, it.lo, it.hi)
        if isinstance(it, tuple):
            b = it[0]
            i0 = it[1]
            i1 = it[2] if len(it) > 2 else i0 + 1
            return ((b.key, i0, i1), b.space, b.lo + i0 * b.sub_bytes, b.lo + i1 * b.sub_bytes)
        return (it, it, 0, 1)

    def _related(self, n):
        key, space, lo, hi = n
        sp = self.spaces.setdefault(space, {})
        if key not in sp:
            sp[key] = (lo, hi)
        return [k for k, (l, h) in sp.items() if l < hi and lo < h]

    def op(self, eng, fn, r=(), w=(), dma=None):
        rn = [self._norm(x) for x in r]
        wn = [self._norm(x) for x in w]
        deps = []
        for n in rn:
            for k in self._related(n):
                if k in self.lastw:
                    deps.append(self.lastw[k])
        for n in wn:
            for k in self._related(n):
                if k in self.lastw:
                    deps.append(self.lastw[k])
                deps.extend(self.readers.get(k, {}).values())
        waits = []
        for tok in deps:
            if tok[0] == "dma":
                _, sem, val = tok
                cur = self.waited[eng].get(id(sem), 0)
                if val > cur:
                    self.waited[eng][id(sem)] = val
                    waits.append(tok)
            else:
                _, seng, cidx = tok
                if seng == "pe" and eng == "pe" and dma is None:
                    continue
                cur = self.waited[eng].get(seng, -1)
                if cidx > cur:
                    self.waited[eng][seng] = cidx
                    waits.append(tok)
                    self.sig[seng].add(cidx)
        if dma is not None:
            if dma not in self.dsems:
                self.dsems[dma] = [self.sem("d_" + str(dma)), 0]
            ds = self.dsems[dma]
            ds[1] += 16
            tok = ("dma", ds[0], ds[1])
            self.ops[eng].append((waits, fn, ("dma", ds[0])))
            rkey = id(ds[0])
        else:
            c = self.count[eng]
            self.count[eng] = c + 1
            tok = ("c", eng, c)
            self.ops[eng].append((waits, fn, ("c", c)))
            rkey = eng
        for n in wn:
            self.lastw[n[0]] = tok
            self.readers[n[0]] = {}
        for n in rn:
            self.readers.setdefault(n[0], {})[rkey] = tok
        return tok

    def emit(self):
        semval = {}
        for eng in self.ENG:
            k = 0
            for c in range(self.count[eng]):
                if c in self.sig[eng]:
                    ep = k // self.EPOCH
                    while len(self.esems[eng]) <= ep:
                        self.esems[eng].append(self.sem("e_%s_%d" % (eng, len(self.esems[eng]))))
                    semval[(eng, c)] = (self.esems[eng][ep], k - ep * self.EPOCH + 1)
                    k += 1
        with self.nc.Block() as block:
            def mk(engname):
                def body(e):
                    for waits, fn, inc in self.ops[engname]:
                        for tok in waits:
                            if tok[0] == "dma":
                                e.wait_ge(tok[1], tok[2])
                            else:
                                s_, v_ = semval[(tok[1], tok[2])]
                                e.wait_ge(s_, v_)
                        ins = fn(e)
                        if inc[0] == "dma":
                            ins.then_inc(inc[1], 16)
                        elif (engname, inc[1]) in semval:
                            ins.then_inc(semval[(engname, inc[1])][0], 1)
                    if engname == "sp":
                        for tok in self.final.values():
                            e.wait_ge(tok[1], tok[2])
                return body
            block.tensor(mk("pe"))
            block.scalar(mk("act"))
            block.vector(mk("dve"))
            block.gpsimd(mk("pool"))
            block.sync(mk("sp"))


WSHAPES = {"ffn1_w_gate": (D, DFF), "ffn1_w_up": (D, DFF), "ffn1_w_down": (DFF, D),
           "w_in": (D, DIN), "s5_w_glu": (D, D), "w_proj_s5": (D, D), "w_proj_ssd": (2 * D, D),
           "w_out": (D, D),
           "ffn2_w_gate": (D, DFF), "ffn2_w_up": (D, DFF), "ffn2_w_down": (DFF, D)}
PSHAPES = {"pc8": [128, 6, 8], "pc16": [128, 3, 16], "convp": [128, 32, 5], "hp": [32, 2],
           "s5a": [128, 3, 64], "s5b": [128, 2, 64, 16], "s5c": [128, 2, 64, 16],
           "cst": [128, 5, 128], "cstv": [128, 4]}


def build_program():
    nc = bass.Bass("TRN2", target_bir_lowering=False)
    NTOK = (NT_PRE + NT_OWN) * TT
    din = {}
    for n, s in WSHAPES.items():
        din[n] = nc.dram_tensor(n, list(s), F32, kind="ExternalInput").ap()
    for n, s in PSHAPES.items():
        din[n] = nc.dram_tensor(n, list(s), F32, kind="ExternalInput").ap()
    xs = nc.dram_tensor("xs", [NTOK, D], F32, kind="ExternalInput").ap()
    out = nc.dram_tensor("out", [NT_OWN * TT, D], F32, kind="ExternalOutput").ap()
    SEGS = [("ffn1_w_gate", 0, DFF, 8, 512), ("ffn1_w_up", 0, DFF, 8, 512), ("ffn1_w_down", 0, D, 22, 128),
            ("w_in", 0, 1024, 8, 512), ("w_in", 3072, 4096, 8, 512), ("w_in", 7168, 32, 8, 128),
            ("s5_w_glu", 0, 1024, 8, 512), ("w_in", 1024, 2048, 8, 512), ("w_in", 7200, 2048, 8, 512),
            ("w_proj_s5", 0, 1024, 8, 512), ("w_proj_ssd", 0, 1024, 16, 256), ("w_out", 0, 1024, 8, 512),
            ("ffn2_w_gate", 0, DFF, 8, 512), ("ffn2_w_up", 0, DFF, 8, 512), ("ffn2_w_down", 0, D, 22, 128)]
    seg = {}
    for (wn_, c0_, nc_, kc_, pw_) in SEGS:
        npan_ = (nc_ + pw_ - 1) // pw_
        seg[(wn_, c0_)] = (nc.dram_tensor("%s_%d_t" % (wn_, c0_), [npan_, 128, kc_ * pw_], BF16, kind="Internal").ap(),
                           nc_, kc_, pw_)
    s5B = nc.dram_tensor("s5B_tab", [8, 128, 2, 8, 128], BF16, kind="Internal").ap()
    s5K = nc.dram_tensor("s5K_tab", [8, 128, 2, 8, 128], BF16, kind="Internal").ap()
    dbg = {}
    if DEBUG:
        for n, c in (("dbg_s5", 8), ("dbg_ssd", 16), ("dbg_m", 8), ("dbg_x1", 8), ("dbg_xs", 16), ("dbg_B", 8), ("dbg_C", 8), ("dbg_y", 16), ("dbg_dt", 1)):
            dbg[n] = nc.dram_tensor(n, [128, c, TT], F32, kind="ExternalOutput").ap()

    with ExitStack() as es:
        P = Prog(nc, es)
        psb = [P.psum("ps%d" % i) for i in range(8)]
        NROT = 7

        rot = {"n": NROT}

        def next_ps():
            P.nps = (P.nps + 1) % rot["n"]
            return psb[P.nps]

        def psbf(ps):
            return ps.ap.bitcast(BF16)

        cst = P.alloc("cst", [128, 5, 128], F32)
        IDN, MU, M1, ONE, KMASK = 0, 1, 2, 3, 4
        cstv = P.alloc("cstv", [128, 4], F32)
        SGN, SELT, SELB, FLAG = 0, 1, 2, 3
        pc8 = P.alloc("pc8", [128, 6, 8], F32)
        G_FFN1, G_MIX, G_FFN2, G_FIN, V_S5D, V_BGLU = range(6)
        pc16 = P.alloc("pc16", [128, 3, 16], F32)
        V_SSDN, V_BGATE, V_SSDD = range(3)
        convp = P.alloc("convp", [128, 32, 5], F32)
        hp = P.alloc("hp", [32, 2], F32)
        aneg = P.alloc("aneg", [32, 1], F32)
        identb = P.alloc("identb", [128, 128], BF16)
        onesb = P.alloc("onesb", [128, 128], BF16)
        mub = P.alloc("mub", [128, 128], BF16)
        m1b = P.alloc("m1b", [128, 128], BF16)
        hSb = P.alloc("hSb", [128, 32, 64], BF16)
        a8r = P.alloc("a8r", [128, 64], F32)
        a8s = P.alloc("a8s", [128, 64], F32)
        a8n = P.alloc("a8n", [128, 64], F32)
        Sst = [P.alloc("Sst%d" % i, [128, 64], F32) for i in range(2)]
        Ssw = [P.alloc("Ssw%d" % i, [128, 64], F32) for i in range(2)]
        hS = P.alloc("hS", [128, 32, 64], F32)
        tail = P.alloc("tail", [128, 32, 3], BF16)
        xres = P.alloc("xres", [128, 8, TT], F32)
        hT = P.alloc("hT", [128, 8, TT], BF16)
        rstd = P.alloc("rstd", [128, TT], F32)
        gsb = [P.alloc("gsb%d" % i, [128, TT], F32) for i in range(2)]
        NPAN = 3
        pans = [P.alloc("pan%d" % i, [128, 4096], BF16) for i in range(NPAN)]
        s5tab = P.alloc("s5tab", [128, 2, 8, 128], BF16)
        xin = P.alloc("xin", [128, D], F32)
        osb = [P.alloc("osb%d" % i, [128, D], F32) for i in range(1)]
        BASE = P.top
        st = {"pan": 0, "g": 0, "o": 0, "s": 0}
        S5T_OFF = ARENA - 8 * TT * 2
        MRG_OFF = BASE + 16 * TT * 2 + 64

        class Phase:
            def __init__(self, base=BASE):
                self.top = base

            def alloc(self, name, shape, dt):
                b = P.view(name, self.top, shape, dt)
                self.top = (b.hi + 31) // 32 * 32
                return b

        def load_params(items, semkey):
            for b, src in items:
                P.op("sp", lambda e, b=b, src=src: e.dma_start(out=b.ap, in_=src), w=[b], dma=semkey)
            sem, total = P.dsems[semkey]
            for b, _ in items:
                P.lastw[b.key] = ("dma", sem, total)

        load_params([(cst, din["cst"]), (cstv, din["cstv"]), (pc8, din["pc8"]), (pc16, din["pc16"]),
                     (convp, din["convp"]), (hp, din["hp"])], "par")
        P.op("dve", lambda e: e.tensor_copy(out=identb.ap, in_=cst[:, IDN, :]), r=[cst], w=[identb])
        P.op("pool", lambda e: e.memset(onesb.ap, 1.0), w=[onesb])
        P.op("dve", lambda e: e.tensor_copy(out=mub.ap, in_=cst[:, MU, :]), r=[cst], w=[mub])
        P.op("dve", lambda e: e.tensor_copy(out=m1b.ap, in_=cst[:, M1, :]), r=[cst], w=[m1b])
        P.op("pool", lambda e: e.memset(hSb.ap, 0.0), w=[hSb])
        P.op("pool", lambda e: e.memset(hS.ap, 0.0), w=[hS])
        P.op("pool", lambda e: e.memset(tail.ap, 0.0), w=[tail])
        P.op("pool", lambda e: e.memset(Sst[0].ap, 0.0), w=[Sst[0]])
        P.op("pool", lambda e: e.memset(Ssw[0].ap, 0.0), w=[Ssw[0]])
        P.op("act", lambda e: e.activation(out=aneg.ap, in_=hp[:, 0:1], func=AF.Exp), r=[hp], w=[aneg])
        P.op("dve", lambda e: e.tensor_scalar(out=aneg.ap, in0=aneg.ap, scalar1=-1.0, scalar2=None, op0=ALU.mult),
             r=[aneg], w=[aneg])

        for (wn_, c0_, nc_, kc_, pw_) in SEGS:
            sap = seg[(wn_, c0_)][0]
            for pn, p0 in enumerate(range(0, nc_, pw_)):
                w_ = min(pw_, nc_ - p0)
                P.op("pool", lambda e, sap=sap, pn=pn, wn_=wn_, c0_=c0_, p0=p0, w_=w_, kc_=kc_: e.dma_start(
                    out=sap[pn][:, 0:kc_ * w_].rearrange("p (c n) -> p c n", c=kc_),
                    in_=din[wn_][:, c0_ + p0:c0_ + p0 + w_].rearrange("(c p) n -> p c n", p=128)),
                    w=["%s_%d_t" % (wn_, c0_)], dma="cast_%s_%d" % (wn_, c0_))

        def get_pan():
            p = pans[st["pan"] % NPAN]
            st["pan"] += 1
            return p

        def proj(wname, col0, ncols, KC, rhs, consume, pw=None):
            sap, nc_, kc_, pw_ = seg[(wname, col0)]
            if pw is None:
                pw = min(512, (4096 // KC) // 128 * 128)
            assert (nc_, kc_, pw_) == (ncols, KC, pw), (wname, col0, nc_, kc_, pw_, ncols, KC, pw)
            skey = "%s_%d_t" % (wname, col0)
            j = 0
            for pn, p0 in enumerate(range(0, ncols, pw)):
                w_ = min(pw, ncols - p0)
                pan = get_pan()
                pv = pan.ap[:, 0:KC * w_].rearrange("p (c n) -> p c n", c=KC)
                P.op("sp", lambda e, pan=pan, pn=pn, w_=w_: e.dma_start(
                    out=pan.ap[:, 0:KC * w_], in_=sap[pn][:, 0:KC * w_]),
                    r=[skey], w=[pan], dma=pan.key)
                for q0 in range(0, w_, 128):
                    cw = min(128, w_ - q0)
                    ps = next_ps()
                    for c in range(KC):
                        rap, rdep = rhs(c)
                        P.op("pe", lambda e, pv=pv, ps=ps, c=c, q0=q0, cw=cw, rap=rap: e.matmul(
                            ps[0:cw, :], lhsT=pv[:, c, q0:q0 + cw], rhs=rap, start=(c == 0), stop=(c == KC - 1)),
                            r=[pan, rdep], w=[ps])
                    consume(j, ps, cw)
                    j += 1

        def rstd_from(ps, scale):
            P.op("dve", lambda e, ps=ps: e.tensor_scalar(out=rstd.ap, in0=ps[:, :], scalar1=scale, scalar2=EPS,
                                                         op0=ALU.mult, op1=ALU.add), r=[ps], w=[rstd])
            P.op("act", lambda e: e.activation(out=rstd.ap, in_=rstd.ap, func=AF.Ln), r=[rstd], w=[rstd])
            P.op("act", lambda e: e.activation(out=rstd.ap, in_=rstd.ap, func=AF.Exp, scale=-0.5), r=[rstd], w=[rstd])

        def rmsnorm_to(dst, gcol, sqbuf):
            for c in range(8):
                P.op("act", lambda e, c=c: e.activation(out=sqbuf[:, c, :], in_=xres[:, c, :], func=AF.Square),
                     r=[(xres, c)], w=[(sqbuf, c)])
            ps = next_ps()
            for c in range(8):
                P.op("pe", lambda e, c=c, ps=ps: e.matmul(ps[:, :], lhsT=onesb.ap, rhs=sqbuf[:, c, :],
                                                            start=(c == 0), stop=(c == 7)),
                     r=[onesb, (sqbuf, c)], w=[ps])
            rstd_from(ps, 1.0 / D)
            for c in range(8):
                P.op("dve", lambda e, c=c: e.scalar_tensor_tensor(
                    out=dst[:, c, :], in0=xres[:, c, :], scalar=pc8[:, gcol, c:c + 1], in1=rstd.ap,
                    op0=ALU.mult, op1=ALU.mult), r=[(xres, c), pc8, rstd], w=[(dst, c)])

        def load_x(tile):
            for s in range(4):
                src = xs[tile * TT + s * 128: tile * TT + (s + 1) * 128, :]
                P.op("sp", lambda e, src=src: e.dma_start(out=xin.ap, in_=src), w=[xin], dma="xin")
                for c0 in range(0, 8, 4):
                    ps = next_ps()
                    for c in range(c0, c0 + 4):
                        P.op("pe", lambda e, c=c, c0=c0, ps=ps: e.transpose(
                            out=ps[:, (c - c0) * 128:(c - c0 + 1) * 128], in_=xin[:, c * 128:(c + 1) * 128],
                            identity=cst[:, IDN, :]), r=[xin, cst], w=[ps])
                    P.op("act", lambda e, c0=c0, s=s, ps=ps: e.activation(
                        out=xres[:, c0:c0 + 4, s * 128:(s + 1) * 128],
                        in_=ps[:, :].rearrange("p (c t) -> p c t", c=4), func=AF.Copy),
                        r=[ps], w=[(xres, c0, c0 + 4)])

        def dump(name, buf, nch):
            if not DEBUG:
                return
            for c in range(nch):
                g = gsb[st["g"] % 2]
                st["g"] += 1
                P.op("dve", lambda e, g=g, c=c: e.tensor_copy(out=g.ap, in_=buf[:, c, :]), r=[(buf, c)], w=[g])
                tok = P.op("sp", lambda e, g=g, c=c: e.dma_start(out=dbg[name][:, c, :], in_=g.ap),
                           r=[g], w=[name], dma=g.key)
                P.final[g.key] = tok

        def ffn(pref, gcol):
            ph = Phase()
            act = ph.alloc("act", [128, 22, TT], BF16)
            sqb = ph.alloc("sqb", [128, 8, TT], BF16)
            rmsnorm_to(hT, gcol, sqb)
            for p0 in range(0, DFF, 512):
                pw = min(512, DFF - p0)
                pp = []
                for wn in (pref + "_w_gate", pref + "_w_up"):
                    pan = get_pan()
                    pv = pan.ap[:, 0:8 * pw].rearrange("p (c n) -> p c n", c=8)
                    P.op("sp", lambda e, pan=pan, wn=wn, p0=p0, pw=pw: e.dma_start(
                        out=pan.ap[:, 0:8 * pw], in_=seg[(wn, 0)][0][p0 // 512][:, 0:8 * pw]),
                        r=[wn + "_0_t"], w=[pan], dma=pan.key)
                    pp.append((pan, pv))
                for j in range(pw // 128):
                    oc = (p0 // 128) + j
                    psg, psu = next_ps(), next_ps()
                    for (pan, pv), ps in ((pp[0], psg), (pp[1], psu)):
                        for c in range(8):
                            P.op("pe", lambda e, pv=pv, ps=ps, c=c, j=j: e.matmul(
                                ps[:, :], lhsT=pv[:, c, j * 128:(j + 1) * 128], rhs=hT[:, c, :],
                                start=(c == 0), stop=(c == 7)), r=[pan, (hT, c)], w=[ps])
                    g = gsb[st["g"] % 2]
                    st["g"] += 1
                    P.op("act", lambda e, g=g, psg=psg: e.activation(out=g.ap, in_=psg[:, :], func=AF.Silu),
                         r=[psg], w=[g])
                    P.op("dve", lambda e, g=g, psu=psu, oc=oc: e.tensor_tensor(
                        out=act[:, oc, :], in0=psu[:, :], in1=g.ap, op=ALU.mult), r=[psu, g], w=[(act, oc)])

            def fin(j, ps, cw):
                P.op("dve", lambda e, ps=ps, j=j: e.scalar_tensor_tensor(
                    out=xres[:, j, :], in0=ps[:, :], scalar=0.5, in1=xres[:, j, :],
                    op0=ALU.mult, op1=ALU.add), r=[ps, (xres, j)], w=[(xres, j)])
            proj(pref + "_w_down", 0, D, 22, lambda c: (act[:, c, :], (act, c)), fin, pw=128)

        def s5_setup():
            ph = Phase()
            A = ph.alloc("s5a", [128, 3, 64], F32)
            Bi = ph.alloc("s5b", [128, 2, 64, 16], F32)
            Ci = ph.alloc("s5c", [128, 2, 64, 16], F32)
            load_params([(A, din["s5a"]), (Bi, din["s5b"]), (Ci, din["s5c"])], "par2")
            T = {}
            TMP = Buf(None, "s5tmp", "arena", BASE, ARENA, [128, ARENA - BASE], U8)
            RD = [A, Bi, Ci, TMP, cstv]

            def t(name):
                if name not in T:
                    T[name] = ph.alloc("t_" + name, [128, 64], F32).ap
                return T[name]

            def tt(o, a, b, op):
                P.op("dve", lambda e: e.tensor_tensor(out=o, in0=a, in1=b, op=op), r=RD, w=[TMP])

            def ts(o, a, s1, s2, op0, op1=None):
                if op1 is None:
                    P.op("dve", lambda e: e.tensor_scalar(out=o, in0=a, scalar1=s1, scalar2=None, op0=op0), r=RD, w=[TMP])
                else:
                    P.op("dve", lambda e: e.tensor_scalar(out=o, in0=a, scalar1=s1, scalar2=s2, op0=op0, op1=op1),
                         r=RD, w=[TMP])

            def stt(o, a, s, b, op0, op1):
                P.op("dve", lambda e: e.scalar_tensor_tensor(out=o, in0=a, scalar=s, in1=b, op0=op0, op1=op1),
                     r=RD, w=[TMP])

            def ac(o, a, func, scale=1.0):
                P.op("act", lambda e: e.activation(out=o, in_=a, func=func, scale=scale), r=RD, w=[TMP])

            def rcp(o, a):
                P.op("dve", lambda e: e.reciprocal(out=o, in_=a), r=RD, w=[TMP])

            import os as _os
            SL = int(_os.environ.get("S5STOP", "99"))
            lr, li, ldt = A[:, 0, :], A[:, 1, :], A[:, 2, :]
            ac(t("dt"), ldt, AF.Exp)
            tt(t("lrdt"), lr, t("dt"), ALU.mult)
            tt(t("th"), li, t("dt"), ALU.mult)
            ac(t("mag"), t("lrdt"), AF.Exp)
            ts(t("v"), t("th"), 1.0 / (2 * np.pi), None, ALU.mult)
            if SL <= 1:
                return
            vi = ph.alloc("t_vi", [128, 64], I32).ap
            P.op("dve", lambda e: e.tensor_copy(out=vi, in_=t("v")), r=RD, w=[TMP])
            P.op("dve", lambda e: e.tensor_copy(out=t("vf"), in_=vi), r=RD, w=[TMP])
            tt(t("fr"), t("v"), t("vf"), ALU.subtract)
            ts(t("g"), t("fr"), 0.5, None, ALU.is_gt)
            tt(t("fr"), t("fr"), t("g"), ALU.subtract)
            ts(t("g"), t("fr"), -0.5, None, ALU.is_lt)
            tt(t("fr"), t("fr"), t("g"), ALU.add)
            if SL <= 2:
                return
            ac(t("sin"), t("fr"), AF.Sin, scale=6.28318)
            ts(t("frc"), t("fr"), 0.25, None, ALU.add)
            ts(t("g"), t("frc"), 0.5, None, ALU.is_gt)
            tt(t("frc"), t("frc"), t("g"), ALU.subtract)
            ac(t("cos"), t("frc"), AF.Sin, scale=6.28318)
            if SL <= 3:
                return
            ar, ai = t("ar"), t("ai")
            tt(ar, t("mag"), t("cos"), ALU.mult)
            tt(ai, t("mag"), t("sin"), ALU.mult)
            tt(t("den"), lr, lr, ALU.mult)
            tt(t("d2"), li, li, ALU.mult)
            tt(t("den"), t("den"), t("d2"), ALU.add)
            rcp(t("rden"), t("den"))
            ts(t("am1"), ar, -1.0, None, ALU.add)
            tt(t("x1"), t("am1"), lr, ALU.mult)
            tt(t("x2"), ai, li, ALU.mult)
            tt(t("x1"), t("x1"), t("x2"), ALU.add)
            tt(t("cr"), t("x1"), t("rden"), ALU.mult)
            tt(t("x1"), ai, lr, ALU.mult)
            tt(t("x2"), t("am1"), li, ALU.mult)
            tt(t("x1"), t("x1"), t("x2"), ALU.subtract)
            tt(t("ci"), t("x1"), t("rden"), ALU.mult)
            pr = ph.alloc("t_pr", [128, 9, 64], F32)
            pi = ph.alloc("t_pi", [128, 9, 64], F32)
            P.op("dve", lambda e: e.memset(pr[:, 0, :], 1.0), r=RD, w=[TMP])
            P.op("dve", lambda e: e.memset(pi[:, 0, :], 0.0), r=RD, w=[TMP])

            def cmul(orr, oi, xr, xi, yr, yi):
                tt(t("m1"), xr, yr, ALU.mult)
                tt(t("m2"), xi, yi, ALU.mult)
                tt(t("m3"), xr, yi, ALU.mult)
                tt(t("m4"), xi, yr, ALU.mult)
                tt(orr, t("m1"), t("m2"), ALU.subtract)
                tt(oi, t("m3"), t("m4"), ALU.add)
            for k in range(8):
                cmul(pr[:, k + 1, :], pi[:, k + 1, :], pr[:, k, :], pi[:, k, :], ar, ai)
            tt(t("m1"), pr[:, 8, :], pr[:, 8, :], ALU.mult)
            tt(t("m2"), pi[:, 8, :], pi[:, 8, :], ALU.mult)
            tt(t("m1"), t("m1"), t("m2"), ALU.add)
            rcp(t("rm"), t("m1"))
            tt(t("ir"), pr[:, 8, :], t("rm"), ALU.mult)
            tt(t("ii"), pi[:, 8, :], t("rm"), ALU.mult)
            ts(t("ii"), t("ii"), -1.0, None, ALU.mult)
            P.op("dve", lambda e: e.tensor_copy(out=a8r.ap, in_=pr[:, 8, :]), r=RD, w=[a8r])
            P.op("dve", lambda e: e.tensor_scalar(out=a8s.ap, in0=pi[:, 8, :], scalar1=cstv[:, SGN:SGN + 1], scalar2=None,
                                                  op0=ALU.mult), r=RD, w=[a8s])
            P.op("dve", lambda e: e.tensor_scalar(out=a8n.ap, in0=a8s.ap, scalar1=-1.0, scalar2=None, op0=ALU.mult),
                 r=[a8s], w=[a8n])
            if SL <= 4:
                return
            big = lambda name: ph.alloc(name, [128, 64, 16], F32).ap
            bbr, bbi, BB, BBs, CC1, CC2, w1, w2 = [big(n) for n in ("bbr", "bbi", "BB", "BBs", "CC1", "CC2", "w1", "w2")]

            def bc(ap2d):
                return ap2d.unsqueeze(2).to_broadcast([128, 64, 16])
            Br, Bim, Cr, Cim = Bi[:, 0, :, :], Bi[:, 1, :, :], Ci[:, 0, :, :], Ci[:, 1, :, :]
            crb, cib = bc(t("cr")), bc(t("ci"))
            tt(w1, Br, crb, ALU.mult); tt(w2, Bim, cib, ALU.mult); tt(bbr, w1, w2, ALU.subtract)
            tt(w1, Bim, crb, ALU.mult); tt(w2, Br, cib, ALU.mult); tt(bbi, w1, w2, ALU.add)
            sT, sB = cstv[:, SELT:SELT + 1], cstv[:, SELB:SELB + 1]
            ts(BB, bbr, sT, None, ALU.mult); stt(BB, bbi, sB, BB, ALU.mult, ALU.add)
            ts(BBs, bbi, sT, -1.0, ALU.mult, ALU.mult); stt(BBs, bbr, sB, BBs, ALU.mult, ALU.add)
            ts(CC1, Cim, sB, -1.0, ALU.mult, ALU.mult); stt(CC1, Cr, sT, CC1, ALU.mult, ALU.add)
            ts(CC2, Cim, sT, -1.0, ALU.mult, ALU.mult); ts(w1, Cr, sB, -1.0, ALU.mult, ALU.mult)
            tt(CC2, CC2, w1, ALU.add)
            if SL <= 5:
                return
            NH = 16
            Bpw = ph.alloc("Bpw", [128, NH, 8, 16], F32)
            BJ = ph.alloc("BJ", [128, NH, 8, 16], F32)
            Cpw = ph.alloc("Cpw", [128, NH, 8, 16], F32)
            stB = ph.alloc("stB", [128, 2, 8, 128], BF16)
            stK = ph.alloc("stK", [128, 2, 8, 128], BF16)
            BJh = ph.alloc("BJh", [128, NH, 8, 16], BF16)
            BJl = ph.alloc("BJl", [128, NH, 8, 16], BF16)
            Cph = ph.alloc("Cph", [128, NH, 8, 16], BF16)
            Cpl = ph.alloc("Cpl", [128, NH, 8, 16], BF16)
            fl = lambda b, g: b[:, g, :, :].rearrange("p a b -> p (a b)")

            def bch(ap2d, gs):
                return ap2d[:, gs].unsqueeze(2).to_broadcast([128, NH, 16])
            for hf in range(64 // NH):
                gs = slice(hf * NH, (hf + 1) * NH)
                w1h, w2h = w1[:, 0:NH, :], w2[:, 0:NH, :]
                for j in range(8):
                    zr, zi = bch(pr[:, 7 - j, :], gs), bch(pi[:, 7 - j, :], gs)
                    tt(w1h, BB[:, gs, :], zr, ALU.mult); tt(w2h, BBs[:, gs, :], zi, ALU.mult)
                    tt(Bpw[:, :, j, :], w1h, w2h, ALU.add)
                    cmul(t("qr"), t("qi"), pr[:, 7 - j, :], pi[:, 7 - j, :], t("ir"), t("ii"))
                    zr, zi = bch(t("qr"), gs), bch(t("qi"), gs)
                    tt(w1h, BB[:, gs, :], zr, ALU.mult); tt(w2h, BBs[:, gs, :], zi, ALU.mult)
                    tt(BJ[:, :, j, :], w1h, w2h, ALU.add)
                    zr, zi = bch(pr[:, j + 1, :], gs), bch(pi[:, j + 1, :], gs)
                    tt(w1h, CC1[:, gs, :], zr, ALU.mult); tt(w2h, CC2[:, gs, :], zi, ALU.mult)
                    tt(Cpw[:, :, j, :], w1h, w2h, ALU.add)
                if SL <= 6:
                    continue
                for src_, hi_, lo_ in ((BJ, BJh, BJl), (Cpw, Cph, Cpl)):
                    P.op("dve", lambda e, src_=src_, hi_=hi_: e.tensor_copy(out=hi_.ap, in_=src_.ap), r=RD, w=[TMP])
                    P.op("dve", lambda e, src_=src_, hi_=hi_: e.tensor_tensor(out=src_.ap, in0=src_.ap, in1=hi_.ap,
                                                                           op=ALU.subtract), r=RD, w=[TMP])
                    P.op("dve", lambda e, src_=src_, lo_=lo_: e.tensor_copy(out=lo_.ap, in_=src_.ap), r=RD, w=[TMP])
                for q in range(hf * NH // 8, (hf + 1) * NH // 8):
                    for gl in range(8):
                        g = 8 * q + gl - hf * NH
                        ps = next_ps()
                        psk = next_ps()
                        P.op("pe", lambda e, g=g, ps=ps: e.transpose(out=ps[:, 0:128], in_=fl(Bpw, g), identity=cst[:, IDN, :]),
                             r=[TMP, cst], w=[ps])
                        for ii, (l_, r_) in enumerate(((BJh, Cph), (BJh, Cpl), (BJl, Cph))):
                            P.op("pe", lambda e, g=g, psk=psk, ii=ii, l_=l_, r_=r_: e.matmul(
                                psk[:, 0:128], lhsT=fl(l_, g), rhs=fl(r_, g), start=(ii == 0), stop=(ii == 2)),
                                r=[TMP], w=[psk])
                        P.op("act", lambda e, gl=gl, ps=ps: e.activation(out=stB[:, 0, gl, :], in_=ps[:, 0:128], func=AF.Copy),
                             r=[ps], w=[stB])
                        P.op("act", lambda e, gl=gl, ps=ps: e.activation(out=stB[:, 1, gl, 0:64], in_=ps[:, 64:128], func=AF.Copy),
                             r=[ps], w=[stB])
                        P.op("act", lambda e, gl=gl, ps=ps: e.activation(out=stB[:, 1, gl, 64:128], in_=ps[:, 0:64], func=AF.Copy),
                             r=[ps], w=[stB])
                        P.op("dve", lambda e, gl=gl, psk=psk: e.tensor_tensor(
                            out=stK[:, 0, gl, :], in0=psk[:, 0:128], in1=cst[:, KMASK, :], op=ALU.mult),
                            r=[psk, cst], w=[stK])
                        P.op("dve", lambda e, gl=gl, g=g: e.tensor_copy(out=stK[:, 1, gl, :], in_=fl(Cph, g)),
                             r=[TMP], w=[stK])
                    if int(_os.environ.get("S7", "9")) >= 3:
                        P.op("sp", lambda e, q=q: e.dma_start(out=s5B[q], in_=stB.ap), r=[stB], w=["s5B"], dma="s5B")
                        P.op("sp", lambda e, q=q: e.dma_start(out=s5K[q], in_=stK.ap), r=[stK], w=["s5K"], dma="s5K")
            print("setup phase top", ph.top, "BASE", BASE)

        def s5_tile(light):
            ph = Phase()
            uT = ph.alloc("uT", [128, 8, TT], BF16)
            U2 = ph.alloc("U2", [64, 64, 8, 16], BF16)
            Ucol = ph.alloc("Ucol", [128, 64, 64], BF16)
            cS = ph.alloc("cS", [128, 64, 64], F32)
            cW = ph.alloc("cW", [128, 64, 64], F32)
            Xprev = ph.alloc("Xprev", [128, 64, 64], BF16)
            Ycol = ph.alloc("Ycol", [128, 64, 64], BF16)
            ytmp = ph.alloc("ytmp", [128, TT], F32)
            gtmp = ph.alloc("gtmp", [128, TT], F32)
            r1 = [ph.alloc("r1_%d" % i, [128, 64], F32) for i in range(2)]
            r2 = [ph.alloc("r2_%d" % i, [128, 64], F32) for i in range(2)]

            def cons_u(j, ps, cw):
                P.op("act", lambda e: e.activation(out=uT[:, j, :], in_=ps[:, :], func=AF.Copy), r=[ps], w=[(uT, j)])
            proj("w_in", 0, 1024, 8, lambda c: (hT[:, c, :], (hT, c)), cons_u)
            for q in range(8):
                ps = next_ps()
                pb = psbf(ps)
                for j in range(8):
                    P.op("pe", lambda e, q=q, j=j, pb=pb: e.transpose(
                        out=pb[0:64, j * 128:(j + 1) * 128], in_=uT[:, q, j:TT:8], identity=identb.ap),
                        r=[(uT, q), identb], w=[ps])
                P.op("dve", lambda e, q=q, pb=pb: e.tensor_copy(
                    out=U2[:, 8 * q:8 * q + 8, :, :],
                    in_=pb[0:64, :].rearrange("p (j g m) -> p g j m", j=8, g=8)), r=[ps], w=[U2])
            for g0 in range(0, 64, 16):
                ps = next_ps()
                pb = psbf(ps)
                for g in range(g0, g0 + 16):
                    P.op("pe", lambda e, g=g, g0=g0, pb=pb: e.transpose(
                        out=pb[:, (g - g0) * 64:(g - g0 + 1) * 64], in_=U2[:, g, :, :].rearrange("p a b -> p (a b)"),
                        identity=identb[0:64, 0:64]), r=[U2, identb], w=[ps])
                P.op("act", lambda e, g0=g0, pb=pb: e.activation(
                    out=Ucol[:, g0:g0 + 16, :], in_=pb[:, :].rearrange("p (g b) -> p g b", g=16), func=AF.Copy),
                    r=[ps], w=[(Ucol, g0, g0 + 16)])
            for q in range(8):
                P.op("sp", lambda e, q=q: e.dma_start(out=s5tab.ap, in_=s5B[q]), r=["s5B"], w=[s5tab], dma="s5tab")
                for which, dst in ((0, cS), (1, cW)):
                    ps = next_ps()
                    for gl in range(8):
                        g = 8 * q + gl
                        P.op("pe", lambda e, which=which, gl=gl, g=g, ps=ps: e.matmul(
                            ps[:, gl * 64:(gl + 1) * 64], lhsT=s5tab[:, which, gl, :], rhs=Ucol[:, g, :],
                            start=True, stop=True), r=[s5tab, (Ucol, g)], w=[ps])
                    P.op("act", lambda e, q=q, dst=dst, ps=ps: e.activation(
                        out=dst[:, 8 * q:8 * q + 8, :], in_=ps[:, :].rearrange("p (g b) -> p g b", g=8), func=AF.Copy),
                        r=[ps], w=[(dst, 8 * q, 8 * q + 8)])
            for b in range(64):
                S0, W0 = Sst[st["s"] % 2], Ssw[st["s"] % 2]
                S1, W1 = Sst[(st["s"] + 1) % 2], Ssw[(st["s"] + 1) % 2]
                st["s"] += 1
                if not light:
                    P.op("act", lambda e, b=b, S0=S0: e.activation(out=Xprev[:, :, b], in_=S0.ap, func=AF.Copy),
                         r=[S0], w=[Xprev])
                A, Bq = r1[b % 2], r2[b % 2]
                P.op("dve", lambda e, A=A, S0=S0: e.tensor_tensor(out=A.ap, in0=S0.ap, in1=a8r.ap, op=ALU.mult),
                     r=[S0, a8r], w=[A])
                P.op("pool", lambda e, Bq=Bq, W0=W0: e.tensor_tensor(out=Bq.ap, in0=W0.ap, in1=a8r.ap, op=ALU.mult),
                     r=[W0, a8r], w=[Bq])
                P.op("dve", lambda e, W0=W0, S1=S1: e.tensor_tensor(out=S1.ap, in0=W0.ap, in1=a8s.ap, op=ALU.mult),
                     r=[W0, a8s], w=[S1])
                P.op("pool", lambda e, S0=S0, W1=W1: e.tensor_tensor(out=W1.ap, in0=S0.ap, in1=a8n.ap, op=ALU.mult),
                     r=[S0, a8n], w=[W1])
                P.op("dve", lambda e, A=A, S1=S1: e.tensor_tensor(out=S1.ap, in0=S1.ap, in1=A.ap, op=ALU.add),
                     r=[S1, A], w=[S1])
                P.op("dve", lambda e, Bq=Bq, W1=W1: e.tensor_tensor(out=W1.ap, in0=W1.ap, in1=Bq.ap, op=ALU.add),
                     r=[W1, Bq], w=[W1])
                P.op("dve", lambda e, b=b, S1=S1: e.tensor_tensor(out=S1.ap, in0=S1.ap, in1=cS[:, :, b], op=ALU.add),
                     r=[S1, cS], w=[S1])
                P.op("dve", lambda e, b=b, W1=W1: e.tensor_tensor(out=W1.ap, in0=W1.ap, in1=cW[:, :, b], op=ALU.add),
                     r=[W1, cW], w=[W1])
            if light:
                return None
            s5T = P.view("s5T", S5T_OFF, [128, 8, TT], BF16)
            gel = ph.alloc("gel", [128, 8, TT], BF16)
            assert ph.top <= S5T_OFF, ph.top
            for q in range(8):
                P.op("sp", lambda e, q=q: e.dma_start(out=s5tab.ap, in_=s5K[q]), r=["s5K"], w=[s5tab], dma="s5tab")
                ps = next_ps()
                for gl in range(8):
                    g = 8 * q + gl
                    P.op("pe", lambda e, gl=gl, g=g, ps=ps: e.matmul(
                        ps[:, gl * 64:(gl + 1) * 64], lhsT=s5tab[:, 0, gl, :], rhs=Ucol[:, g, :],
                        start=True, stop=False), r=[s5tab, (Ucol, g)], w=[ps])
                    P.op("pe", lambda e, gl=gl, g=g, ps=ps: e.matmul(
                        ps[:, gl * 64:(gl + 1) * 64], lhsT=s5tab[:, 1, gl, :], rhs=Xprev[:, g, :],
                        start=False, stop=True), r=[s5tab, Xprev], w=[ps])
                P.op("act", lambda e, q=q, ps=ps: e.activation(
                    out=Ycol[:, 8 * q:8 * q + 8, :], in_=ps[:, :].rearrange("p (g b) -> p g b", g=8), func=AF.Copy),
                    r=[ps], w=[(Ycol, 8 * q, 8 * q + 8)])
            Y2 = P.view("U2", U2.lo, [64, 8, 64, 16], BF16)
            for g0 in range(0, 64, 8):
                ps = next_ps()
                pb = psbf(ps)
                for g in range(g0, g0 + 8):
                    P.op("pe", lambda e, g=g, g0=g0, pb=pb: e.transpose(
                        out=pb[0:64, (g - g0) * 128:(g - g0 + 1) * 128], in_=Ycol[:, g, :], identity=identb.ap),
                        r=[(Ycol, g), identb], w=[ps])
                P.op("dve", lambda e, g0=g0, pb=pb: e.tensor_copy(
                    out=Y2[:, :, g0:g0 + 8, :],
                    in_=pb[0:64, :].rearrange("p (g j m) -> p j g m", g=8, j=8)), r=[ps], w=[Y2])
            for q in range(8):
                ps = next_ps()
                pb = psbf(ps)
                for j in range(8):
                    P.op("pe", lambda e, q=q, j=j, pb=pb: e.transpose(
                        out=pb[:, j * 64:(j + 1) * 64],
                        in_=Y2[:, j, 8 * q:8 * q + 8, :].rearrange("p a b -> p (a b)"),
                        identity=identb[0:64, 0:64]), r=[Y2, identb], w=[ps])
                P.op("dve", lambda e, q=q, pb=pb: e.scalar_tensor_tensor(
                    out=ytmp.ap.rearrange("p (b j) -> p j b", j=8),
                    in0=uT[:, q, :].rearrange("p (b j) -> p j b", j=8),
                    scalar=pc8[:, V_S5D, q:q + 1],
                    in1=pb[:, 0:512].rearrange("p (j b) -> p j b", j=8),
                    op0=ALU.mult, op1=ALU.add), r=[ps, (uT, q), pc8], w=[ytmp])
                P.op("pool", lambda e: e.tensor_tensor(out=gtmp.ap, in0=ytmp.ap, in1=ytmp.ap, op=ALU.mult),
                     r=[ytmp], w=[gtmp])
                P.op("pool", lambda e: e.tensor_scalar(out=gtmp.ap, in0=gtmp.ap, scalar1=0.044715, scalar2=1.0,
                                                       op0=ALU.mult, op1=ALU.add), r=[gtmp], w=[gtmp])
                P.op("pool", lambda e: e.tensor_tensor(out=gtmp.ap, in0=gtmp.ap, in1=ytmp.ap, op=ALU.mult),
                     r=[gtmp, ytmp], w=[gtmp])
                P.op("act", lambda e: e.activation(out=gtmp.ap, in_=gtmp.ap, func=AF.Sigmoid, scale=1.5957691216),
                     r=[gtmp], w=[gtmp])
                P.op("dve", lambda e, q=q: e.tensor_tensor(out=gel[:, q, :], in0=gtmp.ap, in1=ytmp.ap, op=ALU.mult),
                     r=[gtmp, ytmp], w=[(gel, q)])

            def cons_glu(j, ps, cw):
                g = gsb[st["g"] % 2]
                st["g"] += 1
                P.op("act", lambda e: e.activation(out=g.ap, in_=ps[:, :], func=AF.Sigmoid,
                                                   bias=pc8[:, V_BGLU, j:j + 1]), r=[ps, pc8], w=[g])
                P.op("dve", lambda e: e.tensor_tensor(out=s5T[:, j, :], in0=gel[:, j, :], in1=g.ap, op=ALU.mult),
                     r=[g, (gel, j)], w=[(s5T, j)])
            proj("s5_w_glu", 0, 1024, 8, lambda c: (gel[:, c, :], (gel, c)), cons_glu)
            return s5T

        def ssd_tile(light):
            ph = Phase()
            xbcp = ph.alloc("xbcp", [128, 32, TT + 3], BF16)
            ySS = P.view("ySS", xbcp.lo, [128, 16, TT], BF16)
            xsT = ph.alloc("xsT", [128, 16, TT], BF16)
            Bfm = ph.alloc("Bfm", [128, 8, TT], BF16)
            Cfm = ph.alloc("Cfm", [128, 8, TT], BF16)
            xtok = ph.alloc("xtok", [128, 2048], BF16)
            Btok = ph.alloc("Btok", [128, 1024], BF16)
            dtT = ph.alloc("dtT", [32, TT], F32)
            dAT = ph.alloc("dAT", [32, TT], F32)
            dtok = ph.alloc("dtok", [128, 4, 64], F32)
            dAh = ph.alloc("dAh", [128, 4, 32], BF16)
            dAl = ph.alloc("dAl", [128, 4, 32], BF16)
            dAr = ph.alloc("dAr", [128, 4, 32], F32)
            acs = ph.alloc("acs", [128, 64], F32)
            wgt = ph.alloc("wgt", [128, 32], F32)
            cdec = ph.alloc("cdec", [128, 32], F32)
            dg = [ph.alloc("dg%d" % i, [128, 4, 128], BF16) for i in range(4)]
            cbm = [ph.alloc("cbm%d" % i, [128, 128], F32) for i in range(2)]
            rhsM = [ph.alloc("rhsM%d" % i, [128, 2, 128], BF16) for i in range(4)]
            ed = [ph.alloc("ed%d" % i, [128, 256], F32) for i in range(4)]
            wTb = [ph.alloc("wT%d" % i, [128, 128], BF16) for i in range(4)]
            Csb = [ph.alloc("Cs%d" % i, [128, 128], BF16) for i in range(4)]
            xwb = [ph.alloc("xw%d" % i, [128, 256], BF16) for i in range(2)]
            htmp = [ph.alloc("htmp%d" % i, [128, 4, 64], F32) for i in range(2)]
            sqz = [ph.alloc("sqz%d" % i, [128, TT], BF16) for i in range(2)]
            assert ph.top <= S5T_OFF, ph.top
            cnt = {"h": 0, "g": 0}

            P.op("pool", lambda e: e.tensor_copy(out=xbcp[:, :, 0:3], in_=tail.ap), r=[tail], w=[xbcp])

            def cons_xbc(j, ps, cw):
                P.op("dve", lambda e: e.tensor_copy(out=xbcp[:, j, 3:TT + 3], in_=ps[:, :]),
                     r=[ps], w=[(xbcp, j)])
            proj("w_in", 3072, 4096, 8, lambda c: (hT[:, c, :], (hT, c)), cons_xbc)
            P.op("pool", lambda e: e.tensor_copy(out=tail.ap, in_=xbcp[:, :, TT:TT + 3]), r=[xbcp], w=[tail])
            for oc in range(32):
                d = dg[oc % 4]
                for k in range(4):
                    P.op("act", lambda e, d=d, k=k, oc=oc: e.activation(
                        out=d[:, k, :], in_=identb.ap, func=AF.Copy, scale=convp[:, oc, k:k + 1]),
                        r=[identb, convp], w=[(d, k)])
                ps = next_ps()
                for k in range(4):
                    P.op("pe", lambda e, d=d, k=k, oc=oc, ps=ps: e.matmul(
                        ps[:, :], lhsT=d[:, k, :], rhs=xbcp[:, oc, k:k + TT], start=(k == 0), stop=(k == 3)),
                        r=[(d, k), (xbcp, oc)], w=[ps])
                if oc < 16:
                    dst, dk = xsT[:, oc, :], (xsT, oc)
                elif oc < 24:
                    dst, dk = Bfm[:, oc - 16, :], (Bfm, oc - 16)
                else:
                    dst, dk = Cfm[:, oc - 24, :], (Cfm, oc - 24)
                P.op("act", lambda e, dst=dst, ps=ps, oc=oc: e.activation(
                    out=dst, in_=ps[:, :], func=AF.Silu, bias=convp[:, oc, 4:5]), r=[ps, convp], w=[dk])

            def cons_dt(j, ps, cw):
                P.op("act", lambda e: e.activation(out=dtT.ap, in_=ps[0:32, :], func=AF.Exp, bias=hp[:, 1:2]),
                     r=[ps, hp], w=[dtT])
                P.op("act", lambda e: e.activation(out=dtT.ap, in_=dtT.ap, func=AF.Ln, bias=1.0), r=[dtT], w=[dtT])
                P.op("dve", lambda e: e.tensor_scalar(out=dAT.ap, in0=dtT.ap, scalar1=aneg[:, 0:1], scalar2=None,
                                                      op0=ALU.mult), r=[dtT, aneg], w=[dAT])
            proj("w_in", 7168, 32, 8, lambda c: (hT[:, c, :], (hT, c)), cons_dt, pw=128)
            ps = next_ps()
            for s in range(4):
                P.op("pe", lambda e, s=s, ps=ps: e.transpose(
                    out=ps[:, s * 64:s * 64 + 32], in_=dtT[:, s * 128:(s + 1) * 128], identity=cst[0:32, IDN, 0:32]),
                    r=[dtT, cst], w=[ps])
                P.op("pe", lambda e, s=s, ps=ps: e.transpose(
                    out=ps[:, s * 64 + 32:s * 64 + 64], in_=dAT[:, s * 128:(s + 1) * 128], identity=cst[0:32, IDN, 0:32]),
                    r=[dAT, cst], w=[ps])
            P.op("dve", lambda e, ps=ps: e.tensor_copy(out=dtok.ap, in_=ps[:, 0:256].rearrange("p (s c) -> p s c", s=4)),
                 r=[ps], w=[dtok])
            P.op("dve", lambda e: e.tensor_copy(out=dAh.ap, in_=dtok[:, :, 32:64]), r=[dtok], w=[dAh])
            P.op("dve", lambda e: e.tensor_tensor(out=dAr.ap, in0=dtok[:, :, 32:64], in1=dAh.ap, op=ALU.subtract),
                 r=[dtok, dAh], w=[dAr])
            P.op("dve", lambda e: e.tensor_copy(out=dAl.ap, in_=dAr.ap), r=[dAr], w=[dAl])

            for s in range(4):
                tk = slice(s * 128, (s + 1) * 128)
                for c0 in range(0, 24, 8):
                    ps = next_ps()
                    pb = psbf(ps)
                    for c in range(c0, c0 + 8):
                        src, sk = (xsT[:, c, tk], (xsT, c)) if c < 16 else (Bfm[:, c - 16, tk], (Bfm, c - 16))
                        P.op("pe", lambda e, c=c, c0=c0, pb=pb, src=src: e.transpose(
                            out=pb[:, (c - c0) * 128:(c - c0 + 1) * 128], in_=src, identity=identb.ap),
                            r=[sk, identb], w=[ps])
                    if c0 < 16:
                        P.op("act", lambda e, c0=c0, pb=pb: e.activation(
                            out=xtok[:, c0 * 128:(c0 + 8) * 128], in_=pb[:, :], func=AF.Copy), r=[ps], w=[xtok])
                    else:
                        P.op("act", lambda e, pb=pb: e.activation(out=Btok.ap, in_=pb[:, :], func=AF.Copy),
                             r=[ps], w=[Btok])
                ps = next_ps()
                for ii, dd in enumerate((dAh, dAl)):
                    P.op("pe", lambda e, s=s, ps=ps, ii=ii, dd=dd: e.matmul(ps[:, 0:32], lhsT=mub.ap, rhs=dd[:, s, :],
                                                                            start=(ii == 0), stop=(ii == 1)), r=[mub, dd], w=[ps])
                for ii, dd in enumerate((dAh, dAl)):
                    P.op("pe", lambda e, s=s, ps=ps, ii=ii, dd=dd: e.matmul(ps[:, 32:64], lhsT=onesb.ap, rhs=dd[:, s, :],
                                                                            start=(ii == 0), stop=(ii == 1)), r=[onesb, dd], w=[ps])
                P.op("act", lambda e, ps=ps: e.activation(out=acs.ap, in_=ps[:, 0:64], func=AF.Copy), r=[ps], w=[acs])
                P.op("dve", lambda e: e.tensor_tensor(out=wgt.ap, in0=acs[:, 32:64], in1=acs[:, 0:32], op=ALU.subtract),
                     r=[acs], w=[wgt])
                P.op("act", lambda e: e.activation(out=wgt.ap, in_=wgt.ap, func=AF.Exp), r=[wgt], w=[wgt])
                P.op("dve", lambda e, s=s: e.tensor_tensor(out=wgt.ap, in0=wgt.ap, in1=dtok[:, s, 0:32], op=ALU.mult),
                     r=[wgt, dtok], w=[wgt])
                P.op("act", lambda e: e.activation(out=cdec.ap, in_=acs[:, 32:64], func=AF.Exp), r=[acs], w=[cdec])
                rot["n"] = 5
                LAG = 2
                hbuf = {}

                def G0(g, tk=tk):
                    cb = cbm[g % 2]
                    ps = next_ps()
                    P.op("pe", lambda e: e.matmul(ps[:, 0:128], lhsT=Bfm[:, g, tk], rhs=Cfm[:, g, tk],
                                                  start=True, stop=True), r=[(Bfm, g), (Cfm, g)], w=[ps])
                    P.op("dve", lambda e: e.tensor_tensor(out=cb.ap, in0=ps[:, 0:128], in1=cst[:, MU, :],
                                                          op=ALU.mult), r=[ps, cst], w=[cb])

                def A(h, s=s):
                    i = h % 4
                    rm, e_ = rhsM[i], ed[i]
                    for ii, dd in enumerate((dAh, dAl)):
                        P.op("dve", lambda e, ii=ii, dd=dd: e.tensor_scalar(
                            out=rm[:, ii, :], in0=mub.ap, scalar1=dd[:, s, h:h + 1], scalar2=None,
                            op0=ALU.mult), r=[mub, dd], w=[(rm, ii)])
                    ps2 = next_ps()
                    for ii in range(2):
                        P.op("pe", lambda e, ii=ii: e.matmul(
                            ps2[:, 0:128], lhsT=m1b.ap, rhs=rm[:, ii, :], start=(ii == 0), stop=(ii == 1)),
                            r=[m1b, (rm, ii)], w=[ps2])
                    for ii in range(2):
                        P.op("pe", lambda e, ii=ii: e.matmul(
                            ps2[:, 128:256], lhsT=onesb.ap, rhs=rm[:, ii, :], start=(ii == 0), stop=(ii == 1)),
                            r=[onesb, (rm, ii)], w=[ps2])
                    P.op("act", lambda e: e.activation(out=e_.ap, in_=ps2[:, 0:256], func=AF.Exp), r=[ps2], w=[e_])

                def B(h, s=s, tk=tk):
                    g, k = h // 4, h % 4
                    i = h % 4
                    e_, wt, cs = ed[i], wTb[i], Csb[i]
                    cb = cbm[g % 2]
                    psy = psb[5 + (g % 2)]
                    P.op("dve", lambda e: e.scalar_tensor_tensor(
                        out=wt.ap, in0=e_[:, 0:128], scalar=dtok[:, s, h:h + 1], in1=cb.ap,
                        op0=ALU.mult, op1=ALU.mult), r=[e_, cb, dtok], w=[wt])
                    P.op("pool", lambda e: e.tensor_tensor(
                        out=cs.ap, in0=Cfm[:, g, tk], in1=e_[:, 128:256], op=ALU.mult),
                        r=[(Cfm, g), e_], w=[cs])
                    po, co = 64 * (k % 2), 128 * (k // 2)
                    P.op("pe", lambda e: e.matmul(
                        psy[po:po + 64, co:co + 128], lhsT=xtok[:, h * 64:(h + 1) * 64], rhs=wt.ap,
                        start=True, stop=False), r=[xtok, wt], w=[psy])
                    P.op("pe", lambda e: e.matmul(
                        psy[po:po + 64, co:co + 128], lhsT=hSb[:, h, :], rhs=cs.ap,
                        start=False, stop=True), r=[(hSb, h), cs], w=[psy])

                def G1(g, tk=tk):
                    if not light:
                        psy = psb[5 + (g % 2)]
                        for cl in range(2):
                            cc = 2 * g + cl
                            P.op("dve", lambda e, cl=cl, cc=cc: e.scalar_tensor_tensor(
                                out=ySS[:, cc, tk], in0=xsT[:, cc, tk], scalar=pc16[:, V_SSDD, cc:cc + 1],
                                in1=psy[:, cl * 128:(cl + 1) * 128], op0=ALU.mult, op1=ALU.add),
                                r=[psy, (xsT, cc), pc16], w=[(ySS, cc)])
                    xw = xwb[g % 2]
                    ht = htmp[g % 2]
                    P.op("dve", lambda e: e.tensor_tensor(
                        out=xw.ap.rearrange("p (k d) -> p k d", k=4),
                        in0=xtok[:, g * 256:(g + 1) * 256].rearrange("p (k d) -> p k d", k=4),
                        in1=wgt[:, 4 * g:4 * g + 4].unsqueeze(2).to_broadcast([128, 4, 64]), op=ALU.mult),
                        r=[xtok, wgt], w=[xw])
                    ps3 = next_ps()
                    P.op("pe", lambda e: e.matmul(
                        ps3[:, 0:256], lhsT=Btok[:, g * 128:(g + 1) * 128], rhs=xw.ap, start=True, stop=True),
                        r=[Btok, xw], w=[ps3])
                    P.op("pool", lambda e: e.tensor_tensor(
                        out=ht.ap, in0=hS[:, 4 * g:4 * g + 4, :],
                        in1=cdec[:, 4 * g:4 * g + 4].unsqueeze(2).to_broadcast([128, 4, 64]), op=ALU.mult),
                        r=[(hS, 4 * g, 4 * g + 4), cdec], w=[ht])
                    P.op("dve", lambda e: e.tensor_tensor(
                        out=hS[:, 4 * g:4 * g + 4, :], in0=ps3[:, 0:256].rearrange("p (k d) -> p k d", k=4),
                        in1=ht.ap, op=ALU.add), r=[ps3, ht], w=[(hS, 4 * g, 4 * g + 4)])
                    P.op("act", lambda e: e.activation(out=hSb[:, 4 * g:4 * g + 4, :], in_=hS[:, 4 * g:4 * g + 4, :],
                                                       func=AF.Copy), r=[(hS, 4 * g, 4 * g + 4)], w=[(hSb, 4 * g, 4 * g + 4)])

                if light:
                    for g in range(8):
                        G1(g)
                else:
                    for idx in range(32 + LAG):
                        if idx < 32:
                            if idx % 4 == 0:
                                G0(idx // 4)
                            A(idx)
                        if idx >= LAG:
                            hb = idx - LAG
                            B(hb)
                            if hb % 4 == 3:
                                G1(hb // 4)
                rot["n"] = NROT
            if light:
                return None
            if DEBUG and st.get("dbg_ssd_done") is None:
                st["dbg_ssd_done"] = 1
                dump("dbg_xs", xsT, 16); dump("dbg_B", Bfm, 8); dump("dbg_C", Cfm, 8); dump("dbg_y", ySS, 16)
            pss = psb[7]

            def cons_z(j, ps, cw):
                g = gsb[st["g"] % 2]
                st["g"] += 1
                sq_ = sqz[j % 2]
                P.op("act", lambda e: e.activation(out=g.ap, in_=ps[:, :], func=AF.Silu), r=[ps], w=[g])
                P.op("dve", lambda e: e.tensor_tensor(out=ySS[:, j, :], in0=ySS[:, j, :], in1=g.ap, op=ALU.mult),
                     r=[(ySS, j), g], w=[(ySS, j)])
                P.op("act", lambda e: e.activation(out=sq_.ap, in_=ySS[:, j, :], func=AF.Square), r=[(ySS, j)], w=[sq_])
                P.op("pe", lambda e: e.matmul(pss[:, :], lhsT=onesb.ap, rhs=sq_.ap, start=(j == 0), stop=(j == 15)),
                     r=[onesb, sq_], w=[pss])
            proj("w_in", 1024, 2048, 8, lambda c: (hT[:, c, :], (hT, c)), cons_z)
            rstd_from(pss, 1.0 / 2048)
            for c in range(16):
                P.op("dve", lambda e, c=c: e.scalar_tensor_tensor(
                    out=ySS[:, c, :], in0=ySS[:, c, :], scalar=pc16[:, V_SSDN, c:c + 1], in1=rstd.ap,
                    op0=ALU.mult, op1=ALU.mult), r=[(ySS, c), pc16, rstd], w=[(ySS, c)])
            return ySS

        def merge_tile(s5T, ynT):
            G = P.view("G", MRG_OFF, [128, 16, TT], BF16)
            Q = P.view("Q", MRG_OFF + 16 * TT * 2, [128, 8, TT], BF16)
            mT = P.view("mT", MRG_OFF + 24 * TT * 2, [128, 8, TT], BF16)
            assert MRG_OFF + 32 * TT * 2 <= S5T_OFF and MRG_OFF >= ynT.hi, (MRG_OFF, ynT.hi, S5T_OFF)

            def cons_g(j, ps, cw):
                P.op("act", lambda e: e.activation(out=G[:, j, :], in_=ps[:, :], func=AF.Sigmoid,
                                                   bias=pc16[:, V_BGATE, j:j + 1]), r=[ps, pc16], w=[(G, j)])
            proj("w_in", 7200, 2048, 8, lambda c: (hT[:, c, :], (hT, c)), cons_g)

            def cons_q(j, ps, cw):
                P.op("dve", lambda e: e.tensor_tensor(out=Q[:, j, :], in0=ps[:, :], in1=G[:, j, :], op=ALU.mult),
                     r=[ps, (G, j)], w=[(Q, j)])
            proj("w_proj_s5", 0, 1024, 8, lambda c: (s5T[:, c, :], (s5T, c)), cons_q)

            def cons_m(j, ps, cw):
                g = gsb[st["g"] % 2]
                st["g"] += 1
                P.op("dve", lambda e: e.tensor_tensor(out=g.ap, in0=ps[:, :], in1=G[:, 8 + j, :], op=ALU.mult),
                     r=[ps, (G, 8 + j)], w=[g])
                P.op("pool", lambda e: e.tensor_tensor(out=mT[:, j, :], in0=g.ap, in1=Q[:, j, :], op=ALU.add),
                     r=[g, (Q, j)], w=[(mT, j)])
            proj("w_proj_ssd", 0, 1024, 16, lambda c: (ynT[:, c, :], (ynT, c)), cons_m)

            def cons_o(j, ps, cw):
                P.op("dve", lambda e: e.tensor_tensor(out=xres[:, j, :], in0=ps[:, :], in1=xres[:, j, :], op=ALU.add),
                     r=[ps, (xres, j)], w=[(xres, j)])
            proj("w_out", 0, 1024, 8, lambda c: (mT[:, c, :], (mT, c)), cons_o)
            return mT

        def store_out(t_own):
            ph = Phase()
            sqb = ph.alloc("sqb", [128, 8, TT], BF16)
            rmsnorm_to(xres, G_FIN, sqb)
            for s in range(4):
                ob = osb[0]
                for c0 in range(0, 8, 4):
                    ps = next_ps()
                    for c in range(c0, c0 + 4):
                        P.op("pe", lambda e, ps=ps, c=c, c0=c0, s=s: e.transpose(
                            out=ps[:, (c - c0) * 128:(c - c0 + 1) * 128], in_=xres[:, c, s * 128:(s + 1) * 128],
                            identity=cst[:, IDN, :]), r=[(xres, c), cst], w=[ps])
                    P.op("act", lambda e, ps=ps, c0=c0, ob=ob: e.activation(
                        out=ob[:, c0 * 128:(c0 + 4) * 128], in_=ps[:, :], func=AF.Copy), r=[ps], w=[ob])
                r0 = t_own * TT + s * 128
                tok = P.op("sp", lambda e, ob=ob, r0=r0: e.dma_start(out=out[r0:r0 + 128, :], in_=ob.ap),
                           r=[ob], w=["outdram"], dma=ob.key)
                P.final[ob.key] = tok

        import os as _os
        STOP = int(_os.environ.get("KSTOP", "9"))
        if STOP >= 1 or STOP == -1:
            s5_setup()
        for tile in range(NT_PRE + NT_OWN if STOP >= 0 else 0):
            own = tile >= NT_PRE
            light = LIGHT and not own
            if tile == NT_PRE:
                fl_ = cstv[:, FLAG:FLAG + 1]
                for b_ in (Sst[st["s"] % 2], Ssw[st["s"] % 2], hS, hSb, tail):
                    P.op("dve", lambda e, b_=b_: e.tensor_scalar(out=b_.ap, in0=b_.ap, scalar1=fl_, scalar2=None,
                                                                 op0=ALU.mult), r=[b_, cstv], w=[b_])
            load_x(tile)
            ffn("ffn1", G_FFN1)
            if tile == NT_PRE:
                dump("dbg_x1", xres, 8)
            s5T = ynT = None
            if STOP >= 2:
                ph = Phase()
                sqb = ph.alloc("sqb", [128, 8, TT], BF16)
                rmsnorm_to(hT, G_MIX, sqb)
                s5T = s5_tile(light)
            if STOP >= 3:
                ynT = ssd_tile(light)
            if own:
                if STOP >= 4:
                    if tile == NT_PRE:
                        dump("dbg_s5", s5T, 8)
                        dump("dbg_ssd", ynT, 16)
                    mT = merge_tile(s5T, ynT)
                    if tile == NT_PRE:
                        dump("dbg_m", mT, 8)
                    ffn("ffn2", G_FFN2)
                store_out(tile - NT_PRE)
        P.emit()
    return nc


_CACHE = {}


def _pc(v, n):
    return np.ascontiguousarray(np.asarray(v, np.float32).reshape(n, 128).T)


def prep_inputs(inputs):
    f = lambda k: np.asarray(inputs[k], dtype=np.float32)
    common = {}
    for n in WSHAPES:
        common[n] = np.ascontiguousarray(f(n)[0])
    pc8 = np.stack([_pc(f("ffn1_norm")[0], 8), _pc(f("mix_norm")[0], 8), _pc(f("ffn2_norm")[0], 8),
                    _pc(f("final_norm"), 8), _pc(f("s5_D")[0], 8), _pc(f("s5_b_glu")[0], 8)], axis=1)
    common["pc8"] = np.ascontiguousarray(pc8)
    pc16 = np.stack([_pc(f("ssd_norm")[0], 16), _pc(f("b_gate")[0], 16),
                     _pc(np.repeat(f("ssd_D")[0], 64), 16)], axis=1)
    common["pc16"] = np.ascontiguousarray(pc16)
    cw = f("conv_w")[0]
    convp = np.zeros((128, 32, 5), np.float32)
    for k in range(4):
        convp[:, :, k] = _pc(cw[k], 32)
    convp[:, :, 4] = _pc(f("conv_b")[0], 32)
    common["convp"] = convp
    common["hp"] = np.ascontiguousarray(np.stack([f("ssd_A_log")[0], f("ssd_dt_bias")[0]], axis=1))
    dup = lambda a: np.concatenate([a, a], axis=0)
    s5a = np.stack([dup(f("s5_A_re")[0].T), dup(f("s5_A_im")[0].T),
                    np.broadcast_to(f("s5_log_dt")[0][None, :], (128, 64))], axis=1)
    common["s5a"] = np.ascontiguousarray(s5a)
    common["s5b"] = np.ascontiguousarray(np.stack([dup(f("s5_B_re")[0].transpose(1, 0, 2)),
                                                   dup(f("s5_B_im")[0].transpose(1, 0, 2))], axis=1))
    common["s5c"] = np.ascontiguousarray(np.stack([dup(f("s5_C_re")[0].transpose(2, 0, 1)),
                                                   dup(f("s5_C_im")[0].transpose(2, 0, 1))], axis=1))
    i = np.arange(128)
    cst = np.zeros((128, 5, 128), np.float32)
    cst[:, 0, :] = np.eye(128)
    cst[:, 1, :] = (i[:, None] <= i[None, :])
    cst[:, 2, :] = (i[:, None] > i[None, :])
    cst[:, 3, :] = 1.0
    cst[:, 4, :] = ((i[None, :] // 16) >= (i[:, None] // 16))
    common["cst"] = cst
    return common


def kernel(**inputs):
    x = np.asarray(inputs["x"], dtype=np.float32)
    B, L, _ = x.shape
    half = L // 2
    if "nc" not in _CACHE:
        _CACHE["nc"] = build_program()
    nc = _CACHE["nc"]
    common = prep_inputs(inputs)
    in_maps = []
    for c in range(8):
        b, h = c // 2, c % 2
        xs = np.zeros((L, D), np.float32)
        if h == 0:
            xs[half:] = x[b, :half]
        else:
            xs[:] = x[b]
        m = dict(common)
        m["xs"] = xs
        cv = np.zeros((128, 4), np.float32)
        cv[:64, 0], cv[64:, 0] = -1.0, 1.0
        cv[:64, 1] = 1.0
        cv[64:, 2] = 1.0
        cv[:, 3] = float(h)
        m["cstv"] = cv
        in_maps.append(m)
    res = run_bass_kernel_spmd(nc, in_maps, core_ids=list(range(8)))
    _CACHE["res"] = res
    outp = np.zeros((B, L, D), np.float32)
    for c in range(8):
        b, h = c // 2, c % 2
        outp[b, h * half:(h + 1) * half] = np.asarray(res.results[c]["out"], dtype=np.float32)
    return outp
```
